# Optimizing a Trainium2 kernel written in Bass

```python
import math
import jax, jax.numpy as jnp
from jax import lax
import numpy as np

D_MODEL = 1024
BATCH = 8
SEQ = 4096
DEPTH = 1

PLE_DIM = 256
SSM_WIDTH = D_MODEL // 2
SSM_GROUP = 16
SSM_GROUPS = SSM_WIDTH // SSM_GROUP
SSM_STATE = 64
ATTN_WIDTH = D_MODEL - SSM_WIDTH
N_HEADS = 4
HEAD_DIM = 64
V_HEAD_DIM = 2 * HEAD_DIM
ROT_DIM = HEAD_DIM // 4
ROPE_THETA = 500000.0
Q_BLOCK = 128
EPS = 1e-6
IN_COLS = 2 * SSM_WIDTH + 4 * ATTN_WIDTH
DT_MIN = 0.001
DT_MAX = 0.1

kernel_name = "hymba_s5_diffattn_ple_block"


def rmsnorm(x, w):
    xf = x.astype(jnp.float32)
    y = xf * lax.rsqrt(jnp.mean(xf * xf, axis=-1, keepdims=True) + EPS)
    return (y * w.astype(jnp.float32)).astype(x.dtype)


def partial_rope(t, cos, sin):
    half = ROT_DIM // 2
    t1 = t[..., :half].astype(jnp.float32)
    t2 = t[..., half:ROT_DIM].astype(jnp.float32)
    rot = jnp.concatenate([t1 * cos - t2 * sin, t2 * cos + t1 * sin], axis=-1)
    return jnp.concatenate([rot.astype(t.dtype), t[..., ROT_DIM:]], axis=-1)


def s5_branch(u, lam_re, lam_im, log_dt, b_re, b_im, c_re, c_im, d_skip, glu_w, glu_b):
    f32 = jnp.float32
    bsz, s_len, _ = u.shape
    uf = u.astype(f32)
    ug = uf.reshape(bsz, s_len, SSM_GROUPS, SSM_GROUP)
    lr = lam_re.astype(f32)
    li = lam_im.astype(f32)
    dt = jnp.exp(log_dt.astype(f32))[:, None]
    mag = jnp.exp(lr * dt)
    ab_re = mag * jnp.cos(li * dt)
    ab_im = mag * jnp.sin(li * dt)
    nr = ab_re - 1.0
    ni = ab_im
    den = lr * lr + li * li
    coef_re = (nr * lr + ni * li) / den
    coef_im = (ni * lr - nr * li) / den
    br = b_re.astype(f32)
    bi = b_im.astype(f32)
    bb_re = coef_re[..., None] * br - coef_im[..., None] * bi
    bb_im = coef_re[..., None] * bi + coef_im[..., None] * br
    bu_re = jnp.einsum('bsgp,gnp->bsgn', ug, bb_re)
    bu_im = jnp.einsum('bsgp,gnp->bsgn', ug, bb_im)
    a_re = jnp.broadcast_to(ab_re, bu_re.shape)
    a_im = jnp.broadcast_to(ab_im, bu_im.shape)

    def combine(e1, e2):
        a1r, a1i, b1r, b1i = e1
        a2r, a2i, b2r, b2i = e2
        ar = a1r * a2r - a1i * a2i
        ai = a1r * a2i + a1i * a2r
        nbr = a2r * b1r - a2i * b1i + b2r
        nbi = a2r * b1i + a2i * b1r + b2i
        return (ar, ai, nbr, nbi)

    _, _, xr, xi = lax.associative_scan(combine, (a_re, a_im, bu_re, bu_im), axis=1)
    y = (jnp.einsum('bsgn,gpn->bsgp', xr, c_re.astype(f32))
         - jnp.einsum('bsgn,gpn->bsgp', xi, c_im.astype(f32)))
    y = y.reshape(bsz, s_len, SSM_WIDTH) + d_skip.astype(f32) * uf
    y = jax.nn.gelu(y)
    y = y * jax.nn.sigmoid(y @ glu_w.astype(f32) + glu_b.astype(f32))
    return y


def diff_attention(q, k, v, positions, q_norm_w, k_norm_w, lq1, lk1, lq2, lk2,
                   subln_w, lambda_init):
    f32 = jnp.float32
    bsz, s_len, _ = q.shape
    n_blocks = s_len // Q_BLOCK
    q = q.reshape(bsz, s_len, N_HEADS, 2, HEAD_DIM)
    k = k.reshape(bsz, s_len, N_HEADS, 2, HEAD_DIM)
    v = v.reshape(bsz, s_len, N_HEADS, V_HEAD_DIM)
    inv_freq = ROPE_THETA ** (-jnp.arange(0, ROT_DIM, 2, dtype=f32) / ROT_DIM)
    ang = positions.astype(f32)[..., None] * inv_freq
    cos = jnp.cos(ang)[:, :, None, None, :]
    sin = jnp.sin(ang)[:, :, None, None, :]
    q = partial_rope(rmsnorm(q, q_norm_w), cos, sin).astype(f32) * (HEAD_DIM ** -0.5)
    k = partial_rope(rmsnorm(k, k_norm_w), cos, sin).astype(f32)
    lam = (jnp.exp(jnp.sum(lq1.astype(f32) * lk1.astype(f32)))
           - jnp.exp(jnp.sum(lq2.astype(f32) * lk2.astype(f32))) + lambda_init)

    qb = q.reshape(bsz, n_blocks, Q_BLOCK, N_HEADS, 2, HEAD_DIM).transpose(1, 0, 3, 4, 2, 5)
    kt = k.transpose(0, 2, 3, 1, 4)
    vt = v.astype(f32).transpose(0, 2, 1, 3)
    key_idx = jnp.arange(s_len)
    starts = jnp.arange(n_blocks) * Q_BLOCK

    def block(args):
        qblk, start = args
        sc = jnp.einsum('bhcqd,bhckd->bhcqk', qblk, kt)
        q_idx = start + jnp.arange(Q_BLOCK)
        mask = key_idx[None, :] <= q_idx[:, None]
        sc = jnp.where(mask, sc, jnp.finfo(f32).min)
        pr = jax.nn.softmax(sc, axis=-1)
        w = pr[:, :, 0] - lam * pr[:, :, 1]
        return jnp.einsum('bhqk,bhkd->bhqd', w, vt)

    out = lax.map(block, (qb, starts))
    out = out.transpose(1, 0, 3, 2, 4).reshape(bsz, s_len, N_HEADS, V_HEAD_DIM)
    out = rmsnorm(out, subln_w) * (1.0 - lambda_init)
    return out.reshape(bsz, s_len, ATTN_WIDTH)


def setup_inputs(seed: int = 0) -> dict:
    key = jax.random.key(seed)
    ks = jax.random.split(key, 24)
    f32 = jnp.float32
    nrm = lambda k, shape, scale: (jax.random.normal(k, shape, f32) * scale)
    x = jax.random.normal(ks[0], (BATCH, SEQ, D_MODEL), f32)
    p = jax.random.normal(ks[1], (DEPTH, BATCH, SEQ, PLE_DIM), f32)
    positions = jnp.broadcast_to(jnp.arange(SEQ, dtype=jnp.int32)[None, :], (BATCH, SEQ))
    norm_w = 1.0 + nrm(ks[2], (DEPTH, D_MODEL), 0.02)
    w_in = nrm(ks[3], (DEPTH, D_MODEL, IN_COLS), D_MODEL ** -0.5)
    n_idx = jnp.arange(SSM_STATE, dtype=f32)
    ssm_lambda_re = -0.5 + nrm(ks[4], (DEPTH, SSM_GROUPS, SSM_STATE), 0.01)
    ssm_lambda_im = math.pi * n_idx + nrm(ks[5], (DEPTH, SSM_GROUPS, SSM_STATE), 0.01)
    ssm_log_dt = jax.random.uniform(ks[6], (DEPTH, SSM_GROUPS), f32,
                                    math.log(DT_MIN), math.log(DT_MAX))
    ssm_b_re = nrm(ks[7], (DEPTH, SSM_GROUPS, SSM_STATE, SSM_GROUP), (2 * SSM_GROUP) ** -0.5)
    ssm_b_im = nrm(ks[8], (DEPTH, SSM_GROUPS, SSM_STATE, SSM_GROUP), (2 * SSM_GROUP) ** -0.5)
    ssm_c_re = nrm(ks[9], (DEPTH, SSM_GROUPS, SSM_GROUP, SSM_STATE), SSM_STATE ** -0.5)
    ssm_c_im = nrm(ks[10], (DEPTH, SSM_GROUPS, SSM_GROUP, SSM_STATE), SSM_STATE ** -0.5)
    ssm_d = nrm(ks[11], (DEPTH, SSM_WIDTH), 1.0)
    glu_w = nrm(ks[12], (DEPTH, SSM_WIDTH, SSM_WIDTH), SSM_WIDTH ** -0.5)
    glu_b = nrm(ks[13], (DEPTH, SSM_WIDTH), 0.01)
    q_norm_w = 1.0 + nrm(ks[14], (DEPTH, HEAD_DIM), 0.02)
    k_norm_w = 1.0 + nrm(ks[15], (DEPTH, HEAD_DIM), 0.02)
    lambda_q1 = nrm(ks[16], (DEPTH, HEAD_DIM), 0.1)
    lambda_k1 = nrm(ks[17], (DEPTH, HEAD_DIM), 0.1)
    lambda_q2 = nrm(ks[18], (DEPTH, HEAD_DIM), 0.1)
    lambda_k2 = nrm(ks[19], (DEPTH, HEAD_DIM), 0.1)
    subln_w = 1.0 + nrm(ks[20], (DEPTH, V_HEAD_DIM), 0.02)
    w_out = nrm(ks[21], (DEPTH, SSM_WIDTH + ATTN_WIDTH, D_MODEL), (SSM_WIDTH + ATTN_WIDTH) ** -0.5)
    ple_w_proj = nrm(ks[22], (DEPTH, PLE_DIM, D_MODEL), 0.5 * PLE_DIM ** -0.5)
    ple_w_gate = nrm(ks[23], (DEPTH, D_MODEL, D_MODEL), D_MODEL ** -0.5)
    return {
        "x": x, "p": p, "positions": positions, "norm_w": norm_w, "w_in": w_in,
        "ssm_lambda_re": ssm_lambda_re, "ssm_lambda_im": ssm_lambda_im,
        "ssm_log_dt": ssm_log_dt, "ssm_b_re": ssm_b_re, "ssm_b_im": ssm_b_im,
        "ssm_c_re": ssm_c_re, "ssm_c_im": ssm_c_im, "ssm_d": ssm_d,
        "glu_w": glu_w, "glu_b": glu_b, "q_norm_w": q_norm_w, "k_norm_w": k_norm_w,
        "lambda_q1": lambda_q1, "lambda_k1": lambda_k1, "lambda_q2": lambda_q2,
        "lambda_k2": lambda_k2, "subln_w": subln_w, "w_out": w_out,
        "ple_w_proj": ple_w_proj, "ple_w_gate": ple_w_gate,
    }


def reference(x, p, positions, norm_w, w_in, ssm_lambda_re, ssm_lambda_im, ssm_log_dt,
              ssm_b_re, ssm_b_im, ssm_c_re, ssm_c_im, ssm_d, glu_w, glu_b,
              q_norm_w, k_norm_w, lambda_q1, lambda_k1, lambda_q2, lambda_k2,
              subln_w, w_out, ple_w_proj, ple_w_gate):
    splits = [SSM_WIDTH, 2 * SSM_WIDTH, 2 * SSM_WIDTH + ATTN_WIDTH,
              2 * SSM_WIDTH + 2 * ATTN_WIDTH, 2 * SSM_WIDTH + 3 * ATTN_WIDTH]
    for i in range(DEPTH):
        lambda_init = 0.8 - 0.6 * math.exp(-0.3 * i)
        h = rmsnorm(x, norm_w[i])
        proj = h @ w_in[i]
        u, z_s, q, k, v, z_a = jnp.split(proj, splits, axis=-1)
        y_s = s5_branch(u, ssm_lambda_re[i], ssm_lambda_im[i], ssm_log_dt[i],
                        ssm_b_re[i], ssm_b_im[i], ssm_c_re[i], ssm_c_im[i],
                        ssm_d[i], glu_w[i], glu_b[i])
        y_s = (y_s * jax.nn.silu(z_s.astype(jnp.float32))).astype(x.dtype)
        y_a = diff_attention(q, k, v, positions, q_norm_w[i], k_norm_w[i],
                             lambda_q1[i], lambda_k1[i], lambda_q2[i], lambda_k2[i],
                             subln_w[i], lambda_init)
        y_a = (y_a * jax.nn.silu(z_a.astype(jnp.float32))).astype(x.dtype)
        x = x + jnp.concatenate([y_s, y_a], axis=-1) @ w_out[i]
        gate = jax.nn.sigmoid((x @ ple_w_gate[i]).astype(jnp.float32)).astype(x.dtype)
        x = x + gate * (p[i] @ ple_w_proj[i])
    return x
```

```python
import math
import contextlib
import numpy as np
import concourse.bass as bass
import concourse.mybir as mybir
from concourse.bass_utils import run_bass_kernel_spmd

F32 = mybir.dt.float32
BF16 = mybir.dt.bfloat16
I32 = mybir.dt.int32
ALU = mybir.AluOpType
AF = mybir.ActivationFunctionType
AX = mybir.AxisListType

SAME_ENGINE_SYNC = True
N_DMA_SEMS = 48
DEBUG = False

S = 4096
D = 1024
NBLK = 8
MS = 2
L = 8 * MS
NG = 8 * MS + 7
NH = 8 * MS + 1
NE = NG + NH
CPB = 512 // L
SCB = 64
EPS = 1e-6
TWO_PI = 2.0 * math.pi
CW1 = 6.28125
CW2 = TWO_PI - 6.28125
LAMBDA_INIT = 0.8 - 0.6 * math.exp(0.0)


class _Op:
    __slots__ = ("eng", "fn", "deps", "is_dma", "sem", "semval", "signal", "signo", "idx")


class Prog:
    ENGS = ("pe", "act", "dve", "pool", "sp")

    def __init__(self, nc):
        self.nc = nc
        self.ops = []
        self.last_w = {}
        self.readers = {}
        self.dma_rr = 0
        self.dma_sem_total = [0] * N_DMA_SEMS
        self.dma_sem_lastop = [None] * N_DMA_SEMS
        self.bar_deps = []
        self.need_bar = {e: False for e in self.ENGS}
        self.last_eng_op = {}

    def barrier(self):
        deps = [o for o in self.last_eng_op.values()]
        deps += [o for o in self.dma_sem_lastop if o is not None]
        self.bar_deps = deps
        for e in self.ENGS:
            self.need_bar[e] = True

    def _add(self, eng, fn, R, W, is_dma):
        op = _Op()
        op.eng, op.fn, op.is_dma = eng, fn, is_dma
        op.signal = False
        op.signo = 0
        op.sem = None
        op.semval = 0
        op.idx = len(self.ops)
        deps = []
        if self.need_bar[eng]:
            deps += self.bar_deps
            self.need_bar[eng] = False
        for k in R:
            w = self.last_w.get(k)
            if w is not None:
                deps.append(w)
        for k in W:
            w = self.last_w.get(k)
            if w is not None:
                deps.append(w)
            for r in self.readers.get(k, ()):
                deps.append(r)
        if is_dma:
            s = self.dma_rr
            self.dma_rr = (self.dma_rr + 1) % N_DMA_SEMS
            prev = self.dma_sem_lastop[s]
            if prev is not None:
                deps.append(prev)
            self.dma_sem_total[s] += 16
            op.sem = s
            op.semval = self.dma_sem_total[s]
            self.dma_sem_lastop[s] = op
        seen = set()
        dd = []
        for d in deps:
            if d is op or id(d) in seen:
                continue
            seen.add(id(d))
            if (not d.is_dma) and d.eng == eng and (eng == "pe" or not SAME_ENGINE_SYNC):
                continue
            dd.append(d)
            if not d.is_dma:
                d.signal = True
        op.deps = dd
        for k in W:
            self.last_w[k] = op
            self.readers[k] = []
        for k in R:
            if k not in W:
                self.readers.setdefault(k, []).append(op)
        self.ops.append(op)
        if not is_dma:
            self.last_eng_op[eng] = op
        return op

    def op(self, eng, fn, R=(), W=()):
        return self._add(eng, fn, tuple(R), tuple(W), False)

    def dma(self, q, out, in_, R=(), W=()):
        return self._add(q, lambda e: e.dma_start(out=out, in_=in_), tuple(R), tuple(W), True)

    def emit(self, final_keys=()):
        nc = self.nc
        self._add("sp", None, tuple(final_keys), (), False)
        cnt = {e: 0 for e in self.ENGS}
        for o in self.ops:
            if (not o.is_dma) and o.signal:
                cnt[o.eng] += 1
                o.signo = cnt[o.eng]
        with contextlib.ExitStack() as st:
            esem = {e: st.enter_context(nc.semaphore("sem_" + e)) for e in self.ENGS}
            dsem = [st.enter_context(nc.semaphore("dsem%d" % i)) for i in range(N_DMA_SEMS)]
            block = st.enter_context(nc.Block())
            per = {e: [o for o in self.ops if o.eng == e] for e in self.ENGS}

            def replay(e, eng):
                waited = {}
                for o in per[e]:
                    for d in o.deps:
                        if d.is_dma:
                            key, val, sem = ("d", d.sem), d.semval, dsem[d.sem]
                        else:
                            key, val, sem = ("e", d.eng), d.signo, esem[d.eng]
                        if waited.get(key, 0) >= val:
                            continue
                        waited[key] = val
                        eng.wait_ge(sem, val)
                    if o.fn is None:
                        continue
                    ins = o.fn(eng)
                    if o.is_dma:
                        ins.then_inc(dsem[o.sem], 16)
                    elif o.signal:
                        ins.then_inc(esem[e], 1)

            @block.sync
            def _(eng):
                replay("sp", eng)

            @block.scalar
            def _(eng):
                replay("act", eng)

            @block.vector
            def _(eng):
                replay("dve", eng)

            @block.gpsimd
            def _(eng):
                replay("pool", eng)

            @block.tensor
            def _(eng):
                replay("pe", eng)


class Region:
    def __init__(self, arena, base, size):
        self.arena, self.base, self.size, self.cur = arena, base, size, 0

    def reset(self):
        self.cur = 0

    def alloc(self, shape, dt, parts=None):
        esz = 2 if dt == BF16 else 4
        n = 1
        for s_ in shape[1:]:
            n *= s_
        nbytes = (n * esz + 31) // 32 * 32
        off = self.base + self.cur
        self.cur += nbytes
        assert self.cur <= self.size, ("region overflow", self.cur, self.size)
        v = self.arena[0:shape[0], off // 4:(off + nbytes) // 4]
        if dt != F32:
            v = v.bitcast(dt)
        v = v[:, 0:n]
        if len(shape) == 3:
            v = v.rearrange("p (a b) -> p a b", a=shape[1])
        elif len(shape) == 4:
            v = v.rearrange("p (a b c) -> p a b c", a=shape[1], b=shape[2])
        return v


def build_program(debug=False):
    nc = bass.Bass("TRN2", target_bir_lowering=False)
    P = Prog(nc)

    def din(name, shape, dt=F32):
        return nc.dram_tensor(name, list(shape), dt, kind="ExternalInput").ap()

    x_d = din("x", [S, D])
    p_d = din("p", [S, 256])
    pos_d = din("pos", [32, 128], I32)
    vec_d = din("vecs", [17, 128])
    row_d = din("rows", [1, 512])
    win_d = din("w_in", [D, 3072])
    lam_d = din("lam", [16, 2, 128])
    ldt_d = din("log_dt", [16, 2])
    bre_d = din("b_re", [32, 64, 16])
    bim_d = din("b_im", [32, 64, 16])
    cre_d = din("c_re", [32, 16, 64])
    cim_d = din("c_im", [32, 16, 64])
    glu_d = din("glu_w", [512, 512])
    wout_d = din("w_out", [D, D])
    pp_d = din("ple_w_proj", [256, D])
    pg_d = din("ple_w_gate", [D, D])
    out_d = nc.dram_tensor("out", [S, D], F32, kind="ExternalOutput").ap()
    ys_s = nc.dram_tensor("ys_s", [4, 128, S], BF16, kind="Internal").ap()
    u_s = nc.dram_tensor("u_s", [4, 128, S], BF16, kind="Internal").ap()
    zs_s = nc.dram_tensor("zs_s", [4, 128, S], BF16, kind="Internal").ap()
    za_s = nc.dram_tensor("za_s", [S, 512], BF16, kind="Internal").ap()
    qT_s = nc.dram_tensor("qT_s", [4, 128, S], BF16, kind="Internal").ap()
    kT_s = nc.dram_tensor("kT_s", [4, 128, S], BF16, kind="Internal").ap()
    v_s = nc.dram_tensor("v_s", [S, 512], BF16, kind="Internal").ap()
    dbg = {}
    if debug:
        for nm in ("ys", "qT", "kT"):
            dbg[nm] = nc.dram_tensor("dbg_" + nm, [4, 128, S], BF16, kind="ExternalOutput").ap()
        dbg["v"] = nc.dram_tensor("dbg_v", [S, 512], BF16, kind="ExternalOutput").ap()
        dbg["za"] = nc.dram_tensor("dbg_za", [S, 512], BF16, kind="ExternalOutput").ap()

    with contextlib.ExitStack() as st:
        ARENA_BYTES = 212480
        arena = st.enter_context(nc.sbuf_tensor("arena", [128, ARENA_BYTES // 4], F32))
        PQ = [st.enter_context(nc.psum_tensor("pq%d" % i, [128, 1024], F32)) for i in range(4)]
        PS = [PQ[i // 2][:, 512 * (i % 2):512 * (i % 2) + 512] for i in range(8)]
        PSK = ["ps%d" % i for i in range(8)]

        def psb(i):
            return PS[i].bitcast(BF16)

        PER = Region(arena, 0, 59136)
        RA = Region(arena, 59136, 65536)
        RB = Region(arena, 124672, 33792)
        RC = Region(arena, 158464, ARENA_BYTES - 158464)
        ident_bf = PER.alloc([128, 128], BF16)
        ident_f = PER.alloc([128, 128], F32)
        ones_f = PER.alloc([128, 128], F32)
        ones_bf = PER.alloc([128, 128], BF16)
        Wsel = PER.alloc([128, 8, 240], BF16)
        maskT = PER.alloc([128, MS, 128], BF16)
        cmask = PER.alloc([128, 4, 512], BF16)
        glu_bf = PER.alloc([128, 4, 512], BF16)
        W2_base = PER.cur
        wout_bf = PER.alloc([128, 8, 1024], BF16)
        pg_bf = PER.alloc([128, 8, 1024], BF16)
        pp_bf = PER.alloc([128, 2, 1024], BF16)
        W2 = Region(arena, W2_base, PER.cur - W2_base)
        vecT = PER.alloc([128, 17], F32)
        bcq = PER.alloc([128, 512], F32)
        bck = PER.alloc([128, 512], F32)
        cosT = PER.alloc([128, 32, 8], F32)
        sinT = PER.alloc([128, 32, 8], F32)
        APr = PER.alloc([128, 6, 16], F32)
        APi = PER.alloc([128, 6, 16], F32)
        carry = PER.alloc([128, 2, 16], F32)
        lamv = PER.alloc([128, 4], F32)
        sw08 = PER.alloc([128, 1], F32)
        epsv = PER.alloc([128, 1], F32)
        bcsw = PER.alloc([128, 128], F32)
        u1 = PER.alloc([128, 2, 16], F32)
        rmag = PER.alloc([128, 16], F32)
        nd = vecT[:, 0:8]
        dsk = vecT[:, 8:12]
        glb = vecT[:, 12:16]
        win_bf = RA.alloc([128, 8, 3072], BF16)
        M1 = RA.alloc([128, 32, MS, 128], BF16)
        Hr = RB.alloc([128, 16, NH, 16], BF16)
        nHi = RB.alloc([128, 16, NH, 16], BF16)
        Tt = RB.alloc([128, 32, MS, 128], BF16)

        def mm(out, lhsT, rhs, start, stop, R, W, tp=None):
            if tp is None:
                P.op("pe", lambda e: e.matmul(out, lhsT=lhsT, rhs=rhs, start=start, stop=stop), R, W)
            else:
                P.op("pe", lambda e: e.matmul(out, lhsT=lhsT, rhs=rhs, start=start, stop=stop, tile_position=tp), R, W)

        def tr(out, in_, ident, R, W):
            P.op("pe", lambda e: e.transpose(out, in_, ident), R, W)

        def act(out, in_, func, R, W, bias=None, scale=None, accum=None):
            kw = {}
            if bias is not None:
                kw["bias"] = bias
            if scale is not None:
                kw["scale"] = scale
            if accum is not None:
                kw["accum_out"] = accum
            P.op("act", lambda e: e.activation(out, in_, func, **kw), R, W)

        def tt(eng, out, a, b, op, R, W):
            P.op(eng, lambda e: e.tensor_tensor(out, a, b, op), R, W)

        def ts(eng, out, a, s1, s2, op0, op1, R, W):
            if op1 is None:
                P.op(eng, lambda e: e.tensor_scalar(out, a, s1, None, op0), R, W)
            else:
                P.op(eng, lambda e: e.tensor_scalar(out, a, s1, s2, op0, op1), R, W)

        def stt(out, a, s, b, op0, op1, R, W):
            P.op("dve", lambda e: e.scalar_tensor_tensor(out, a, s, b, op0, op1), R, W)

        def cp(eng, out, in_, R, W):
            if eng == "act":
                act(out, in_, AF.Copy, R, W)
            else:
                P.op(eng, lambda e: e.tensor_copy(out, in_), R, W)

        def iota(out, pattern, base, cm, W):
            P.op("pool", lambda e: e.iota(out, pattern=pattern, base=base, channel_multiplier=cm), (), W)

        def memset(eng, out, val, W):
            P.op(eng, lambda e: e.memset(out, val), (), W)

        def bc3(ap2, shape, axis):
            return ap2.unsqueeze(axis).to_broadcast(shape)

        def sincos(x, q, qi, r, s_out, c_out, key):
            ts("dve", q, x, 1.0 / TWO_PI, None, ALU.mult, None, [key + "x"], [key + "q"])
            cp("dve", qi, q, [key + "q"], [key + "qi"])
            cp("dve", q, qi, [key + "qi"], [key + "q"])
            stt(r, q, -CW1, x, ALU.mult, ALU.add, [key + "q", key + "x"], [key + "r"])
            stt(r, q, -CW2, r, ALU.mult, ALU.add, [key + "q", key + "r"], [key + "r"])
            ts("dve", x, r, -math.pi, math.pi, ALU.max, ALU.min, [key + "r"], [key + "x"])
            act(s_out, x, AF.Sin, [key + "x"], [key + "s"])
            ts("dve", x, r, math.pi / 2, None, ALU.add, None, [key + "r", key + "s"], [key + "x"])
            ts("dve", q, x, math.pi, -TWO_PI, ALU.is_gt, ALU.mult, [key + "x"], [key + "q"])
            tt("dve", x, x, q, ALU.add, [key + "q", key + "x"], [key + "x"])
            ts("dve", x, x, -math.pi, math.pi, ALU.max, ALU.min, [key + "x"], [key + "x"])
            act(c_out, x, AF.Sin, [key + "x"], [key + "c"])

        RC.reset()
        W2.reset()
        R0 = Region(arena, RA.base, RA.size)
        io_i = R0.alloc([128, 1920], I32)
        io_f = R0.alloc([128, 1920], F32)
        msk_f = R0.alloc([128, 1920], F32)
        iota(io_i[:, 0:128], [[1, 128]], 0, -1, ["io_i"])
        ts("dve", ident_f, io_i[:, 0:128], 0.0, None, ALU.is_equal, None, ["io_i"], ["ident_f"])
        cp("dve", ident_bf, ident_f, ["ident_f"], ["ident_bf"])
        memset("dve", ones_f, 1.0, ["ones_f"])
        memset("dve", ones_bf, 1.0, ["ones_bf"])
        memset("dve", epsv, EPS, ["epsv"])
        iota(io_i[:, 0:1920].rearrange("p (a b) -> p a b", a=8), [[16, 8], [1, 240]], -112, -1, ["io_i"])
        ts("dve", io_f[:, 0:1920], io_i[:, 0:1920], 0.0, None, ALU.is_equal, None, ["io_i"], ["io_f"])
        iota(io_i[:, 0:1920].rearrange("p (a b) -> p a b", a=8), [[0, 8], [1, 240]], 0, 0, ["io_i"])
        ts("dve", msk_f[:, 0:1920], io_i[:, 0:1920], 112.0, None, ALU.is_ge, None, ["io_i"], ["msk_f"])
        tt("dve", io_f[:, 0:1920], io_f[:, 0:1920], msk_f[:, 0:1920], ALU.mult, ["io_f", "msk_f"], ["io_f"])
        ts("dve", msk_f[:, 0:1920], io_i[:, 0:1920], 127.0, None, ALU.is_le, None, ["io_i"], ["msk_f"])
        tt("dve", Wsel.rearrange("p a b -> p (a b)"), io_f[:, 0:1920], msk_f[:, 0:1920], ALU.mult,
           ["io_f", "msk_f"], ["Wsel"])
        iota(io_i[:, 0:128].rearrange("p (a b) -> p a b", a=8), [[16, 8], [0, 16]], 0, -1, ["io_i"])
        memset("dve", maskT.rearrange("p a b -> p (a b)"), 1.0, ["maskT"])
        ts("dve", maskT[:, 0, :], io_i[:, 0:128], -15.0, None, ALU.is_ge, None, ["io_i", "maskT"], ["maskT"])
        for j in range(4):
            iota(io_i[:, 0:512], [[1, 512]], -128 * j, -1, ["io_i"])
            ts("dve", cmask[:, j, :], io_i[:, 0:512], 0.0, None, ALU.is_ge, None, ["io_i", "cmask"], ["cmask"])

        vec16 = R0.alloc([17, 128], F32)
        rowv = R0.alloc([1, 512], F32)
        lam16 = R0.alloc([16, 3, 128], F32)
        ldt16 = R0.alloc([16, 2], F32)
        pos_i = R0.alloc([32, 128], I32)
        pos_f = R0.alloc([32, 128], F32)
        P.dma("sp", vec16, vec_d, [], ["vec16"])
        P.dma("sp", rowv, row_d, [], ["rowv"])
        P.dma("sp", lam16[:, 0:2, :], lam_d, [], ["lam16a"])
        P.dma("sp", ldt16, ldt_d, [], ["ldt16"])
        P.dma("sp", pos_i, pos_d, [], ["pos_i"])
        cp("dve", lam16[:, 2, :].rearrange("p (a b) -> p a b", a=2), bc3(ldt16, [16, 2, 64], 2),
           ["ldt16"], ["lam16b"])
        cp("dve", pos_f, pos_i, ["pos_i"], ["pos_f"])
        tr(PS[0][:, 0:17], vec16, ident_f[0:17, 0:17], ["vec16", "ident_f"], [PSK[0]])
        cp("dve", vecT, PS[0][:, 0:17], [PSK[0]], ["vecT"])
        par = R0.alloc([128, 3, 16], F32)
        for i in range(3):
            tr(PS[1][:, 16 * i:16 * i + 16], lam16[:, i, :], ident_f[0:16, 0:16],
               ["lam16a", "lam16b", "ident_f"], [PSK[1]])
        cp("dve", par.rearrange("p a b -> p (a b)"), PS[1][:, 0:48], [PSK[1]], ["par"])
        posT = R0.alloc([128, 32], F32)
        tr(PS[2][:, 0:32], pos_f, ident_f[0:32, 0:32], ["pos_f", "ident_f"], [PSK[2]])
        cp("dve", posT, PS[2][:, 0:32], [PSK[2]], ["posT"])
        bcr = R0.alloc([128, 512], F32)
        mm(PS[3][:, 0:512], ones_f[0:1, :], rowv, True, True, ["ones_f", "rowv"], [PSK[3]])
        cp("dve", bcr, PS[3][:, 0:512], [PSK[3]], ["bcr"])
        ts("dve", bcsw, bcr[:, 384:512], 1.0 - LAMBDA_INIT, None, ALU.mult, None, ["bcr"], ["bcsw"])
        ts("dve", bcq.rearrange("p (a b) -> p a b", a=8), bc3(bcr[:, 0:64], [128, 8, 64], 1),
           0.125, None, ALU.mult, None, ["bcr"], ["bcq"])
        cp("dve", bck.rearrange("p (a b) -> p a b", a=8), bc3(bcr[:, 64:128], [128, 8, 64], 1), ["bcr"], ["bck"])
        lsc = R0.alloc([128, 128], F32)
        lsum = R0.alloc([128, 2], F32)
        tt("dve", lsc[:, 0:64], bcr[:, 128:192], bcr[:, 192:256], ALU.mult, ["bcr"], ["lsc"])
        tt("dve", lsc[:, 64:128], bcr[:, 256:320], bcr[:, 320:384], ALU.mult, ["bcr", "lsc"], ["lsc"])
        P.op("dve", lambda e: e.tensor_reduce(lsum, lsc.rearrange("p (a b) -> p a b", a=2), AX.X, ALU.add),
             ["lsc"], ["lsum"])
        act(lsum, lsum, AF.Exp, ["lsum"], ["lsum"])
        tt("dve", lamv[:, 0:1], lsum[:, 0:1], lsum[:, 1:2], ALU.subtract, ["lsum"], ["lamv"])
        ts("dve", lamv[:, 0:1], lamv[:, 0:1], LAMBDA_INIT, None, ALU.add, None, ["lamv"], ["lamv"])
        ts("dve", lamv[:, 1:2], lamv[:, 0:1], -1.0, None, ALU.mult, None, ["lamv"], ["lamv"])
        ts("dve", sw08, vecT[:, 16:17], 1.0 - LAMBDA_INIT, None, ALU.mult, None, ["vecT"], ["sw08"])
        invf = R0.alloc([128, 8], F32)
        for i in range(8):
            memset("dve", invf[:, i:i + 1], float(np.float32(500000.0) ** np.float32(-(2.0 * i) / 16.0)), ["invf"])
        ang = R0.alloc([128, 256], F32)
        aq = R0.alloc([128, 256], F32)
        aqi = R0.alloc([128, 256], I32)
        ar_ = R0.alloc([128, 256], F32)
        tt("dve", ang.rearrange("p (a b) -> p a b", a=32), bc3(posT, [128, 32, 8], 2), bc3(invf, [128, 32, 8], 1),
           ALU.mult, ["posT", "invf"], ["angx"])
        sincos(ang, aq, aqi, ar_, sinT.rearrange("p a b -> p (a b)"), cosT.rearrange("p a b -> p (a b)"), "ang")

        NEt = 16 * NE
        kv_i = R0.alloc([128, NE], I32)
        kv = R0.alloc([128, NE], F32)
        iota(kv_i[:, 0:NG], [[-1, NG]], L - 1, 0, ["kv_i"])
        iota(kv_i[:, NG:NE], [[1, NH]], 0, 0, ["kv_i"])
        cp("dve", kv, kv_i, ["kv_i"], ["kv"])
        dtv = R0.alloc([128, 16], F32)
        act(dtv, par[:, 2, :], AF.Exp, ["par"], ["dtv"])
        lrdt = R0.alloc([128, 16], F32)
        thv = R0.alloc([128, 16], F32)
        tt("dve", lrdt, par[:, 0, :], dtv, ALU.mult, ["par", "dtv"], ["lrdt"])
        tt("dve", thv, par[:, 1, :], dtv, ALU.mult, ["par", "dtv"], ["thv"])
        Emag = R0.alloc([128, 16, NE], F32)
        Eph = R0.alloc([128, 16, NE], F32)
        Eq = R0.alloc([128, 16, NE], F32)
        Eqi = R0.alloc([128, 16, NE], I32)
        Er = R0.alloc([128, 16, NE], F32)
        Ei = R0.alloc([128, 16, NE], F32)
        Ert = R0.alloc([128, 16, NE], F32)
        shp = [128, 16, NE]
        tt("dve", Emag, bc3(lrdt, shp, 2), bc3(kv, shp, 1), ALU.mult, ["lrdt", "kv"], ["Emag"])
        act(Emag, Emag, AF.Exp, ["Emag"], ["Emag"])
        tt("dve", Eph, bc3(thv, shp, 2), bc3(kv, shp, 1), ALU.mult, ["thv", "kv"], ["Ephx"])
        f2 = lambda a: a.rearrange("p a b -> p (a b)")
        sincos(f2(Eph), f2(Eq), f2(Eqi), f2(Ert), f2(Ei), f2(Er), "Eph")
        tt("dve", f2(Er), f2(Er), f2(Emag), ALU.mult, ["Ephc", "Emag"], ["Er"])
        tt("dve", f2(Ei), f2(Ei), f2(Emag), ALU.mult, ["Ephs", "Emag"], ["Ei"])
        c_nr = R0.alloc([128, 16], F32)
        c_den = R0.alloc([128, 16], F32)
        c_t = R0.alloc([128, 16], F32)
        c_r = R0.alloc([128, 16], F32)
        c_i = R0.alloc([128, 16], F32)
        lr, li = par[:, 0, :], par[:, 1, :]
        ni = Ei[:, :, NG + 1]
        ts("dve", c_nr, Er[:, :, NG + 1], -1.0, None, ALU.add, None, ["Er"], ["c_nr"])
        tt("dve", c_den, lr, lr, ALU.mult, ["par"], ["c_den"])
        tt("dve", c_t, li, li, ALU.mult, ["par"], ["c_t"])
        tt("dve", c_den, c_den, c_t, ALU.add, ["c_den", "c_t"], ["c_den"])
        P.op("dve", lambda e: e.reciprocal(c_den, c_den), ["c_den"], ["c_den"])
        tt("dve", c_r, c_nr, lr, ALU.mult, ["c_nr", "par"], ["c_r"])
        tt("dve", c_t, ni, li, ALU.mult, ["Ei", "par", "c_t"], ["c_t"])
        tt("dve", c_r, c_r, c_t, ALU.add, ["c_r", "c_t"], ["c_r"])
        tt("dve", c_r, c_r, c_den, ALU.mult, ["c_r", "c_den"], ["c_r"])
        tt("dve", c_i, ni, lr, ALU.mult, ["Ei", "par"], ["c_i"])
        tt("dve", c_t, c_nr, li, ALU.mult, ["c_nr", "par", "c_t"], ["c_t"])
        tt("dve", c_i, c_i, c_t, ALU.subtract, ["c_i", "c_t"], ["c_i"])
        tt("dve", c_i, c_i, c_den, ALU.mult, ["c_i", "c_den"], ["c_i"])
        cp("dve", APr[:, 0, :], Er[:, :, NG + L], ["Er"], ["APr"])
        cp("dve", APi[:, 0, :], Ei[:, :, NG + L], ["Ei"], ["APi"])
        sq1 = R0.alloc([128, 16], F32)
        sq2 = R0.alloc([128, 16], F32)
        for d_ in range(1, 6):
            tt("dve", sq1, APr[:, d_ - 1, :], APr[:, d_ - 1, :], ALU.mult, ["APr", "sq1"], ["sq1"])
            tt("dve", sq2, APi[:, d_ - 1, :], APi[:, d_ - 1, :], ALU.mult, ["APi", "sq2"], ["sq2"])
            tt("dve", APr[:, d_, :], sq1, sq2, ALU.subtract, ["sq1", "sq2", "APr"], ["APr"])
            tt("dve", sq1, APr[:, d_ - 1, :], APi[:, d_ - 1, :], ALU.mult, ["APr", "APi", "sq1"], ["sq1"])
            ts("dve", APi[:, d_, :], sq1, 2.0, None, ALU.mult, None, ["sq1", "APi"], ["APi"])
        cp("dve", rmag, Emag[:, :, NG + L], ["Emag"], ["rmag"])
        P.op("dve", lambda e: e.reciprocal(sq1, rmag), ["rmag", "sq1"], ["sq1"])
        tt("dve", u1[:, 0, :], APr[:, 0, :], sq1, ALU.mult, ["APr", "sq1"], ["u1"])
        tt("dve", u1[:, 1, :], APi[:, 0, :], sq1, ALU.mult, ["APi", "sq1", "u1"], ["u1"])
        Bre = R0.alloc([128, 16, 16], F32)
        Bim = R0.alloc([128, 16, 16], F32)
        bbr = R0.alloc([128, 16, 16], F32)
        bbi = R0.alloc([128, 16, 16], F32)
        bt = R0.alloc([128, 16, 16], F32)
        P.dma("sp", Bre, bre_d.rearrange("(gp g2) n q -> (g2 n) gp q", g2=2), [], ["Bre"])
        P.dma("sp", Bim, bim_d.rearrange("(gp g2) n q -> (g2 n) gp q", g2=2), [], ["Bim"])
        s3 = [128, 16, 16]
        tt("dve", bbr, Bre, bc3(c_r, s3, 2), ALU.mult, ["Bre", "c_r"], ["bbr"])
        tt("dve", bt, Bim, bc3(c_i, s3, 2), ALU.mult, ["Bim", "c_i"], ["bt"])
        tt("dve", bbr, bbr, bt, ALU.subtract, ["bbr", "bt"], ["bbr"])
        tt("dve", bbi, Bim, bc3(c_r, s3, 2), ALU.mult, ["Bim", "c_r"], ["bbi"])
        tt("dve", bt, Bre, bc3(c_i, s3, 2), ALU.mult, ["Bre", "c_i", "bt"], ["bt"])
        tt("dve", bbi, bbi, bt, ALU.add, ["bbi", "bt"], ["bbi"])
        Xc = R0.alloc([128, 4, 128], F32)
        Ctr = R0.alloc([128, 16, 16], F32)
        Cti = R0.alloc([128, 16, 16], F32)
        for ri, cd in enumerate((cre_d, cim_d)):
            for hf in range(2):
                for gpl in range(8):
                    for g2 in range(2):
                        g = 2 * (8 * hf + gpl) + g2
                        P.dma("sp" if (gpl % 2 == 0) else "act", Xc[16 * gpl:16 * gpl + 16, 2 * ri + hf, 64 * g2:64 * g2 + 64],
                              cd[g], [], [("Xc", ri, hf, gpl, g2)])
                tr(PS[4 + 2 * ri + hf][:, 0:128], Xc[:, 2 * ri + hf, :], ident_f,
                   [("Xc", ri, hf, a, b) for a in range(8) for b in range(2)] + ["ident_f"], [PSK[4 + 2 * ri + hf]])
                dst = (Ctr, Cti)[ri]
                cp("dve", dst[:, 8 * hf:8 * hf + 8, :].rearrange("p a b -> p (a b)"), PS[4 + 2 * ri + hf][:, 0:128],
                   [PSK[4 + 2 * ri + hf]], [("Ct", ri, hf)])
        CtK = [("Ct", ri, hf) for ri in range(2) for hf in range(2)]
        Gr = RC.alloc([128, 16, NG, 16], BF16)
        Gi = RC.alloc([128, 16, NG, 16], BF16)
        GC = 2
        g1 = W2.alloc([128, GC, NG, 16], F32)
        g2t = W2.alloc([128, GC, NG, 16], F32)
        g3 = W2.alloc([128, GC, NG, 16], F32)
        g4 = W2.alloc([128, GC, NG, 16], F32)
        for c0 in range(0, 16, GC):
            sl = slice(c0, c0 + GC)
            for (E0, n0, nn, Xr_, Xi_, Or_, Oi_, neg, kx) in (
                    (0, 0, NG, bbr, bbi, Gr, Gi, False, ["bbr", "bbi"]),
                    (NG, 0, NH, Ctr, Cti, Hr, nHi, True, CtK)):
                shp4 = [128, GC, nn, 16]
                er = Er[:, sl, E0:E0 + nn].unsqueeze(3).to_broadcast(shp4)
                ei = Ei[:, sl, E0:E0 + nn].unsqueeze(3).to_broadcast(shp4)
                xr = Xr_[:, sl, :].unsqueeze(2).to_broadcast(shp4)
                xi = Xi_[:, sl, :].unsqueeze(2).to_broadcast(shp4)
                a1, a2 = g1[:, :, 0:nn, :], g2t[:, :, 0:nn, :]
                a3, a4 = g3[:, :, 0:nn, :], g4[:, :, 0:nn, :]
                tt("dve", a1, er, xr, ALU.mult, ["Er"] + kx + ["g1"], ["g1"])
                tt("dve", a2, ei, xi, ALU.mult, ["Ei"] + kx + ["g2"], ["g2"])
                tt("dve", Or_[:, sl, :, :], a1, a2, ALU.subtract, ["g1", "g2"], [("GH", E0, c0, 0)])
                tt("pool", a3, er, xi, ALU.mult, ["Er"] + kx + ["g3"], ["g3"])
                tt("pool", a4, ei, xr, ALU.mult, ["Ei"] + kx + ["g4"], ["g4"])
                if neg:
                    fl = lambda a: a.rearrange("p a b c -> p a (b c)")
                    stt(fl(Oi_[:, sl, :, :]), fl(a3), -1.0, fl(a4), ALU.mult, ALU.subtract, ["g3", "g4"], [("GH", E0, c0, 1)])
                else:
                    tt("pool", Oi_[:, sl, :, :], a3, a4, ALU.add, ["g3", "g4"], [("GH", E0, c0, 1)])
        GK = [("GH", 0, c0, i) for c0 in range(0, 16, GC) for i in range(2)]
        HK = [("GH", NG, c0, i) for c0 in range(0, 16, GC) for i in range(2)]
        P.barrier()
        for g in range(32):
            gp, hf = g // 2, g % 2
            hs = slice(64 * hf, 64 * hf + 64)
            pb = 4 + (g % 2)
            for dl in range(MS):
                r0 = (L - 1) - 8 * dl
                o = PS[pb][:, 128 * dl:128 * dl + 128]
                mm(o, Gr[hs, gp, r0:r0 + 8, :].rearrange("p a b -> p (a b)"),
                   Hr[hs, gp, 0:8, :].rearrange("p a b -> p (a b)"), True, False, GK + HK, [PSK[pb]])
                mm(o, Gi[hs, gp, r0:r0 + 8, :].rearrange("p a b -> p (a b)"),
                   nHi[hs, gp, 0:8, :].rearrange("p a b -> p (a b)"), False, True, GK + HK, [PSK[pb]])
            tt("dve", Tt[:, g, :, :].rearrange("p a b -> p (a b)"), PS[pb][:, 0:128 * MS],
               maskT.rearrange("p a b -> p (a b)"), ALU.mult, [PSK[pb], "maskT"], ["Tt"])
            pt = 6 + (g % 2)
            for j in range(MS):
                for ri, Gx in enumerate((Gr, Gi)):
                    c0 = (j * 2 + ri) * 64
                    tr(psb(pt)[:, c0:c0 + 64], Gx[hs, gp, 8 * j:8 * j + 8, :].rearrange("p a b -> p (a b)"),
                       ident_bf[hs, hs], GK + ["ident_bf"], [PSK[pt]])
            cp("act", M1[:, g, :, :].rearrange("p a b -> p (a b)"), psb(pt)[:, 0:128 * MS], [PSK[pt]], ["M1"])
        wst = [RC.alloc([128, 1024], F32) for _ in range(2)]
        wi = 0

        def load_w(dst, src, rows_chunks, ncols, key, scale_col=None):
            nonlocal wi
            for c in range(rows_chunks):
                for n0 in range(0, ncols, 1024):
                    nn = min(1024, ncols - n0)
                    b = wi % 2
                    wi += 1
                    P.dma("sp", wst[b][:, 0:nn], src[128 * c:128 * c + 128, n0:n0 + nn], [], [("wst", b)])
                    eng = "act" if (wi % 2) else "dve"
                    if scale_col is not None:
                        if eng == "act":
                            act(dst[:, c, n0:n0 + nn], wst[b][:, 0:nn], AF.Copy, [("wst", b), "vecT"], [key],
                                scale=scale_col[:, c:c + 1])
                        else:
                            ts("dve", dst[:, c, n0:n0 + nn], wst[b][:, 0:nn], scale_col[:, c:c + 1], None,
                               ALU.mult, None, [("wst", b), "vecT"], [key])
                    else:
                        cp(eng, dst[:, c, n0:n0 + nn], wst[b][:, 0:nn], [("wst", b)], [key])

        load_w(win_bf, win_d, 8, 3072, "win_bf", scale_col=nd)
        load_w(glu_bf, glu_d, 4, 512, "glu_bf")
        W2.reset()
        CH_ = 2048 // L
        PTr = W2.alloc([128, 16, CH_], F32)
        PTi = W2.alloc([128, 16, CH_], F32)
        Rm = W2.alloc([128, 16, CH_], F32)
        pw = RC.alloc([128, 8, 2, 16], F32)
        dA = RC.alloc([128, 16, CH_ // 2], F32)
        dB = RC.alloc([128, 16, CH_ // 2], F32)
        memset("dve", PTr[:, :, 0:1], 1.0, ["PT"])
        memset("dve", PTi[:, :, 0:1], 0.0, ["PT"])
        cp("dve", pw[:, 0, :, :], u1, ["u1"], ["pw"])
        k_ = 0
        while (1 << k_) < CH_:
            m_ = 1 << k_
            if k_ > 0:
                pr, pi_ = pw[:, k_ - 1, 0, :], pw[:, k_ - 1, 1, :]
                tt("dve", dA[:, :, 0], pr, pr, ALU.mult, ["pw", "dA"], ["dA"])
                tt("dve", dB[:, :, 0], pi_, pi_, ALU.mult, ["pw", "dB"], ["dB"])
                tt("dve", pw[:, k_, 0, :], dA[:, :, 0], dB[:, :, 0], ALU.subtract, ["dA", "dB", "pw"], ["pw"])
                tt("dve", dA[:, :, 0], pr, pi_, ALU.mult, ["pw", "dA"], ["dA"])
                ts("dve", pw[:, k_, 1, :], dA[:, :, 0], 2.0, None, ALU.mult, None, ["dA", "pw"], ["pw"])
            shp_ = [128, 16, m_]
            br = pw[:, k_, 0, :].unsqueeze(2).to_broadcast(shp_)
            bi = pw[:, k_, 1, :].unsqueeze(2).to_broadcast(shp_)
            sr, si = PTr[:, :, 0:m_], PTi[:, :, 0:m_]
            a_, b_2 = dA[:, :, 0:m_], dB[:, :, 0:m_]
            tt("dve", a_, sr, br, ALU.mult, ["PT", "pw", "dA"], ["dA"])
            tt("dve", b_2, si, bi, ALU.mult, ["PT", "pw", "dB"], ["dB"])
            tt("dve", PTr[:, :, m_:2 * m_], a_, b_2, ALU.subtract, ["dA", "dB", "PT"], ["PT"])
            tt("dve", a_, sr, bi, ALU.mult, ["PT", "pw", "dA"], ["dA"])
            tt("dve", b_2, si, br, ALU.mult, ["PT", "pw", "dB"], ["dB"])
            tt("dve", PTi[:, :, m_:2 * m_], a_, b_2, ALU.add, ["dA", "dB", "PT"], ["PT"])
            k_ += 1
        cp("dve", Rm, rmag.unsqueeze(2).to_broadcast([128, 16, CH_]), ["rmag"], ["Rm"])
        memset("dve", Rm[:, :, 0:1], 0.0, ["Rm"])
        memset("dve", carry.rearrange("p a b -> p (a b)"), 0.0, ["carry"])
        P.barrier()

        RC.reset()
        xt = [RC.alloc([128, 1024], F32) for _ in range(2)]
        hbf = RC.alloc([128, 1024], BF16)
        hT = RC.alloc([128, 8, 512], BF16)
        uT = RC.alloc([128, 4, 512], BF16)
        zsT = RC.alloc([128, 4, 512], BF16)
        qsq = RC.alloc([128, 512], F32)
        w2_mark = W2.cur
        qns = [W2.alloc([128, 512], F32) for _ in range(2)]
        qtoks = [[W2.alloc([128, 512], BF16) for _ in range(2)] for _ in range(2)]
        qTb = RC.alloc([128, 4, 512], BF16)
        kTb = RC.alloc([128, 4, 512], BF16)
        vtok = [RC.alloc([128, 512], BF16) for _ in range(2)]
        st8 = RC.alloc([128, 8], F32)
        rt1 = RC.alloc([128, 8, 8], F32)
        rt2 = RC.alloc([128, 8, 8], F32)
        ss1 = RC.alloc([128, 4], F32)

        hTs = [hT, RC.alloc([128, 8, 512], BF16)]
        mhalf = RC.alloc([128, 8], F32)
        memset("dve", mhalf, -0.5, ["mhalf"])

        hbfs = [hbf, RC.alloc([128, 1024], BF16)]

        def rms_a(b, i):
            t0 = 512 * b
            xb_ = xt[i % 2]
            xk = ("xt", i % 2)
            hb, hk = hbfs[i % 2], ("hbf", i % 2)
            P.dma("sp", xb_, x_d[t0 + 128 * i:t0 + 128 * i + 128, :], [], [xk])
            act(hb, xb_, AF.Square, [xk], [hk, "ss1"], accum=ss1[:, 0:1])
            ts("dve", ss1[:, 1:2], ss1[:, 0:1], 1.0 / D, EPS, ALU.mult, ALU.add, ["ss1"], ["ss1b"])
            tt("pool", ss1[:, 3:4], ss1[:, 1:2], mhalf[:, 0:1], ALU.pow, ["ss1b", "mhalf"], ["ss1d"])
            act(hb, xb_, AF.Copy, [xk, "ss1d"], [hk], scale=ss1[:, 3:4])

        def rms_b(b, i):
            hb, hk = hbfs[i % 2], ("hbf", i % 2)
            for c in range(8):
                tr(psb(0)[:, 128 * c:128 * c + 128], hb[:, 128 * c:128 * c + 128], ident_bf,
                   [hk, "ident_bf"], [PSK[0]])
            cp("act", hTs[b % 2][:, :, 128 * i:128 * i + 128], psb(0).rearrange("p (a b) -> p a b", a=8),
               [PSK[0]], [("hT", b % 2, i)])

        def rms_sched(b, slot):
            order = {0: [("a", 0)], 1: [("a", 1)], 2: [("b", 0)], 3: [("a", 2)], 4: [("b", 1)], 5: [("a", 3)],
                     6: [("b", 2)], 7: [("b", 3)]}
            for kind, i in order[slot]:
                (rms_a if kind == "a" else rms_b)(b, i)

        for slot in range(8):
            rms_sched(0, slot)
        for b in range(NBLK):
            t0 = 512 * b
            hT = hTs[b % 2]
            hTK = [("hT", b % 2, i) for i in range(4)]
            for fi, f in enumerate(range(8)):
                pb = 1 + (fi % 2)
                for c in range(8):
                    mm(PS[pb][:, :], win_bf[:, c, 128 * f:128 * f + 128], hT[:, c, :], c == 0, c == 7,
                       ["win_bf"] + hTK, [PSK[pb]])
                if f < 4:
                    cp("act", uT[:, f, :], PS[pb][:, :], [PSK[pb]], [("uT", f)])
                    P.dma("act", u_s[f, :, t0:t0 + 512], uT[:, f, :], [("uT", f)], [("u_s", f, b)])
                else:
                    act(zsT[:, f - 4, :], PS[pb][:, :], AF.Silu, [PSK[pb]], [("zsT", f - 4)])
                    P.dma("act", zs_s[f - 4, :, t0:t0 + 512], zsT[:, f - 4, :], [("zsT", f - 4)], [("zs_s", f - 4, b)])
                if b + 1 < NBLK:
                    rms_sched(b + 1, fi)
            def qk_transposes(i):
                ts_ = slice(128 * i, 128 * i + 128)
                for wi_, which in enumerate(("q", "k")):
                    qt_ = qtoks[wi_][i % 2]
                    qk_ = ("qtok", wi_, i % 2)
                    pk3 = ("ps3", wi_)
                    for h in range(4):
                        tr(psb(3)[:, 512 * wi_ + 128 * h:512 * wi_ + 128 * h + 128], qt_[:, 128 * h:128 * h + 128], ident_bf,
                           [qk_, "ident_bf"], [pk3])
                    dstT = qTb if which == "q" else kTb
                    cp("act", dstT[:, :, ts_], psb(3)[:, 512 * wi_:512 * wi_ + 512].rearrange("p (a b) -> p a b", a=4),
                       [pk3], [(which + "Tb", i)])

            for i in range(4):
                ts_ = slice(128 * i, 128 * i + 128)
                for wi_, (which, col0) in enumerate((("q", 1024), ("k", 1536), ("v", 2048), ("za", 2560))):
                    pb = (4 + wi_ + 2 * (i % 2)) if wi_ < 2 else (wi_ - 1)
                    for c in range(8):
                        mm(PS[pb][:, :], hT[:, c, ts_], win_bf[:, c, col0:col0 + 512], c == 0, c == 7,
                           ["win_bf", ("hT", b % 2, i)], [PSK[pb]])
                    if which == "v":
                        vb = vtok[0]
                        cp("act", vb, PS[pb][:, :], [PSK[pb]], [("vtok", 0)])
                        P.dma("act", v_s[t0 + 128 * i:t0 + 128 * i + 128, :], vb, [("vtok", 0)],
                              [("v_s", b, i)])
                        continue
                    if which == "za":
                        vb = vtok[1]
                        act(vb, PS[pb][:, :], AF.Silu, [PSK[pb]], [("vtok", 1)])
                        P.dma("act", za_s[t0 + 128 * i:t0 + 128 * i + 128, :], vb, [("vtok", 1)],
                              [("za_s", b, i)])
                        continue
                    wq = bcq if which == "q" else bck
                    qn = qns[wi_]
                    qnk = "qn%d" % wi_
                    act(qsq, PS[pb][:, :], AF.Square, [PSK[pb]], ["qsq"])
                    P.op("dve", lambda e: e.tensor_reduce(st8, qsq.rearrange("p (a b) -> p a b", a=8), AX.X, ALU.add),
                         ["qsq"], ["st8"])
                    ts("dve", st8, st8, 1.0 / 64, EPS, ALU.mult, ALU.add, ["st8"], ["st8"])
                    tt("pool", st8, st8, mhalf, ALU.pow, ["st8", "mhalf"], ["st8"])
                    q3 = qn.rearrange("p (a b) -> p a b", a=8)
                    tt("dve", q3, PS[pb][:, :].rearrange("p (a b) -> p a b", a=8), bc3(st8, [128, 8, 64], 2),
                       ALU.mult, [PSK[pb], "st8"], [qnk])
                    tt("dve", qn, qn, wq, ALU.mult, [qnk, "bcq", "bck"], [qnk])
                    tile_idx = 4 * b + i
                    cs = cosT[:, tile_idx, :].unsqueeze(1).to_broadcast([128, 8, 8])
                    sn = sinT[:, tile_idx, :].unsqueeze(1).to_broadcast([128, 8, 8])
                    x1, x2 = q3[:, :, 0:8], q3[:, :, 8:16]
                    tt("dve", rt1, x1, cs, ALU.mult, [qnk, "angc"], ["rt1"])
                    tt("dve", rt2, x2, sn, ALU.mult, [qnk, "angs"], ["rt2"])
                    tt("dve", rt1, rt1, rt2, ALU.subtract, ["rt1", "rt2"], ["rt1"])
                    tt("dve", rt2, x2, cs, ALU.mult, [qnk, "angc", "rt2"], ["rt2"])
                    tt("dve", x2, x1, sn, ALU.mult, [qnk, "angs"], [qnk])
                    tt("dve", x2, x2, rt2, ALU.add, [qnk, "rt2"], [qnk])
                    cp("dve", x1, rt1, ["rt1", qnk], [qnk])
                    cp("pool", qtoks[wi_][i % 2], qn, [qnk], [("qtok", wi_, i % 2)])
                if i >= 1:
                    qk_transposes(i - 1)
            qk_transposes(3)
            for h in range(4):
                P.dma("act", qT_s[h, :, t0:t0 + 512], qTb[:, h, :], [("qTb", i) for i in range(4)], [("qT_s", h, b)])
                P.dma("act", kT_s[h, :, t0:t0 + 512], kTb[:, h, :], [("kTb", i) for i in range(4)], [("kT_s", h, b)])
        P.barrier()
        HT = 2048
        CH = HT // L
        SCH = HT // 8
        RA1 = Region(arena, RA.base, 49152)
        RC.reset()
        uTh = RA1.alloc([128, 4, HT], BF16)
        Ub = RA1.alloc([128, 32, SCH], BF16)
        Zt = [RA1.alloc([128, 16, CH], F32) for _ in range(2)]
        Zt += [RC.alloc([128, 16, CH], F32) for _ in range(4)]
        Zr, Zi, Zmr, Zmi, tA, tB = Zt
        zsl = [RC.alloc([128, 4, 512], BF16) for _ in range(2)]
        y2ps = [RC.alloc([128, SCH, 2], F32) for _ in range(2)]
        ge1s = [RC.alloc([128, SCH, 2], F32) for _ in range(2)]
        gsg = RC.alloc([128, 512], F32)
        ysb = [RC.alloc([128, 512], BF16) for _ in range(2)]
        W2.cur = w2_mark
        Xbf = [W2.alloc([128, 16, CH], BF16) for _ in range(2)]
        y2b_h = Ub.rearrange("p a b -> p (a b)").rearrange("p (c t) -> p c t", c=4)
        UK = [("Ub", g) for g in range(32)]
        f3 = lambda a: a.rearrange("p a b -> p (a b)")
        for hh in range(2):
            T0 = HT * hh
            for f in range(4):
                P.dma("sp", uTh[:, f, :], u_s[f, :, T0:T0 + HT], [("u_s", f, b_) for b_ in range(4 * hh, 4 * hh + 4)],
                      [("uTh", f)])
            for g in range(32):
                g8, gl = g // 8, g % 8
                pb = (g // 2) % 2
                for s_ in range(8):
                    j_, e_ = s_ // 2, s_ % 2
                    mm(PS[pb][32 * j_:32 * j_ + 32, SCH * (g % 2):SCH * (g % 2) + SCH],
                       Wsel[:, gl, 112 - 16 * e_:144 - 16 * e_], uTh[:, g8, s_:HT:8], e_ == 0, e_ == 1,
                       ["Wsel", ("uTh", g8)], [PSK[pb]], tp=(0, 32 * j_))
                if g % 2 == 1:
                    cp("act", Ub[:, g - 1:g + 1, :].rearrange("p a b -> p (a b)"), PS[pb][:, :], [PSK[pb]],
                       [("Ub", g - 1), ("Ub", g)])
            for gq in range(4):
                pz = 4 + 2 * (gq % 2)
                for g in range(8 * gq, 8 * gq + 8):
                    gp, hf = g // 2, g % 2
                    hs = slice(64 * hf, 64 * hf + 64)
                    for ri in range(2):
                        for j in range(MS):
                            mm(PS[pz + ri][hs, CH * (gp % 4):CH * (gp % 4) + CH], M1[:, g, j, 64 * ri:64 * ri + 64],
                               Ub[:, g, j:SCH:MS], j == 0, j == MS - 1, ["M1", ("Ub", g)], [PSK[pz + ri]])
                cp("dve", f3(Zr[:, 4 * gq:4 * gq + 4, :]), PS[pz][:, :], [PSK[pz]], [("Zr", gq)])
                cp("act", f3(Zi[:, 4 * gq:4 * gq + 4, :]), PS[pz + 1][:, :], [PSK[pz + 1]], [("Zi", gq)])
            ZrK = [("Zr", q_) for q_ in range(4)]
            ZiK = [("Zi", q_) for q_ in range(4)]
            a0r, a0i = APr[:, 0, :], APi[:, 0, :]
            cr_, ci_ = carry[:, 0, :], carry[:, 1, :]
            t1, t2 = tA[:, :, 0], tB[:, :, 0]
            tt("dve", t1, a0r, cr_, ALU.mult, ["APr", "carry", "tA"], ["tA"])
            tt("dve", t2, a0i, ci_, ALU.mult, ["APi", "carry", "tB"], ["tB"])
            tt("dve", t1, t1, t2, ALU.subtract, ["tA", "tB"], ["tA"])
            tt("dve", Zr[:, :, 0], Zr[:, :, 0], t1, ALU.add, ZrK + ["tA"], ZrK)
            tt("dve", t1, a0r, ci_, ALU.mult, ["APr", "carry", "tA"], ["tA"])
            tt("dve", t2, a0i, cr_, ALU.mult, ["APi", "carry", "tB"], ["tB"])
            tt("dve", t1, t1, t2, ALU.add, ["tA", "tB"], ["tA"])
            tt("dve", Zi[:, :, 0], Zi[:, :, 0], t1, ALU.add, ZiK + ["tA"], ZiK)
            tt("dve", tA, PTr, Zr, ALU.mult, ["PT", "tA"] + ZrK, ["tA"])
            tt("pool", tB, PTi, Zi, ALU.mult, ["PT", "tB"] + ZiK, ["tB"])
            tt("dve", Zmr, tA, tB, ALU.add, ["tA", "tB", "Zmr"], ["Zmr"])
            tt("dve", tA, PTr, Zi, ALU.mult, ["PT", "tA"] + ZiK, ["tA"])
            tt("pool", tB, PTi, Zr, ALU.mult, ["PT", "tB"] + ZrK, ["tB"])
            tt("dve", Zmi, tA, tB, ALU.subtract, ["tA", "tB", "Zmi"], ["Zmi"])
            P.op("dve", lambda e: e.tensor_tensor_scan(f3(Zr), f3(Rm), f3(Zmr), 0.0, ALU.mult, ALU.add),
                 ["Rm", "Zmr"] + ZrK, ZrK)
            P.op("dve", lambda e: e.tensor_tensor_scan(f3(Zi), f3(Rm), f3(Zmi), 0.0, ALU.mult, ALU.add),
                 ["Rm", "Zmi"] + ZiK, ZiK)
            tt("dve", tA, PTr, Zr, ALU.mult, ["PT", "tA"] + ZrK, ["tA"])
            tt("pool", tB, PTi, Zi, ALU.mult, ["PT", "tB"] + ZiK, ["tB"])
            tt("dve", Zmr, tA, tB, ALU.subtract, ["tA", "tB", "Zmr"], ["Zmr"])
            tt("dve", tA, PTr, Zi, ALU.mult, ["PT", "tA"] + ZiK, ["tA"])
            tt("pool", tB, PTi, Zr, ALU.mult, ["PT", "tB"] + ZrK, ["tB"])
            tt("dve", Zmi, tA, tB, ALU.add, ["tA", "tB", "Zmi"], ["Zmi"])
            for ri, (Xs, xk_) in enumerate(((Zmr, "Zmr"), (Zmi, "Zmi"))):
                cp("dve", Xbf[ri][:, :, 1:CH], Xs[:, :, 0:CH - 1], [xk_], [("Xbf", ri)])
                cp("dve", Xbf[ri][:, :, 0], carry[:, ri, :], ["carry", ("Xbf", ri)], [("Xbf", ri)])
            for ri, (Xs, xk_) in enumerate(((Zmr, "Zmr"), (Zmi, "Zmi"))):
                cp("dve", carry[:, ri, :], Xs[:, :, CH - 1], [xk_, ("Xbf", 0), ("Xbf", 1), "carry"], ["carry"])
            for g in range(32):
                gp, hf = g // 2, g % 2
                hs = slice(64 * hf, 64 * hf + 64)
                pb = (g // 2) % 2
                for j in range(MS):
                    o = PS[pb][:, SCH * (g % 2) + j:SCH * (g % 2) + SCH:MS]
                    for jp in range(j + 1):
                        mm(o, Tt[:, g, j - jp, :], Ub[:, g, jp:SCH:MS], jp == 0, False, ["Tt", ("Ub", g)], [PSK[pb]])
                    mm(o, Hr[hs, gp, 8 * j + 1:8 * j + 9, :].rearrange("p a b -> p (a b)"), Xbf[0][hs, gp, :],
                       False, False, HK + [("Xbf", 0)], [PSK[pb]])
                    mm(o, nHi[hs, gp, 8 * j + 1:8 * j + 9, :].rearrange("p a b -> p (a b)"), Xbf[1][hs, gp, :],
                       False, True, HK + [("Xbf", 1)], [PSK[pb]])
                if g % 2 == 1:
                    cp("act", Ub[:, g - 1:g + 1, :].rearrange("p a b -> p (a b)"), PS[pb][:, :], [PSK[pb]],
                       [("Ub", g - 1), ("Ub", g)])
            for ct in range(4):
                CK = [("Ub", 8 * ct + gl) for gl in range(8)]
                for t_ in range(8):
                    pbk = 4 + t_ // 2
                    for gl in range(8):
                        j_, e_ = gl // 2, gl % 2
                        mm(PS[pbk][32 * j_:32 * j_ + 32, SCH * (t_ % 2):SCH * (t_ % 2) + SCH],
                           Wsel[:, t_, 112 - 16 * e_:144 - 16 * e_], Ub[:, 8 * ct + gl, :], e_ == 0, e_ == 1,
                           ["Wsel", ("Ub", 8 * ct + gl)], [PSK[pbk]], tp=(0, 32 * j_))
                for tq in range(4):
                    pbk = 4 + tq
                    y2p, ge1 = y2ps[tq % 2], ge1s[tq % 2]
                    yk, gk = ("y2p", tq % 2), ("ge1", tq % 2)
                    uview = uTh[:, ct, :].rearrange("p (a b) -> p a b", b=8)[:, :, 2 * tq:2 * tq + 2]
                    stt(y2p, uview, dsk[:, ct:ct + 1], PS[pbk][:, :].rearrange("p (b a) -> p a b", b=2),
                        ALU.mult, ALU.add, [("uTh", ct), "vecT", PSK[pbk]], [yk])
                    yf, gf = f3(y2p), f3(ge1)
                    tt("pool", gf, yf, yf, ALU.mult, [yk], [gk])
                    ts("dve", gf, gf, 0.044715, 1.0, ALU.mult, ALU.add, [gk], [gk])
                    tt("dve", gf, gf, yf, ALU.mult, [gk, yk], [gk])
                    act(gf, gf, AF.Sigmoid, [gk], [gk], scale=1.5957691216057308)
                    tt("dve", y2b_h[:, ct, :].rearrange("p (a b) -> p a b", b=8)[:, :, 2 * tq:2 * tq + 2], y2p, ge1,
                       ALU.mult, [yk, gk] + CK, CK)
            for bi in range(4):
                bg = 4 * hh + bi
                tb0 = 512 * bi
                zl = zsl[bi % 2]
                for f in range(4):
                    P.dma("sp", zl[:, f, :], zs_s[f, :, T0 + tb0:T0 + tb0 + 512], [("zs_s", f, bg)], [("zsl", bi % 2, f)])
                for fo in range(4):
                    pb = fo % 2
                    for ci in range(4):
                        mm(PS[pb][:, :], glu_bf[:, ci, 128 * fo:128 * fo + 128], y2b_h[:, ci, tb0:tb0 + 512], ci == 0, ci == 3,
                           ["glu_bf"] + UK, [PSK[pb]])
                    act(gsg, PS[pb][:, :], AF.Sigmoid, [PSK[pb]], ["gsg"], bias=glb[:, fo:fo + 1])
                    tt("dve", gsg, gsg, y2b_h[:, fo, tb0:tb0 + 512], ALU.mult, ["gsg"] + UK, ["gsg"])
                    yo = ysb[fo % 2]
                    tt("dve", yo, gsg, zl[:, fo, :], ALU.mult, ["gsg", ("zsl", bi % 2, fo), ("ysb", fo % 2)], [("ysb", fo % 2)])
                    P.dma("sp", ys_s[fo, :, T0 + tb0:T0 + tb0 + 512], yo, [("ysb", fo % 2)], [("ys_s", fo, bg)])
        P.barrier()
        fin_keys = []
        if debug:
            RC.reset()
            dtile = RC.alloc([128, 4096], BF16)
            for nm, src in (("ys", ys_s), ("qT", qT_s), ("kT", kT_s)):
                for f in range(4):
                    P.dma("sp", dtile, src[f], [], ["dtile"])
                    P.dma("sp", dbg[nm][f], dtile, ["dtile"], [("dbg", nm, f)])
                    fin_keys.append(("dbg", nm, f))
            for nm, src in (("v", v_s), ("za", za_s)):
                for i in range(32):
                    P.dma("sp", dtile[:, 0:512], src[128 * i:128 * i + 128, :], [], ["dtile"])
                    P.dma("sp", dbg[nm][128 * i:128 * i + 128, :], dtile[:, 0:512], ["dtile"], [("dbg", nm, i)])
                    fin_keys.append(("dbg", nm, i))
            P.barrier()

        RAB = Region(arena, RA.base, RA.size + RB.size)
        RC.reset()
        kT_res = RAB.alloc([128, 4, S], BF16)
        v_res = RAB.alloc([128, 32, 4, 129], BF16)
        qTl = [RAB.alloc([128, 4, 512], BF16) for _ in range(2)]
        zatok = RAB.alloc([128, 4, 512], BF16)
        ysl = RAB.alloc([128, 4, 512], BF16)
        PTt = [RAB.alloc([128, 2, 512], BF16) for _ in range(4)]
        yaT = RAB.alloc([128, 4, 512], BF16)
        rs = RC.alloc([128, 8], F32)
        rsn = RC.alloc([128, 4], F32)
        o_all = RC.alloc([128, 4, 512], F32)
        sqt = RC.alloc([128, 512], F32)
        ss4 = RC.alloc([128, 4], F32)
        yatok = RC.alloc([128, 512], BF16)
        xl = [RC.alloc([128, 1024], F32) for _ in range(2)]
        xnews = [RC.alloc([128, 1024], F32) for _ in range(2)]
        xnbs = [RC.alloc([128, 1024], BF16) for _ in range(2)]
        xnT = RC.alloc([128, 8, 128], BF16)
        pl = [RC.alloc([128, 256], F32) for _ in range(2)]
        pT = RC.alloc([128, 2, 128], BF16)
        gate = RC.alloc([128, 1024], F32)
        wst = [RC.alloc([128, 1024], F32) for _ in range(2)]
        tri = cmask[:, 0, 0:128]
        mhalf4 = RC.alloc([128, 4], F32)
        memset("dve", mhalf4, -0.5, ["mhalf4"])
        for h in range(4):
            P.dma("sp", kT_res[:, h, :], kT_s[h], [("kT_s", h, b_) for b_ in range(NBLK)], [("kT_res", h)])
        memset("dve", v_res[:, :, :, 128:129], 1.0, ["v_ones"])
        for i in range(32):
            P.dma("sp", v_res[:, i, :, 0:128], v_s[128 * i:128 * i + 128, :].rearrange("p (a b) -> p a b", a=4),
                  [("v_s", i // 4, i % 4)], [("v_res", i)])

        def load_q(b):
            for h in range(4):
                P.dma("sp", qTl[b % 2][:, h, :], qT_s[h, :, 512 * b:512 * b + 512], [("qT_s", h, b)], [("qTl", b % 2, h)])

        def load_x(b, i):
            tok = slice(512 * b + 128 * i, 512 * b + 128 * i + 128)
            P.dma("sp", xl[i % 2], x_d[tok, :], [], [("xl", i % 2)])
            P.dma("sp", pl[i % 2], p_d[tok, :], [], [("pl", i % 2)])

        load_q(0)
        load_w(wout_bf, wout_d, 8, 1024, "wout_bf")
        load_w(pg_bf, pg_d, 8, 1024, "pg_bf")
        load_w(pp_bf, pp_d, 2, 1024, "pp_bf")
        OBk = [PS[4], PS[5]]
        zatoks = [zatok, RAB.alloc([128, 4, 512], BF16)]
        ysls = [ysl, RC.alloc([128, 4, 512], BF16)]
        pti = 0

        def make_tail_units(b):
            t0 = 512 * b
            zat, ysl_ = zatoks[b % 2], ysls[b % 2]

            def stage_a(i):
                ts_ = slice(128 * i, 128 * i + 128)
                oK = [("o_all", i, h) for h in range(4)]
                oq = o_all[:, i, :]
                act(sqt, oq, AF.Square, oK, ["sqt"])
                P.op("dve", lambda e: e.tensor_reduce(ss4, sqt.rearrange("p (a b) -> p a b", a=4), AX.X, ALU.add),
                     ["sqt"], ["ss4"])
                ts("dve", ss4, ss4, 1.0 / 128, EPS, ALU.mult, ALU.add, ["ss4"], ["ss4"])
                tt("pool", ss4, ss4, mhalf4, ALU.pow, ["ss4", "mhalf4"], ["ss4"])
                o3 = oq.rearrange("p (a b) -> p a b", a=4)
                tt("dve", o3, o3, bc3(ss4, [128, 4, 128], 2), ALU.mult, oK + ["ss4"], oK)
                tt("dve", o3, o3, bcsw.unsqueeze(1).to_broadcast([128, 4, 128]), ALU.mult, oK + ["bcsw"], oK)
                tt("dve", yatok, oq, zat[:, i, :], ALU.mult, oK + [("zatok", b % 2, i)], ["yatok"])
                yield
                for h in range(4):
                    tr(psb(7)[:, 128 * h:128 * h + 128], yatok[:, 128 * h:128 * h + 128], ident_bf,
                       ["yatok", "ident_bf"], [PSK[7]])
                    if h % 2 == 1:
                        yield
                cp("dve", yaT[:, :, ts_], psb(7)[:, 0:512].rearrange("p (a b) -> p a b", a=4), [PSK[7]], [("yaT", i)])
                yield

            def stage_b(i):
                ts_ = slice(128 * i, 128 * i + 128)
                xb_ = xl[i % 2]
                xk = ("xl", i % 2)
                xn = xnews[i % 2]
                for hf in range(2):
                    for c in range(8):
                        src = ysl_ if c < 4 else yaT
                        kk = ("ysl", b % 2, c) if c < 4 else ("yaT", i)
                        mm(PS[6][:, :], src[:, c % 4, ts_], wout_bf[:, c, 512 * hf:512 * hf + 512], c == 0, c == 7,
                           [kk, "wout_bf"], [PSK[6]])
                        if c % 2 == 1:
                            yield
                    tt("dve", xn[:, 512 * hf:512 * hf + 512], PS[6][:, :], xb_[:, 512 * hf:512 * hf + 512],
                       ALU.add, [PSK[6], xk], [("xnew", i % 2, hf)])
                    cp("pool", xnbs[i % 2][:, 512 * hf:512 * hf + 512], xn[:, 512 * hf:512 * hf + 512],
                       [("xnew", i % 2, hf)], [("xnb", i % 2, hf)])
                    yield

            def stage_c1(i):
                plk = ("pl", i % 2)
                for c in range(8):
                    tr(psb(7)[:, 128 * c:128 * c + 128], xnbs[i % 2][:, 128 * c:128 * c + 128], ident_bf,
                       [("xnb", i % 2, c // 4), "ident_bf"], [PSK[7]])
                    if c % 2 == 1:
                        yield
                cp("dve", xnT.rearrange("p a b -> p (a b)"), psb(7)[:, :], [PSK[7]], ["xnT"])
                for c in range(2):
                    tr(PS[6][:, 128 * c:128 * c + 128], pl[i % 2][:, 128 * c:128 * c + 128], ident_f, [plk, "ident_f"],
                       [PSK[6]])
                cp("dve", pT.rearrange("p a b -> p (a b)"), PS[6][:, 0:256], [PSK[6]], ["pT"])
                yield

            def stage_c2(i, hf):
                xb_ = xl[i % 2]
                xk = ("xl", i % 2)
                xn = xnews[i % 2]
                hsl = slice(512 * hf, 512 * hf + 512)
                pk_ = PSK[6]
                for c in range(8):
                    mm(PS[6][:, :], xnT[:, c, :], pg_bf[:, c, hsl], c == 0, c == 7, ["xnT", "pg_bf"], [pk_])
                    if c % 2 == 1:
                        yield
                act(gate[:, hsl], PS[6][:, :], AF.Tanh, [pk_], [("gate", hf)], scale=0.5)
                for c in range(2):
                    mm(PS[6][:, :], pT[:, c, :], pp_bf[:, c, hsl], c == 0, c == 1, ["pT", "pp_bf"], [pk_])
                stt(gate[:, hsl], gate[:, hsl], 1.0, PS[6][:, :], ALU.add, ALU.mult, [("gate", hf), pk_], [("gate", hf)])
                stt(xb_[:, hsl], gate[:, hsl], 0.5, xn[:, hsl], ALU.mult, ALU.add, [("gate", hf), ("xnew", i % 2, hf), xk], [xk])
                yield

            def store(i):
                tok = slice(t0 + 128 * i, t0 + 128 * i + 128)
                P.dma("sp", out_d[tok, :], xl[i % 2], [("xl", i % 2)], [("out", b, i)])
                fin_keys.append(("out", b, i))

            def gen():
                load_x(b, 0)
                load_x(b, 1)
                for i in range(4):
                    yield from stage_a(i)
                for i in range(4):
                    yield from stage_b(i)
                    yield from stage_c1(i)
                    yield from stage_c2(i, 0)
                    yield from stage_c2(i, 1)
                    store(i)
                    if i + 2 < 4:
                        load_x(b, i + 2)
                    yield

            return gen(), 112

        pending, pend_left = None, 0

        def advance(n):
            nonlocal pending, pend_left
            for _ in range(n):
                if pending is None:
                    return
                try:
                    next(pending)
                    pend_left = max(pend_left - 1, 1)
                except StopIteration:
                    pending, pend_left = None, 0

        for b in range(NBLK):
            t0 = 512 * b
            qb = qTl[b % 2]
            if b + 1 < NBLK:
                load_q(b + 1)
            for i in range(4):
                P.dma("sp", zatoks[b % 2][:, i, :], za_s[t0 + 128 * i:t0 + 128 * i + 128, :], [("za_s", b, i)],
                      [("zatok", b % 2, i)])
            for h in range(4):
                P.dma("sp", ysls[b % 2][:, h, :], ys_s[h, :, t0:t0 + 512], [("ys_s", h, b)], [("ysl", b % 2, h)])
            nkt = 4 * (b + 1)
            iters = []
            for h in range(4):
                for qh in range(2):
                    for kt in range(nkt):
                        j = kt - 4 * b
                        if j >= 0 and 128 * j >= 256 * (qh + 1):
                            continue
                        iters.append((h, kt, qh))
            n_it = len(iters)

            def scores(it):
                h, kt, qh = it
                buf = scores.cnt % 2
                scores.cnt += 1
                j = kt - 4 * b
                q0 = max(128 * max(j, 0) - 256 * qh, 0)
                ks = slice(128 * kt, 128 * kt + 128)
                for c in range(2):
                    hs = slice(64 * c, 64 * c + 64)
                    mm(PQ[buf][:, 512 * c + q0:512 * c + 256], kT_res[hs, h, ks],
                       qb[hs, h, 256 * qh + q0:256 * qh + 256], True, True,
                       [("kT_res", h), ("qTl", b % 2, h)], [("SC", buf)])
                return buf, q0

            scores.cnt = 0
            LA = 2
            sq_ = [scores(iters[k_]) for k_ in range(min(LA, n_it))]
            started = {}
            for idx, it in enumerate(iters):
                h, kt, qh = it
                buf, q0 = sq_.pop(0)
                j = kt - 4 * b
                pt_i = pti % 4
                pti += 1
                pt = PTt[pt_i]
                pk = ("PT", pt_i)
                act(pt[:, :, q0:256], PQ[buf][:, :].rearrange("p (c q) -> p c q", c=2)[:, :, q0:256], AF.Exp,
                    [("SC", buf)], [pk])
                if idx + LA < n_it:
                    sq_.append(scores(iters[idx + LA]))
                if j >= 0 and 128 * j >= 256 * qh:
                    tt("pool", pt[:, :, q0:q0 + 128], pt[:, :, q0:q0 + 128],
                       tri.unsqueeze(1).to_broadcast([128, 2, 128]), ALU.mult, [pk, "cmask"], [pk])
                for qt in range(max(j, 2 * qh), 2 * qh + 2):
                    for c in range(2):
                        r = 2 * (qt - 2 * qh) + c
                        bank, col0 = r // 3, (r % 3) * 129
                        st_ = (h, qh, bank) not in started
                        started[(h, qh, bank)] = True
                        ql = 128 * (qt - 2 * qh)
                        lhs = pt[:, c, ql:ql + 128]
                        o_ap = OBk[bank][:, col0:col0 + 129]
                        rhs_ = v_res[:, kt, h, :]
                        P.op("pe", lambda e, o_ap=o_ap, lhs=lhs, rhs_=rhs_, st_=st_, sp_=False:
                             e.matmul(o_ap, lhsT=lhs, rhs=rhs_, start=st_, stop=sp_, skip_group_check=True),
                             [pk, ("v_res", kt), "v_ones"], [("OB", bank)])
                last_of_head = (idx + 1 == n_it) or (iters[idx + 1][0] != h) or (iters[idx + 1][2] != qh)
                if last_of_head:
                    for bank in range(2):
                        nreg = 3 if bank < 1 else 1
                        src = OBk[bank][:, 128:128 + 129 * (nreg - 1) + 1:129]
                        dst = rs[:, 3 * bank:3 * bank + nreg]
                        P.op("dve", lambda e, dst=dst, src=src: e.reciprocal(dst, src), [("OB", bank)], ["rs"])
                    ts("dve", rsn[:, 0:2], rs[:, 1:4:2], lamv[:, 1:2], None, ALU.mult, None, ["rs", "lamv"], ["rsn"])
                    for qt in range(2 * qh, 2 * qh + 2):
                        r0_, r1_ = 2 * (qt - 2 * qh), 2 * (qt - 2 * qh) + 1
                        oa = o_all[:, qt, 128 * h:128 * h + 128]
                        ts("dve", oa, OBk[r0_ // 3][:, (r0_ % 3) * 129:(r0_ % 3) * 129 + 128], rs[:, r0_:r0_ + 1], None,
                           ALU.mult, None, [("OB", r0_ // 3), "rs"], [("o_all", qt, h)])
                        stt(oa, OBk[r1_ // 3][:, (r1_ % 3) * 129:(r1_ % 3) * 129 + 128], rsn[:, qt - 2 * qh:qt - 2 * qh + 1], oa,
                            ALU.mult, ALU.add, [("OB", r1_ // 3), "rsn", ("o_all", qt, h)], [("o_all", qt, h)])
                if pending is not None:
                    advance(-(-pend_left // max(n_it - idx - 8, 1)))
            advance(10 ** 6)
            pending, pend_left = make_tail_units(b)
        advance(10 ** 6)
        P.emit(final_keys=fin_keys)
    return nc


_NC_CACHE = {}


def _core_inputs(b, x, p, positions, norm_w, w_in, ssm_lambda_re, ssm_lambda_im, ssm_log_dt,
                 ssm_b_re, ssm_b_im, ssm_c_re, ssm_c_im, ssm_d, glu_w, glu_b,
                 q_norm_w, k_norm_w, lambda_q1, lambda_k1, lambda_q2, lambda_k2,
                 subln_w, w_out, ple_w_proj, ple_w_gate):
    f = lambda a: np.ascontiguousarray(np.asarray(a, dtype=np.float32))
    vecs = np.concatenate([f(norm_w[0]).reshape(8, 128), f(ssm_d[0]).reshape(4, 128),
                           f(glu_b[0]).reshape(4, 128), f(subln_w[0]).reshape(1, 128)], axis=0)
    rows = np.concatenate([f(q_norm_w[0]), f(k_norm_w[0]), f(lambda_q1[0]), f(lambda_k1[0]),
                           f(lambda_q2[0]), f(lambda_k2[0]), f(subln_w[0])]).reshape(1, 512)
    lam = np.stack([f(ssm_lambda_re[0]).reshape(16, 128), f(ssm_lambda_im[0]).reshape(16, 128)], axis=1)
    return {
        "x": f(x[b]), "p": f(p[0, b]),
        "pos": np.ascontiguousarray(np.asarray(positions[b], dtype=np.int32).reshape(32, 128)),
        "vecs": np.ascontiguousarray(vecs), "rows": np.ascontiguousarray(rows),
        "w_in": f(w_in[0]), "lam": np.ascontiguousarray(lam), "log_dt": f(ssm_log_dt[0]).reshape(16, 2),
        "b_re": f(ssm_b_re[0]), "b_im": f(ssm_b_im[0]), "c_re": f(ssm_c_re[0]), "c_im": f(ssm_c_im[0]),
        "glu_w": f(glu_w[0]), "w_out": f(w_out[0]), "ple_w_proj": f(ple_w_proj[0]), "ple_w_gate": f(ple_w_gate[0]),
    }


def kernel(**inputs):
    if "nc" not in _NC_CACHE:
        _NC_CACHE["nc"] = build_program(DEBUG)
    nc = _NC_CACHE["nc"]
    in_maps = [_core_inputs(b, **inputs) for b in range(8)]
    res = run_bass_kernel_spmd(nc, in_maps, core_ids=list(range(8)))
    out = np.stack([np.asarray(r["out"], dtype=np.float32) for r in res.results], axis=0)
    return out
```

```python
import math
import contextlib
import numpy as np
import concourse.bass as bass
import concourse.mybir as mybir
from concourse.bass_utils import run_bass_kernel_spmd

F32 = mybir.dt.float32
BF16 = mybir.dt.bfloat16
I32 = mybir.dt.int32
ALU = mybir.AluOpType
AF = mybir.ActivationFunctionType
AX = mybir.AxisListType

SAME_ENGINE_SYNC = True
N_DMA_SEMS = 48
DEBUG = False

S = 4096
D = 1024
NBLK = 8
MS = 2
L = 8 * MS
NG = 8 * MS + 7
NH = 8 * MS + 1
NE = NG + NH
CPB = 512 // L
SCB = 64
EPS = 1e-6
TWO_PI = 2.0 * math.pi
CW1 = 6.28125
CW2 = TWO_PI - 6.28125
LAMBDA_INIT = 0.8 - 0.6 * math.exp(0.0)


class _Op:
    __slots__ = ("eng", "fn", "deps", "is_dma", "sem", "semval", "signal", "signo", "idx")


class Prog:
    ENGS = ("pe", "act", "dve", "pool", "sp")

    def __init__(self, nc):
        self.nc = nc
        self.ops = []
        self.last_w = {}
        self.readers = {}
        self.dma_rr = 0
        self.dma_sem_total = [0] * N_DMA_SEMS
        self.dma_sem_lastop = [None] * N_DMA_SEMS
        self.bar_deps = []
        self.need_bar = {e: False for e in self.ENGS}
        self.last_eng_op = {}

    def barrier(self):
        deps = [o for o in self.last_eng_op.values()]
        deps += [o for o in self.dma_sem_lastop if o is not None]
        self.bar_deps = deps
        for e in self.ENGS:
            self.need_bar[e] = True

    def _add(self, eng, fn, R, W, is_dma):
        op = _Op()
        op.eng, op.fn, op.is_dma = eng, fn, is_dma
        op.signal = False
        op.signo = 0
        op.sem = None
        op.semval = 0
        op.idx = len(self.ops)
        deps = []
        if self.need_bar[eng]:
            deps += self.bar_deps
            self.need_bar[eng] = False
        for k in R:
            w = self.last_w.get(k)
            if w is not None:
                deps.append(w)
        for k in W:
            w = self.last_w.get(k)
            if w is not None:
                deps.append(w)
            for r in self.readers.get(k, ()):
                deps.append(r)
        if is_dma:
            s = self.dma_rr
            self.dma_rr = (self.dma_rr + 1) % N_DMA_SEMS
            prev = self.dma_sem_lastop[s]
            if prev is not None:
                deps.append(prev)
            self.dma_sem_total[s] += 16
            op.sem = s
            op.semval = self.dma_sem_total[s]
            self.dma_sem_lastop[s] = op
        seen = set()
        dd = []
        for d in deps:
            if d is op or id(d) in seen:
                continue
            seen.add(id(d))
            if (not d.is_dma) and d.eng == eng and (eng == "pe" or not SAME_ENGINE_SYNC):
                continue
            dd.append(d)
            if not d.is_dma:
                d.signal = True
        op.deps = dd
        for k in W:
            self.last_w[k] = op
            self.readers[k] = []
        for k in R:
            if k not in W:
                self.readers.setdefault(k, []).append(op)
        self.ops.append(op)
        if not is_dma:
            self.last_eng_op[eng] = op
        return op

    def op(self, eng, fn, R=(), W=()):
        return self._add(eng, fn, tuple(R), tuple(W), False)

    def dma(self, q, out, in_, R=(), W=()):
        return self._add(q, lambda e: e.dma_start(out=out, in_=in_), tuple(R), tuple(W), True)

    def emit(self, final_keys=()):
        nc = self.nc
        self._add("sp", None, tuple(final_keys), (), False)
        cnt = {e: 0 for e in self.ENGS}
        for o in self.ops:
            if (not o.is_dma) and o.signal:
                cnt[o.eng] += 1
                o.signo = cnt[o.eng]
        with contextlib.ExitStack() as st:
            esem = {e: st.enter_context(nc.semaphore("sem_" + e)) for e in self.ENGS}
            dsem = [st.enter_context(nc.semaphore("dsem%d" % i)) for i in range(N_DMA_SEMS)]
            block = st.enter_context(nc.Block())
            per = {e: [o for o in self.ops if o.eng == e] for e in self.ENGS}

            def replay(e, eng):
                waited = {}
                for o in per[e]:
                    for d in o.deps:
                        if d.is_dma:
                            key, val, sem = ("d", d.sem), d.semval, dsem[d.sem]
                        else:
                            key, val, sem = ("e", d.eng), d.signo, esem[d.eng]
                        if waited.get(key, 0) >= val:
                            continue
                        waited[key] = val
                        eng.wait_ge(sem, val)
                    if o.fn is None:
                        continue
                    ins = o.fn(eng)
                    if o.is_dma:
                        ins.then_inc(dsem[o.sem], 16)
                    elif o.signal:
                        ins.then_inc(esem[e], 1)

            @block.sync
            def _(eng):
                replay("sp", eng)

            @block.scalar
            def _(eng):
                replay("act", eng)

            @block.vector
            def _(eng):
                replay("dve", eng)

            @block.gpsimd
            def _(eng):
                replay("pool", eng)

            @block.tensor
            def _(eng):
                replay("pe", eng)


class Region:
    def __init__(self, arena, base, size):
        self.arena, self.base, self.size, self.cur = arena, base, size, 0

    def reset(self):
        self.cur = 0

    def alloc(self, shape, dt, parts=None):
        esz = 2 if dt == BF16 else 4
        n = 1
        for s_ in shape[1:]:
            n *= s_
        nbytes = (n * esz + 31) // 32 * 32
        off = self.base + self.cur
        self.cur += nbytes
        assert self.cur <= self.size, ("region overflow", self.cur, self.size)
        v = self.arena[0:shape[0], off // 4:(off + nbytes) // 4]
        if dt != F32:
            v = v.bitcast(dt)
        v = v[:, 0:n]
        if len(shape) == 3:
            v = v.rearrange("p (a b) -> p a b", a=shape[1])
        elif len(shape) == 4:
            v = v.rearrange("p (a b c) -> p a b c", a=shape[1], b=shape[2])
        return v


def build_program(debug=False):
    nc = bass.Bass("TRN2", target_bir_lowering=False)
    P = Prog(nc)

    def din(name, shape, dt=F32):
        return nc.dram_tensor(name, list(shape), dt, kind="ExternalInput").ap()

    x_d = din("x", [S, D])
    p_d = din("p", [S, 256])
    pos_d = din("pos", [32, 128], I32)
    vec_d = din("vecs", [17, 128])
    row_d = din("rows", [1, 512])
    win_d = din("w_in", [D, 3072])
    lam_d = din("lam", [16, 2, 128])
    ldt_d = din("log_dt", [16, 2])
    bre_d = din("b_re", [32, 64, 16])
    bim_d = din("b_im", [32, 64, 16])
    cre_d = din("c_re", [32, 16, 64])
    cim_d = din("c_im", [32, 16, 64])
    glu_d = din("glu_w", [512, 512])
    wout_d = din("w_out", [D, D])
    pp_d = din("ple_w_proj", [256, D])
    pg_d = din("ple_w_gate", [D, D])
    out_d = nc.dram_tensor("out", [S, D], F32, kind="ExternalOutput").ap()
    ys_s = nc.dram_tensor("ys_s", [4, 128, S], BF16, kind="Internal").ap()
    u_s = nc.dram_tensor("u_s", [4, 128, S], BF16, kind="Internal").ap()
    zs_s = nc.dram_tensor("zs_s", [4, 128, S], BF16, kind="Internal").ap()
    za_s = nc.dram_tensor("za_s", [S, 512], BF16, kind="Internal").ap()
    qT_s = nc.dram_tensor("qT_s", [4, 128, S], BF16, kind="Internal").ap()
    kT_s = nc.dram_tensor("kT_s", [4, 128, S], BF16, kind="Internal").ap()
    v_s = nc.dram_tensor("v_s", [S, 512], BF16, kind="Internal").ap()
    dbg = {}
    if debug:
        for nm in ("ys", "qT", "kT"):
            dbg[nm] = nc.dram_tensor("dbg_" + nm, [4, 128, S], BF16, kind="ExternalOutput").ap()
        dbg["v"] = nc.dram_tensor("dbg_v", [S, 512], BF16, kind="ExternalOutput").ap()
        dbg["za"] = nc.dram_tensor("dbg_za", [S, 512], BF16, kind="ExternalOutput").ap()

    with contextlib.ExitStack() as st:
        ARENA_BYTES = 212480
        arena = st.enter_context(nc.sbuf_tensor("arena", [128, ARENA_BYTES // 4], F32))
        PQ = [st.enter_context(nc.psum_tensor("pq%d" % i, [128, 1024], F32)) for i in range(4)]
        PS = [PQ[i // 2][:, 512 * (i % 2):512 * (i % 2) + 512] for i in range(8)]
        PSK = ["ps%d" % i for i in range(8)]

        def psb(i):
            return PS[i].bitcast(BF16)

        PER = Region(arena, 0, 59136)
        RA = Region(arena, 59136, 65536)
        RB = Region(arena, 124672, 33792)
        RC = Region(arena, 158464, ARENA_BYTES - 158464)
        ident_bf = PER.alloc([128, 128], BF16)
        ident_f = PER.alloc([128, 128], F32)
        ones_f = PER.alloc([128, 128], F32)
        ones_bf = PER.alloc([128, 128], BF16)
        Wsel = PER.alloc([128, 8, 240], BF16)
        maskT = PER.alloc([128, MS, 128], BF16)
        cmask = PER.alloc([128, 4, 512], BF16)
        glu_bf = PER.alloc([128, 4, 512], BF16)
        W2_base = PER.cur
        wout_bf = PER.alloc([128, 8, 1024], BF16)
        pg_bf = PER.alloc([128, 8, 1024], BF16)
        pp_bf = PER.alloc([128, 2, 1024], BF16)
        W2 = Region(arena, W2_base, PER.cur - W2_base)
        vecT = PER.alloc([128, 17], F32)
        bcq = PER.alloc([128, 512], F32)
        bck = PER.alloc([128, 512], F32)
        cosT = PER.alloc([128, 32, 8], F32)
        sinT = PER.alloc([128, 32, 8], F32)
        APr = PER.alloc([128, 6, 16], F32)
        APi = PER.alloc([128, 6, 16], F32)
        carry = PER.alloc([128, 2, 16], F32)
        lamv = PER.alloc([128, 4], F32)
        sw08 = PER.alloc([128, 1], F32)
        epsv = PER.alloc([128, 1], F32)
        bcsw = PER.alloc([128, 128], F32)
        u1 = PER.alloc([128, 2, 16], F32)
        rmag = PER.alloc([128, 16], F32)
        nd = vecT[:, 0:8]
        dsk = vecT[:, 8:12]
        glb = vecT[:, 12:16]
        win_bf = RA.alloc([128, 8, 3072], BF16)
        M1 = RA.alloc([128, 32, MS, 128], BF16)
        Hr = RB.alloc([128, 16, NH, 16], BF16)
        nHi = RB.alloc([128, 16, NH, 16], BF16)
        Tt = RB.alloc([128, 32, MS, 128], BF16)

        def mm(out, lhsT, rhs, start, stop, R, W, tp=None):
            if tp is None:
                P.op("pe", lambda e: e.matmul(out, lhsT=lhsT, rhs=rhs, start=start, stop=stop), R, W)
            else:
                P.op("pe", lambda e: e.matmul(out, lhsT=lhsT, rhs=rhs, start=start, stop=stop, tile_position=tp), R, W)

        def tr(out, in_, ident, R, W):
            P.op("pe", lambda e: e.transpose(out, in_, ident), R, W)

        def act(out, in_, func, R, W, bias=None, scale=None, accum=None):
            kw = {}
            if bias is not None:
                kw["bias"] = bias
            if scale is not None:
                kw["scale"] = scale
            if accum is not None:
                kw["accum_out"] = accum
            P.op("act", lambda e: e.activation(out, in_, func, **kw), R, W)

        def tt(eng, out, a, b, op, R, W):
            P.op(eng, lambda e: e.tensor_tensor(out, a, b, op), R, W)

        def ts(eng, out, a, s1, s2, op0, op1, R, W):
            if op1 is None:
                P.op(eng, lambda e: e.tensor_scalar(out, a, s1, None, op0), R, W)
            else:
                P.op(eng, lambda e: e.tensor_scalar(out, a, s1, s2, op0, op1), R, W)

        def stt(out, a, s, b, op0, op1, R, W):
            P.op("dve", lambda e: e.scalar_tensor_tensor(out, a, s, b, op0, op1), R, W)

        def cp(eng, out, in_, R, W):
            if eng == "act":
                act(out, in_, AF.Copy, R, W)
            else:
                P.op(eng, lambda e: e.tensor_copy(out, in_), R, W)

        def iota(out, pattern, base, cm, W):
            P.op("pool", lambda e: e.iota(out, pattern=pattern, base=base, channel_multiplier=cm), (), W)

        def memset(eng, out, val, W):
            P.op(eng, lambda e: e.memset(out, val), (), W)

        def bc3(ap2, shape, axis):
            return ap2.unsqueeze(axis).to_broadcast(shape)

        def sincos(x, q, qi, r, s_out, c_out, key):
            ts("dve", q, x, 1.0 / TWO_PI, None, ALU.mult, None, [key + "x"], [key + "q"])
            cp("dve", qi, q, [key + "q"], [key + "qi"])
            cp("dve", q, qi, [key + "qi"], [key + "q"])
            stt(r, q, -CW1, x, ALU.mult, ALU.add, [key + "q", key + "x"], [key + "r"])
            stt(r, q, -CW2, r, ALU.mult, ALU.add, [key + "q", key + "r"], [key + "r"])
            ts("dve", x, r, -math.pi, math.pi, ALU.max, ALU.min, [key + "r"], [key + "x"])
            act(s_out, x, AF.Sin, [key + "x"], [key + "s"])
            ts("dve", x, r, math.pi / 2, None, ALU.add, None, [key + "r", key + "s"], [key + "x"])
            ts("dve", q, x, math.pi, -TWO_PI, ALU.is_gt, ALU.mult, [key + "x"], [key + "q"])
            tt("dve", x, x, q, ALU.add, [key + "q", key + "x"], [key + "x"])
            ts("dve", x, x, -math.pi, math.pi, ALU.max, ALU.min, [key + "x"], [key + "x"])
            act(c_out, x, AF.Sin, [key + "x"], [key + "c"])

        RC.reset()
        W2.reset()
        R0 = Region(arena, RA.base, RA.size)
        io_i = R0.alloc([128, 1920], I32)
        io_f = R0.alloc([128, 1920], F32)
        msk_f = R0.alloc([128, 1920], F32)
        iota(io_i[:, 0:128], [[1, 128]], 0, -1, ["io_i"])
        ts("dve", ident_f, io_i[:, 0:128], 0.0, None, ALU.is_equal, None, ["io_i"], ["ident_f"])
        cp("dve", ident_bf, ident_f, ["ident_f"], ["ident_bf"])
        memset("dve", ones_f, 1.0, ["ones_f"])
        memset("dve", ones_bf, 1.0, ["ones_bf"])
        memset("dve", epsv, EPS, ["epsv"])
        iota(io_i[:, 0:1920].rearrange("p (a b) -> p a b", a=8), [[16, 8], [1, 240]], -112, -1, ["io_i"])
        ts("dve", io_f[:, 0:1920], io_i[:, 0:1920], 0.0, None, ALU.is_equal, None, ["io_i"], ["io_f"])
        iota(io_i[:, 0:1920].rearrange("p (a b) -> p a b", a=8), [[0, 8], [1, 240]], 0, 0, ["io_i"])
        ts("dve", msk_f[:, 0:1920], io_i[:, 0:1920], 112.0, None, ALU.is_ge, None, ["io_i"], ["msk_f"])
        tt("dve", io_f[:, 0:1920], io_f[:, 0:1920], msk_f[:, 0:1920], ALU.mult, ["io_f", "msk_f"], ["io_f"])
        ts("dve", msk_f[:, 0:1920], io_i[:, 0:1920], 127.0, None, ALU.is_le, None, ["io_i"], ["msk_f"])
        tt("dve", Wsel.rearrange("p a b -> p (a b)"), io_f[:, 0:1920], msk_f[:, 0:1920], ALU.mult,
           ["io_f", "msk_f"], ["Wsel"])
        iota(io_i[:, 0:128].rearrange("p (a b) -> p a b", a=8), [[16, 8], [0, 16]], 0, -1, ["io_i"])
        memset("dve", maskT.rearrange("p a b -> p (a b)"), 1.0, ["maskT"])
        ts("dve", maskT[:, 0, :], io_i[:, 0:128], -15.0, None, ALU.is_ge, None, ["io_i", "maskT"], ["maskT"])
        for j in range(4):
            iota(io_i[:, 0:512], [[1, 512]], -128 * j, -1, ["io_i"])
            ts("dve", cmask[:, j, :], io_i[:, 0:512], 0.0, None, ALU.is_ge, None, ["io_i", "cmask"], ["cmask"])

        vec16 = R0.alloc([17, 128], F32)
        rowv = R0.alloc([1, 512], F32)
        lam16 = R0.alloc([16, 3, 128], F32)
        ldt16 = R0.alloc([16, 2], F32)
        pos_i = R0.alloc([32, 128], I32)
        pos_f = R0.alloc([32, 128], F32)
        P.dma("sp", vec16, vec_d, [], ["vec16"])
        P.dma("sp", rowv, row_d, [], ["rowv"])
        P.dma("sp", lam16[:, 0:2, :], lam_d, [], ["lam16a"])
        P.dma("sp", ldt16, ldt_d, [], ["ldt16"])
        P.dma("sp", pos_i, pos_d, [], ["pos_i"])
        cp("dve", lam16[:, 2, :].rearrange("p (a b) -> p a b", a=2), bc3(ldt16, [16, 2, 64], 2),
           ["ldt16"], ["lam16b"])
        cp("dve", pos_f, pos_i, ["pos_i"], ["pos_f"])
        tr(PS[0][:, 0:17], vec16, ident_f[0:17, 0:17], ["vec16", "ident_f"], [PSK[0]])
        cp("dve", vecT, PS[0][:, 0:17], [PSK[0]], ["vecT"])
        par = R0.alloc([128, 3, 16], F32)
        for i in range(3):
            tr(PS[1][:, 16 * i:16 * i + 16], lam16[:, i, :], ident_f[0:16, 0:16],
               ["lam16a", "lam16b", "ident_f"], [PSK[1]])
        cp("dve", par.rearrange("p a b -> p (a b)"), PS[1][:, 0:48], [PSK[1]], ["par"])
        posT = R0.alloc([128, 32], F32)
        tr(PS[2][:, 0:32], pos_f, ident_f[0:32, 0:32], ["pos_f", "ident_f"], [PSK[2]])
        cp("dve", posT, PS[2][:, 0:32], [PSK[2]], ["posT"])
        bcr = R0.alloc([128, 512], F32)
        mm(PS[3][:, 0:512], ones_f[0:1, :], rowv, True, True, ["ones_f", "rowv"], [PSK[3]])
        cp("dve", bcr, PS[3][:, 0:512], [PSK[3]], ["bcr"])
        ts("dve", bcsw, bcr[:, 384:512], 1.0 - LAMBDA_INIT, None, ALU.mult, None, ["bcr"], ["bcsw"])
        ts("dve", bcq.rearrange("p (a b) -> p a b", a=8), bc3(bcr[:, 0:64], [128, 8, 64], 1),
           0.125, None, ALU.mult, None, ["bcr"], ["bcq"])
        cp("dve", bck.rearrange("p (a b) -> p a b", a=8), bc3(bcr[:, 64:128], [128, 8, 64], 1), ["bcr"], ["bck"])
        lsc = R0.alloc([128, 128], F32)
        lsum = R0.alloc([128, 2], F32)
        tt("dve", lsc[:, 0:64], bcr[:, 128:192], bcr[:, 192:256], ALU.mult, ["bcr"], ["lsc"])
        tt("dve", lsc[:, 64:128], bcr[:, 256:320], bcr[:, 320:384], ALU.mult, ["bcr", "lsc"], ["lsc"])
        P.op("dve", lambda e: e.tensor_reduce(lsum, lsc.rearrange("p (a b) -> p a b", a=2), AX.X, ALU.add),
             ["lsc"], ["lsum"])
        act(lsum, lsum, AF.Exp, ["lsum"], ["lsum"])
        tt("dve", lamv[:, 0:1], lsum[:, 0:1], lsum[:, 1:2], ALU.subtract, ["lsum"], ["lamv"])
        ts("dve", lamv[:, 0:1], lamv[:, 0:1], LAMBDA_INIT, None, ALU.add, None, ["lamv"], ["lamv"])
        ts("dve", lamv[:, 1:2], lamv[:, 0:1], -1.0, None, ALU.mult, None, ["lamv"], ["lamv"])
        ts("dve", sw08, vecT[:, 16:17], 1.0 - LAMBDA_INIT, None, ALU.mult, None, ["vecT"], ["sw08"])
        invf = R0.alloc([128, 8], F32)
        for i in range(8):
            memset("dve", invf[:, i:i + 1], float(np.float32(500000.0) ** np.float32(-(2.0 * i) / 16.0)), ["invf"])
        ang = R0.alloc([128, 256], F32)
        aq = R0.alloc([128, 256], F32)
        aqi = R0.alloc([128, 256], I32)
        ar_ = R0.alloc([128, 256], F32)
        tt("dve", ang.rearrange("p (a b) -> p a b", a=32), bc3(posT, [128, 32, 8], 2), bc3(invf, [128, 32, 8], 1),
           ALU.mult, ["posT", "invf"], ["angx"])
        sincos(ang, aq, aqi, ar_, sinT.rearrange("p a b -> p (a b)"), cosT.rearrange("p a b -> p (a b)"), "ang")

        NEt = 16 * NE
        kv_i = R0.alloc([128, NE], I32)
        kv = R0.alloc([128, NE], F32)
        iota(kv_i[:, 0:NG], [[-1, NG]], L - 1, 0, ["kv_i"])
        iota(kv_i[:, NG:NE], [[1, NH]], 0, 0, ["kv_i"])
        cp("dve", kv, kv_i, ["kv_i"], ["kv"])
        dtv = R0.alloc([128, 16], F32)
        act(dtv, par[:, 2, :], AF.Exp, ["par"], ["dtv"])
        lrdt = R0.alloc([128, 16], F32)
        thv = R0.alloc([128, 16], F32)
        tt("dve", lrdt, par[:, 0, :], dtv, ALU.mult, ["par", "dtv"], ["lrdt"])
        tt("dve", thv, par[:, 1, :], dtv, ALU.mult, ["par", "dtv"], ["thv"])
        Emag = R0.alloc([128, 16, NE], F32)
        Eph = R0.alloc([128, 16, NE], F32)
        Eq = R0.alloc([128, 16, NE], F32)
        Eqi = R0.alloc([128, 16, NE], I32)
        Er = R0.alloc([128, 16, NE], F32)
        Ei = R0.alloc([128, 16, NE], F32)
        Ert = R0.alloc([128, 16, NE], F32)
        shp = [128, 16, NE]
        tt("dve", Emag, bc3(lrdt, shp, 2), bc3(kv, shp, 1), ALU.mult, ["lrdt", "kv"], ["Emag"])
        act(Emag, Emag, AF.Exp, ["Emag"], ["Emag"])
        tt("dve", Eph, bc3(thv, shp, 2), bc3(kv, shp, 1), ALU.mult, ["thv", "kv"], ["Ephx"])
        f2 = lambda a: a.rearrange("p a b -> p (a b)")
        sincos(f2(Eph), f2(Eq), f2(Eqi), f2(Ert), f2(Ei), f2(Er), "Eph")
        tt("dve", f2(Er), f2(Er), f2(Emag), ALU.mult, ["Ephc", "Emag"], ["Er"])
        tt("dve", f2(Ei), f2(Ei), f2(Emag), ALU.mult, ["Ephs", "Emag"], ["Ei"])
        c_nr = R0.alloc([128, 16], F32)
        c_den = R0.alloc([128, 16], F32)
        c_t = R0.alloc([128, 16], F32)
        c_r = R0.alloc([128, 16], F32)
        c_i = R0.alloc([128, 16], F32)
        lr, li = par[:, 0, :], par[:, 1, :]
        ni = Ei[:, :, NG + 1]
        ts("dve", c_nr, Er[:, :, NG + 1], -1.0, None, ALU.add, None, ["Er"], ["c_nr"])
        tt("dve", c_den, lr, lr, ALU.mult, ["par"], ["c_den"])
        tt("dve", c_t, li, li, ALU.mult, ["par"], ["c_t"])
        tt("dve", c_den, c_den, c_t, ALU.add, ["c_den", "c_t"], ["c_den"])
        P.op("dve", lambda e: e.reciprocal(c_den, c_den), ["c_den"], ["c_den"])
        tt("dve", c_r, c_nr, lr, ALU.mult, ["c_nr", "par"], ["c_r"])
        tt("dve", c_t, ni, li, ALU.mult, ["Ei", "par", "c_t"], ["c_t"])
        tt("dve", c_r, c_r, c_t, ALU.add, ["c_r", "c_t"], ["c_r"])
        tt("dve", c_r, c_r, c_den, ALU.mult, ["c_r", "c_den"], ["c_r"])
        tt("dve", c_i, ni, lr, ALU.mult, ["Ei", "par"], ["c_i"])
        tt("dve", c_t, c_nr, li, ALU.mult, ["c_nr", "par", "c_t"], ["c_t"])
        tt("dve", c_i, c_i, c_t, ALU.subtract, ["c_i", "c_t"], ["c_i"])
        tt("dve", c_i, c_i, c_den, ALU.mult, ["c_i", "c_den"], ["c_i"])
        cp("dve", APr[:, 0, :], Er[:, :, NG + L], ["Er"], ["APr"])
        cp("dve", APi[:, 0, :], Ei[:, :, NG + L], ["Ei"], ["APi"])
        sq1 = R0.alloc([128, 16], F32)
        sq2 = R0.alloc([128, 16], F32)
        for d_ in range(1, 6):
            tt("dve", sq1, APr[:, d_ - 1, :], APr[:, d_ - 1, :], ALU.mult, ["APr", "sq1"], ["sq1"])
            tt("dve", sq2, APi[:, d_ - 1, :], APi[:, d_ - 1, :], ALU.mult, ["APi", "sq2"], ["sq2"])
            tt("dve", APr[:, d_, :], sq1, sq2, ALU.subtract, ["sq1", "sq2", "APr"], ["APr"])
            tt("dve", sq1, APr[:, d_ - 1, :], APi[:, d_ - 1, :], ALU.mult, ["APr", "APi", "sq1"], ["sq1"])
            ts("dve", APi[:, d_, :], sq1, 2.0, None, ALU.mult, None, ["sq1", "APi"], ["APi"])
        cp("dve", rmag, Emag[:, :, NG + L], ["Emag"], ["rmag"])
        P.op("dve", lambda e: e.reciprocal(sq1, rmag), ["rmag", "sq1"], ["sq1"])
        tt("dve", u1[:, 0, :], APr[:, 0, :], sq1, ALU.mult, ["APr", "sq1"], ["u1"])
        tt("dve", u1[:, 1, :], APi[:, 0, :], sq1, ALU.mult, ["APi", "sq1", "u1"], ["u1"])
        Bre = R0.alloc([128, 16, 16], F32)
        Bim = R0.alloc([128, 16, 16], F32)
        bbr = R0.alloc([128, 16, 16], F32)
        bbi = R0.alloc([128, 16, 16], F32)
        bt = R0.alloc([128, 16, 16], F32)
        P.dma("sp", Bre, bre_d.rearrange("(gp g2) n q -> (g2 n) gp q", g2=2), [], ["Bre"])
        P.dma("sp", Bim, bim_d.rearrange("(gp g2) n q -> (g2 n) gp q", g2=2), [], ["Bim"])
        s3 = [128, 16, 16]
        tt("dve", bbr, Bre, bc3(c_r, s3, 2), ALU.mult, ["Bre", "c_r"], ["bbr"])
        tt("dve", bt, Bim, bc3(c_i, s3, 2), ALU.mult, ["Bim", "c_i"], ["bt"])
        tt("dve", bbr, bbr, bt, ALU.subtract, ["bbr", "bt"], ["bbr"])
        tt("dve", bbi, Bim, bc3(c_r, s3, 2), ALU.mult, ["Bim", "c_r"], ["bbi"])
        tt("dve", bt, Bre, bc3(c_i, s3, 2), ALU.mult, ["Bre", "c_i", "bt"], ["bt"])
        tt("dve", bbi, bbi, bt, ALU.add, ["bbi", "bt"], ["bbi"])
        Xc = R0.alloc([128, 4, 128], F32)
        Ctr = R0.alloc([128, 16, 16], F32)
        Cti = R0.alloc([128, 16, 16], F32)
        for ri, cd in enumerate((cre_d, cim_d)):
            for hf in range(2):
                for gpl in range(8):
                    for g2 in range(2):
                        g = 2 * (8 * hf + gpl) + g2
                        P.dma("sp" if (gpl % 2 == 0) else "act", Xc[16 * gpl:16 * gpl + 16, 2 * ri + hf, 64 * g2:64 * g2 + 64],
                              cd[g], [], [("Xc", ri, hf, gpl, g2)])
                tr(PS[4 + 2 * ri + hf][:, 0:128], Xc[:, 2 * ri + hf, :], ident_f,
                   [("Xc", ri, hf, a, b) for a in range(8) for b in range(2)] + ["ident_f"], [PSK[4 + 2 * ri + hf]])
                dst = (Ctr, Cti)[ri]
                cp("dve", dst[:, 8 * hf:8 * hf + 8, :].rearrange("p a b -> p (a b)"), PS[4 + 2 * ri + hf][:, 0:128],
                   [PSK[4 + 2 * ri + hf]], [("Ct", ri, hf)])
        CtK = [("Ct", ri, hf) for ri in range(2) for hf in range(2)]
        Gr = RC.alloc([128, 16, NG, 16], BF16)
        Gi = RC.alloc([128, 16, NG, 16], BF16)
        GC = 2
        g1 = W2.alloc([128, GC, NG, 16], F32)
        g2t = W2.alloc([128, GC, NG, 16], F32)
        g3 = W2.alloc([128, GC, NG, 16], F32)
        g4 = W2.alloc([128, GC, NG, 16], F32)
        for c0 in range(0, 16, GC):
            sl = slice(c0, c0 + GC)
            for (E0, n0, nn, Xr_, Xi_, Or_, Oi_, neg, kx) in (
                    (0, 0, NG, bbr, bbi, Gr, Gi, False, ["bbr", "bbi"]),
                    (NG, 0, NH, Ctr, Cti, Hr, nHi, True, CtK)):
                shp4 = [128, GC, nn, 16]
                er = Er[:, sl, E0:E0 + nn].unsqueeze(3).to_broadcast(shp4)
                ei = Ei[:, sl, E0:E0 + nn].unsqueeze(3).to_broadcast(shp4)
                xr = Xr_[:, sl, :].unsqueeze(2).to_broadcast(shp4)
                xi = Xi_[:, sl, :].unsqueeze(2).to_broadcast(shp4)
                a1, a2 = g1[:, :, 0:nn, :], g2t[:, :, 0:nn, :]
                a3, a4 = g3[:, :, 0:nn, :], g4[:, :, 0:nn, :]
                tt("dve", a1, er, xr, ALU.mult, ["Er"] + kx + ["g1"], ["g1"])
                tt("dve", a2, ei, xi, ALU.mult, ["Ei"] + kx + ["g2"], ["g2"])
                tt("dve", Or_[:, sl, :, :], a1, a2, ALU.subtract, ["g1", "g2"], [("GH", E0, c0, 0)])
                tt("pool", a3, er, xi, ALU.mult, ["Er"] + kx + ["g3"], ["g3"])
                tt("pool", a4, ei, xr, ALU.mult, ["Ei"] + kx + ["g4"], ["g4"])
                if neg:
                    fl = lambda a: a.rearrange("p a b c -> p a (b c)")
                    stt(fl(Oi_[:, sl, :, :]), fl(a3), -1.0, fl(a4), ALU.mult, ALU.subtract, ["g3", "g4"], [("GH", E0, c0, 1)])
                else:
                    tt("pool", Oi_[:, sl, :, :], a3, a4, ALU.add, ["g3", "g4"], [("GH", E0, c0, 1)])
        GK = [("GH", 0, c0, i) for c0 in range(0, 16, GC) for i in range(2)]
        HK = [("GH", NG, c0, i) for c0 in range(0, 16, GC) for i in range(2)]
        P.barrier()
        for g in range(32):
            gp, hf = g // 2, g % 2
            hs = slice(64 * hf, 64 * hf + 64)
            pb = 4 + (g % 2)
            for dl in range(MS):
                r0 = (L - 1) - 8 * dl
                o = PS[pb][:, 128 * dl:128 * dl + 128]
                mm(o, Gr[hs, gp, r0:r0 + 8, :].rearrange("p a b -> p (a b)"),
                   Hr[hs, gp, 0:8, :].rearrange("p a b -> p (a b)"), True, False, GK + HK, [PSK[pb]])
                mm(o, Gi[hs, gp, r0:r0 + 8, :].rearrange("p a b -> p (a b)"),
                   nHi[hs, gp, 0:8, :].rearrange("p a b -> p (a b)"), False, True, GK + HK, [PSK[pb]])
            tt("dve", Tt[:, g, :, :].rearrange("p a b -> p (a b)"), PS[pb][:, 0:128 * MS],
               maskT.rearrange("p a b -> p (a b)"), ALU.mult, [PSK[pb], "maskT"], ["Tt"])
            pt = 6 + (g % 2)
            for j in range(MS):
                for ri, Gx in enumerate((Gr, Gi)):
                    c0 = (j * 2 + ri) * 64
                    tr(psb(pt)[:, c0:c0 + 64], Gx[hs, gp, 8 * j:8 * j + 8, :].rearrange("p a b -> p (a b)"),
                       ident_bf[hs, hs], GK + ["ident_bf"], [PSK[pt]])
            cp("act", M1[:, g, :, :].rearrange("p a b -> p (a b)"), psb(pt)[:, 0:128 * MS], [PSK[pt]], ["M1"])
        wst = [RC.alloc([128, 1024], F32) for _ in range(2)]
        wi = 0

        def load_w(dst, src, rows_chunks, ncols, key, scale_col=None):
            nonlocal wi
            for c in range(rows_chunks):
                for n0 in range(0, ncols, 1024):
                    nn = min(1024, ncols - n0)
                    b = wi % 2
                    wi += 1
                    P.dma("sp", wst[b][:, 0:nn], src[128 * c:128 * c + 128, n0:n0 + nn], [], [("wst", b)])
                    eng = "act" if (wi % 2) else "dve"
                    if scale_col is not None:
                        if eng == "act":
                            act(dst[:, c, n0:n0 + nn], wst[b][:, 0:nn], AF.Copy, [("wst", b), "vecT"], [key],
                                scale=scale_col[:, c:c + 1])
                        else:
                            ts("dve", dst[:, c, n0:n0 + nn], wst[b][:, 0:nn], scale_col[:, c:c + 1], None,
                               ALU.mult, None, [("wst", b), "vecT"], [key])
                    else:
                        cp(eng, dst[:, c, n0:n0 + nn], wst[b][:, 0:nn], [("wst", b)], [key])

        load_w(win_bf, win_d, 8, 3072, "win_bf", scale_col=nd)
        load_w(glu_bf, glu_d, 4, 512, "glu_bf")
        W2.reset()
        CH_ = 2048 // L
        PTr = W2.alloc([128, 16, CH_], F32)
        PTi = W2.alloc([128, 16, CH_], F32)
        Rm = W2.alloc([128, 16, CH_], F32)
        pw = RC.alloc([128, 8, 2, 16], F32)
        dA = RC.alloc([128, 16, CH_ // 2], F32)
        dB = RC.alloc([128, 16, CH_ // 2], F32)
        memset("dve", PTr[:, :, 0:1], 1.0, ["PT"])
        memset("dve", PTi[:, :, 0:1], 0.0, ["PT"])
        cp("dve", pw[:, 0, :, :], u1, ["u1"], ["pw"])
        k_ = 0
        while (1 << k_) < CH_:
            m_ = 1 << k_
            if k_ > 0:
                pr, pi_ = pw[:, k_ - 1, 0, :], pw[:, k_ - 1, 1, :]
                tt("dve", dA[:, :, 0], pr, pr, ALU.mult, ["pw", "dA"], ["dA"])
                tt("dve", dB[:, :, 0], pi_, pi_, ALU.mult, ["pw", "dB"], ["dB"])
                tt("dve", pw[:, k_, 0, :], dA[:, :, 0], dB[:, :, 0], ALU.subtract, ["dA", "dB", "pw"], ["pw"])
                tt("dve", dA[:, :, 0], pr, pi_, ALU.mult, ["pw", "dA"], ["dA"])
                ts("dve", pw[:, k_, 1, :], dA[:, :, 0], 2.0, None, ALU.mult, None, ["dA", "pw"], ["pw"])
            shp_ = [128, 16, m_]
            br = pw[:, k_, 0, :].unsqueeze(2).to_broadcast(shp_)
            bi = pw[:, k_, 1, :].unsqueeze(2).to_broadcast(shp_)
            sr, si = PTr[:, :, 0:m_], PTi[:, :, 0:m_]
            a_, b_2 = dA[:, :, 0:m_], dB[:, :, 0:m_]
            tt("dve", a_, sr, br, ALU.mult, ["PT", "pw", "dA"], ["dA"])
            tt("dve", b_2, si, bi, ALU.mult, ["PT", "pw", "dB"], ["dB"])
            tt("dve", PTr[:, :, m_:2 * m_], a_, b_2, ALU.subtract, ["dA", "dB", "PT"], ["PT"])
            tt("dve", a_, sr, bi, ALU.mult, ["PT", "pw", "dA"], ["dA"])
            tt("dve", b_2, si, br, ALU.mult, ["PT", "pw", "dB"], ["dB"])
            tt("dve", PTi[:, :, m_:2 * m_], a_, b_2, ALU.add, ["dA", "dB", "PT"], ["PT"])
            k_ += 1
        cp("dve", Rm, rmag.unsqueeze(2).to_broadcast([128, 16, CH_]), ["rmag"], ["Rm"])
        memset("dve", Rm[:, :, 0:1], 0.0, ["Rm"])
        memset("dve", carry.rearrange("p a b -> p (a b)"), 0.0, ["carry"])
        P.barrier()

        RC.reset()
        xt = [RC.alloc([128, 1024], F32) for _ in range(2)]
        hbf = RC.alloc([128, 1024], BF16)
        hT = RC.alloc([128, 8, 512], BF16)
        uT = RC.alloc([128, 4, 512], BF16)
        zsT = RC.alloc([128, 4, 512], BF16)
        qsq = RC.alloc([128, 512], F32)
        w2_mark = W2.cur
        qns = [W2.alloc([128, 512], F32) for _ in range(2)]
        qtoks = [[W2.alloc([128, 512], BF16) for _ in range(2)] for _ in range(2)]
        qTb = RC.alloc([128, 4, 512], BF16)
        kTb = RC.alloc([128, 4, 512], BF16)
        vtok = [RC.alloc([128, 512], BF16) for _ in range(2)]
        st8 = RC.alloc([128, 8], F32)
        rt1 = RC.alloc([128, 8, 8], F32)
        rt2 = RC.alloc([128, 8, 8], F32)
        ss1 = RC.alloc([128, 4], F32)

        hTs = [hT, RC.alloc([128, 8, 512], BF16)]
        mhalf = RC.alloc([128, 8], F32)
        memset("dve", mhalf, -0.5, ["mhalf"])

        hbfs = [hbf, RC.alloc([128, 1024], BF16)]

        def rms_a(b, i):
            t0 = 512 * b
            xb_ = xt[i % 2]
            xk = ("xt", i % 2)
            hb, hk = hbfs[i % 2], ("hbf", i % 2)
            P.dma("sp", xb_, x_d[t0 + 128 * i:t0 + 128 * i + 128, :], [], [xk])
            act(hb, xb_, AF.Square, [xk], [hk, "ss1"], accum=ss1[:, 0:1])
            ts("dve", ss1[:, 1:2], ss1[:, 0:1], 1.0 / D, EPS, ALU.mult, ALU.add, ["ss1"], ["ss1b"])
            tt("pool", ss1[:, 3:4], ss1[:, 1:2], mhalf[:, 0:1], ALU.pow, ["ss1b", "mhalf"], ["ss1d"])
            act(hb, xb_, AF.Copy, [xk, "ss1d"], [hk], scale=ss1[:, 3:4])

        def rms_b(b, i):
            hb, hk = hbfs[i % 2], ("hbf", i % 2)
            for c in range(8):
                tr(psb(0)[:, 128 * c:128 * c + 128], hb[:, 128 * c:128 * c + 128], ident_bf,
                   [hk, "ident_bf"], [PSK[0]])
            cp("act", hTs[b % 2][:, :, 128 * i:128 * i + 128], psb(0).rearrange("p (a b) -> p a b", a=8),
               [PSK[0]], [("hT", b % 2, i)])

        def rms_sched(b, slot):
            order = {0: [("a", 0)], 1: [("a", 1)], 2: [("b", 0)], 3: [("a", 2)], 4: [("b", 1)], 5: [("a", 3)],
                     6: [("b", 2)], 7: [("b", 3)]}
            for kind, i in order[slot]:
                (rms_a if kind == "a" else rms_b)(b, i)

        for slot in range(8):
            rms_sched(0, slot)
        for b in range(NBLK):
            t0 = 512 * b
            hT = hTs[b % 2]
            hTK = [("hT", b % 2, i) for i in range(4)]
            for fi, f in enumerate(range(8)):
                pb = 1 + (fi % 2)
                for c in range(8):
                    mm(PS[pb][:, :], win_bf[:, c, 128 * f:128 * f + 128], hT[:, c, :], c == 0, c == 7,
                       ["win_bf"] + hTK, [PSK[pb]])
                if f < 4:
                    cp("act", uT[:, f, :], PS[pb][:, :], [PSK[pb]], [("uT", f)])
                    P.dma("act", u_s[f, :, t0:t0 + 512], uT[:, f, :], [("uT", f)], [("u_s", f, b)])
                else:
                    act(zsT[:, f - 4, :], PS[pb][:, :], AF.Silu, [PSK[pb]], [("zsT", f - 4)])
                    P.dma("act", zs_s[f - 4, :, t0:t0 + 512], zsT[:, f - 4, :], [("zsT", f - 4)], [("zs_s", f - 4, b)])
                if b + 1 < NBLK:
                    rms_sched(b + 1, fi)
            def qk_transposes(i):
                ts_ = slice(128 * i, 128 * i + 128)
                for wi_, which in enumerate(("q", "k")):
                    qt_ = qtoks[wi_][i % 2]
                    qk_ = ("qtok", wi_, i % 2)
                    pk3 = ("ps3", wi_)
                    for h in range(4):
                        tr(psb(3)[:, 512 * wi_ + 128 * h:512 * wi_ + 128 * h + 128], qt_[:, 128 * h:128 * h + 128], ident_bf,
                           [qk_, "ident_bf"], [pk3])
                    dstT = qTb if which == "q" else kTb
                    cp("act", dstT[:, :, ts_], psb(3)[:, 512 * wi_:512 * wi_ + 512].rearrange("p (a b) -> p a b", a=4),
                       [pk3], [(which + "Tb", i)])

            for i in range(4):
                ts_ = slice(128 * i, 128 * i + 128)
                for wi_, (which, col0) in enumerate((("q", 1024), ("k", 1536), ("v", 2048), ("za", 2560))):
                    pb = (4 + wi_ + 2 * (i % 2)) if wi_ < 2 else (wi_ - 1)
                    for c in range(8):
                        mm(PS[pb][:, :], hT[:, c, ts_], win_bf[:, c, col0:col0 + 512], c == 0, c == 7,
                           ["win_bf", ("hT", b % 2, i)], [PSK[pb]])
                    if which == "v":
                        vb = vtok[0]
                        cp("act", vb, PS[pb][:, :], [PSK[pb]], [("vtok", 0)])
                        P.dma("act", v_s[t0 + 128 * i:t0 + 128 * i + 128, :], vb, [("vtok", 0)],
                              [("v_s", b, i)])
                        continue
                    if which == "za":
                        vb = vtok[1]
                        act(vb, PS[pb][:, :], AF.Silu, [PSK[pb]], [("vtok", 1)])
                        P.dma("act", za_s[t0 + 128 * i:t0 + 128 * i + 128, :], vb, [("vtok", 1)],
                              [("za_s", b, i)])
                        continue
                    wq = bcq if which == "q" else bck
                    qn = qns[wi_]
                    qnk = "qn%d" % wi_
                    act(qsq, PS[pb][:, :], AF.Square, [PSK[pb]], ["qsq"])
                    P.op("dve", lambda e: e.tensor_reduce(st8, qsq.rearrange("p (a b) -> p a b", a=8), AX.X, ALU.add),
                         ["qsq"], ["st8"])
                    ts("dve", st8, st8, 1.0 / 64, EPS, ALU.mult, ALU.add, ["st8"], ["st8"])
                    tt("pool", st8, st8, mhalf, ALU.pow, ["st8", "mhalf"], ["st8"])
                    q3 = qn.rearrange("p (a b) -> p a b", a=8)
                    tt("dve", q3, PS[pb][:, :].rearrange("p (a b) -> p a b", a=8), bc3(st8, [128, 8, 64], 2),
                       ALU.mult, [PSK[pb], "st8"], [qnk])
                    tt("dve", qn, qn, wq, ALU.mult, [qnk, "bcq", "bck"], [qnk])
                    tile_idx = 4 * b + i
                    cs = cosT[:, tile_idx, :].unsqueeze(1).to_broadcast([128, 8, 8])
                    sn = sinT[:, tile_idx, :].unsqueeze(1).to_broadcast([128, 8, 8])
                    x1, x2 = q3[:, :, 0:8], q3[:, :, 8:16]
                    tt("dve", rt1, x1, cs, ALU.mult, [qnk, "angc"], ["rt1"])
                    tt("dve", rt2, x2, sn, ALU.mult, [qnk, "angs"], ["rt2"])
                    tt("dve", rt1, rt1, rt2, ALU.subtract, ["rt1", "rt2"], ["rt1"])
                    tt("dve", rt2, x2, cs, ALU.mult, [qnk, "angc", "rt2"], ["rt2"])
                    tt("dve", x2, x1, sn, ALU.mult, [qnk, "angs"], [qnk])
                    tt("dve", x2, x2, rt2, ALU.add, [qnk, "rt2"], [qnk])
                    cp("dve", x1, rt1, ["rt1", qnk], [qnk])
                    cp("pool", qtoks[wi_][i % 2], qn, [qnk], [("qtok", wi_, i % 2)])
                if i >= 1:
                    qk_transposes(i - 1)
            qk_transposes(3)
            for h in range(4):
                P.dma("act", qT_s[h, :, t0:t0 + 512], qTb[:, h, :], [("qTb", i) for i in range(4)], [("qT_s", h, b)])
                P.dma("act", kT_s[h, :, t0:t0 + 512], kTb[:, h, :], [("kTb", i) for i in range(4)], [("kT_s", h, b)])
        P.barrier()
        HT = 2048
        CH = HT // L
        SCH = HT // 8
        RA1 = Region(arena, RA.base, 49152)
        RC.reset()
        uTh = RA1.alloc([128, 4, HT], BF16)
        Ub = RA1.alloc([128, 32, SCH], BF16)
        Zt = [RA1.alloc([128, 16, CH], F32) for _ in range(2)]
        Zt += [RC.alloc([128, 16, CH], F32) for _ in range(4)]
        Zr, Zi, Zmr, Zmi, tA, tB = Zt
        zsl = [RC.alloc([128, 4, 512], BF16) for _ in range(2)]
        y2ps = [RC.alloc([128, SCH, 2], F32) for _ in range(2)]
        ge1s = [RC.alloc([128, SCH, 2], F32) for _ in range(2)]
        gsg = RC.alloc([128, 512], F32)
        ysb = [RC.alloc([128, 512], BF16) for _ in range(2)]
        W2.cur = w2_mark
        Xbf = [W2.alloc([128, 16, CH], BF16) for _ in range(2)]
        y2b_h = Ub.rearrange("p a b -> p (a b)").rearrange("p (c t) -> p c t", c=4)
        UK = [("Ub", g) for g in range(32)]
        f3 = lambda a: a.rearrange("p a b -> p (a b)")
        for hh in range(2):
            T0 = HT * hh
            for f in range(4):
                P.dma("sp", uTh[:, f, :], u_s[f, :, T0:T0 + HT], [("u_s", f, b_) for b_ in range(4 * hh, 4 * hh + 4)],
                      [("uTh", f)])
            for g in range(32):
                g8, gl = g // 8, g % 8
                pb = (g // 2) % 2
                for s_ in range(8):
                    j_, e_ = s_ // 2, s_ % 2
                    mm(PS[pb][32 * j_:32 * j_ + 32, SCH * (g % 2):SCH * (g % 2) + SCH],
                       Wsel[:, gl, 112 - 16 * e_:144 - 16 * e_], uTh[:, g8, s_:HT:8], e_ == 0, e_ == 1,
                       ["Wsel", ("uTh", g8)], [PSK[pb]], tp=(0, 32 * j_))
                if g % 2 == 1:
                    cp("act", Ub[:, g - 1:g + 1, :].rearrange("p a b -> p (a b)"), PS[pb][:, :], [PSK[pb]],
                       [("Ub", g - 1), ("Ub", g)])
            for gq in range(4):
                pz = 4 + 2 * (gq % 2)
                for g in range(8 * gq, 8 * gq + 8):
                    gp, hf = g // 2, g % 2
                    hs = slice(64 * hf, 64 * hf + 64)
                    for ri in range(2):
                        for j in range(MS):
                            mm(PS[pz + ri][hs, CH * (gp % 4):CH * (gp % 4) + CH], M1[:, g, j, 64 * ri:64 * ri + 64],
                               Ub[:, g, j:SCH:MS], j == 0, j == MS - 1, ["M1", ("Ub", g)], [PSK[pz + ri]])
                cp("dve", f3(Zr[:, 4 * gq:4 * gq + 4, :]), PS[pz][:, :], [PSK[pz]], [("Zr", gq)])
                cp("act", f3(Zi[:, 4 * gq:4 * gq + 4, :]), PS[pz + 1][:, :], [PSK[pz + 1]], [("Zi", gq)])
            ZrK = [("Zr", q_) for q_ in range(4)]
            ZiK = [("Zi", q_) for q_ in range(4)]
            a0r, a0i = APr[:, 0, :], APi[:, 0, :]
            cr_, ci_ = carry[:, 0, :], carry[:, 1, :]
            t1, t2 = tA[:, :, 0], tB[:, :, 0]
            tt("dve", t1, a0r, cr_, ALU.mult, ["APr", "carry", "tA"], ["tA"])
            tt("dve", t2, a0i, ci_, ALU.mult, ["APi", "carry", "tB"], ["tB"])
            tt("dve", t1, t1, t2, ALU.subtract, ["tA", "tB"], ["tA"])
            tt("dve", Zr[:, :, 0], Zr[:, :, 0], t1, ALU.add, ZrK + ["tA"], ZrK)
            tt("dve", t1, a0r, ci_, ALU.mult, ["APr", "carry", "tA"], ["tA"])
            tt("dve", t2, a0i, cr_, ALU.mult, ["APi", "carry", "tB"], ["tB"])
            tt("dve", t1, t1, t2, ALU.add, ["tA", "tB"], ["tA"])
            tt("dve", Zi[:, :, 0], Zi[:, :, 0], t1, ALU.add, ZiK + ["tA"], ZiK)
            tt("dve", tA, PTr, Zr, ALU.mult, ["PT", "tA"] + ZrK, ["tA"])
            tt("pool", tB, PTi, Zi, ALU.mult, ["PT", "tB"] + ZiK, ["tB"])
            tt("dve", Zmr, tA, tB, ALU.add, ["tA", "tB", "Zmr"], ["Zmr"])
            tt("dve", tA, PTr, Zi, ALU.mult, ["PT", "tA"] + ZiK, ["tA"])
            tt("pool", tB, PTi, Zr, ALU.mult, ["PT", "tB"] + ZrK, ["tB"])
            tt("dve", Zmi, tA, tB, ALU.subtract, ["tA", "tB", "Zmi"], ["Zmi"])
            P.op("dve", lambda e: e.tensor_tensor_scan(f3(Zr), f3(Rm), f3(Zmr), 0.0, ALU.mult, ALU.add),
                 ["Rm", "Zmr"] + ZrK, ZrK)
            P.op("dve", lambda e: e.tensor_tensor_scan(f3(Zi), f3(Rm), f3(Zmi), 0.0, ALU.mult, ALU.add),
                 ["Rm", "Zmi"] + ZiK, ZiK)
            tt("dve", tA, PTr, Zr, ALU.mult, ["PT", "tA"] + ZrK, ["tA"])
            tt("pool", tB, PTi, Zi, ALU.mult, ["PT", "tB"] + ZiK, ["tB"])
            tt("dve", Zmr, tA, tB, ALU.subtract, ["tA", "tB", "Zmr"], ["Zmr"])
            tt("dve", tA, PTr, Zi, ALU.mult, ["PT", "tA"] + ZiK, ["tA"])
            tt("pool", tB, PTi, Zr, ALU.mult, ["PT", "tB"] + ZrK, ["tB"])
            tt("dve", Zmi, tA, tB, ALU.add, ["tA", "tB", "Zmi"], ["Zmi"])
            for ri, (Xs, xk_) in enumerate(((Zmr, "Zmr"), (Zmi, "Zmi"))):
                cp("dve", Xbf[ri][:, :, 1:CH], Xs[:, :, 0:CH - 1], [xk_], [("Xbf", ri)])
                cp("dve", Xbf[ri][:, :, 0], carry[:, ri, :], ["carry", ("Xbf", ri)], [("Xbf", ri)])
            for ri, (Xs, xk_) in enumerate(((Zmr, "Zmr"), (Zmi, "Zmi"))):
                cp("dve", carry[:, ri, :], Xs[:, :, CH - 1], [xk_, ("Xbf", 0), ("Xbf", 1), "carry"], ["carry"])
            for g in range(32):
                gp, hf = g // 2, g % 2
                hs = slice(64 * hf, 64 * hf + 64)
                pb = (g // 2) % 2
                for j in range(MS):
                    o = PS[pb][:, SCH * (g % 2) + j:SCH * (g % 2) + SCH:MS]
                    for jp in range(j + 1):
                        mm(o, Tt[:, g, j - jp, :], Ub[:, g, jp:SCH:MS], jp == 0, False, ["Tt", ("Ub", g)], [PSK[pb]])
                    mm(o, Hr[hs, gp, 8 * j + 1:8 * j + 9, :].rearrange("p a b -> p (a b)"), Xbf[0][hs, gp, :],
                       False, False, HK + [("Xbf", 0)], [PSK[pb]])
                    mm(o, nHi[hs, gp, 8 * j + 1:8 * j + 9, :].rearrange("p a b -> p (a b)"), Xbf[1][hs, gp, :],
                       False, True, HK + [("Xbf", 1)], [PSK[pb]])
                if g % 2 == 1:
                    cp("act", Ub[:, g - 1:g + 1, :].rearrange("p a b -> p (a b)"), PS[pb][:, :], [PSK[pb]],
                       [("Ub", g - 1), ("Ub", g)])
            for ct in range(4):
                CK = [("Ub", 8 * ct + gl) for gl in range(8)]
                for t_ in range(8):
                    pbk = 4 + t_ // 2
                    for gl in range(8):
                        j_, e_ = gl // 2, gl % 2
                        mm(PS[pbk][32 * j_:32 * j_ + 32, SCH * (t_ % 2):SCH * (t_ % 2) + SCH],
                           Wsel[:, t_, 112 - 16 * e_:144 - 16 * e_], Ub[:, 8 * ct + gl, :], e_ == 0, e_ == 1,
                           ["Wsel", ("Ub", 8 * ct + gl)], [PSK[pbk]], tp=(0, 32 * j_))
                for tq in range(4):
                    pbk = 4 + tq
                    y2p, ge1 = y2ps[tq % 2], ge1s[tq % 2]
                    yk, gk = ("y2p", tq % 2), ("ge1", tq % 2)
                    uview = uTh[:, ct, :].rearrange("p (a b) -> p a b", b=8)[:, :, 2 * tq:2 * tq + 2]
                    stt(y2p, uview, dsk[:, ct:ct + 1], PS[pbk][:, :].rearrange("p (b a) -> p a b", b=2),
                        ALU.mult, ALU.add, [("uTh", ct), "vecT", PSK[pbk]], [yk])
                    yf, gf = f3(y2p), f3(ge1)
                    tt("pool", gf, yf, yf, ALU.mult, [yk], [gk])
                    ts("dve", gf, gf, 0.044715, 1.0, ALU.mult, ALU.add, [gk], [gk])
                    tt("dve", gf, gf, yf, ALU.mult, [gk, yk], [gk])
                    act(gf, gf, AF.Sigmoid, [gk], [gk], scale=1.5957691216057308)
                    tt("dve", y2b_h[:, ct, :].rearrange("p (a b) -> p a b", b=8)[:, :, 2 * tq:2 * tq + 2], y2p, ge1,
                       ALU.mult, [yk, gk] + CK, CK)
            for bi in range(4):
                bg = 4 * hh + bi
                tb0 = 512 * bi
                zl = zsl[bi % 2]
                for f in range(4):
                    P.dma("sp", zl[:, f, :], zs_s[f, :, T0 + tb0:T0 + tb0 + 512], [("zs_s", f, bg)], [("zsl", bi % 2, f)])
                for fo in range(4):
                    pb = fo % 2
                    for ci in range(4):
                        mm(PS[pb][:, :], glu_bf[:, ci, 128 * fo:128 * fo + 128], y2b_h[:, ci, tb0:tb0 + 512], ci == 0, ci == 3,
                           ["glu_bf"] + UK, [PSK[pb]])
                    act(gsg, PS[pb][:, :], AF.Sigmoid, [PSK[pb]], ["gsg"], bias=glb[:, fo:fo + 1])
                    tt("dve", gsg, gsg, y2b_h[:, fo, tb0:tb0 + 512], ALU.mult, ["gsg"] + UK, ["gsg"])
                    yo = ysb[fo % 2]
                    tt("dve", yo, gsg, zl[:, fo, :], ALU.mult, ["gsg", ("zsl", bi % 2, fo), ("ysb", fo % 2)], [("ysb", fo % 2)])
                    P.dma("sp", ys_s[fo, :, T0 + tb0:T0 + tb0 + 512], yo, [("ysb", fo % 2)], [("ys_s", fo, bg)])
        P.barrier()
        fin_keys = []
        if debug:
            RC.reset()
            dtile = RC.alloc([128, 4096], BF16)
            for nm, src in (("ys", ys_s), ("qT", qT_s), ("kT", kT_s)):
                for f in range(4):
                    P.dma("sp", dtile, src[f], [], ["dtile"])
                    P.dma("sp", dbg[nm][f], dtile, ["dtile"], [("dbg", nm, f)])
                    fin_keys.append(("dbg", nm, f))
            for nm, src in (("v", v_s), ("za", za_s)):
                for i in range(32):
                    P.dma("sp", dtile[:, 0:512], src[128 * i:128 * i + 128, :], [], ["dtile"])
                    P.dma("sp", dbg[nm][128 * i:128 * i + 128, :], dtile[:, 0:512], ["dtile"], [("dbg", nm, i)])
                    fin_keys.append(("dbg", nm, i))
            P.barrier()

        RAB = Region(arena, RA.base, RA.size + RB.size)
        RC.reset()
        kT_res = RAB.alloc([128, 4, S], BF16)
        v_res = RAB.alloc([128, 32, 4, 129], BF16)
        qTl = [RAB.alloc([128, 4, 512], BF16) for _ in range(2)]
        zatok = RAB.alloc([128, 4, 512], BF16)
        ysl = RAB.alloc([128, 4, 512], BF16)
        PTt = [RAB.alloc([128, 2, 512], BF16) for _ in range(4)]
        yaT = RAB.alloc([128, 4, 512], BF16)
        rs = RC.alloc([128, 8], F32)
        rsn = RC.alloc([128, 4], F32)
        o_all = RC.alloc([128, 4, 512], F32)
        sqt = RC.alloc([128, 512], F32)
        ss4 = RC.alloc([128, 4], F32)
        yatok = RC.alloc([128, 512], BF16)
        xl = [RC.alloc([128, 1024], F32) for _ in range(2)]
        xnews = [RC.alloc([128, 1024], F32) for _ in range(2)]
        xnbs = [RC.alloc([128, 1024], BF16) for _ in range(2)]
        xnT = RC.alloc([128, 8, 128], BF16)
        pl = [RC.alloc([128, 256], F32) for _ in range(2)]
        pT = RC.alloc([128, 2, 128], BF16)
        gate = RC.alloc([128, 1024], F32)
        wst = [RC.alloc([128, 1024], F32) for _ in range(2)]
        tri = cmask[:, 0, 0:128]
        mhalf4 = RC.alloc([128, 4], F32)
        memset("dve", mhalf4, -0.5, ["mhalf4"])
        for h in range(4):
            P.dma("sp", kT_res[:, h, :], kT_s[h], [("kT_s", h, b_) for b_ in range(NBLK)], [("kT_res", h)])
        memset("dve", v_res[:, :, :, 128:129], 1.0, ["v_ones"])
        for i in range(32):
            P.dma("sp", v_res[:, i, :, 0:128], v_s[128 * i:128 * i + 128, :].rearrange("p (a b) -> p a b", a=4),
                  [("v_s", i // 4, i % 4)], [("v_res", i)])

        def load_q(b):
            for h in range(4):
                P.dma("sp", qTl[b % 2][:, h, :], qT_s[h, :, 512 * b:512 * b + 512], [("qT_s", h, b)], [("qTl", b % 2, h)])

        def load_x(b, i):
            tok = slice(512 * b + 128 * i, 512 * b + 128 * i + 128)
            P.dma("sp", xl[i % 2], x_d[tok, :], [], [("xl", i % 2)])
            P.dma("sp", pl[i % 2], p_d[tok, :], [], [("pl", i % 2)])

        load_q(0)
        load_w(wout_bf, wout_d, 8, 1024, "wout_bf")
        load_w(pg_bf, pg_d, 8, 1024, "pg_bf")
        load_w(pp_bf, pp_d, 2, 1024, "pp_bf")
        OBk = [PS[4], PS[5]]
        zatoks = [zatok, RAB.alloc([128, 4, 512], BF16)]
        ysls = [ysl, RC.alloc([128, 4, 512], BF16)]
        pti = 0

        def make_tail_units(b):
            t0 = 512 * b
            zat, ysl_ = zatoks[b % 2], ysls[b % 2]

            def stage_a(i):
                ts_ = slice(128 * i, 128 * i + 128)
                oK = [("o_all", i, h) for h in range(4)]
                oq = o_all[:, i, :]
                act(sqt, oq, AF.Square, oK, ["sqt"])
                P.op("dve", lambda e: e.tensor_reduce(ss4, sqt.rearrange("p (a b) -> p a b", a=4), AX.X, ALU.add),
                     ["sqt"], ["ss4"])
                ts("dve", ss4, ss4, 1.0 / 128, EPS, ALU.mult, ALU.add, ["ss4"], ["ss4"])
                tt("pool", ss4, ss4, mhalf4, ALU.pow, ["ss4", "mhalf4"], ["ss4"])
                o3 = oq.rearrange("p (a b) -> p a b", a=4)
                tt("dve", o3, o3, bc3(ss4, [128, 4, 128], 2), ALU.mult, oK + ["ss4"], oK)
                tt("dve", o3, o3, bcsw.unsqueeze(1).to_broadcast([128, 4, 128]), ALU.mult, oK + ["bcsw"], oK)
                tt("dve", yatok, oq, zat[:, i, :], ALU.mult, oK + [("zatok", b % 2, i)], ["yatok"])
                yield
                for h in range(4):
                    tr(psb(7)[:, 128 * h:128 * h + 128], yatok[:, 128 * h:128 * h + 128], ident_bf,
                       ["yatok", "ident_bf"], [PSK[7]])
                    if h % 2 == 1:
                        yield
                cp("dve", yaT[:, :, ts_], psb(7)[:, 0:512].rearrange("p (a b) -> p a b", a=4), [PSK[7]], [("yaT", i)])
                yield

            def stage_b(i):
                ts_ = slice(128 * i, 128 * i + 128)
                xb_ = xl[i % 2]
                xk = ("xl", i % 2)
                xn = xnews[i % 2]
                for hf in range(2):
                    for c in range(8):
                        src = ysl_ if c < 4 else yaT
                        kk = ("ysl", b % 2, c) if c < 4 else ("yaT", i)
                        mm(PS[6][:, :], src[:, c % 4, ts_], wout_bf[:, c, 512 * hf:512 * hf + 512], c == 0, c == 7,
                           [kk, "wout_bf"], [PSK[6]])
                        if c % 2 == 1:
                            yield
                    tt("dve", xn[:, 512 * hf:512 * hf + 512], PS[6][:, :], xb_[:, 512 * hf:512 * hf + 512],
                       ALU.add, [PSK[6], xk], [("xnew", i % 2, hf)])
                    cp("pool", xnbs[i % 2][:, 512 * hf:512 * hf + 512], xn[:, 512 * hf:512 * hf + 512],
                       [("xnew", i % 2, hf)], [("xnb", i % 2, hf)])
                    yield

            def stage_c1(i):
                plk = ("pl", i % 2)
                for c in range(8):
                    tr(psb(7)[:, 128 * c:128 * c + 128], xnbs[i % 2][:, 128 * c:128 * c + 128], ident_bf,
                       [("xnb", i % 2, c // 4), "ident_bf"], [PSK[7]])
                    if c % 2 == 1:
                        yield
                cp("dve", xnT.rearrange("p a b -> p (a b)"), psb(7)[:, :], [PSK[7]], ["xnT"])
                for c in range(2):
                    tr(PS[6][:, 128 * c:128 * c + 128], pl[i % 2][:, 128 * c:128 * c + 128], ident_f, [plk, "ident_f"],
                       [PSK[6]])
                cp("dve", pT.rearrange("p a b -> p (a b)"), PS[6][:, 0:256], [PSK[6]], ["pT"])
                yield

            def stage_c2(i, hf):
                xb_ = xl[i % 2]
                xk = ("xl", i % 2)
                xn = xnews[i % 2]
                hsl = slice(512 * hf, 512 * hf + 512)
                pk_ = PSK[6]
                for c in range(8):
                    mm(PS[6][:, :], xnT[:, c, :], pg_bf[:, c, hsl], c == 0, c == 7, ["xnT", "pg_bf"], [pk_])
                    if c % 2 == 1:
                        yield
                act(gate[:, hsl], PS[6][:, :], AF.Tanh, [pk_], [("gate", hf)], scale=0.5)
                for c in range(2):
                    mm(PS[6][:, :], pT[:, c, :], pp_bf[:, c, hsl], c == 0, c == 1, ["pT", "pp_bf"], [pk_])
                stt(gate[:, hsl], gate[:, hsl], 1.0, PS[6][:, :], ALU.add, ALU.mult, [("gate", hf), pk_], [("gate", hf)])
                stt(xb_[:, hsl], gate[:, hsl], 0.5, xn[:, hsl], ALU.mult, ALU.add, [("gate", hf), ("xnew", i % 2, hf), xk], [xk])
                yield

            def store(i):
                tok = slice(t0 + 128 * i, t0 + 128 * i + 128)
                P.dma("sp", out_d[tok, :], xl[i % 2], [("xl", i % 2)], [("out", b, i)])
                fin_keys.append(("out", b, i))

            def gen():
                load_x(b, 0)
                load_x(b, 1)
                for i in range(4):
                    yield from stage_a(i)
                for i in range(4):
                    yield from stage_b(i)
                    yield from stage_c1(i)
                    yield from stage_c2(i, 0)
                    yield from stage_c2(i, 1)
                    store(i)
                    if i + 2 < 4:
                        load_x(b, i + 2)
                    yield

            return gen(), 112

        qzall = wst[1].bitcast(BF16)
        qz = [[qzall[:, 512 * (2 * c_ + hb_):512 * (2 * c_ + hb_) + 512] for hb_ in range(2)] for c_ in range(2)]
        for c_ in range(2):
            for hb_ in range(2):
                memset("pool", qz[c_][hb_], 0.0, [("qz", c_, hb_), ("wst", 1)])
        pending, pend_left = None, 0

        def advance(n):
            nonlocal pending, pend_left
            for _ in range(n):
                if pending is None:
                    return
                try:
                    next(pending)
                    pend_left = max(pend_left - 1, 1)
                except StopIteration:
                    pending, pend_left = None, 0

        for b in range(NBLK):
            t0 = 512 * b
            qb = qTl[b % 2]
            if b + 1 < NBLK:
                load_q(b + 1)
            for i in range(4):
                P.dma("sp", zatoks[b % 2][:, i, :], za_s[t0 + 128 * i:t0 + 128 * i + 128, :], [("za_s", b, i)],
                      [("zatok", b % 2, i)])
            for h in range(4):
                P.dma("sp", ysls[b % 2][:, h, :], ys_s[h, :, t0:t0 + 512], [("ys_s", h, b)], [("ysl", b % 2, h)])
            nkt = 4 * (b + 1)
            iters = []
            for h in range(4):
                for qh in range(2):
                    for kt in range(nkt):
                        j = kt - 4 * b
                        if j >= 0 and 128 * j >= 256 * (qh + 1):
                            continue
                        iters.append((h, kt, qh))
            n_it = len(iters)

            qz_done = set()

            def scores(it):
                h, kt, qh = it
                buf = scores.cnt % 4
                scores.cnt += 1
                j = kt - 4 * b
                q0 = max(128 * max(j, 0) - 256 * qh, 0)
                ks = slice(128 * kt, 128 * kt + 128)
                if h not in qz_done:
                    qz_done.add(h)
                    for c in range(2):
                        hs = slice(64 * c, 64 * c + 64)
                        cp("pool", qz[c][h % 2][hs, :], qb[hs, h, :], [("qTl", b % 2, h)], [("qz", c, h % 2)])
                for c in range(2):
                    mm(PS[buf][:, 256 * c + q0:256 * c + 256], kT_res[:, h, ks],
                       qz[c][h % 2][:, 256 * qh + q0:256 * qh + 256], True, True,
                       [("kT_res", h), ("qz", c, h % 2)], [("SC", buf)])
                return buf, q0

            scores.cnt = 0
            LA = 3
            sq_ = [scores(iters[k_]) for k_ in range(min(LA, n_it))]
            started = {}
            for idx, it in enumerate(iters):
                h, kt, qh = it
                if idx + LA < n_it:
                    sq_.append(scores(iters[idx + LA]))
                buf, q0 = sq_.pop(0)
                j = kt - 4 * b
                pt_i = pti % 4
                pti += 1
                pt = PTt[pt_i]
                pk = ("PT", pt_i)
                act(pt[:, :, q0:256], PS[buf].rearrange("p (c q) -> p c q", c=2)[:, :, q0:256], AF.Exp,
                    [("SC", buf)], [pk])
                if j >= 0 and 128 * j >= 256 * qh:
                    tt("pool", pt[:, :, q0:q0 + 128], pt[:, :, q0:q0 + 128],
                       tri.unsqueeze(1).to_broadcast([128, 2, 128]), ALU.mult, [pk, "cmask"], [pk])
                for qt in range(max(j, 2 * qh), 2 * qh + 2):
                    for c in range(2):
                        r = 2 * (qt - 2 * qh) + c
                        bank, col0 = r // 3, (r % 3) * 129
                        st_ = (h, qh, bank) not in started
                        started[(h, qh, bank)] = True
                        ql = 128 * (qt - 2 * qh)
                        lhs = pt[:, c, ql:ql + 128]
                        o_ap = OBk[bank][:, col0:col0 + 129]
                        rhs_ = v_res[:, kt, h, :]
                        P.op("pe", lambda e, o_ap=o_ap, lhs=lhs, rhs_=rhs_, st_=st_, sp_=False:
                             e.matmul(o_ap, lhsT=lhs, rhs=rhs_, start=st_, stop=sp_, skip_group_check=True),
                             [pk, ("v_res", kt), "v_ones"], [("OB", bank)])
                last_of_head = (idx + 1 == n_it) or (iters[idx + 1][0] != h) or (iters[idx + 1][2] != qh)
                if last_of_head:
                    for bank in range(2):
                        nreg = 3 if bank < 1 else 1
                        src = OBk[bank][:, 128:128 + 129 * (nreg - 1) + 1:129]
                        dst = rs[:, 3 * bank:3 * bank + nreg]
                        P.op("dve", lambda e, dst=dst, src=src: e.reciprocal(dst, src), [("OB", bank)], ["rs"])
                    ts("dve", rsn[:, 0:2], rs[:, 1:4:2], lamv[:, 1:2], None, ALU.mult, None, ["rs", "lamv"], ["rsn"])
                    for qt in range(2 * qh, 2 * qh + 2):
                        r0_, r1_ = 2 * (qt - 2 * qh), 2 * (qt - 2 * qh) + 1
                        oa = o_all[:, qt, 128 * h:128 * h + 128]
                        ts("dve", oa, OBk[r0_ // 3][:, (r0_ % 3) * 129:(r0_ % 3) * 129 + 128], rs[:, r0_:r0_ + 1], None,
                           ALU.mult, None, [("OB", r0_ // 3), "rs"], [("o_all", qt, h)])
                        stt(oa, OBk[r1_ // 3][:, (r1_ % 3) * 129:(r1_ % 3) * 129 + 128], rsn[:, qt - 2 * qh:qt - 2 * qh + 1], oa,
                            ALU.mult, ALU.add, [("OB", r1_ // 3), "rsn", ("o_all", qt, h)], [("o_all", qt, h)])
                if pending is not None:
                    advance(-(-pend_left // max(n_it - idx - 8, 1)))
            advance(10 ** 6)
            pending, pend_left = make_tail_units(b)
        advance(10 ** 6)
        P.emit(final_keys=fin_keys)
    return nc


_NC_CACHE = {}


def _core_inputs(b, x, p, positions, norm_w, w_in, ssm_lambda_re, ssm_lambda_im, ssm_log_dt,
                 ssm_b_re, ssm_b_im, ssm_c_re, ssm_c_im, ssm_d, glu_w, glu_b,
                 q_norm_w, k_norm_w, lambda_q1, lambda_k1, lambda_q2, lambda_k2,
                 subln_w, w_out, ple_w_proj, ple_w_gate):
    f = lambda a: np.ascontiguousarray(np.asarray(a, dtype=np.float32))
    vecs = np.concatenate([f(norm_w[0]).reshape(8, 128), f(ssm_d[0]).reshape(4, 128),
                           f(glu_b[0]).reshape(4, 128), f(subln_w[0]).reshape(1, 128)], axis=0)
    rows = np.concatenate([f(q_norm_w[0]), f(k_norm_w[0]), f(lambda_q1[0]), f(lambda_k1[0]),
                           f(lambda_q2[0]), f(lambda_k2[0]), f(subln_w[0])]).reshape(1, 512)
    lam = np.stack([f(ssm_lambda_re[0]).reshape(16, 128), f(ssm_lambda_im[0]).reshape(16, 128)], axis=1)
    return {
        "x": f(x[b]), "p": f(p[0, b]),
        "pos": np.ascontiguousarray(np.asarray(positions[b], dtype=np.int32).reshape(32, 128)),
        "vecs": np.ascontiguousarray(vecs), "rows": np.ascontiguousarray(rows),
        "w_in": f(w_in[0]), "lam": np.ascontiguousarray(lam), "log_dt": f(ssm_log_dt[0]).reshape(16, 2),
        "b_re": f(ssm_b_re[0]), "b_im": f(ssm_b_im[0]), "c_re": f(ssm_c_re[0]), "c_im": f(ssm_c_im[0]),
        "glu_w": f(glu_w[0]), "w_out": f(w_out[0]), "ple_w_proj": f(ple_w_proj[0]), "ple_w_gate": f(ple_w_gate[0]),
    }


def kernel(**inputs):
    if "nc" not in _NC_CACHE:
        _NC_CACHE["nc"] = build_program(DEBUG)
    nc = _NC_CACHE["nc"]
    in_maps = [_core_inputs(b, **inputs) for b in range(8)]
    res = run_bass_kernel_spmd(nc, in_maps, core_ids=list(range(8)))
    out = np.stack([np.asarray(r["out"], dtype=np.float32) for r in res.results], axis=0)
    return out
```

```python
import math
import contextlib
import numpy as np
import concourse.bass as bass
import concourse.mybir as mybir
from concourse.bass_utils import run_bass_kernel_spmd

F32 = mybir.dt.float32
BF16 = mybir.dt.bfloat16
I32 = mybir.dt.int32
ALU = mybir.AluOpType
AF = mybir.ActivationFunctionType
AX = mybir.AxisListType

SAME_ENGINE_SYNC = True
N_DMA_SEMS = 48
DEBUG = False

S = 4096
D = 1024
NBLK = 8
MS = 2
L = 8 * MS
NG = 8 * MS + 7
NH = 8 * MS + 1
NE = NG + NH
CPB = 512 // L
SCB = 64
EPS = 1e-6
TWO_PI = 2.0 * math.pi
CW1 = 6.28125
CW2 = TWO_PI - 6.28125
LAMBDA_INIT = 0.8 - 0.6 * math.exp(0.0)


class _Op:
    __slots__ = ("eng", "fn", "deps", "is_dma", "sem", "semval", "signal", "signo", "idx")


class Prog:
    ENGS = ("pe", "act", "dve", "pool", "sp")

    def __init__(self, nc):
        self.nc = nc
        self.ops = []
        self.last_w = {}
        self.readers = {}
        self.dma_rr = 0
        self.dma_sem_total = [0] * N_DMA_SEMS
        self.dma_sem_lastop = [None] * N_DMA_SEMS
        self.bar_deps = []
        self.need_bar = {e: False for e in self.ENGS}
        self.last_eng_op = {}

    def barrier(self):
        deps = [o for o in self.last_eng_op.values()]
        deps += [o for o in self.dma_sem_lastop if o is not None]
        self.bar_deps = deps
        for e in self.ENGS:
            self.need_bar[e] = True

    def _add(self, eng, fn, R, W, is_dma):
        op = _Op()
        op.eng, op.fn, op.is_dma = eng, fn, is_dma
        op.signal = False
        op.signo = 0
        op.sem = None
        op.semval = 0
        op.idx = len(self.ops)
        deps = []
        if self.need_bar[eng]:
            deps += self.bar_deps
            self.need_bar[eng] = False
        for k in R:
            w = self.last_w.get(k)
            if w is not None:
                deps.append(w)
        for k in W:
            w = self.last_w.get(k)
            if w is not None:
                deps.append(w)
            for r in self.readers.get(k, ()):
                deps.append(r)
        if is_dma:
            s = self.dma_rr
            self.dma_rr = (self.dma_rr + 1) % N_DMA_SEMS
            prev = self.dma_sem_lastop[s]
            if prev is not None:
                deps.append(prev)
            self.dma_sem_total[s] += 16
            op.sem = s
            op.semval = self.dma_sem_total[s]
            self.dma_sem_lastop[s] = op
        seen = set()
        dd = []
        for d in deps:
            if d is op or id(d) in seen:
                continue
            seen.add(id(d))
            if (not d.is_dma) and d.eng == eng and (eng == "pe" or not SAME_ENGINE_SYNC):
                continue
            dd.append(d)
            if not d.is_dma:
                d.signal = True
        op.deps = dd
        for k in W:
            self.last_w[k] = op
            self.readers[k] = []
        for k in R:
            if k not in W:
                self.readers.setdefault(k, []).append(op)
        self.ops.append(op)
        if not is_dma:
            self.last_eng_op[eng] = op
        return op

    def op(self, eng, fn, R=(), W=()):
        return self._add(eng, fn, tuple(R), tuple(W), False)

    def dma(self, q, out, in_, R=(), W=()):
        return self._add(q, lambda e: e.dma_start(out=out, in_=in_), tuple(R), tuple(W), True)

    def emit(self, final_keys=()):
        nc = self.nc
        self._add("sp", None, tuple(final_keys), (), False)
        cnt = {e: 0 for e in self.ENGS}
        for o in self.ops:
            if (not o.is_dma) and o.signal:
                cnt[o.eng] += 1
                o.signo = cnt[o.eng]
        with contextlib.ExitStack() as st:
            esem = {e: st.enter_context(nc.semaphore("sem_" + e)) for e in self.ENGS}
            dsem = [st.enter_context(nc.semaphore("dsem%d" % i)) for i in range(N_DMA_SEMS)]
            block = st.enter_context(nc.Block())
            per = {e: [o for o in self.ops if o.eng == e] for e in self.ENGS}

            def replay(e, eng):
                waited = {}
                for o in per[e]:
                    for d in o.deps:
                        if d.is_dma:
                            key, val, sem = ("d", d.sem), d.semval, dsem[d.sem]
                        else:
                            key, val, sem = ("e", d.eng), d.signo, esem[d.eng]
                        if waited.get(key, 0) >= val:
                            continue
                        waited[key] = val
                        eng.wait_ge(sem, val)
                    if o.fn is None:
                        continue
                    ins = o.fn(eng)
                    if o.is_dma:
                        ins.then_inc(dsem[o.sem], 16)
                    elif o.signal:
                        ins.then_inc(esem[e], 1)

            @block.sync
            def _(eng):
                replay("sp", eng)

            @block.scalar
            def _(eng):
                replay("act", eng)

            @block.vector
            def _(eng):
                replay("dve", eng)

            @block.gpsimd
            def _(eng):
                replay("pool", eng)

            @block.tensor
            def _(eng):
                replay("pe", eng)


class Region:
    def __init__(self, arena, base, size):
        self.arena, self.base, self.size, self.cur = arena, base, size, 0

    def reset(self):
        self.cur = 0

    def alloc(self, shape, dt, parts=None):
        esz = 2 if dt == BF16 else 4
        n = 1
        for s_ in shape[1:]:
            n *= s_
        nbytes = (n * esz + 31) // 32 * 32
        off = self.base + self.cur
        self.cur += nbytes
        assert self.cur <= self.size, ("region overflow", self.cur, self.size)
        v = self.arena[0:shape[0], off // 4:(off + nbytes) // 4]
        if dt != F32:
            v = v.bitcast(dt)
        v = v[:, 0:n]
        if len(shape) == 3:
            v = v.rearrange("p (a b) -> p a b", a=shape[1])
        elif len(shape) == 4:
            v = v.rearrange("p (a b c) -> p a b c", a=shape[1], b=shape[2])
        return v


def build_program(debug=False):
    nc = bass.Bass("TRN2", target_bir_lowering=False)
    P = Prog(nc)

    def din(name, shape, dt=F32):
        return nc.dram_tensor(name, list(shape), dt, kind="ExternalInput").ap()

    x_d = din("x", [S, D])
    p_d = din("p", [S, 256])
    pos_d = din("pos", [32, 128], I32)
    vec_d = din("vecs", [17, 128])
    row_d = din("rows", [1, 512])
    win_d = din("w_in", [D, 3072])
    lam_d = din("lam", [16, 2, 128])
    ldt_d = din("log_dt", [16, 2])
    bre_d = din("b_re", [32, 64, 16])
    bim_d = din("b_im", [32, 64, 16])
    cre_d = din("c_re", [32, 16, 64])
    cim_d = din("c_im", [32, 16, 64])
    glu_d = din("glu_w", [512, 512])
    wout_d = din("w_out", [D, D])
    pp_d = din("ple_w_proj", [256, D])
    pg_d = din("ple_w_gate", [D, D])
    out_d = nc.dram_tensor("out", [S, D], F32, kind="ExternalOutput").ap()
    ys_s = nc.dram_tensor("ys_s", [4, 128, S], BF16, kind="Internal").ap()
    u_s = nc.dram_tensor("u_s", [4, 128, S], BF16, kind="Internal").ap()
    zs_s = nc.dram_tensor("zs_s", [4, 128, S], BF16, kind="Internal").ap()
    za_s = nc.dram_tensor("za_s", [S, 512], BF16, kind="Internal").ap()
    qT_s = nc.dram_tensor("qT_s", [4, 128, S], BF16, kind="Internal").ap()
    kT_s = nc.dram_tensor("kT_s", [4, 128, S], BF16, kind="Internal").ap()
    v_s = nc.dram_tensor("v_s", [S, 512], BF16, kind="Internal").ap()
    dbg = {}
    if debug:
        for nm in ("ys", "qT", "kT"):
            dbg[nm] = nc.dram_tensor("dbg_" + nm, [4, 128, S], BF16, kind="ExternalOutput").ap()
        dbg["v"] = nc.dram_tensor("dbg_v", [S, 512], BF16, kind="ExternalOutput").ap()
        dbg["za"] = nc.dram_tensor("dbg_za", [S, 512], BF16, kind="ExternalOutput").ap()

    with contextlib.ExitStack() as st:
        ARENA_BYTES = 212480
        arena = st.enter_context(nc.sbuf_tensor("arena", [128, ARENA_BYTES // 4], F32))
        PQ = [st.enter_context(nc.psum_tensor("pq%d" % i, [128, 1024], F32)) for i in range(4)]
        PS = [PQ[i // 2][:, 512 * (i % 2):512 * (i % 2) + 512] for i in range(8)]
        PSK = ["ps%d" % i for i in range(8)]

        def psb(i):
            return PS[i].bitcast(BF16)

        PER = Region(arena, 0, 59136)
        RA = Region(arena, 59136, 65536)
        RB = Region(arena, 124672, 33792)
        RC = Region(arena, 158464, ARENA_BYTES - 158464)
        ident_bf = PER.alloc([128, 128], BF16)
        ident_f = PER.alloc([128, 128], F32)
        ones_f = PER.alloc([128, 128], F32)
        ones_bf = PER.alloc([128, 128], BF16)
        Wsel = PER.alloc([128, 8, 240], BF16)
        maskT = PER.alloc([128, MS, 128], BF16)
        cmask = PER.alloc([128, 4, 512], BF16)
        glu_bf = PER.alloc([128, 4, 512], BF16)
        W2_base = PER.cur
        wout_bf = PER.alloc([128, 8, 1024], BF16)
        pg_bf = PER.alloc([128, 8, 1024], BF16)
        pp_bf = PER.alloc([128, 2, 1024], BF16)
        W2 = Region(arena, W2_base, PER.cur - W2_base)
        vecT = PER.alloc([128, 17], F32)
        bcq = PER.alloc([128, 512], F32)
        bck = PER.alloc([128, 512], F32)
        cosT = PER.alloc([128, 32, 8], F32)
        sinT = PER.alloc([128, 32, 8], F32)
        APr = PER.alloc([128, 6, 16], F32)
        APi = PER.alloc([128, 6, 16], F32)
        carry = PER.alloc([128, 2, 16], F32)
        lamv = PER.alloc([128, 4], F32)
        sw08 = PER.alloc([128, 1], F32)
        epsv = PER.alloc([128, 1], F32)
        bcsw = PER.alloc([128, 128], F32)
        u1 = PER.alloc([128, 2, 16], F32)
        rmag = PER.alloc([128, 16], F32)
        nd = vecT[:, 0:8]
        dsk = vecT[:, 8:12]
        glb = vecT[:, 12:16]
        win_bf = RA.alloc([128, 8, 3072], BF16)
        M1 = RA.alloc([128, 32, MS, 128], BF16)
        Hr = RB.alloc([128, 16, NH, 16], BF16)
        nHi = RB.alloc([128, 16, NH, 16], BF16)
        Tt = RB.alloc([128, 32, MS, 128], BF16)

        def mm(out, lhsT, rhs, start, stop, R, W, tp=None):
            if tp is None:
                P.op("pe", lambda e: e.matmul(out, lhsT=lhsT, rhs=rhs, start=start, stop=stop), R, W)
            else:
                P.op("pe", lambda e: e.matmul(out, lhsT=lhsT, rhs=rhs, start=start, stop=stop, tile_position=tp), R, W)

        def tr(out, in_, ident, R, W):
            P.op("pe", lambda e: e.transpose(out, in_, ident), R, W)

        def act(out, in_, func, R, W, bias=None, scale=None, accum=None):
            kw = {}
            if bias is not None:
                kw["bias"] = bias
            if scale is not None:
                kw["scale"] = scale
            if accum is not None:
                kw["accum_out"] = accum
            P.op("act", lambda e: e.activation(out, in_, func, **kw), R, W)

        def tt(eng, out, a, b, op, R, W):
            P.op(eng, lambda e: e.tensor_tensor(out, a, b, op), R, W)

        def ts(eng, out, a, s1, s2, op0, op1, R, W):
            if op1 is None:
                P.op(eng, lambda e: e.tensor_scalar(out, a, s1, None, op0), R, W)
            else:
                P.op(eng, lambda e: e.tensor_scalar(out, a, s1, s2, op0, op1), R, W)

        def stt(out, a, s, b, op0, op1, R, W):
            P.op("dve", lambda e: e.scalar_tensor_tensor(out, a, s, b, op0, op1), R, W)

        def cp(eng, out, in_, R, W):
            if eng == "act":
                act(out, in_, AF.Copy, R, W)
            else:
                P.op(eng, lambda e: e.tensor_copy(out, in_), R, W)

        def iota(out, pattern, base, cm, W):
            P.op("pool", lambda e: e.iota(out, pattern=pattern, base=base, channel_multiplier=cm), (), W)

        def memset(eng, out, val, W):
            P.op(eng, lambda e: e.memset(out, val), (), W)

        def bc3(ap2, shape, axis):
            return ap2.unsqueeze(axis).to_broadcast(shape)

        def sincos(x, q, qi, r, s_out, c_out, key):
            ts("dve", q, x, 1.0 / TWO_PI, None, ALU.mult, None, [key + "x"], [key + "q"])
            cp("dve", qi, q, [key + "q"], [key + "qi"])
            cp("dve", q, qi, [key + "qi"], [key + "q"])
            stt(r, q, -CW1, x, ALU.mult, ALU.add, [key + "q", key + "x"], [key + "r"])
            stt(r, q, -CW2, r, ALU.mult, ALU.add, [key + "q", key + "r"], [key + "r"])
            ts("dve", x, r, -math.pi, math.pi, ALU.max, ALU.min, [key + "r"], [key + "x"])
            act(s_out, x, AF.Sin, [key + "x"], [key + "s"])
            ts("dve", x, r, math.pi / 2, None, ALU.add, None, [key + "r", key + "s"], [key + "x"])
            ts("dve", q, x, math.pi, -TWO_PI, ALU.is_gt, ALU.mult, [key + "x"], [key + "q"])
            tt("dve", x, x, q, ALU.add, [key + "q", key + "x"], [key + "x"])
            ts("dve", x, x, -math.pi, math.pi, ALU.max, ALU.min, [key + "x"], [key + "x"])
            act(c_out, x, AF.Sin, [key + "x"], [key + "c"])

        RC.reset()
        W2.reset()
        R0 = Region(arena, RA.base, RA.size)
        io_i = R0.alloc([128, 1920], I32)
        io_f = R0.alloc([128, 1920], F32)
        msk_f = R0.alloc([128, 1920], F32)
        iota(io_i[:, 0:128], [[1, 128]], 0, -1, ["io_i"])
        ts("dve", ident_f, io_i[:, 0:128], 0.0, None, ALU.is_equal, None, ["io_i"], ["ident_f"])
        cp("dve", ident_bf, ident_f, ["ident_f"], ["ident_bf"])
        memset("dve", ones_f, 1.0, ["ones_f"])
        memset("dve", ones_bf, 1.0, ["ones_bf"])
        memset("dve", epsv, EPS, ["epsv"])
        iota(io_i[:, 0:1920].rearrange("p (a b) -> p a b", a=8), [[16, 8], [1, 240]], -112, -1, ["io_i"])
        ts("dve", io_f[:, 0:1920], io_i[:, 0:1920], 0.0, None, ALU.is_equal, None, ["io_i"], ["io_f"])
        iota(io_i[:, 0:1920].rearrange("p (a b) -> p a b", a=8), [[0, 8], [1, 240]], 0, 0, ["io_i"])
        ts("dve", msk_f[:, 0:1920], io_i[:, 0:1920], 112.0, None, ALU.is_ge, None, ["io_i"], ["msk_f"])
        tt("dve", io_f[:, 0:1920], io_f[:, 0:1920], msk_f[:, 0:1920], ALU.mult, ["io_f", "msk_f"], ["io_f"])
        ts("dve", msk_f[:, 0:1920], io_i[:, 0:1920], 127.0, None, ALU.is_le, None, ["io_i"], ["msk_f"])
        tt("dve", Wsel.rearrange("p a b -> p (a b)"), io_f[:, 0:1920], msk_f[:, 0:1920], ALU.mult,
           ["io_f", "msk_f"], ["Wsel"])
        iota(io_i[:, 0:128].rearrange("p (a b) -> p a b", a=8), [[16, 8], [0, 16]], 0, -1, ["io_i"])
        memset("dve", maskT.rearrange("p a b -> p (a b)"), 1.0, ["maskT"])
        ts("dve", maskT[:, 0, :], io_i[:, 0:128], -15.0, None, ALU.is_ge, None, ["io_i", "maskT"], ["maskT"])
        for j in range(4):
            iota(io_i[:, 0:512], [[1, 512]], -128 * j, -1, ["io_i"])
            ts("dve", cmask[:, j, :], io_i[:, 0:512], 0.0, None, ALU.is_ge, None, ["io_i", "cmask"], ["cmask"])

        vec16 = R0.alloc([17, 128], F32)
        rowv = R0.alloc([1, 512], F32)
        lam16 = R0.alloc([16, 3, 128], F32)
        ldt16 = R0.alloc([16, 2], F32)
        pos_i = R0.alloc([32, 128], I32)
        pos_f = R0.alloc([32, 128], F32)
        P.dma("sp", vec16, vec_d, [], ["vec16"])
        P.dma("sp", rowv, row_d, [], ["rowv"])
        P.dma("sp", lam16[:, 0:2, :], lam_d, [], ["lam16a"])
        P.dma("sp", ldt16, ldt_d, [], ["ldt16"])
        P.dma("sp", pos_i, pos_d, [], ["pos_i"])
        cp("dve", lam16[:, 2, :].rearrange("p (a b) -> p a b", a=2), bc3(ldt16, [16, 2, 64], 2),
           ["ldt16"], ["lam16b"])
        cp("dve", pos_f, pos_i, ["pos_i"], ["pos_f"])
        tr(PS[0][:, 0:17], vec16, ident_f[0:17, 0:17], ["vec16", "ident_f"], [PSK[0]])
        cp("dve", vecT, PS[0][:, 0:17], [PSK[0]], ["vecT"])
        par = R0.alloc([128, 3, 16], F32)
        for i in range(3):
            tr(PS[1][:, 16 * i:16 * i + 16], lam16[:, i, :], ident_f[0:16, 0:16],
               ["lam16a", "lam16b", "ident_f"], [PSK[1]])
        cp("dve", par.rearrange("p a b -> p (a b)"), PS[1][:, 0:48], [PSK[1]], ["par"])
        posT = R0.alloc([128, 32], F32)
        tr(PS[2][:, 0:32], pos_f, ident_f[0:32, 0:32], ["pos_f", "ident_f"], [PSK[2]])
        cp("dve", posT, PS[2][:, 0:32], [PSK[2]], ["posT"])
        bcr = R0.alloc([128, 512], F32)
        mm(PS[3][:, 0:512], ones_f[0:1, :], rowv, True, True, ["ones_f", "rowv"], [PSK[3]])
        cp("dve", bcr, PS[3][:, 0:512], [PSK[3]], ["bcr"])
        ts("dve", bcsw, bcr[:, 384:512], 1.0 - LAMBDA_INIT, None, ALU.mult, None, ["bcr"], ["bcsw"])
        ts("dve", bcq.rearrange("p (a b) -> p a b", a=8), bc3(bcr[:, 0:64], [128, 8, 64], 1),
           0.125, None, ALU.mult, None, ["bcr"], ["bcq"])
        cp("dve", bck.rearrange("p (a b) -> p a b", a=8), bc3(bcr[:, 64:128], [128, 8, 64], 1), ["bcr"], ["bck"])
        lsc = R0.alloc([128, 128], F32)
        lsum = R0.alloc([128, 2], F32)
        tt("dve", lsc[:, 0:64], bcr[:, 128:192], bcr[:, 192:256], ALU.mult, ["bcr"], ["lsc"])
        tt("dve", lsc[:, 64:128], bcr[:, 256:320], bcr[:, 320:384], ALU.mult, ["bcr", "lsc"], ["lsc"])
        P.op("dve", lambda e: e.tensor_reduce(lsum, lsc.rearrange("p (a b) -> p a b", a=2), AX.X, ALU.add),
             ["lsc"], ["lsum"])
        act(lsum, lsum, AF.Exp, ["lsum"], ["lsum"])
        tt("dve", lamv[:, 0:1], lsum[:, 0:1], lsum[:, 1:2], ALU.subtract, ["lsum"], ["lamv"])
        ts("dve", lamv[:, 0:1], lamv[:, 0:1], LAMBDA_INIT, None, ALU.add, None, ["lamv"], ["lamv"])
        ts("dve", lamv[:, 1:2], lamv[:, 0:1], -1.0, None, ALU.mult, None, ["lamv"], ["lamv"])
        ts("dve", sw08, vecT[:, 16:17], 1.0 - LAMBDA_INIT, None, ALU.mult, None, ["vecT"], ["sw08"])
        invf = R0.alloc([128, 8], F32)
        for i in range(8):
            memset("dve", invf[:, i:i + 1], float(np.float32(500000.0) ** np.float32(-(2.0 * i) / 16.0)), ["invf"])
        ang = R0.alloc([128, 256], F32)
        aq = R0.alloc([128, 256], F32)
        aqi = R0.alloc([128, 256], I32)
        ar_ = R0.alloc([128, 256], F32)
        tt("dve", ang.rearrange("p (a b) -> p a b", a=32), bc3(posT, [128, 32, 8], 2), bc3(invf, [128, 32, 8], 1),
           ALU.mult, ["posT", "invf"], ["angx"])
        sincos(ang, aq, aqi, ar_, sinT.rearrange("p a b -> p (a b)"), cosT.rearrange("p a b -> p (a b)"), "ang")

        NEt = 16 * NE
        kv_i = R0.alloc([128, NE], I32)
        kv = R0.alloc([128, NE], F32)
        iota(kv_i[:, 0:NG], [[-1, NG]], L - 1, 0, ["kv_i"])
        iota(kv_i[:, NG:NE], [[1, NH]], 0, 0, ["kv_i"])
        cp("dve", kv, kv_i, ["kv_i"], ["kv"])
        dtv = R0.alloc([128, 16], F32)
        act(dtv, par[:, 2, :], AF.Exp, ["par"], ["dtv"])
        lrdt = R0.alloc([128, 16], F32)
        thv = R0.alloc([128, 16], F32)
        tt("dve", lrdt, par[:, 0, :], dtv, ALU.mult, ["par", "dtv"], ["lrdt"])
        tt("dve", thv, par[:, 1, :], dtv, ALU.mult, ["par", "dtv"], ["thv"])
        Emag = R0.alloc([128, 16, NE], F32)
        Eph = R0.alloc([128, 16, NE], F32)
        Eq = R0.alloc([128, 16, NE], F32)
        Eqi = R0.alloc([128, 16, NE], I32)
        Er = R0.alloc([128, 16, NE], F32)
        Ei = R0.alloc([128, 16, NE], F32)
        Ert = R0.alloc([128, 16, NE], F32)
        shp = [128, 16, NE]
        tt("dve", Emag, bc3(lrdt, shp, 2), bc3(kv, shp, 1), ALU.mult, ["lrdt", "kv"], ["Emag"])
        act(Emag, Emag, AF.Exp, ["Emag"], ["Emag"])
        tt("dve", Eph, bc3(thv, shp, 2), bc3(kv, shp, 1), ALU.mult, ["thv", "kv"], ["Ephx"])
        f2 = lambda a: a.rearrange("p a b -> p (a b)")
        sincos(f2(Eph), f2(Eq), f2(Eqi), f2(Ert), f2(Ei), f2(Er), "Eph")
        tt("dve", f2(Er), f2(Er), f2(Emag), ALU.mult, ["Ephc", "Emag"], ["Er"])
        tt("dve", f2(Ei), f2(Ei), f2(Emag), ALU.mult, ["Ephs", "Emag"], ["Ei"])
        c_nr = R0.alloc([128, 16], F32)
        c_den = R0.alloc([128, 16], F32)
        c_t = R0.alloc([128, 16], F32)
        c_r = R0.alloc([128, 16], F32)
        c_i = R0.alloc([128, 16], F32)
        lr, li = par[:, 0, :], par[:, 1, :]
        ni = Ei[:, :, NG + 1]
        ts("dve", c_nr, Er[:, :, NG + 1], -1.0, None, ALU.add, None, ["Er"], ["c_nr"])
        tt("dve", c_den, lr, lr, ALU.mult, ["par"], ["c_den"])
        tt("dve", c_t, li, li, ALU.mult, ["par"], ["c_t"])
        tt("dve", c_den, c_den, c_t, ALU.add, ["c_den", "c_t"], ["c_den"])
        P.op("dve", lambda e: e.reciprocal(c_den, c_den), ["c_den"], ["c_den"])
        tt("dve", c_r, c_nr, lr, ALU.mult, ["c_nr", "par"], ["c_r"])
        tt("dve", c_t, ni, li, ALU.mult, ["Ei", "par", "c_t"], ["c_t"])
        tt("dve", c_r, c_r, c_t, ALU.add, ["c_r", "c_t"], ["c_r"])
        tt("dve", c_r, c_r, c_den, ALU.mult, ["c_r", "c_den"], ["c_r"])
        tt("dve", c_i, ni, lr, ALU.mult, ["Ei", "par"], ["c_i"])
        tt("dve", c_t, c_nr, li, ALU.mult, ["c_nr", "par", "c_t"], ["c_t"])
        tt("dve", c_i, c_i, c_t, ALU.subtract, ["c_i", "c_t"], ["c_i"])
        tt("dve", c_i, c_i, c_den, ALU.mult, ["c_i", "c_den"], ["c_i"])
        cp("dve", APr[:, 0, :], Er[:, :, NG + L], ["Er"], ["APr"])
        cp("dve", APi[:, 0, :], Ei[:, :, NG + L], ["Ei"], ["APi"])
        sq1 = R0.alloc([128, 16], F32)
        sq2 = R0.alloc([128, 16], F32)
        for d_ in range(1, 6):
            tt("dve", sq1, APr[:, d_ - 1, :], APr[:, d_ - 1, :], ALU.mult, ["APr", "sq1"], ["sq1"])
            tt("dve", sq2, APi[:, d_ - 1, :], APi[:, d_ - 1, :], ALU.mult, ["APi", "sq2"], ["sq2"])
            tt("dve", APr[:, d_, :], sq1, sq2, ALU.subtract, ["sq1", "sq2", "APr"], ["APr"])
            tt("dve", sq1, APr[:, d_ - 1, :], APi[:, d_ - 1, :], ALU.mult, ["APr", "APi", "sq1"], ["sq1"])
            ts("dve", APi[:, d_, :], sq1, 2.0, None, ALU.mult, None, ["sq1", "APi"], ["APi"])
        cp("dve", rmag, Emag[:, :, NG + L], ["Emag"], ["rmag"])
        P.op("dve", lambda e: e.reciprocal(sq1, rmag), ["rmag", "sq1"], ["sq1"])
        tt("dve", u1[:, 0, :], APr[:, 0, :], sq1, ALU.mult, ["APr", "sq1"], ["u1"])
        tt("dve", u1[:, 1, :], APi[:, 0, :], sq1, ALU.mult, ["APi", "sq1", "u1"], ["u1"])
        Bre = R0.alloc([128, 16, 16], F32)
        Bim = R0.alloc([128, 16, 16], F32)
        bbr = R0.alloc([128, 16, 16], F32)
        bbi = R0.alloc([128, 16, 16], F32)
        bt = R0.alloc([128, 16, 16], F32)
        P.dma("sp", Bre, bre_d.rearrange("(gp g2) n q -> (g2 n) gp q", g2=2), [], ["Bre"])
        P.dma("sp", Bim, bim_d.rearrange("(gp g2) n q -> (g2 n) gp q", g2=2), [], ["Bim"])
        s3 = [128, 16, 16]
        tt("dve", bbr, Bre, bc3(c_r, s3, 2), ALU.mult, ["Bre", "c_r"], ["bbr"])
        tt("dve", bt, Bim, bc3(c_i, s3, 2), ALU.mult, ["Bim", "c_i"], ["bt"])
        tt("dve", bbr, bbr, bt, ALU.subtract, ["bbr", "bt"], ["bbr"])
        tt("dve", bbi, Bim, bc3(c_r, s3, 2), ALU.mult, ["Bim", "c_r"], ["bbi"])
        tt("dve", bt, Bre, bc3(c_i, s3, 2), ALU.mult, ["Bre", "c_i", "bt"], ["bt"])
        tt("dve", bbi, bbi, bt, ALU.add, ["bbi", "bt"], ["bbi"])
        Xc = R0.alloc([128, 4, 128], F32)
        Ctr = R0.alloc([128, 16, 16], F32)
        Cti = R0.alloc([128, 16, 16], F32)
        for ri, cd in enumerate((cre_d, cim_d)):
            for hf in range(2):
                for gpl in range(8):
                    for g2 in range(2):
                        g = 2 * (8 * hf + gpl) + g2
                        P.dma("sp" if (gpl % 2 == 0) else "act", Xc[16 * gpl:16 * gpl + 16, 2 * ri + hf, 64 * g2:64 * g2 + 64],
                              cd[g], [], [("Xc", ri, hf, gpl, g2)])
                tr(PS[4 + 2 * ri + hf][:, 0:128], Xc[:, 2 * ri + hf, :], ident_f,
                   [("Xc", ri, hf, a, b) for a in range(8) for b in range(2)] + ["ident_f"], [PSK[4 + 2 * ri + hf]])
                dst = (Ctr, Cti)[ri]
                cp("dve", dst[:, 8 * hf:8 * hf + 8, :].rearrange("p a b -> p (a b)"), PS[4 + 2 * ri + hf][:, 0:128],
                   [PSK[4 + 2 * ri + hf]], [("Ct", ri, hf)])
        CtK = [("Ct", ri, hf) for ri in range(2) for hf in range(2)]
        Gr = RC.alloc([128, 16, NG, 16], BF16)
        Gi = RC.alloc([128, 16, NG, 16], BF16)
        GC = 2
        g1 = W2.alloc([128, GC, NG, 16], F32)
        g2t = W2.alloc([128, GC, NG, 16], F32)
        g3 = W2.alloc([128, GC, NG, 16], F32)
        g4 = W2.alloc([128, GC, NG, 16], F32)
        for c0 in range(0, 16, GC):
            sl = slice(c0, c0 + GC)
            for (E0, n0, nn, Xr_, Xi_, Or_, Oi_, neg, kx) in (
                    (0, 0, NG, bbr, bbi, Gr, Gi, False, ["bbr", "bbi"]),
                    (NG, 0, NH, Ctr, Cti, Hr, nHi, True, CtK)):
                shp4 = [128, GC, nn, 16]
                er = Er[:, sl, E0:E0 + nn].unsqueeze(3).to_broadcast(shp4)
                ei = Ei[:, sl, E0:E0 + nn].unsqueeze(3).to_broadcast(shp4)
                xr = Xr_[:, sl, :].unsqueeze(2).to_broadcast(shp4)
                xi = Xi_[:, sl, :].unsqueeze(2).to_broadcast(shp4)
                a1, a2 = g1[:, :, 0:nn, :], g2t[:, :, 0:nn, :]
                a3, a4 = g3[:, :, 0:nn, :], g4[:, :, 0:nn, :]
                tt("dve", a1, er, xr, ALU.mult, ["Er"] + kx + ["g1"], ["g1"])
                tt("dve", a2, ei, xi, ALU.mult, ["Ei"] + kx + ["g2"], ["g2"])
                tt("dve", Or_[:, sl, :, :], a1, a2, ALU.subtract, ["g1", "g2"], [("GH", E0, c0, 0)])
                tt("pool", a3, er, xi, ALU.mult, ["Er"] + kx + ["g3"], ["g3"])
                tt("pool", a4, ei, xr, ALU.mult, ["Ei"] + kx + ["g4"], ["g4"])
                if neg:
                    fl = lambda a: a.rearrange("p a b c -> p a (b c)")
                    stt(fl(Oi_[:, sl, :, :]), fl(a3), -1.0, fl(a4), ALU.mult, ALU.subtract, ["g3", "g4"], [("GH", E0, c0, 1)])
                else:
                    tt("pool", Oi_[:, sl, :, :], a3, a4, ALU.add, ["g3", "g4"], [("GH", E0, c0, 1)])
        GK = [("GH", 0, c0, i) for c0 in range(0, 16, GC) for i in range(2)]
        HK = [("GH", NG, c0, i) for c0 in range(0, 16, GC) for i in range(2)]
        P.barrier()
        for g in range(32):
            gp, hf = g // 2, g % 2
            hs = slice(64 * hf, 64 * hf + 64)
            pb = 4 + (g % 2)
            for dl in range(MS):
                r0 = (L - 1) - 8 * dl
                o = PS[pb][:, 128 * dl:128 * dl + 128]
                mm(o, Gr[hs, gp, r0:r0 + 8, :].rearrange("p a b -> p (a b)"),
                   Hr[hs, gp, 0:8, :].rearrange("p a b -> p (a b)"), True, False, GK + HK, [PSK[pb]])
                mm(o, Gi[hs, gp, r0:r0 + 8, :].rearrange("p a b -> p (a b)"),
                   nHi[hs, gp, 0:8, :].rearrange("p a b -> p (a b)"), False, True, GK + HK, [PSK[pb]])
            tt("dve", Tt[:, g, :, :].rearrange("p a b -> p (a b)"), PS[pb][:, 0:128 * MS],
               maskT.rearrange("p a b -> p (a b)"), ALU.mult, [PSK[pb], "maskT"], ["Tt"])
            pt = 6 + (g % 2)
            for j in range(MS):
                for ri, Gx in enumerate((Gr, Gi)):
                    c0 = (j * 2 + ri) * 64
                    tr(psb(pt)[:, c0:c0 + 64], Gx[hs, gp, 8 * j:8 * j + 8, :].rearrange("p a b -> p (a b)"),
                       ident_bf[hs, hs], GK + ["ident_bf"], [PSK[pt]])
            cp("act", M1[:, g, :, :].rearrange("p a b -> p (a b)"), psb(pt)[:, 0:128 * MS], [PSK[pt]], ["M1"])
        wst = [RC.alloc([128, 1024], F32) for _ in range(2)]
        wi = 0

        def load_w(dst, src, rows_chunks, ncols, key, scale_col=None):
            nonlocal wi
            for c in range(rows_chunks):
                for n0 in range(0, ncols, 1024):
                    nn = min(1024, ncols - n0)
                    b = wi % 2
                    wi += 1
                    P.dma("sp", wst[b][:, 0:nn], src[128 * c:128 * c + 128, n0:n0 + nn], [], [("wst", b)])
                    eng = "act" if (wi % 2) else "dve"
                    if scale_col is not None:
                        if eng == "act":
                            act(dst[:, c, n0:n0 + nn], wst[b][:, 0:nn], AF.Copy, [("wst", b), "vecT"], [key],
                                scale=scale_col[:, c:c + 1])
                        else:
                            ts("dve", dst[:, c, n0:n0 + nn], wst[b][:, 0:nn], scale_col[:, c:c + 1], None,
                               ALU.mult, None, [("wst", b), "vecT"], [key])
                    else:
                        cp(eng, dst[:, c, n0:n0 + nn], wst[b][:, 0:nn], [("wst", b)], [key])

        load_w(win_bf, win_d, 8, 3072, "win_bf", scale_col=nd)
        load_w(glu_bf, glu_d, 4, 512, "glu_bf")
        W2.reset()
        CH_ = 2048 // L
        PTr = W2.alloc([128, 16, CH_], F32)
        PTi = W2.alloc([128, 16, CH_], F32)
        Rm = W2.alloc([128, 16, CH_], F32)
        pw = RC.alloc([128, 8, 2, 16], F32)
        dA = RC.alloc([128, 16, CH_ // 2], F32)
        dB = RC.alloc([128, 16, CH_ // 2], F32)
        memset("dve", PTr[:, :, 0:1], 1.0, ["PT"])
        memset("dve", PTi[:, :, 0:1], 0.0, ["PT"])
        cp("dve", pw[:, 0, :, :], u1, ["u1"], ["pw"])
        k_ = 0
        while (1 << k_) < CH_:
            m_ = 1 << k_
            if k_ > 0:
                pr, pi_ = pw[:, k_ - 1, 0, :], pw[:, k_ - 1, 1, :]
                tt("dve", dA[:, :, 0], pr, pr, ALU.mult, ["pw", "dA"], ["dA"])
                tt("dve", dB[:, :, 0], pi_, pi_, ALU.mult, ["pw", "dB"], ["dB"])
                tt("dve", pw[:, k_, 0, :], dA[:, :, 0], dB[:, :, 0], ALU.subtract, ["dA", "dB", "pw"], ["pw"])
                tt("dve", dA[:, :, 0], pr, pi_, ALU.mult, ["pw", "dA"], ["dA"])
                ts("dve", pw[:, k_, 1, :], dA[:, :, 0], 2.0, None, ALU.mult, None, ["dA", "pw"], ["pw"])
            shp_ = [128, 16, m_]
            br = pw[:, k_, 0, :].unsqueeze(2).to_broadcast(shp_)
            bi = pw[:, k_, 1, :].unsqueeze(2).to_broadcast(shp_)
            sr, si = PTr[:, :, 0:m_], PTi[:, :, 0:m_]
            a_, b_2 = dA[:, :, 0:m_], dB[:, :, 0:m_]
            tt("dve", a_, sr, br, ALU.mult, ["PT", "pw", "dA"], ["dA"])
            tt("dve", b_2, si, bi, ALU.mult, ["PT", "pw", "dB"], ["dB"])
            tt("dve", PTr[:, :, m_:2 * m_], a_, b_2, ALU.subtract, ["dA", "dB", "PT"], ["PT"])
            tt("dve", a_, sr, bi, ALU.mult, ["PT", "pw", "dA"], ["dA"])
            tt("dve", b_2, si, br, ALU.mult, ["PT", "pw", "dB"], ["dB"])
            tt("dve", PTi[:, :, m_:2 * m_], a_, b_2, ALU.add, ["dA", "dB", "PT"], ["PT"])
            k_ += 1
        cp("dve", Rm, rmag.unsqueeze(2).to_broadcast([128, 16, CH_]), ["rmag"], ["Rm"])
        memset("dve", Rm[:, :, 0:1], 0.0, ["Rm"])
        memset("dve", carry.rearrange("p a b -> p (a b)"), 0.0, ["carry"])
        P.barrier()

        RC.reset()
        xt = [RC.alloc([128, 1024], F32) for _ in range(2)]
        hbf = RC.alloc([128, 1024], BF16)
        hT = RC.alloc([128, 8, 512], BF16)
        uT = RC.alloc([128, 4, 512], BF16)
        zsT = RC.alloc([128, 4, 512], BF16)
        qsq = RC.alloc([128, 512], F32)
        w2_mark = W2.cur
        qns = [W2.alloc([128, 512], F32) for _ in range(2)]
        qtoks = [[W2.alloc([128, 512], BF16) for _ in range(2)] for _ in range(2)]
        qTb = RC.alloc([128, 4, 512], BF16)
        kTb = RC.alloc([128, 4, 512], BF16)
        vtok = [RC.alloc([128, 512], BF16) for _ in range(2)]
        st8 = RC.alloc([128, 8], F32)
        rt1 = RC.alloc([128, 8, 8], F32)
        rt2 = RC.alloc([128, 8, 8], F32)
        ss1 = RC.alloc([128, 4], F32)

        hTs = [hT, RC.alloc([128, 8, 512], BF16)]
        mhalf = RC.alloc([128, 8], F32)
        memset("dve", mhalf, -0.5, ["mhalf"])

        hbfs = [hbf, RC.alloc([128, 1024], BF16)]

        def rms_a(b, i):
            t0 = 512 * b
            xb_ = xt[i % 2]
            xk = ("xt", i % 2)
            hb, hk = hbfs[i % 2], ("hbf", i % 2)
            P.dma("sp", xb_, x_d[t0 + 128 * i:t0 + 128 * i + 128, :], [], [xk])
            act(hb, xb_, AF.Square, [xk], [hk, "ss1"], accum=ss1[:, 0:1])
            ts("dve", ss1[:, 1:2], ss1[:, 0:1], 1.0 / D, EPS, ALU.mult, ALU.add, ["ss1"], ["ss1b"])
            tt("pool", ss1[:, 3:4], ss1[:, 1:2], mhalf[:, 0:1], ALU.pow, ["ss1b", "mhalf"], ["ss1d"])
            act(hb, xb_, AF.Copy, [xk, "ss1d"], [hk], scale=ss1[:, 3:4])

        def rms_b(b, i):
            hb, hk = hbfs[i % 2], ("hbf", i % 2)
            for c in range(8):
                tr(psb(0)[:, 128 * c:128 * c + 128], hb[:, 128 * c:128 * c + 128], ident_bf,
                   [hk, "ident_bf"], [PSK[0]])
            cp("act", hTs[b % 2][:, :, 128 * i:128 * i + 128], psb(0).rearrange("p (a b) -> p a b", a=8),
               [PSK[0]], [("hT", b % 2, i)])

        def rms_sched(b, slot):
            order = {0: [("a", 0)], 1: [("a", 1)], 2: [("b", 0)], 3: [("a", 2)], 4: [("b", 1)], 5: [("a", 3)],
                     6: [("b", 2)], 7: [("b", 3)]}
            for kind, i in order[slot]:
                (rms_a if kind == "a" else rms_b)(b, i)

        for slot in range(8):
            rms_sched(0, slot)
        for b in range(NBLK):
            t0 = 512 * b
            hT = hTs[b % 2]
            hTK = [("hT", b % 2, i) for i in range(4)]
            for fi, f in enumerate(range(8)):
                pb = 1 + (fi % 2)
                for c in range(8):
                    mm(PS[pb][:, :], win_bf[:, c, 128 * f:128 * f + 128], hT[:, c, :], c == 0, c == 7,
                       ["win_bf"] + hTK, [PSK[pb]])
                if f < 4:
                    cp("act", uT[:, f, :], PS[pb][:, :], [PSK[pb]], [("uT", f)])
                    P.dma("act", u_s[f, :, t0:t0 + 512], uT[:, f, :], [("uT", f)], [("u_s", f, b)])
                else:
                    act(zsT[:, f - 4, :], PS[pb][:, :], AF.Silu, [PSK[pb]], [("zsT", f - 4)])
                    P.dma("act", zs_s[f - 4, :, t0:t0 + 512], zsT[:, f - 4, :], [("zsT", f - 4)], [("zs_s", f - 4, b)])
                if b + 1 < NBLK:
                    rms_sched(b + 1, fi)
            def qk_transposes(i):
                ts_ = slice(128 * i, 128 * i + 128)
                for wi_, which in enumerate(("q", "k")):
                    qt_ = qtoks[wi_][i % 2]
                    qk_ = ("qtok", wi_, i % 2)
                    pk3 = ("ps3", wi_)
                    for h in range(4):
                        tr(psb(3)[:, 512 * wi_ + 128 * h:512 * wi_ + 128 * h + 128], qt_[:, 128 * h:128 * h + 128], ident_bf,
                           [qk_, "ident_bf"], [pk3])
                    dstT = qTb if which == "q" else kTb
                    cp("act", dstT[:, :, ts_], psb(3)[:, 512 * wi_:512 * wi_ + 512].rearrange("p (a b) -> p a b", a=4),
                       [pk3], [(which + "Tb", i)])

            for i in range(4):
                ts_ = slice(128 * i, 128 * i + 128)
                for wi_, (which, col0) in enumerate((("q", 1024), ("k", 1536), ("v", 2048), ("za", 2560))):
                    pb = (4 + wi_ + 2 * (i % 2)) if wi_ < 2 else (wi_ - 1)
                    for c in range(8):
                        mm(PS[pb][:, :], hT[:, c, ts_], win_bf[:, c, col0:col0 + 512], c == 0, c == 7,
                           ["win_bf", ("hT", b % 2, i)], [PSK[pb]])
                    if which == "v":
                        vb = vtok[0]
                        cp("act", vb, PS[pb][:, :], [PSK[pb]], [("vtok", 0)])
                        P.dma("act", v_s[t0 + 128 * i:t0 + 128 * i + 128, :], vb, [("vtok", 0)],
                              [("v_s", b, i)])
                        continue
                    if which == "za":
                        vb = vtok[1]
                        act(vb, PS[pb][:, :], AF.Silu, [PSK[pb]], [("vtok", 1)])
                        P.dma("act", za_s[t0 + 128 * i:t0 + 128 * i + 128, :], vb, [("vtok", 1)],
                              [("za_s", b, i)])
                        continue
                    wq = bcq if which == "q" else bck
                    qn = qns[wi_]
                    qnk = "qn%d" % wi_
                    act(qsq, PS[pb][:, :], AF.Square, [PSK[pb]], ["qsq"])
                    P.op("dve", lambda e: e.tensor_reduce(st8, qsq.rearrange("p (a b) -> p a b", a=8), AX.X, ALU.add),
                         ["qsq"], ["st8"])
                    ts("dve", st8, st8, 1.0 / 64, EPS, ALU.mult, ALU.add, ["st8"], ["st8"])
                    tt("pool", st8, st8, mhalf, ALU.pow, ["st8", "mhalf"], ["st8"])
                    q3 = qn.rearrange("p (a b) -> p a b", a=8)
                    tt("dve", q3, PS[pb][:, :].rearrange("p (a b) -> p a b", a=8), bc3(st8, [128, 8, 64], 2),
                       ALU.mult, [PSK[pb], "st8"], [qnk])
                    tt("dve", qn, qn, wq, ALU.mult, [qnk, "bcq", "bck"], [qnk])
                    tile_idx = 4 * b + i
                    cs = cosT[:, tile_idx, :].unsqueeze(1).to_broadcast([128, 8, 8])
                    sn = sinT[:, tile_idx, :].unsqueeze(1).to_broadcast([128, 8, 8])
                    x1, x2 = q3[:, :, 0:8], q3[:, :, 8:16]
                    tt("dve", rt1, x1, cs, ALU.mult, [qnk, "angc"], ["rt1"])
                    tt("dve", rt2, x2, sn, ALU.mult, [qnk, "angs"], ["rt2"])
                    tt("dve", rt1, rt1, rt2, ALU.subtract, ["rt1", "rt2"], ["rt1"])
                    tt("dve", rt2, x2, cs, ALU.mult, [qnk, "angc", "rt2"], ["rt2"])
                    tt("dve", x2, x1, sn, ALU.mult, [qnk, "angs"], [qnk])
                    tt("dve", x2, x2, rt2, ALU.add, [qnk, "rt2"], [qnk])
                    cp("dve", x1, rt1, ["rt1", qnk], [qnk])
                    cp("pool", qtoks[wi_][i % 2], qn, [qnk], [("qtok", wi_, i % 2)])
                if i >= 1:
                    qk_transposes(i - 1)
            qk_transposes(3)
            for h in range(4):
                P.dma("act", qT_s[h, :, t0:t0 + 512], qTb[:, h, :], [("qTb", i) for i in range(4)], [("qT_s", h, b)])
                P.dma("act", kT_s[h, :, t0:t0 + 512], kTb[:, h, :], [("kTb", i) for i in range(4)], [("kT_s", h, b)])
        P.barrier()
        HT = 2048
        CH = HT // L
        SCH = HT // 8
        RA1 = Region(arena, RA.base, 49152)
        RC.reset()
        uTh = RA1.alloc([128, 4, HT], BF16)
        Ub = RA1.alloc([128, 32, SCH], BF16)
        Zt = [RA1.alloc([128, 16, CH], F32) for _ in range(2)]
        Zt += [RC.alloc([128, 16, CH], F32) for _ in range(4)]
        Zr, Zi, Zmr, Zmi, tA, tB = Zt
        zsl = [RC.alloc([128, 4, 512], BF16) for _ in range(2)]
        y2ps = [RC.alloc([128, SCH, 2], F32) for _ in range(2)]
        ge1s = [RC.alloc([128, SCH, 2], F32) for _ in range(2)]
        gsg = RC.alloc([128, 512], F32)
        ysb = [RC.alloc([128, 512], BF16) for _ in range(2)]
        W2.cur = w2_mark
        Xbf = [W2.alloc([128, 16, CH], BF16) for _ in range(2)]
        y2b_h = Ub.rearrange("p a b -> p (a b)").rearrange("p (c t) -> p c t", c=4)
        UK = [("Ub", g) for g in range(32)]
        f3 = lambda a: a.rearrange("p a b -> p (a b)")
        for hh in range(2):
            T0 = HT * hh
            for f in range(4):
                P.dma("sp", uTh[:, f, :], u_s[f, :, T0:T0 + HT], [("u_s", f, b_) for b_ in range(4 * hh, 4 * hh + 4)],
                      [("uTh", f)])
            for g in range(32):
                g8, gl = g // 8, g % 8
                pb = (g // 2) % 2
                for s_ in range(8):
                    j_, e_ = s_ // 2, s_ % 2
                    mm(PS[pb][32 * j_:32 * j_ + 32, SCH * (g % 2):SCH * (g % 2) + SCH],
                       Wsel[:, gl, 112 - 16 * e_:144 - 16 * e_], uTh[:, g8, s_:HT:8], e_ == 0, e_ == 1,
                       ["Wsel", ("uTh", g8)], [PSK[pb]], tp=(0, 32 * j_))
                if g % 2 == 1:
                    cp("act", Ub[:, g - 1:g + 1, :].rearrange("p a b -> p (a b)"), PS[pb][:, :], [PSK[pb]],
                       [("Ub", g - 1), ("Ub", g)])
            for gq in range(4):
                pz = 4 + 2 * (gq % 2)
                for g in range(8 * gq, 8 * gq + 8):
                    gp, hf = g // 2, g % 2
                    hs = slice(64 * hf, 64 * hf + 64)
                    for ri in range(2):
                        for j in range(MS):
                            mm(PS[pz + ri][hs, CH * (gp % 4):CH * (gp % 4) + CH], M1[:, g, j, 64 * ri:64 * ri + 64],
                               Ub[:, g, j:SCH:MS], j == 0, j == MS - 1, ["M1", ("Ub", g)], [PSK[pz + ri]])
                cp("dve", f3(Zr[:, 4 * gq:4 * gq + 4, :]), PS[pz][:, :], [PSK[pz]], [("Zr", gq)])
                cp("act", f3(Zi[:, 4 * gq:4 * gq + 4, :]), PS[pz + 1][:, :], [PSK[pz + 1]], [("Zi", gq)])
            ZrK = [("Zr", q_) for q_ in range(4)]
            ZiK = [("Zi", q_) for q_ in range(4)]
            a0r, a0i = APr[:, 0, :], APi[:, 0, :]
            cr_, ci_ = carry[:, 0, :], carry[:, 1, :]
            t1, t2 = tA[:, :, 0], tB[:, :, 0]
            tt("dve", t1, a0r, cr_, ALU.mult, ["APr", "carry", "tA"], ["tA"])
            tt("dve", t2, a0i, ci_, ALU.mult, ["APi", "carry", "tB"], ["tB"])
            tt("dve", t1, t1, t2, ALU.subtract, ["tA", "tB"], ["tA"])
            tt("dve", Zr[:, :, 0], Zr[:, :, 0], t1, ALU.add, ZrK + ["tA"], ZrK)
            tt("dve", t1, a0r, ci_, ALU.mult, ["APr", "carry", "tA"], ["tA"])
            tt("dve", t2, a0i, cr_, ALU.mult, ["APi", "carry", "tB"], ["tB"])
            tt("dve", t1, t1, t2, ALU.add, ["tA", "tB"], ["tA"])
            tt("dve", Zi[:, :, 0], Zi[:, :, 0], t1, ALU.add, ZiK + ["tA"], ZiK)
            tt("dve", tA, PTr, Zr, ALU.mult, ["PT", "tA"] + ZrK, ["tA"])
            tt("pool", tB, PTi, Zi, ALU.mult, ["PT", "tB"] + ZiK, ["tB"])
            tt("dve", Zmr, tA, tB, ALU.add, ["tA", "tB", "Zmr"], ["Zmr"])
            tt("dve", tA, PTr, Zi, ALU.mult, ["PT", "tA"] + ZiK, ["tA"])
            tt("pool", tB, PTi, Zr, ALU.mult, ["PT", "tB"] + ZrK, ["tB"])
            tt("dve", Zmi, tA, tB, ALU.subtract, ["tA", "tB", "Zmi"], ["Zmi"])
            P.op("dve", lambda e: e.tensor_tensor_scan(f3(Zr), f3(Rm), f3(Zmr), 0.0, ALU.mult, ALU.add),
                 ["Rm", "Zmr"] + ZrK, ZrK)
            P.op("dve", lambda e: e.tensor_tensor_scan(f3(Zi), f3(Rm), f3(Zmi), 0.0, ALU.mult, ALU.add),
                 ["Rm", "Zmi"] + ZiK, ZiK)
            tt("dve", tA, PTr, Zr, ALU.mult, ["PT", "tA"] + ZrK, ["tA"])
            tt("pool", tB, PTi, Zi, ALU.mult, ["PT", "tB"] + ZiK, ["tB"])
            tt("dve", Zmr, tA, tB, ALU.subtract, ["tA", "tB", "Zmr"], ["Zmr"])
            tt("dve", tA, PTr, Zi, ALU.mult, ["PT", "tA"] + ZiK, ["tA"])
            tt("pool", tB, PTi, Zr, ALU.mult, ["PT", "tB"] + ZrK, ["tB"])
            tt("dve", Zmi, tA, tB, ALU.add, ["tA", "tB", "Zmi"], ["Zmi"])
            for ri, (Xs, xk_) in enumerate(((Zmr, "Zmr"), (Zmi, "Zmi"))):
                cp("dve", Xbf[ri][:, :, 1:CH], Xs[:, :, 0:CH - 1], [xk_], [("Xbf", ri)])
                cp("dve", Xbf[ri][:, :, 0], carry[:, ri, :], ["carry", ("Xbf", ri)], [("Xbf", ri)])
            for ri, (Xs, xk_) in enumerate(((Zmr, "Zmr"), (Zmi, "Zmi"))):
                cp("dve", carry[:, ri, :], Xs[:, :, CH - 1], [xk_, ("Xbf", 0), ("Xbf", 1), "carry"], ["carry"])
            for g in range(32):
                gp, hf = g // 2, g % 2
                hs = slice(64 * hf, 64 * hf + 64)
                pb = (g // 2) % 2
                for j in range(MS):
                    o = PS[pb][:, SCH * (g % 2) + j:SCH * (g % 2) + SCH:MS]
                    for jp in range(j + 1):
                        mm(o, Tt[:, g, j - jp, :], Ub[:, g, jp:SCH:MS], jp == 0, False, ["Tt", ("Ub", g)], [PSK[pb]])
                    mm(o, Hr[hs, gp, 8 * j + 1:8 * j + 9, :].rearrange("p a b -> p (a b)"), Xbf[0][hs, gp, :],
                       False, False, HK + [("Xbf", 0)], [PSK[pb]])
                    mm(o, nHi[hs, gp, 8 * j + 1:8 * j + 9, :].rearrange("p a b -> p (a b)"), Xbf[1][hs, gp, :],
                       False, True, HK + [("Xbf", 1)], [PSK[pb]])
                if g % 2 == 1:
                    cp("act", Ub[:, g - 1:g + 1, :].rearrange("p a b -> p (a b)"), PS[pb][:, :], [PSK[pb]],
                       [("Ub", g - 1), ("Ub", g)])
            for ct in range(4):
                CK = [("Ub", 8 * ct + gl) for gl in range(8)]
                for t_ in range(8):
                    pbk = 4 + t_ // 2
                    for gl in range(8):
                        j_, e_ = gl // 2, gl % 2
                        mm(PS[pbk][32 * j_:32 * j_ + 32, SCH * (t_ % 2):SCH * (t_ % 2) + SCH],
                           Wsel[:, t_, 112 - 16 * e_:144 - 16 * e_], Ub[:, 8 * ct + gl, :], e_ == 0, e_ == 1,
                           ["Wsel", ("Ub", 8 * ct + gl)], [PSK[pbk]], tp=(0, 32 * j_))
                for tq in range(4):
                    pbk = 4 + tq
                    y2p, ge1 = y2ps[tq % 2], ge1s[tq % 2]
                    yk, gk = ("y2p", tq % 2), ("ge1", tq % 2)
                    uview = uTh[:, ct, :].rearrange("p (a b) -> p a b", b=8)[:, :, 2 * tq:2 * tq + 2]
                    stt(y2p, uview, dsk[:, ct:ct + 1], PS[pbk][:, :].rearrange("p (b a) -> p a b", b=2),
                        ALU.mult, ALU.add, [("uTh", ct), "vecT", PSK[pbk]], [yk])
                    yf, gf = f3(y2p), f3(ge1)
                    tt("pool", gf, yf, yf, ALU.mult, [yk], [gk])
                    ts("dve", gf, gf, 0.044715, 1.0, ALU.mult, ALU.add, [gk], [gk])
                    tt("dve", gf, gf, yf, ALU.mult, [gk, yk], [gk])
                    act(gf, gf, AF.Sigmoid, [gk], [gk], scale=1.5957691216057308)
                    tt("dve", y2b_h[:, ct, :].rearrange("p (a b) -> p a b", b=8)[:, :, 2 * tq:2 * tq + 2], y2p, ge1,
                       ALU.mult, [yk, gk] + CK, CK)
            for bi in range(4):
                bg = 4 * hh + bi
                tb0 = 512 * bi
                zl = zsl[bi % 2]
                for f in range(4):
                    P.dma("sp", zl[:, f, :], zs_s[f, :, T0 + tb0:T0 + tb0 + 512], [("zs_s", f, bg)], [("zsl", bi % 2, f)])
                for fo in range(4):
                    pb = fo % 2
                    for ci in range(4):
                        mm(PS[pb][:, :], glu_bf[:, ci, 128 * fo:128 * fo + 128], y2b_h[:, ci, tb0:tb0 + 512], ci == 0, ci == 3,
                           ["glu_bf"] + UK, [PSK[pb]])
                    act(gsg, PS[pb][:, :], AF.Sigmoid, [PSK[pb]], ["gsg"], bias=glb[:, fo:fo + 1])
                    tt("dve", gsg, gsg, y2b_h[:, fo, tb0:tb0 + 512], ALU.mult, ["gsg"] + UK, ["gsg"])
                    yo = ysb[fo % 2]
                    tt("dve", yo, gsg, zl[:, fo, :], ALU.mult, ["gsg", ("zsl", bi % 2, fo), ("ysb", fo % 2)], [("ysb", fo % 2)])
                    P.dma("sp", ys_s[fo, :, T0 + tb0:T0 + tb0 + 512], yo, [("ysb", fo % 2)], [("ys_s", fo, bg)])
        P.barrier()
        fin_keys = []
        if debug:
            RC.reset()
            dtile = RC.alloc([128, 4096], BF16)
            for nm, src in (("ys", ys_s), ("qT", qT_s), ("kT", kT_s)):
                for f in range(4):
                    P.dma("sp", dtile, src[f], [], ["dtile"])
                    P.dma("sp", dbg[nm][f], dtile, ["dtile"], [("dbg", nm, f)])
                    fin_keys.append(("dbg", nm, f))
            for nm, src in (("v", v_s), ("za", za_s)):
                for i in range(32):
                    P.dma("sp", dtile[:, 0:512], src[128 * i:128 * i + 128, :], [], ["dtile"])
                    P.dma("sp", dbg[nm][128 * i:128 * i + 128, :], dtile[:, 0:512], ["dtile"], [("dbg", nm, i)])
                    fin_keys.append(("dbg", nm, i))
            P.barrier()

        RAB = Region(arena, RA.base, RA.size + RB.size)
        RC.reset()
        kT_res = RAB.alloc([128, 4, S], BF16)
        v_res = RAB.alloc([128, 32, 4, 129], BF16)
        qTl = [RAB.alloc([128, 4, 512], BF16) for _ in range(2)]
        zatok = RAB.alloc([128, 4, 512], BF16)
        ysl = RAB.alloc([128, 4, 512], BF16)
        PTt = [RAB.alloc([128, 2, 512], BF16) for _ in range(4)]
        yaT = RAB.alloc([128, 4, 512], BF16)
        rs = RC.alloc([128, 8], F32)
        rsn = RC.alloc([128, 4], F32)
        o_all = RC.alloc([128, 4, 512], F32)
        sqt = RC.alloc([128, 512], F32)
        ss4 = RC.alloc([128, 4], F32)
        yatoks = [RC.alloc([128, 512], BF16) for _ in range(2)]
        xl = [RC.alloc([128, 1024], F32) for _ in range(2)]
        xnews = [RC.alloc([128, 1024], F32) for _ in range(2)]
        xnbs = [RC.alloc([128, 1024], BF16) for _ in range(2)]
        xnT = RC.alloc([128, 8, 128], BF16)
        pl = [RC.alloc([128, 256], F32) for _ in range(2)]
        pT = RC.alloc([128, 2, 128], BF16)
        gate = RC.alloc([128, 1024], F32)
        wst = [RC.alloc([128, 1024], F32) for _ in range(2)]
        tri = cmask[:, 0, 0:128]
        mhalf4 = RC.alloc([128, 4], F32)
        memset("dve", mhalf4, -0.5, ["mhalf4"])
        for h in range(4):
            P.dma("sp", kT_res[:, h, :], kT_s[h], [("kT_s", h, b_) for b_ in range(NBLK)], [("kT_res", h)])
        memset("dve", v_res[:, :, :, 128:129], 1.0, ["v_ones"])
        for i in range(32):
            P.dma("sp", v_res[:, i, :, 0:128], v_s[128 * i:128 * i + 128, :].rearrange("p (a b) -> p a b", a=4),
                  [("v_s", i // 4, i % 4)], [("v_res", i)])

        def load_q(b):
            for h in range(4):
                P.dma("sp", qTl[b % 2][:, h, :], qT_s[h, :, 512 * b:512 * b + 512], [("qT_s", h, b)], [("qTl", b % 2, h)])

        def load_x(b, i):
            tok = slice(512 * b + 128 * i, 512 * b + 128 * i + 128)
            P.dma("sp", xl[i % 2], x_d[tok, :], [], [("xl", i % 2)])
            P.dma("sp", pl[i % 2], p_d[tok, :], [], [("pl", i % 2)])

        load_q(0)
        load_w(wout_bf, wout_d, 8, 1024, "wout_bf")
        load_w(pg_bf, pg_d, 8, 1024, "pg_bf")
        load_w(pp_bf, pp_d, 2, 1024, "pp_bf")
        OBk = [PS[4], PS[5]]
        zatoks = [zatok, RAB.alloc([128, 4, 512], BF16)]
        ysls = [ysl, RC.alloc([128, 4, 512], BF16)]
        pti = 0

        def make_tail_units(b):
            t0 = 512 * b
            zat, ysl_ = zatoks[b % 2], ysls[b % 2]

            def stage_a(i):
                ts_ = slice(128 * i, 128 * i + 128)
                oK = [("o_all", i, h) for h in range(4)]
                oq = o_all[:, i, :]
                act(sqt, oq, AF.Square, oK, ["sqt"])
                P.op("dve", lambda e: e.tensor_reduce(ss4, sqt.rearrange("p (a b) -> p a b", a=4), AX.X, ALU.add),
                     ["sqt"], ["ss4"])
                ts("dve", ss4, ss4, 1.0 / 128, EPS, ALU.mult, ALU.add, ["ss4"], ["ss4"])
                tt("pool", ss4, ss4, mhalf4, ALU.pow, ["ss4", "mhalf4"], ["ss4"])
                o3 = oq.rearrange("p (a b) -> p a b", a=4)
                tt("dve", o3, o3, bc3(ss4, [128, 4, 128], 2), ALU.mult, oK + ["ss4"], oK)
                tt("dve", o3, o3, bcsw.unsqueeze(1).to_broadcast([128, 4, 128]), ALU.mult, oK + ["bcsw"], oK)
                tt("dve", yatoks[i % 2], oq, zat[:, i, :], ALU.mult, oK + [("zatok", b % 2, i)], [("yatok", i % 2)])
                yield

            def stage_a2(i):
                ts_ = slice(128 * i, 128 * i + 128)
                for h in range(4):
                    tr(psb(7)[:, 128 * h:128 * h + 128], yatoks[i % 2][:, 128 * h:128 * h + 128], ident_bf,
                       [("yatok", i % 2), "ident_bf"], [PSK[7]])
                    if h % 2 == 1:
                        yield
                cp("dve", yaT[:, :, ts_], psb(7)[:, 0:512].rearrange("p (a b) -> p a b", a=4), [PSK[7]], [("yaT", i)])
                yield

            def stage_b(i):
                ts_ = slice(128 * i, 128 * i + 128)
                xb_ = xl[i % 2]
                xk = ("xl", i % 2)
                xn = xnews[i % 2]
                for hf in range(2):
                    for c in range(8):
                        src = ysl_ if c < 4 else yaT
                        kk = ("ysl", b % 2, c) if c < 4 else ("yaT", i)
                        mm(PS[6][:, :], src[:, c % 4, ts_], wout_bf[:, c, 512 * hf:512 * hf + 512], c == 0, c == 7,
                           [kk, "wout_bf"], [PSK[6]])
                        if c % 2 == 1:
                            yield
                    tt("dve", xn[:, 512 * hf:512 * hf + 512], PS[6][:, :], xb_[:, 512 * hf:512 * hf + 512],
                       ALU.add, [PSK[6], xk], [("xnew", i % 2, hf)])
                    cp("dve", xnbs[i % 2][:, 512 * hf:512 * hf + 512], xn[:, 512 * hf:512 * hf + 512],
                       [("xnew", i % 2, hf)], [("xnb", i % 2, hf)])
                    yield

            def stage_c1(i):
                plk = ("pl", i % 2)
                for c in range(8):
                    tr(psb(7)[:, 128 * c:128 * c + 128], xnbs[i % 2][:, 128 * c:128 * c + 128], ident_bf,
                       [("xnb", i % 2, c // 4), "ident_bf"], [PSK[7]])
                    if c % 2 == 1:
                        yield
                cp("dve", xnT.rearrange("p a b -> p (a b)"), psb(7)[:, :], [PSK[7]], ["xnT"])
                for c in range(2):
                    tr(PS[6][:, 128 * c:128 * c + 128], pl[i % 2][:, 128 * c:128 * c + 128], ident_f, [plk, "ident_f"],
                       [PSK[6]])
                cp("dve", pT.rearrange("p a b -> p (a b)"), PS[6][:, 0:256], [PSK[6]], ["pT"])
                yield

            def stage_c2(i, hf):
                xb_ = xl[i % 2]
                xk = ("xl", i % 2)
                xn = xnews[i % 2]
                hsl = slice(512 * hf, 512 * hf + 512)
                pk_ = PSK[6]
                for c in range(8):
                    mm(PS[6][:, :], xnT[:, c, :], pg_bf[:, c, hsl], c == 0, c == 7, ["xnT", "pg_bf"], [pk_])
                    if c % 2 == 1:
                        yield
                act(gate[:, hsl], PS[6][:, :], AF.Tanh, [pk_], [("gate", hf)], scale=0.5)
                for c in range(2):
                    mm(PS[6][:, :], pT[:, c, :], pp_bf[:, c, hsl], c == 0, c == 1, ["pT", "pp_bf"], [pk_])
                stt(gate[:, hsl], gate[:, hsl], 1.0, PS[6][:, :], ALU.add, ALU.mult, [("gate", hf), pk_], [("gate", hf)])
                stt(xb_[:, hsl], gate[:, hsl], 0.5, xn[:, hsl], ALU.mult, ALU.add, [("gate", hf), ("xnew", i % 2, hf), xk], [xk])
                yield

            def store(i):
                tok = slice(t0 + 128 * i, t0 + 128 * i + 128)
                P.dma("sp", out_d[tok, :], xl[i % 2], [("xl", i % 2)], [("out", b, i)])
                fin_keys.append(("out", b, i))

            def gen():
                load_x(b, 0)
                load_x(b, 1)
                yield from stage_a(0)
                yield from stage_a(1)
                yield from stage_a2(0)
                yield from stage_a(2)
                yield from stage_a2(1)
                yield from stage_a(3)
                yield from stage_a2(2)
                yield from stage_a2(3)

                def fin(i):
                    yield from stage_c1(i)
                    yield from stage_c2(i, 0)
                    yield from stage_c2(i, 1)
                    store(i)
                    if i + 2 < 4:
                        load_x(b, i + 2)
                    yield

                yield from stage_b(0)
                yield from stage_b(1)
                yield from fin(0)
                yield from stage_b(2)
                yield from fin(1)
                yield from stage_b(3)
                yield from fin(2)
                yield from fin(3)

            return gen(), 112

        qzall = wst[1].bitcast(BF16)
        qz = [[qzall[:, 512 * (2 * c_ + hb_):512 * (2 * c_ + hb_) + 512] for hb_ in range(2)] for c_ in range(2)]
        for c_ in range(2):
            for hb_ in range(2):
                memset("pool", qz[c_][hb_], 0.0, [("qz", c_, hb_), ("wst", 1)])
        pending, pend_left = None, 0

        def advance(n):
            nonlocal pending, pend_left
            for _ in range(n):
                if pending is None:
                    return
                try:
                    next(pending)
                    pend_left = max(pend_left - 1, 1)
                except StopIteration:
                    pending, pend_left = None, 0

        for b in range(NBLK):
            t0 = 512 * b
            qb = qTl[b % 2]
            if b + 1 < NBLK:
                load_q(b + 1)
            for i in range(4):
                P.dma("sp", zatoks[b % 2][:, i, :], za_s[t0 + 128 * i:t0 + 128 * i + 128, :], [("za_s", b, i)],
                      [("zatok", b % 2, i)])
            for h in range(4):
                P.dma("sp", ysls[b % 2][:, h, :], ys_s[h, :, t0:t0 + 512], [("ys_s", h, b)], [("ysl", b % 2, h)])
            nkt = 4 * (b + 1)
            iters = []
            for h in range(4):
                for qh in range(2):
                    for kt in range(nkt):
                        j = kt - 4 * b
                        if j >= 0 and 128 * j >= 256 * (qh + 1):
                            continue
                        iters.append((h, kt, qh))
            n_it = len(iters)

            qz_done = set()

            def scores(it):
                h, kt, qh = it
                buf = scores.cnt % 4
                scores.cnt += 1
                j = kt - 4 * b
                q0 = max(128 * max(j, 0) - 256 * qh, 0)
                ks = slice(128 * kt, 128 * kt + 128)
                if h not in qz_done:
                    qz_done.add(h)
                    for c in range(2):
                        hs = slice(64 * c, 64 * c + 64)
                        cp("pool", qz[c][h % 2][hs, :], qb[hs, h, :], [("qTl", b % 2, h)], [("qz", c, h % 2)])
                for c in range(2):
                    mm(PS[buf][:, 256 * c + q0:256 * c + 256], kT_res[:, h, ks],
                       qz[c][h % 2][:, 256 * qh + q0:256 * qh + 256], True, True,
                       [("kT_res", h), ("qz", c, h % 2)], [("SC", buf)])
                return buf, q0

            scores.cnt = 0
            LA = 3
            sq_ = [scores(iters[k_]) for k_ in range(min(LA, n_it))]
            started = {}
            for idx, it in enumerate(iters):
                h, kt, qh = it
                if idx + LA < n_it:
                    sq_.append(scores(iters[idx + LA]))
                buf, q0 = sq_.pop(0)
                j = kt - 4 * b
                pt_i = pti % 4
                pti += 1
                pt = PTt[pt_i]
                pk = ("PT", pt_i)
                act(pt[:, :, q0:256], PS[buf].rearrange("p (c q) -> p c q", c=2)[:, :, q0:256], AF.Exp,
                    [("SC", buf)], [pk])
                if j >= 0 and 128 * j >= 256 * qh:
                    tt("pool", pt[:, :, q0:q0 + 128], pt[:, :, q0:q0 + 128],
                       tri.unsqueeze(1).to_broadcast([128, 2, 128]), ALU.mult, [pk, "cmask"], [pk])
                for qt in range(max(j, 2 * qh), 2 * qh + 2):
                    for c in range(2):
                        r = 2 * (qt - 2 * qh) + c
                        bank, col0 = r // 3, (r % 3) * 129
                        st_ = (h, qh, bank) not in started
                        started[(h, qh, bank)] = True
                        ql = 128 * (qt - 2 * qh)
                        lhs = pt[:, c, ql:ql + 128]
                        o_ap = OBk[bank][:, col0:col0 + 129]
                        rhs_ = v_res[:, kt, h, :]
                        P.op("pe", lambda e, o_ap=o_ap, lhs=lhs, rhs_=rhs_, st_=st_, sp_=False:
                             e.matmul(o_ap, lhsT=lhs, rhs=rhs_, start=st_, stop=sp_, skip_group_check=True),
                             [pk, ("v_res", kt), "v_ones"], [("OB", bank)])
                last_of_head = (idx + 1 == n_it) or (iters[idx + 1][0] != h) or (iters[idx + 1][2] != qh)
                if last_of_head:
                    for bank in range(2):
                        nreg = 3 if bank < 1 else 1
                        src = OBk[bank][:, 128:128 + 129 * (nreg - 1) + 1:129]
                        dst = rs[:, 3 * bank:3 * bank + nreg]
                        P.op("dve", lambda e, dst=dst, src=src: e.reciprocal(dst, src), [("OB", bank)], ["rs"])
                    ts("dve", rsn[:, 0:2], rs[:, 1:4:2], lamv[:, 1:2], None, ALU.mult, None, ["rs", "lamv"], ["rsn"])
                    for qt in range(2 * qh, 2 * qh + 2):
                        r0_, r1_ = 2 * (qt - 2 * qh), 2 * (qt - 2 * qh) + 1
                        oa = o_all[:, qt, 128 * h:128 * h + 128]
                        ts("dve", oa, OBk[r0_ // 3][:, (r0_ % 3) * 129:(r0_ % 3) * 129 + 128], rs[:, r0_:r0_ + 1], None,
                           ALU.mult, None, [("OB", r0_ // 3), "rs"], [("o_all", qt, h)])
                        stt(oa, OBk[r1_ // 3][:, (r1_ % 3) * 129:(r1_ % 3) * 129 + 128], rsn[:, qt - 2 * qh:qt - 2 * qh + 1], oa,
                            ALU.mult, ALU.add, [("OB", r1_ // 3), "rsn", ("o_all", qt, h)], [("o_all", qt, h)])
                if pending is not None:
                    advance(-(-pend_left // max(n_it - idx - 8, 1)))
            advance(10 ** 6)
            pending, pend_left = make_tail_units(b)
        advance(10 ** 6)
        P.emit(final_keys=fin_keys)
    return nc


_NC_CACHE = {}


def _core_inputs(b, x, p, positions, norm_w, w_in, ssm_lambda_re, ssm_lambda_im, ssm_log_dt,
                 ssm_b_re, ssm_b_im, ssm_c_re, ssm_c_im, ssm_d, glu_w, glu_b,
                 q_norm_w, k_norm_w, lambda_q1, lambda_k1, lambda_q2, lambda_k2,
                 subln_w, w_out, ple_w_proj, ple_w_gate):
    f = lambda a: np.ascontiguousarray(np.asarray(a, dtype=np.float32))
    vecs = np.concatenate([f(norm_w[0]).reshape(8, 128), f(ssm_d[0]).reshape(4, 128),
                           f(glu_b[0]).reshape(4, 128), f(subln_w[0]).reshape(1, 128)], axis=0)
    rows = np.concatenate([f(q_norm_w[0]), f(k_norm_w[0]), f(lambda_q1[0]), f(lambda_k1[0]),
                           f(lambda_q2[0]), f(lambda_k2[0]), f(subln_w[0])]).reshape(1, 512)
    lam = np.stack([f(ssm_lambda_re[0]).reshape(16, 128), f(ssm_lambda_im[0]).reshape(16, 128)], axis=1)
    return {
        "x": f(x[b]), "p": f(p[0, b]),
        "pos": np.ascontiguousarray(np.asarray(positions[b], dtype=np.int32).reshape(32, 128)),
        "vecs": np.ascontiguousarray(vecs), "rows": np.ascontiguousarray(rows),
        "w_in": f(w_in[0]), "lam": np.ascontiguousarray(lam), "log_dt": f(ssm_log_dt[0]).reshape(16, 2),
        "b_re": f(ssm_b_re[0]), "b_im": f(ssm_b_im[0]), "c_re": f(ssm_c_re[0]), "c_im": f(ssm_c_im[0]),
        "glu_w": f(glu_w[0]), "w_out": f(w_out[0]), "ple_w_proj": f(ple_w_proj[0]), "ple_w_gate": f(ple_w_gate[0]),
    }


def kernel(**inputs):
    if "nc" not in _NC_CACHE:
        _NC_CACHE["nc"] = build_program(DEBUG)
    nc = _NC_CACHE["nc"]
    in_maps = [_core_inputs(b, **inputs) for b in range(8)]
    res = run_bass_kernel_spmd(nc, in_maps, core_ids=list(range(8)))
    out = np.stack([np.asarray(r["out"], dtype=np.float32) for r in res.results], axis=0)
    return out
```

```python
import math
import contextlib
import numpy as np
import concourse.bass as bass
import concourse.mybir as mybir
from concourse.bass_utils import run_bass_kernel_spmd

F32 = mybir.dt.float32
BF16 = mybir.dt.bfloat16
I32 = mybir.dt.int32
ALU = mybir.AluOpType
AF = mybir.ActivationFunctionType
AX = mybir.AxisListType

SAME_ENGINE_SYNC = True
N_DMA_SEMS = 48
DEBUG = False

S = 4096
D = 1024
NBLK = 8
MS = 2
L = 8 * MS
NG = 8 * MS + 7
NH = 8 * MS + 1
NE = NG + NH
CPB = 512 // L
SCB = 64
EPS = 1e-6
TWO_PI = 2.0 * math.pi
CW1 = 6.28125
CW2 = TWO_PI - 6.28125
LAMBDA_INIT = 0.8 - 0.6 * math.exp(0.0)


class _Op:
    __slots__ = ("eng", "fn", "deps", "is_dma", "sem", "semval", "signal", "signo", "idx")


class Prog:
    ENGS = ("pe", "act", "dve", "pool", "sp")

    def __init__(self, nc):
        self.nc = nc
        self.ops = []
        self.last_w = {}
        self.readers = {}
        self.dma_rr = 0
        self.dma_sem_total = [0] * N_DMA_SEMS
        self.dma_sem_lastop = [None] * N_DMA_SEMS
        self.bar_deps = []
        self.need_bar = {e: False for e in self.ENGS}
        self.last_eng_op = {}

    def barrier(self):
        deps = [o for o in self.last_eng_op.values()]
        deps += [o for o in self.dma_sem_lastop if o is not None]
        self.bar_deps = deps
        for e in self.ENGS:
            self.need_bar[e] = True

    def _add(self, eng, fn, R, W, is_dma):
        op = _Op()
        op.eng, op.fn, op.is_dma = eng, fn, is_dma
        op.signal = False
        op.signo = 0
        op.sem = None
        op.semval = 0
        op.idx = len(self.ops)
        deps = []
        if self.need_bar[eng]:
            deps += self.bar_deps
            self.need_bar[eng] = False
        for k in R:
            w = self.last_w.get(k)
            if w is not None:
                deps.append(w)
        for k in W:
            w = self.last_w.get(k)
            if w is not None:
                deps.append(w)
            for r in self.readers.get(k, ()):
                deps.append(r)
        if is_dma:
            s = self.dma_rr
            self.dma_rr = (self.dma_rr + 1) % N_DMA_SEMS
            prev = self.dma_sem_lastop[s]
            if prev is not None:
                deps.append(prev)
            self.dma_sem_total[s] += 16
            op.sem = s
            op.semval = self.dma_sem_total[s]
            self.dma_sem_lastop[s] = op
        seen = set()
        dd = []
        for d in deps:
            if d is op or id(d) in seen:
                continue
            seen.add(id(d))
            if (not d.is_dma) and d.eng == eng and (eng == "pe" or not SAME_ENGINE_SYNC):
                continue
            dd.append(d)
            if not d.is_dma:
                d.signal = True
        op.deps = dd
        for k in W:
            self.last_w[k] = op
            self.readers[k] = []
        for k in R:
            if k not in W:
                self.readers.setdefault(k, []).append(op)
        self.ops.append(op)
        if not is_dma:
            self.last_eng_op[eng] = op
        return op

    def op(self, eng, fn, R=(), W=()):
        return self._add(eng, fn, tuple(R), tuple(W), False)

    def dma(self, q, out, in_, R=(), W=()):
        return self._add(q, lambda e: e.dma_start(out=out, in_=in_), tuple(R), tuple(W), True)

    def emit(self, final_keys=()):
        nc = self.nc
        self._add("sp", None, tuple(final_keys), (), False)
        cnt = {e: 0 for e in self.ENGS}
        for o in self.ops:
            if (not o.is_dma) and o.signal:
                cnt[o.eng] += 1
                o.signo = cnt[o.eng]
        with contextlib.ExitStack() as st:
            esem = {e: st.enter_context(nc.semaphore("sem_" + e)) for e in self.ENGS}
            dsem = [st.enter_context(nc.semaphore("dsem%d" % i)) for i in range(N_DMA_SEMS)]
            block = st.enter_context(nc.Block())
            per = {e: [o for o in self.ops if o.eng == e] for e in self.ENGS}

            def replay(e, eng):
                waited = {}
                for o in per[e]:
                    for d in o.deps:
                        if d.is_dma:
                            key, val, sem = ("d", d.sem), d.semval, dsem[d.sem]
                        else:
                            key, val, sem = ("e", d.eng), d.signo, esem[d.eng]
                        if waited.get(key, 0) >= val:
                            continue
                        waited[key] = val
                        eng.wait_ge(sem, val)
                    if o.fn is None:
                        continue
                    ins = o.fn(eng)
                    if o.is_dma:
                        ins.then_inc(dsem[o.sem], 16)
                    elif o.signal:
                        ins.then_inc(esem[e], 1)

            @block.sync
            def _(eng):
                replay("sp", eng)

            @block.scalar
            def _(eng):
                replay("act", eng)

            @block.vector
            def _(eng):
                replay("dve", eng)

            @block.gpsimd
            def _(eng):
                replay("pool", eng)

            @block.tensor
            def _(eng):
                replay("pe", eng)


class Region:
    def __init__(self, arena, base, size):
        self.arena, self.base, self.size, self.cur = arena, base, size, 0

    def reset(self):
        self.cur = 0

    def alloc(self, shape, dt, parts=None):
        esz = 2 if dt == BF16 else 4
        n = 1
        for s_ in shape[1:]:
            n *= s_
        nbytes = (n * esz + 31) // 32 * 32
        off = self.base + self.cur
        self.cur += nbytes
        assert self.cur <= self.size, ("region overflow", self.cur, self.size)
        v = self.arena[0:shape[0], off // 4:(off + nbytes) // 4]
        if dt != F32:
            v = v.bitcast(dt)
        v = v[:, 0:n]
        if len(shape) == 3:
            v = v.rearrange("p (a b) -> p a b", a=shape[1])
        elif len(shape) == 4:
            v = v.rearrange("p (a b c) -> p a b c", a=shape[1], b=shape[2])
        return v


def build_program(debug=False):
    nc = bass.Bass("TRN2", target_bir_lowering=False)
    P = Prog(nc)

    def din(name, shape, dt=F32):
        return nc.dram_tensor(name, list(shape), dt, kind="ExternalInput").ap()

    x_d = din("x", [S, D])
    p_d = din("p", [S, 256])
    pos_d = din("pos", [32, 128], I32)
    vec_d = din("vecs", [17, 128])
    row_d = din("rows", [1, 512])
    win_d = din("w_in", [D, 3072])
    lam_d = din("lam", [16, 2, 128])
    ldt_d = din("log_dt", [16, 2])
    bre_d = din("b_re", [32, 64, 16])
    bim_d = din("b_im", [32, 64, 16])
    cre_d = din("c_re", [32, 16, 64])
    cim_d = din("c_im", [32, 16, 64])
    glu_d = din("glu_w", [512, 512])
    wout_d = din("w_out", [D, D])
    pp_d = din("ple_w_proj", [256, D])
    pg_d = din("ple_w_gate", [D, D])
    out_d = nc.dram_tensor("out", [S, D], F32, kind="ExternalOutput").ap()
    ys_s = nc.dram_tensor("ys_s", [4, 128, S], BF16, kind="Internal").ap()
    u_s = nc.dram_tensor("u_s", [4, 128, S], BF16, kind="Internal").ap()
    zs_s = nc.dram_tensor("zs_s", [4, 128, S], BF16, kind="Internal").ap()
    za_s = nc.dram_tensor("za_s", [S, 512], BF16, kind="Internal").ap()
    qT_s = nc.dram_tensor("qT_s", [4, 128, S], BF16, kind="Internal").ap()
    kT_s = nc.dram_tensor("kT_s", [4, 128, S], BF16, kind="Internal").ap()
    v_s = nc.dram_tensor("v_s", [S, 512], BF16, kind="Internal").ap()
    dbg = {}
    if debug:
        for nm in ("ys", "qT", "kT"):
            dbg[nm] = nc.dram_tensor("dbg_" + nm, [4, 128, S], BF16, kind="ExternalOutput").ap()
        dbg["v"] = nc.dram_tensor("dbg_v", [S, 512], BF16, kind="ExternalOutput").ap()
        dbg["za"] = nc.dram_tensor("dbg_za", [S, 512], BF16, kind="ExternalOutput").ap()

    with contextlib.ExitStack() as st:
        ARENA_BYTES = 212480
        arena = st.enter_context(nc.sbuf_tensor("arena", [128, ARENA_BYTES // 4], F32))
        PQ = [st.enter_context(nc.psum_tensor("pq%d" % i, [128, 1024], F32)) for i in range(4)]
        PS = [PQ[i // 2][:, 512 * (i % 2):512 * (i % 2) + 512] for i in range(8)]
        PSK = ["ps%d" % i for i in range(8)]

        def psb(i):
            return PS[i].bitcast(BF16)

        PER = Region(arena, 0, 59136)
        RA = Region(arena, 59136, 65536)
        RB = Region(arena, 124672, 33792)
        RC = Region(arena, 158464, ARENA_BYTES - 158464)
        ident_bf = PER.alloc([128, 128], BF16)
        ident_f = PER.alloc([128, 128], F32)
        ones_f = PER.alloc([128, 128], F32)
        ones_bf = PER.alloc([128, 128], BF16)
        Wsel = PER.alloc([128, 8, 240], BF16)
        maskT = PER.alloc([128, MS, 128], BF16)
        cmask = PER.alloc([128, 4, 512], BF16)
        glu_bf = PER.alloc([128, 4, 512], BF16)
        W2_base = PER.cur
        wout_bf = PER.alloc([128, 8, 1024], BF16)
        pg_bf = PER.alloc([128, 8, 1024], BF16)
        pp_bf = PER.alloc([128, 2, 1024], BF16)
        W2 = Region(arena, W2_base, PER.cur - W2_base)
        vecT = PER.alloc([128, 17], F32)
        bcq = PER.alloc([128, 512], F32)
        bck = PER.alloc([128, 512], F32)
        cosT = PER.alloc([128, 32, 8], F32)
        sinT = PER.alloc([128, 32, 8], F32)
        APr = PER.alloc([128, 6, 16], F32)
        APi = PER.alloc([128, 6, 16], F32)
        carry = PER.alloc([128, 2, 16], F32)
        lamv = PER.alloc([128, 4], F32)
        sw08 = PER.alloc([128, 1], F32)
        epsv = PER.alloc([128, 1], F32)
        bcsw = PER.alloc([128, 128], F32)
        u1 = PER.alloc([128, 2, 16], F32)
        rmag = PER.alloc([128, 16], F32)
        nd = vecT[:, 0:8]
        dsk = vecT[:, 8:12]
        glb = vecT[:, 12:16]
        win_bf = RA.alloc([128, 8, 3072], BF16)
        M1 = RA.alloc([128, 32, MS, 128], BF16)
        Hr = RB.alloc([128, 16, NH, 16], BF16)
        nHi = RB.alloc([128, 16, NH, 16], BF16)
        Tt = RB.alloc([128, 32, MS, 128], BF16)

        def mm(out, lhsT, rhs, start, stop, R, W, tp=None):
            if tp is None:
                P.op("pe", lambda e: e.matmul(out, lhsT=lhsT, rhs=rhs, start=start, stop=stop), R, W)
            else:
                P.op("pe", lambda e: e.matmul(out, lhsT=lhsT, rhs=rhs, start=start, stop=stop, tile_position=tp), R, W)

        def tr(out, in_, ident, R, W):
            P.op("pe", lambda e: e.transpose(out, in_, ident), R, W)

        def act(out, in_, func, R, W, bias=None, scale=None, accum=None):
            kw = {}
            if bias is not None:
                kw["bias"] = bias
            if scale is not None:
                kw["scale"] = scale
            if accum is not None:
                kw["accum_out"] = accum
            P.op("act", lambda e: e.activation(out, in_, func, **kw), R, W)

        def tt(eng, out, a, b, op, R, W):
            P.op(eng, lambda e: e.tensor_tensor(out, a, b, op), R, W)

        def ts(eng, out, a, s1, s2, op0, op1, R, W):
            if op1 is None:
                P.op(eng, lambda e: e.tensor_scalar(out, a, s1, None, op0), R, W)
            else:
                P.op(eng, lambda e: e.tensor_scalar(out, a, s1, s2, op0, op1), R, W)

        def stt(out, a, s, b, op0, op1, R, W):
            P.op("dve", lambda e: e.scalar_tensor_tensor(out, a, s, b, op0, op1), R, W)

        def cp(eng, out, in_, R, W):
            if eng == "act":
                act(out, in_, AF.Copy, R, W)
            else:
                P.op(eng, lambda e: e.tensor_copy(out, in_), R, W)

        def iota(out, pattern, base, cm, W):
            P.op("pool", lambda e: e.iota(out, pattern=pattern, base=base, channel_multiplier=cm), (), W)

        def memset(eng, out, val, W):
            P.op(eng, lambda e: e.memset(out, val), (), W)

        def bc3(ap2, shape, axis):
            return ap2.unsqueeze(axis).to_broadcast(shape)

        def sincos(x, q, qi, r, s_out, c_out, key):
            ts("dve", q, x, 1.0 / TWO_PI, None, ALU.mult, None, [key + "x"], [key + "q"])
            cp("dve", qi, q, [key + "q"], [key + "qi"])
            cp("dve", q, qi, [key + "qi"], [key + "q"])
            stt(r, q, -CW1, x, ALU.mult, ALU.add, [key + "q", key + "x"], [key + "r"])
            stt(r, q, -CW2, r, ALU.mult, ALU.add, [key + "q", key + "r"], [key + "r"])
            ts("dve", x, r, -math.pi, math.pi, ALU.max, ALU.min, [key + "r"], [key + "x"])
            act(s_out, x, AF.Sin, [key + "x"], [key + "s"])
            ts("dve", x, r, math.pi / 2, None, ALU.add, None, [key + "r", key + "s"], [key + "x"])
            ts("dve", q, x, math.pi, -TWO_PI, ALU.is_gt, ALU.mult, [key + "x"], [key + "q"])
            tt("dve", x, x, q, ALU.add, [key + "q", key + "x"], [key + "x"])
            ts("dve", x, x, -math.pi, math.pi, ALU.max, ALU.min, [key + "x"], [key + "x"])
            act(c_out, x, AF.Sin, [key + "x"], [key + "c"])

        RC.reset()
        W2.reset()
        R0 = Region(arena, RA.base, RA.size)
        io_i = R0.alloc([128, 1920], I32)
        io_f = R0.alloc([128, 1920], F32)
        msk_f = R0.alloc([128, 1920], F32)
        iota(io_i[:, 0:128], [[1, 128]], 0, -1, ["io_i"])
        ts("dve", ident_f, io_i[:, 0:128], 0.0, None, ALU.is_equal, None, ["io_i"], ["ident_f"])
        cp("dve", ident_bf, ident_f, ["ident_f"], ["ident_bf"])
        memset("dve", ones_f, 1.0, ["ones_f"])
        memset("dve", ones_bf, 1.0, ["ones_bf"])
        memset("dve", epsv, EPS, ["epsv"])
        iota(io_i[:, 0:1920].rearrange("p (a b) -> p a b", a=8), [[16, 8], [1, 240]], -112, -1, ["io_i"])
        ts("dve", io_f[:, 0:1920], io_i[:, 0:1920], 0.0, None, ALU.is_equal, None, ["io_i"], ["io_f"])
        iota(io_i[:, 0:1920].rearrange("p (a b) -> p a b", a=8), [[0, 8], [1, 240]], 0, 0, ["io_i"])
        ts("dve", msk_f[:, 0:1920], io_i[:, 0:1920], 112.0, None, ALU.is_ge, None, ["io_i"], ["msk_f"])
        tt("dve", io_f[:, 0:1920], io_f[:, 0:1920], msk_f[:, 0:1920], ALU.mult, ["io_f", "msk_f"], ["io_f"])
        ts("dve", msk_f[:, 0:1920], io_i[:, 0:1920], 127.0, None, ALU.is_le, None, ["io_i"], ["msk_f"])
        tt("dve", Wsel.rearrange("p a b -> p (a b)"), io_f[:, 0:1920], msk_f[:, 0:1920], ALU.mult,
           ["io_f", "msk_f"], ["Wsel"])
        iota(io_i[:, 0:128].rearrange("p (a b) -> p a b", a=8), [[16, 8], [0, 16]], 0, -1, ["io_i"])
        memset("dve", maskT.rearrange("p a b -> p (a b)"), 1.0, ["maskT"])
        ts("dve", maskT[:, 0, :], io_i[:, 0:128], -15.0, None, ALU.is_ge, None, ["io_i", "maskT"], ["maskT"])
        for j in range(4):
            iota(io_i[:, 0:512], [[1, 512]], -128 * j, -1, ["io_i"])
            ts("dve", cmask[:, j, :], io_i[:, 0:512], 0.0, None, ALU.is_ge, None, ["io_i", "cmask"], ["cmask"])

        vec16 = R0.alloc([17, 128], F32)
        rowv = R0.alloc([1, 512], F32)
        lam16 = R0.alloc([16, 3, 128], F32)
        ldt16 = R0.alloc([16, 2], F32)
        pos_i = R0.alloc([32, 128], I32)
        pos_f = R0.alloc([32, 128], F32)
        P.dma("sp", vec16, vec_d, [], ["vec16"])
        P.dma("sp", rowv, row_d, [], ["rowv"])
        P.dma("sp", lam16[:, 0:2, :], lam_d, [], ["lam16a"])
        P.dma("sp", ldt16, ldt_d, [], ["ldt16"])
        P.dma("sp", pos_i, pos_d, [], ["pos_i"])
        cp("dve", lam16[:, 2, :].rearrange("p (a b) -> p a b", a=2), bc3(ldt16, [16, 2, 64], 2),
           ["ldt16"], ["lam16b"])
        cp("dve", pos_f, pos_i, ["pos_i"], ["pos_f"])
        tr(PS[0][:, 0:17], vec16, ident_f[0:17, 0:17], ["vec16", "ident_f"], [PSK[0]])
        cp("dve", vecT, PS[0][:, 0:17], [PSK[0]], ["vecT"])
        par = R0.alloc([128, 3, 16], F32)
        for i in range(3):
            tr(PS[1][:, 16 * i:16 * i + 16], lam16[:, i, :], ident_f[0:16, 0:16],
               ["lam16a", "lam16b", "ident_f"], [PSK[1]])
        cp("dve", par.rearrange("p a b -> p (a b)"), PS[1][:, 0:48], [PSK[1]], ["par"])
        posT = R0.alloc([128, 32], F32)
        tr(PS[2][:, 0:32], pos_f, ident_f[0:32, 0:32], ["pos_f", "ident_f"], [PSK[2]])
        cp("dve", posT, PS[2][:, 0:32], [PSK[2]], ["posT"])
        bcr = R0.alloc([128, 512], F32)
        mm(PS[3][:, 0:512], ones_f[0:1, :], rowv, True, True, ["ones_f", "rowv"], [PSK[3]])
        cp("dve", bcr, PS[3][:, 0:512], [PSK[3]], ["bcr"])
        ts("dve", bcsw, bcr[:, 384:512], 1.0 - LAMBDA_INIT, None, ALU.mult, None, ["bcr"], ["bcsw"])
        ts("dve", bcq.rearrange("p (a b) -> p a b", a=8), bc3(bcr[:, 0:64], [128, 8, 64], 1),
           0.125, None, ALU.mult, None, ["bcr"], ["bcq"])
        cp("dve", bck.rearrange("p (a b) -> p a b", a=8), bc3(bcr[:, 64:128], [128, 8, 64], 1), ["bcr"], ["bck"])
        lsc = R0.alloc([128, 128], F32)
        lsum = R0.alloc([128, 2], F32)
        tt("dve", lsc[:, 0:64], bcr[:, 128:192], bcr[:, 192:256], ALU.mult, ["bcr"], ["lsc"])
        tt("dve", lsc[:, 64:128], bcr[:, 256:320], bcr[:, 320:384], ALU.mult, ["bcr", "lsc"], ["lsc"])
        P.op("dve", lambda e: e.tensor_reduce(lsum, lsc.rearrange("p (a b) -> p a b", a=2), AX.X, ALU.add),
             ["lsc"], ["lsum"])
        act(lsum, lsum, AF.Exp, ["lsum"], ["lsum"])
        tt("dve", lamv[:, 0:1], lsum[:, 0:1], lsum[:, 1:2], ALU.subtract, ["lsum"], ["lamv"])
        ts("dve", lamv[:, 0:1], lamv[:, 0:1], LAMBDA_INIT, None, ALU.add, None, ["lamv"], ["lamv"])
        ts("dve", lamv[:, 1:2], lamv[:, 0:1], -1.0, None, ALU.mult, None, ["lamv"], ["lamv"])
        ts("dve", sw08, vecT[:, 16:17], 1.0 - LAMBDA_INIT, None, ALU.mult, None, ["vecT"], ["sw08"])
        invf = R0.alloc([128, 8], F32)
        for i in range(8):
            memset("dve", invf[:, i:i + 1], float(np.float32(500000.0) ** np.float32(-(2.0 * i) / 16.0)), ["invf"])
        ang = R0.alloc([128, 256], F32)
        aq = R0.alloc([128, 256], F32)
        aqi = R0.alloc([128, 256], I32)
        ar_ = R0.alloc([128, 256], F32)
        tt("dve", ang.rearrange("p (a b) -> p a b", a=32), bc3(posT, [128, 32, 8], 2), bc3(invf, [128, 32, 8], 1),
           ALU.mult, ["posT", "invf"], ["angx"])
        sincos(ang, aq, aqi, ar_, sinT.rearrange("p a b -> p (a b)"), cosT.rearrange("p a b -> p (a b)"), "ang")

        NEt = 16 * NE
        kv_i = R0.alloc([128, NE], I32)
        kv = R0.alloc([128, NE], F32)
        iota(kv_i[:, 0:NG], [[-1, NG]], L - 1, 0, ["kv_i"])
        iota(kv_i[:, NG:NE], [[1, NH]], 0, 0, ["kv_i"])
        cp("dve", kv, kv_i, ["kv_i"], ["kv"])
        dtv = R0.alloc([128, 16], F32)
        act(dtv, par[:, 2, :], AF.Exp, ["par"], ["dtv"])
        lrdt = R0.alloc([128, 16], F32)
        thv = R0.alloc([128, 16], F32)
        tt("dve", lrdt, par[:, 0, :], dtv, ALU.mult, ["par", "dtv"], ["lrdt"])
        tt("dve", thv, par[:, 1, :], dtv, ALU.mult, ["par", "dtv"], ["thv"])
        Emag = R0.alloc([128, 16, NE], F32)
        Eph = R0.alloc([128, 16, NE], F32)
        Eq = R0.alloc([128, 16, NE], F32)
        Eqi = R0.alloc([128, 16, NE], I32)
        Er = R0.alloc([128, 16, NE], F32)
        Ei = R0.alloc([128, 16, NE], F32)
        Ert = R0.alloc([128, 16, NE], F32)
        shp = [128, 16, NE]
        tt("dve", Emag, bc3(lrdt, shp, 2), bc3(kv, shp, 1), ALU.mult, ["lrdt", "kv"], ["Emag"])
        act(Emag, Emag, AF.Exp, ["Emag"], ["Emag"])
        tt("dve", Eph, bc3(thv, shp, 2), bc3(kv, shp, 1), ALU.mult, ["thv", "kv"], ["Ephx"])
        f2 = lambda a: a.rearrange("p a b -> p (a b)")
        sincos(f2(Eph), f2(Eq), f2(Eqi), f2(Ert), f2(Ei), f2(Er), "Eph")
        tt("dve", f2(Er), f2(Er), f2(Emag), ALU.mult, ["Ephc", "Emag"], ["Er"])
        tt("dve", f2(Ei), f2(Ei), f2(Emag), ALU.mult, ["Ephs", "Emag"], ["Ei"])
        c_nr = R0.alloc([128, 16], F32)
        c_den = R0.alloc([128, 16], F32)
        c_t = R0.alloc([128, 16], F32)
        c_r = R0.alloc([128, 16], F32)
        c_i = R0.alloc([128, 16], F32)
        lr, li = par[:, 0, :], par[:, 1, :]
        ni = Ei[:, :, NG + 1]
        ts("dve", c_nr, Er[:, :, NG + 1], -1.0, None, ALU.add, None, ["Er"], ["c_nr"])
        tt("dve", c_den, lr, lr, ALU.mult, ["par"], ["c_den"])
        tt("dve", c_t, li, li, ALU.mult, ["par"], ["c_t"])
        tt("dve", c_den, c_den, c_t, ALU.add, ["c_den", "c_t"], ["c_den"])
        P.op("dve", lambda e: e.reciprocal(c_den, c_den), ["c_den"], ["c_den"])
        tt("dve", c_r, c_nr, lr, ALU.mult, ["c_nr", "par"], ["c_r"])
        tt("dve", c_t, ni, li, ALU.mult, ["Ei", "par", "c_t"], ["c_t"])
        tt("dve", c_r, c_r, c_t, ALU.add, ["c_r", "c_t"], ["c_r"])
        tt("dve", c_r, c_r, c_den, ALU.mult, ["c_r", "c_den"], ["c_r"])
        tt("dve", c_i, ni, lr, ALU.mult, ["Ei", "par"], ["c_i"])
        tt("dve", c_t, c_nr, li, ALU.mult, ["c_nr", "par", "c_t"], ["c_t"])
        tt("dve", c_i, c_i, c_t, ALU.subtract, ["c_i", "c_t"], ["c_i"])
        tt("dve", c_i, c_i, c_den, ALU.mult, ["c_i", "c_den"], ["c_i"])
        cp("dve", APr[:, 0, :], Er[:, :, NG + L], ["Er"], ["APr"])
        cp("dve", APi[:, 0, :], Ei[:, :, NG + L], ["Ei"], ["APi"])
        sq1 = R0.alloc([128, 16], F32)
        sq2 = R0.alloc([128, 16], F32)
        for d_ in range(1, 6):
            tt("dve", sq1, APr[:, d_ - 1, :], APr[:, d_ - 1, :], ALU.mult, ["APr", "sq1"], ["sq1"])
            tt("dve", sq2, APi[:, d_ - 1, :], APi[:, d_ - 1, :], ALU.mult, ["APi", "sq2"], ["sq2"])
            tt("dve", APr[:, d_, :], sq1, sq2, ALU.subtract, ["sq1", "sq2", "APr"], ["APr"])
            tt("dve", sq1, APr[:, d_ - 1, :], APi[:, d_ - 1, :], ALU.mult, ["APr", "APi", "sq1"], ["sq1"])
            ts("dve", APi[:, d_, :], sq1, 2.0, None, ALU.mult, None, ["sq1", "APi"], ["APi"])
        cp("dve", rmag, Emag[:, :, NG + L], ["Emag"], ["rmag"])
        P.op("dve", lambda e: e.reciprocal(sq1, rmag), ["rmag", "sq1"], ["sq1"])
        tt("dve", u1[:, 0, :], APr[:, 0, :], sq1, ALU.mult, ["APr", "sq1"], ["u1"])
        tt("dve", u1[:, 1, :], APi[:, 0, :], sq1, ALU.mult, ["APi", "sq1", "u1"], ["u1"])
        Bre = R0.alloc([128, 16, 16], F32)
        Bim = R0.alloc([128, 16, 16], F32)
        bbr = R0.alloc([128, 16, 16], F32)
        bbi = R0.alloc([128, 16, 16], F32)
        bt = R0.alloc([128, 16, 16], F32)
        P.dma("sp", Bre, bre_d.rearrange("(gp g2) n q -> (g2 n) gp q", g2=2), [], ["Bre"])
        P.dma("sp", Bim, bim_d.rearrange("(gp g2) n q -> (g2 n) gp q", g2=2), [], ["Bim"])
        s3 = [128, 16, 16]
        tt("dve", bbr, Bre, bc3(c_r, s3, 2), ALU.mult, ["Bre", "c_r"], ["bbr"])
        tt("dve", bt, Bim, bc3(c_i, s3, 2), ALU.mult, ["Bim", "c_i"], ["bt"])
        tt("dve", bbr, bbr, bt, ALU.subtract, ["bbr", "bt"], ["bbr"])
        tt("dve", bbi, Bim, bc3(c_r, s3, 2), ALU.mult, ["Bim", "c_r"], ["bbi"])
        tt("dve", bt, Bre, bc3(c_i, s3, 2), ALU.mult, ["Bre", "c_i", "bt"], ["bt"])
        tt("dve", bbi, bbi, bt, ALU.add, ["bbi", "bt"], ["bbi"])
        Xc = R0.alloc([128, 4, 128], F32)
        Ctr = R0.alloc([128, 16, 16], F32)
        Cti = R0.alloc([128, 16, 16], F32)
        for ri, cd in enumerate((cre_d, cim_d)):
            for hf in range(2):
                for gpl in range(8):
                    for g2 in range(2):
                        g = 2 * (8 * hf + gpl) + g2
                        P.dma("sp" if (gpl % 2 == 0) else "act", Xc[16 * gpl:16 * gpl + 16, 2 * ri + hf, 64 * g2:64 * g2 + 64],
                              cd[g], [], [("Xc", ri, hf, gpl, g2)])
                tr(PS[4 + 2 * ri + hf][:, 0:128], Xc[:, 2 * ri + hf, :], ident_f,
                   [("Xc", ri, hf, a, b) for a in range(8) for b in range(2)] + ["ident_f"], [PSK[4 + 2 * ri + hf]])
                dst = (Ctr, Cti)[ri]
                cp("dve", dst[:, 8 * hf:8 * hf + 8, :].rearrange("p a b -> p (a b)"), PS[4 + 2 * ri + hf][:, 0:128],
                   [PSK[4 + 2 * ri + hf]], [("Ct", ri, hf)])
        CtK = [("Ct", ri, hf) for ri in range(2) for hf in range(2)]
        Gr = RC.alloc([128, 16, NG, 16], BF16)
        Gi = RC.alloc([128, 16, NG, 16], BF16)
        GC = 2
        g1 = W2.alloc([128, GC, NG, 16], F32)
        g2t = W2.alloc([128, GC, NG, 16], F32)
        g3 = W2.alloc([128, GC, NG, 16], F32)
        g4 = W2.alloc([128, GC, NG, 16], F32)
        for c0 in range(0, 16, GC):
            sl = slice(c0, c0 + GC)
            for (E0, n0, nn, Xr_, Xi_, Or_, Oi_, neg, kx) in (
                    (0, 0, NG, bbr, bbi, Gr, Gi, False, ["bbr", "bbi"]),
                    (NG, 0, NH, Ctr, Cti, Hr, nHi, True, CtK)):
                shp4 = [128, GC, nn, 16]
                er = Er[:, sl, E0:E0 + nn].unsqueeze(3).to_broadcast(shp4)
                ei = Ei[:, sl, E0:E0 + nn].unsqueeze(3).to_broadcast(shp4)
                xr = Xr_[:, sl, :].unsqueeze(2).to_broadcast(shp4)
                xi = Xi_[:, sl, :].unsqueeze(2).to_broadcast(shp4)
                a1, a2 = g1[:, :, 0:nn, :], g2t[:, :, 0:nn, :]
                a3, a4 = g3[:, :, 0:nn, :], g4[:, :, 0:nn, :]
                tt("dve", a1, er, xr, ALU.mult, ["Er"] + kx + ["g1"], ["g1"])
                tt("dve", a2, ei, xi, ALU.mult, ["Ei"] + kx + ["g2"], ["g2"])
                tt("dve", Or_[:, sl, :, :], a1, a2, ALU.subtract, ["g1", "g2"], [("GH", E0, c0, 0)])
                tt("pool", a3, er, xi, ALU.mult, ["Er"] + kx + ["g3"], ["g3"])
                tt("pool", a4, ei, xr, ALU.mult, ["Ei"] + kx + ["g4"], ["g4"])
                if neg:
                    fl = lambda a: a.rearrange("p a b c -> p a (b c)")
                    stt(fl(Oi_[:, sl, :, :]), fl(a3), -1.0, fl(a4), ALU.mult, ALU.subtract, ["g3", "g4"], [("GH", E0, c0, 1)])
                else:
                    tt("pool", Oi_[:, sl, :, :], a3, a4, ALU.add, ["g3", "g4"], [("GH", E0, c0, 1)])
        GK = [("GH", 0, c0, i) for c0 in range(0, 16, GC) for i in range(2)]
        HK = [("GH", NG, c0, i) for c0 in range(0, 16, GC) for i in range(2)]
        P.barrier()
        for g in range(32):
            gp, hf = g // 2, g % 2
            hs = slice(64 * hf, 64 * hf + 64)
            pb = 4 + (g % 2)
            for dl in range(MS):
                r0 = (L - 1) - 8 * dl
                o = PS[pb][:, 128 * dl:128 * dl + 128]
                mm(o, Gr[hs, gp, r0:r0 + 8, :].rearrange("p a b -> p (a b)"),
                   Hr[hs, gp, 0:8, :].rearrange("p a b -> p (a b)"), True, False, GK + HK, [PSK[pb]])
                mm(o, Gi[hs, gp, r0:r0 + 8, :].rearrange("p a b -> p (a b)"),
                   nHi[hs, gp, 0:8, :].rearrange("p a b -> p (a b)"), False, True, GK + HK, [PSK[pb]])
            tt("dve", Tt[:, g, :, :].rearrange("p a b -> p (a b)"), PS[pb][:, 0:128 * MS],
               maskT.rearrange("p a b -> p (a b)"), ALU.mult, [PSK[pb], "maskT"], ["Tt"])
            pt = 6 + (g % 2)
            for j in range(MS):
                for ri, Gx in enumerate((Gr, Gi)):
                    c0 = (j * 2 + ri) * 64
                    tr(psb(pt)[:, c0:c0 + 64], Gx[hs, gp, 8 * j:8 * j + 8, :].rearrange("p a b -> p (a b)"),
                       ident_bf[hs, hs], GK + ["ident_bf"], [PSK[pt]])
            cp("act", M1[:, g, :, :].rearrange("p a b -> p (a b)"), psb(pt)[:, 0:128 * MS], [PSK[pt]], ["M1"])
        wst = [RC.alloc([128, 1024], F32) for _ in range(2)]
        wi = 0

        def load_w(dst, src, rows_chunks, ncols, key, scale_col=None):
            nonlocal wi
            for c in range(rows_chunks):
                for n0 in range(0, ncols, 1024):
                    nn = min(1024, ncols - n0)
                    b = wi % 2
                    wi += 1
                    P.dma("sp", wst[b][:, 0:nn], src[128 * c:128 * c + 128, n0:n0 + nn], [], [("wst", b)])
                    eng = "act" if (wi % 2) else "dve"
                    if scale_col is not None:
                        if eng == "act":
                            act(dst[:, c, n0:n0 + nn], wst[b][:, 0:nn], AF.Copy, [("wst", b), "vecT"], [key],
                                scale=scale_col[:, c:c + 1])
                        else:
                            ts("dve", dst[:, c, n0:n0 + nn], wst[b][:, 0:nn], scale_col[:, c:c + 1], None,
                               ALU.mult, None, [("wst", b), "vecT"], [key])
                    else:
                        cp(eng, dst[:, c, n0:n0 + nn], wst[b][:, 0:nn], [("wst", b)], [key])

        load_w(win_bf, win_d, 8, 3072, "win_bf", scale_col=nd)
        load_w(glu_bf, glu_d, 4, 512, "glu_bf")
        W2.reset()
        CH_ = 2048 // L
        PTr = W2.alloc([128, 16, CH_], F32)
        PTi = W2.alloc([128, 16, CH_], F32)
        Rm = W2.alloc([128, 16, CH_], F32)
        pw = RC.alloc([128, 8, 2, 16], F32)
        dA = RC.alloc([128, 16, CH_ // 2], F32)
        dB = RC.alloc([128, 16, CH_ // 2], F32)
        memset("dve", PTr[:, :, 0:1], 1.0, ["PT"])
        memset("dve", PTi[:, :, 0:1], 0.0, ["PT"])
        cp("dve", pw[:, 0, :, :], u1, ["u1"], ["pw"])
        k_ = 0
        while (1 << k_) < CH_:
            m_ = 1 << k_
            if k_ > 0:
                pr, pi_ = pw[:, k_ - 1, 0, :], pw[:, k_ - 1, 1, :]
                tt("dve", dA[:, :, 0], pr, pr, ALU.mult, ["pw", "dA"], ["dA"])
                tt("dve", dB[:, :, 0], pi_, pi_, ALU.mult, ["pw", "dB"], ["dB"])
                tt("dve", pw[:, k_, 0, :], dA[:, :, 0], dB[:, :, 0], ALU.subtract, ["dA", "dB", "pw"], ["pw"])
                tt("dve", dA[:, :, 0], pr, pi_, ALU.mult, ["pw", "dA"], ["dA"])
                ts("dve", pw[:, k_, 1, :], dA[:, :, 0], 2.0, None, ALU.mult, None, ["dA", "pw"], ["pw"])
            shp_ = [128, 16, m_]
            br = pw[:, k_, 0, :].unsqueeze(2).to_broadcast(shp_)
            bi = pw[:, k_, 1, :].unsqueeze(2).to_broadcast(shp_)
            sr, si = PTr[:, :, 0:m_], PTi[:, :, 0:m_]
            a_, b_2 = dA[:, :, 0:m_], dB[:, :, 0:m_]
            tt("dve", a_, sr, br, ALU.mult, ["PT", "pw", "dA"], ["dA"])
            tt("dve", b_2, si, bi, ALU.mult, ["PT", "pw", "dB"], ["dB"])
            tt("dve", PTr[:, :, m_:2 * m_], a_, b_2, ALU.subtract, ["dA", "dB", "PT"], ["PT"])
            tt("dve", a_, sr, bi, ALU.mult, ["PT", "pw", "dA"], ["dA"])
            tt("dve", b_2, si, br, ALU.mult, ["PT", "pw", "dB"], ["dB"])
            tt("dve", PTi[:, :, m_:2 * m_], a_, b_2, ALU.add, ["dA", "dB", "PT"], ["PT"])
            k_ += 1
        cp("dve", Rm, rmag.unsqueeze(2).to_broadcast([128, 16, CH_]), ["rmag"], ["Rm"])
        memset("dve", Rm[:, :, 0:1], 0.0, ["Rm"])
        memset("dve", carry.rearrange("p a b -> p (a b)"), 0.0, ["carry"])
        P.barrier()

        RC.reset()
        xt = [RC.alloc([128, 1024], F32) for _ in range(2)]
        hbf = RC.alloc([128, 1024], BF16)
        hT = RC.alloc([128, 8, 512], BF16)
        uT = RC.alloc([128, 4, 512], BF16)
        zsT = RC.alloc([128, 4, 512], BF16)
        qsq = RC.alloc([128, 512], F32)
        w2_mark = W2.cur
        qns = [W2.alloc([128, 512], F32) for _ in range(2)]
        qtoks = [[W2.alloc([128, 512], BF16) for _ in range(2)] for _ in range(2)]
        qTb = RC.alloc([128, 4, 512], BF16)
        kTb = RC.alloc([128, 4, 512], BF16)
        vtok = [RC.alloc([128, 512], BF16) for _ in range(2)]
        st8 = RC.alloc([128, 8], F32)
        rt1 = RC.alloc([128, 8, 8], F32)
        rt2 = RC.alloc([128, 8, 8], F32)
        ss1 = RC.alloc([128, 4], F32)

        hTs = [hT, RC.alloc([128, 8, 512], BF16)]
        mhalf = RC.alloc([128, 8], F32)
        memset("dve", mhalf, -0.5, ["mhalf"])

        hbfs = [hbf, RC.alloc([128, 1024], BF16)]

        def rms_a(b, i):
            t0 = 512 * b
            xb_ = xt[i % 2]
            xk = ("xt", i % 2)
            hb, hk = hbfs[i % 2], ("hbf", i % 2)
            P.dma("sp", xb_, x_d[t0 + 128 * i:t0 + 128 * i + 128, :], [], [xk])
            act(hb, xb_, AF.Square, [xk], [hk, "ss1"], accum=ss1[:, 0:1])
            ts("dve", ss1[:, 1:2], ss1[:, 0:1], 1.0 / D, EPS, ALU.mult, ALU.add, ["ss1"], ["ss1b"])
            tt("pool", ss1[:, 3:4], ss1[:, 1:2], mhalf[:, 0:1], ALU.pow, ["ss1b", "mhalf"], ["ss1d"])
            act(hb, xb_, AF.Copy, [xk, "ss1d"], [hk], scale=ss1[:, 3:4])

        def rms_b(b, i):
            hb, hk = hbfs[i % 2], ("hbf", i % 2)
            for c in range(8):
                tr(psb(0)[:, 128 * c:128 * c + 128], hb[:, 128 * c:128 * c + 128], ident_bf,
                   [hk, "ident_bf"], [PSK[0]])
            cp("act", hTs[b % 2][:, :, 128 * i:128 * i + 128], psb(0).rearrange("p (a b) -> p a b", a=8),
               [PSK[0]], [("hT", b % 2, i)])

        def rms_sched(b, slot):
            order = {0: [("a", 0)], 1: [("a", 1)], 2: [("b", 0)], 3: [("a", 2)], 4: [("b", 1)], 5: [("a", 3)],
                     6: [("b", 2)], 7: [("b", 3)]}
            for kind, i in order[slot]:
                (rms_a if kind == "a" else rms_b)(b, i)

        for slot in range(8):
            rms_sched(0, slot)
        for b in range(NBLK):
            t0 = 512 * b
            hT = hTs[b % 2]
            hTK = [("hT", b % 2, i) for i in range(4)]
            for fi, f in enumerate(range(8)):
                pb = 1 + (fi % 2)
                for c in range(8):
                    mm(PS[pb][:, :], win_bf[:, c, 128 * f:128 * f + 128], hT[:, c, :], c == 0, c == 7,
                       ["win_bf"] + hTK, [PSK[pb]])
                if f < 4:
                    cp("act", uT[:, f, :], PS[pb][:, :], [PSK[pb]], [("uT", f)])
                    P.dma("act", u_s[f, :, t0:t0 + 512], uT[:, f, :], [("uT", f)], [("u_s", f, b)])
                else:
                    act(zsT[:, f - 4, :], PS[pb][:, :], AF.Silu, [PSK[pb]], [("zsT", f - 4)])
                    P.dma("act", zs_s[f - 4, :, t0:t0 + 512], zsT[:, f - 4, :], [("zsT", f - 4)], [("zs_s", f - 4, b)])
                if b + 1 < NBLK:
                    rms_sched(b + 1, fi)
            def qk_transposes(i):
                ts_ = slice(128 * i, 128 * i + 128)
                for wi_, which in enumerate(("q", "k")):
                    qt_ = qtoks[wi_][i % 2]
                    qk_ = ("qtok", wi_, i % 2)
                    pk3 = ("ps3", wi_)
                    for h in range(4):
                        tr(psb(3)[:, 512 * wi_ + 128 * h:512 * wi_ + 128 * h + 128], qt_[:, 128 * h:128 * h + 128], ident_bf,
                           [qk_, "ident_bf"], [pk3])
                    dstT = qTb if which == "q" else kTb
                    cp("act", dstT[:, :, ts_], psb(3)[:, 512 * wi_:512 * wi_ + 512].rearrange("p (a b) -> p a b", a=4),
                       [pk3], [(which + "Tb", i)])

            for i in range(4):
                ts_ = slice(128 * i, 128 * i + 128)
                for wi_, (which, col0) in enumerate((("q", 1024), ("k", 1536), ("v", 2048), ("za", 2560))):
                    pb = (4 + wi_ + 2 * (i % 2)) if wi_ < 2 else (wi_ - 1)
                    for c in range(8):
                        mm(PS[pb][:, :], hT[:, c, ts_], win_bf[:, c, col0:col0 + 512], c == 0, c == 7,
                           ["win_bf", ("hT", b % 2, i)], [PSK[pb]])
                    if which == "v":
                        vb = vtok[0]
                        cp("act", vb, PS[pb][:, :], [PSK[pb]], [("vtok", 0)])
                        P.dma("act", v_s[t0 + 128 * i:t0 + 128 * i + 128, :], vb, [("vtok", 0)],
                              [("v_s", b, i)])
                        continue
                    if which == "za":
                        vb = vtok[1]
                        act(vb, PS[pb][:, :], AF.Silu, [PSK[pb]], [("vtok", 1)])
                        P.dma("act", za_s[t0 + 128 * i:t0 + 128 * i + 128, :], vb, [("vtok", 1)],
                              [("za_s", b, i)])
                        continue
                    wq = bcq if which == "q" else bck
                    qn = qns[wi_]
                    qnk = "qn%d" % wi_
                    act(qsq, PS[pb][:, :], AF.Square, [PSK[pb]], ["qsq"])
                    P.op("dve", lambda e: e.tensor_reduce(st8, qsq.rearrange("p (a b) -> p a b", a=8), AX.X, ALU.add),
                         ["qsq"], ["st8"])
                    ts("dve", st8, st8, 1.0 / 64, EPS, ALU.mult, ALU.add, ["st8"], ["st8"])
                    tt("pool", st8, st8, mhalf, ALU.pow, ["st8", "mhalf"], ["st8"])
                    q3 = qn.rearrange("p (a b) -> p a b", a=8)
                    tt("dve", q3, PS[pb][:, :].rearrange("p (a b) -> p a b", a=8), bc3(st8, [128, 8, 64], 2),
                       ALU.mult, [PSK[pb], "st8"], [qnk])
                    tt("dve", qn, qn, wq, ALU.mult, [qnk, "bcq", "bck"], [qnk])
                    tile_idx = 4 * b + i
                    cs = cosT[:, tile_idx, :].unsqueeze(1).to_broadcast([128, 8, 8])
                    sn = sinT[:, tile_idx, :].unsqueeze(1).to_broadcast([128, 8, 8])
                    x1, x2 = q3[:, :, 0:8], q3[:, :, 8:16]
                    tt("dve", rt1, x1, cs, ALU.mult, [qnk, "angc"], ["rt1"])
                    tt("dve", rt2, x2, sn, ALU.mult, [qnk, "angs"], ["rt2"])
                    tt("dve", rt1, rt1, rt2, ALU.subtract, ["rt1", "rt2"], ["rt1"])
                    tt("dve", rt2, x2, cs, ALU.mult, [qnk, "angc", "rt2"], ["rt2"])
                    tt("dve", x2, x1, sn, ALU.mult, [qnk, "angs"], [qnk])
                    tt("dve", x2, x2, rt2, ALU.add, [qnk, "rt2"], [qnk])
                    cp("dve", x1, rt1, ["rt1", qnk], [qnk])
                    cp("pool", qtoks[wi_][i % 2], qn, [qnk], [("qtok", wi_, i % 2)])
                if i >= 1:
                    qk_transposes(i - 1)
            qk_transposes(3)
            for h in range(4):
                P.dma("act", qT_s[h, :, t0:t0 + 512], qTb[:, h, :], [("qTb", i) for i in range(4)], [("qT_s", h, b)])
                P.dma("act", kT_s[h, :, t0:t0 + 512], kTb[:, h, :], [("kTb", i) for i in range(4)], [("kT_s", h, b)])
        P.barrier()
        HT = 2048
        CH = HT // L
        SCH = HT // 8
        RA1 = Region(arena, RA.base, 49152)
        RC.reset()
        uTh = RA1.alloc([128, 4, HT], BF16)
        Ub = RA1.alloc([128, 32, SCH], BF16)
        Zt = [RA1.alloc([128, 16, CH], F32) for _ in range(2)]
        Zt += [RC.alloc([128, 16, CH], F32) for _ in range(4)]
        Zr, Zi, Zmr, Zmi, tA, tB = Zt
        zsl = [RC.alloc([128, 4, 512], BF16) for _ in range(2)]
        y2ps = [RC.alloc([128, SCH, 2], F32) for _ in range(2)]
        ge1s = [RC.alloc([128, SCH, 2], F32) for _ in range(2)]
        gsg = RC.alloc([128, 512], F32)
        ysb = [RC.alloc([128, 512], BF16) for _ in range(2)]
        W2.cur = w2_mark
        Xbf = [W2.alloc([128, 16, CH], BF16) for _ in range(2)]
        y2b_h = Ub.rearrange("p a b -> p (a b)").rearrange("p (c t) -> p c t", c=4)
        UK = [("Ub", g) for g in range(32)]
        f3 = lambda a: a.rearrange("p a b -> p (a b)")
        for hh in range(2):
            T0 = HT * hh
            for f in range(4):
                P.dma("sp", uTh[:, f, :], u_s[f, :, T0:T0 + HT], [("u_s", f, b_) for b_ in range(4 * hh, 4 * hh + 4)],
                      [("uTh", f)])
            for g in range(32):
                g8, gl = g // 8, g % 8
                pb = (g // 2) % 2
                for s_ in range(8):
                    j_, e_ = s_ // 2, s_ % 2
                    mm(PS[pb][32 * j_:32 * j_ + 32, SCH * (g % 2):SCH * (g % 2) + SCH],
                       Wsel[:, gl, 112 - 16 * e_:144 - 16 * e_], uTh[:, g8, s_:HT:8], e_ == 0, e_ == 1,
                       ["Wsel", ("uTh", g8)], [PSK[pb]], tp=(0, 32 * j_))
                if g % 2 == 1:
                    cp("act", Ub[:, g - 1:g + 1, :].rearrange("p a b -> p (a b)"), PS[pb][:, :], [PSK[pb]],
                       [("Ub", g - 1), ("Ub", g)])
            for gq in range(4):
                pz = 4 + 2 * (gq % 2)
                for g in range(8 * gq, 8 * gq + 8):
                    gp, hf = g // 2, g % 2
                    hs = slice(64 * hf, 64 * hf + 64)
                    for ri in range(2):
                        for j in range(MS):
                            mm(PS[pz + ri][hs, CH * (gp % 4):CH * (gp % 4) + CH], M1[:, g, j, 64 * ri:64 * ri + 64],
                               Ub[:, g, j:SCH:MS], j == 0, j == MS - 1, ["M1", ("Ub", g)], [PSK[pz + ri]])
                cp("dve", f3(Zr[:, 4 * gq:4 * gq + 4, :]), PS[pz][:, :], [PSK[pz]], [("Zr", gq)])
                cp("act", f3(Zi[:, 4 * gq:4 * gq + 4, :]), PS[pz + 1][:, :], [PSK[pz + 1]], [("Zi", gq)])
            ZrK = [("Zr", q_) for q_ in range(4)]
            ZiK = [("Zi", q_) for q_ in range(4)]
            a0r, a0i = APr[:, 0, :], APi[:, 0, :]
            cr_, ci_ = carry[:, 0, :], carry[:, 1, :]
            t1, t2 = tA[:, :, 0], tB[:, :, 0]
            tt("dve", t1, a0r, cr_, ALU.mult, ["APr", "carry", "tA"], ["tA"])
            tt("dve", t2, a0i, ci_, ALU.mult, ["APi", "carry", "tB"], ["tB"])
            tt("dve", t1, t1, t2, ALU.subtract, ["tA", "tB"], ["tA"])
            tt("dve", Zr[:, :, 0], Zr[:, :, 0], t1, ALU.add, ZrK + ["tA"], ZrK)
            tt("dve", t1, a0r, ci_, ALU.mult, ["APr", "carry", "tA"], ["tA"])
            tt("dve", t2, a0i, cr_, ALU.mult, ["APi", "carry", "tB"], ["tB"])
            tt("dve", t1, t1, t2, ALU.add, ["tA", "tB"], ["tA"])
            tt("dve", Zi[:, :, 0], Zi[:, :, 0], t1, ALU.add, ZiK + ["tA"], ZiK)
            tt("dve", tA, PTr, Zr, ALU.mult, ["PT", "tA"] + ZrK, ["tA"])
            tt("pool", tB, PTi, Zi, ALU.mult, ["PT", "tB"] + ZiK, ["tB"])
            tt("dve", Zmr, tA, tB, ALU.add, ["tA", "tB", "Zmr"], ["Zmr"])
            tt("dve", tA, PTr, Zi, ALU.mult, ["PT", "tA"] + ZiK, ["tA"])
            tt("pool", tB, PTi, Zr, ALU.mult, ["PT", "tB"] + ZrK, ["tB"])
            tt("dve", Zmi, tA, tB, ALU.subtract, ["tA", "tB", "Zmi"], ["Zmi"])
            P.op("dve", lambda e: e.tensor_tensor_scan(f3(Zr), f3(Rm), f3(Zmr), 0.0, ALU.mult, ALU.add),
                 ["Rm", "Zmr"] + ZrK, ZrK)
            P.op("dve", lambda e: e.tensor_tensor_scan(f3(Zi), f3(Rm), f3(Zmi), 0.0, ALU.mult, ALU.add),
                 ["Rm", "Zmi"] + ZiK, ZiK)
            tt("dve", tA, PTr, Zr, ALU.mult, ["PT", "tA"] + ZrK, ["tA"])
            tt("pool", tB, PTi, Zi, ALU.mult, ["PT", "tB"] + ZiK, ["tB"])
            tt("dve", Zmr, tA, tB, ALU.subtract, ["tA", "tB", "Zmr"], ["Zmr"])
            tt("dve", tA, PTr, Zi, ALU.mult, ["PT", "tA"] + ZiK, ["tA"])
            tt("pool", tB, PTi, Zr, ALU.mult, ["PT", "tB"] + ZrK, ["tB"])
            tt("dve", Zmi, tA, tB, ALU.add, ["tA", "tB", "Zmi"], ["Zmi"])
            for ri, (Xs, xk_) in enumerate(((Zmr, "Zmr"), (Zmi, "Zmi"))):
                cp("dve", Xbf[ri][:, :, 1:CH], Xs[:, :, 0:CH - 1], [xk_], [("Xbf", ri)])
                cp("dve", Xbf[ri][:, :, 0], carry[:, ri, :], ["carry", ("Xbf", ri)], [("Xbf", ri)])
            for ri, (Xs, xk_) in enumerate(((Zmr, "Zmr"), (Zmi, "Zmi"))):
                cp("dve", carry[:, ri, :], Xs[:, :, CH - 1], [xk_, ("Xbf", 0), ("Xbf", 1), "carry"], ["carry"])
            for g in range(32):
                gp, hf = g // 2, g % 2
                hs = slice(64 * hf, 64 * hf + 64)
                pb = (g // 2) % 2
                for j in range(MS):
                    o = PS[pb][:, SCH * (g % 2) + j:SCH * (g % 2) + SCH:MS]
                    for jp in range(j + 1):
                        mm(o, Tt[:, g, j - jp, :], Ub[:, g, jp:SCH:MS], jp == 0, False, ["Tt", ("Ub", g)], [PSK[pb]])
                    mm(o, Hr[hs, gp, 8 * j + 1:8 * j + 9, :].rearrange("p a b -> p (a b)"), Xbf[0][hs, gp, :],
                       False, False, HK + [("Xbf", 0)], [PSK[pb]])
                    mm(o, nHi[hs, gp, 8 * j + 1:8 * j + 9, :].rearrange("p a b -> p (a b)"), Xbf[1][hs, gp, :],
                       False, True, HK + [("Xbf", 1)], [PSK[pb]])
                if g % 2 == 1:
                    cp("act", Ub[:, g - 1:g + 1, :].rearrange("p a b -> p (a b)"), PS[pb][:, :], [PSK[pb]],
                       [("Ub", g - 1), ("Ub", g)])
            for ct in range(4):
                CK = [("Ub", 8 * ct + gl) for gl in range(8)]
                for t_ in range(8):
                    pbk = 4 + t_ // 2
                    for gl in range(8):
                        j_, e_ = gl // 2, gl % 2
                        mm(PS[pbk][32 * j_:32 * j_ + 32, SCH * (t_ % 2):SCH * (t_ % 2) + SCH],
                           Wsel[:, t_, 112 - 16 * e_:144 - 16 * e_], Ub[:, 8 * ct + gl, :], e_ == 0, e_ == 1,
                           ["Wsel", ("Ub", 8 * ct + gl)], [PSK[pbk]], tp=(0, 32 * j_))
                for tq in range(4):
                    pbk = 4 + tq
                    y2p, ge1 = y2ps[tq % 2], ge1s[tq % 2]
                    yk, gk = ("y2p", tq % 2), ("ge1", tq % 2)
                    uview = uTh[:, ct, :].rearrange("p (a b) -> p a b", b=8)[:, :, 2 * tq:2 * tq + 2]
                    stt(y2p, uview, dsk[:, ct:ct + 1], PS[pbk][:, :].rearrange("p (b a) -> p a b", b=2),
                        ALU.mult, ALU.add, [("uTh", ct), "vecT", PSK[pbk]], [yk])
                    yf, gf = f3(y2p), f3(ge1)
                    tt("pool", gf, yf, yf, ALU.mult, [yk], [gk])
                    ts("dve", gf, gf, 0.044715, 1.0, ALU.mult, ALU.add, [gk], [gk])
                    tt("dve", gf, gf, yf, ALU.mult, [gk, yk], [gk])
                    act(gf, gf, AF.Sigmoid, [gk], [gk], scale=1.5957691216057308)
                    tt("dve", y2b_h[:, ct, :].rearrange("p (a b) -> p a b", b=8)[:, :, 2 * tq:2 * tq + 2], y2p, ge1,
                       ALU.mult, [yk, gk] + CK, CK)
            for bi in range(4):
                bg = 4 * hh + bi
                tb0 = 512 * bi
                zl = zsl[bi % 2]
                for f in range(4):
                    P.dma("sp", zl[:, f, :], zs_s[f, :, T0 + tb0:T0 + tb0 + 512], [("zs_s", f, bg)], [("zsl", bi % 2, f)])
                for fo in range(4):
                    pb = fo % 2
                    for ci in range(4):
                        mm(PS[pb][:, :], glu_bf[:, ci, 128 * fo:128 * fo + 128], y2b_h[:, ci, tb0:tb0 + 512], ci == 0, ci == 3,
                           ["glu_bf"] + UK, [PSK[pb]])
                    act(gsg, PS[pb][:, :], AF.Sigmoid, [PSK[pb]], ["gsg"], bias=glb[:, fo:fo + 1])
                    tt("dve", gsg, gsg, y2b_h[:, fo, tb0:tb0 + 512], ALU.mult, ["gsg"] + UK, ["gsg"])
                    yo = ysb[fo % 2]
                    tt("dve", yo, gsg, zl[:, fo, :], ALU.mult, ["gsg", ("zsl", bi % 2, fo), ("ysb", fo % 2)], [("ysb", fo % 2)])
                    P.dma("sp", ys_s[fo, :, T0 + tb0:T0 + tb0 + 512], yo, [("ysb", fo % 2)], [("ys_s", fo, bg)])
        P.barrier()
        fin_keys = []
        if debug:
            RC.reset()
            dtile = RC.alloc([128, 4096], BF16)
            for nm, src in (("ys", ys_s), ("qT", qT_s), ("kT", kT_s)):
                for f in range(4):
                    P.dma("sp", dtile, src[f], [], ["dtile"])
                    P.dma("sp", dbg[nm][f], dtile, ["dtile"], [("dbg", nm, f)])
                    fin_keys.append(("dbg", nm, f))
            for nm, src in (("v", v_s), ("za", za_s)):
                for i in range(32):
                    P.dma("sp", dtile[:, 0:512], src[128 * i:128 * i + 128, :], [], ["dtile"])
                    P.dma("sp", dbg[nm][128 * i:128 * i + 128, :], dtile[:, 0:512], ["dtile"], [("dbg", nm, i)])
                    fin_keys.append(("dbg", nm, i))
            P.barrier()

        RAB = Region(arena, RA.base, RA.size + RB.size)
        RC.reset()
        kT_res = RAB.alloc([128, 4, S], BF16)
        v_res = RAB.alloc([128, 32, 4, 129], BF16)
        qTl = [RAB.alloc([128, 4, 512], BF16) for _ in range(2)]
        zatok = RAB.alloc([128, 4, 512], BF16)
        ysl = RAB.alloc([128, 4, 512], BF16)
        PTt = [RAB.alloc([128, 2, 512], BF16) for _ in range(4)]
        yaT = RAB.alloc([128, 4, 512], BF16)
        rs = RC.alloc([128, 8], F32)
        rsn = RC.alloc([128, 4], F32)
        o_all = RC.alloc([128, 4, 512], F32)
        sqt = RC.alloc([128, 512], F32)
        ss4 = RC.alloc([128, 4], F32)
        yatoks = [RC.alloc([128, 512], BF16) for _ in range(2)]
        xl = [RC.alloc([128, 1024], F32) for _ in range(2)]
        xnews = [RC.alloc([128, 1024], F32) for _ in range(2)]
        xnbs = [RC.alloc([128, 1024], BF16) for _ in range(2)]
        xnT = RC.alloc([128, 8, 128], BF16)
        pl = [RC.alloc([128, 256], F32) for _ in range(2)]
        pT = RC.alloc([128, 2, 128], BF16)
        gate = RC.alloc([128, 1024], F32)
        wst = [RC.alloc([128, 1024], F32) for _ in range(2)]
        tri = cmask[:, 0, 0:128]
        mhalf4 = RC.alloc([128, 4], F32)
        memset("dve", mhalf4, -0.5, ["mhalf4"])
        for h in range(4):
            P.dma("sp", kT_res[:, h, :], kT_s[h], [("kT_s", h, b_) for b_ in range(NBLK)], [("kT_res", h)])
        memset("dve", v_res[:, :, :, 128:129], 1.0, ["v_ones"])
        for i in range(32):
            P.dma("sp", v_res[:, i, :, 0:128], v_s[128 * i:128 * i + 128, :].rearrange("p (a b) -> p a b", a=4),
                  [("v_s", i // 4, i % 4)], [("v_res", i)])

        def load_q(b):
            for h in range(4):
                P.dma("sp", qTl[b % 2][:, h, :], qT_s[h, :, 512 * b:512 * b + 512], [("qT_s", h, b)], [("qTl", b % 2, h)])

        def load_x(b, i):
            tok = slice(512 * b + 128 * i, 512 * b + 128 * i + 128)
            P.dma("sp", xl[i % 2], x_d[tok, :], [], [("xl", i % 2)])
            P.dma("sp", pl[i % 2], p_d[tok, :], [], [("pl", i % 2)])

        load_q(0)
        load_w(wout_bf, wout_d, 8, 1024, "wout_bf")
        load_w(pg_bf, pg_d, 8, 1024, "pg_bf")
        load_w(pp_bf, pp_d, 2, 1024, "pp_bf")
        OBk = [PS[4], PS[5]]
        zatoks = [zatok, RAB.alloc([128, 4, 512], BF16)]
        ysls = [ysl, RC.alloc([128, 4, 512], BF16)]
        pti = 0

        def make_tail_units(b):
            t0 = 512 * b
            zat, ysl_ = zatoks[b % 2], ysls[b % 2]

            def stage_a(i):
                ts_ = slice(128 * i, 128 * i + 128)
                oK = [("o_all", i, h) for h in range(4)]
                oq = o_all[:, i, :]
                act(sqt, oq, AF.Square, oK, ["sqt"])
                P.op("dve", lambda e: e.tensor_reduce(ss4, sqt.rearrange("p (a b) -> p a b", a=4), AX.X, ALU.add),
                     ["sqt"], ["ss4"])
                ts("dve", ss4, ss4, 1.0 / 128, EPS, ALU.mult, ALU.add, ["ss4"], ["ss4"])
                tt("pool", ss4, ss4, mhalf4, ALU.pow, ["ss4", "mhalf4"], ["ss4"])
                o3 = oq.rearrange("p (a b) -> p a b", a=4)
                tt("dve", o3, o3, bc3(ss4, [128, 4, 128], 2), ALU.mult, oK + ["ss4"], oK)
                tt("dve", o3, o3, bcsw.unsqueeze(1).to_broadcast([128, 4, 128]), ALU.mult, oK + ["bcsw"], oK)
                tt("dve", yatoks[i % 2], oq, zat[:, i, :], ALU.mult, oK + [("zatok", b % 2, i)], [("yatok", i % 2)])
                yield

            def stage_a2(i):
                ts_ = slice(128 * i, 128 * i + 128)
                for h in range(4):
                    tr(psb(7)[:, 128 * h:128 * h + 128], yatoks[i % 2][:, 128 * h:128 * h + 128], ident_bf,
                       [("yatok", i % 2), "ident_bf"], [PSK[7]])
                    if h % 2 == 1:
                        yield
                cp("dve", yaT[:, :, ts_], psb(7)[:, 0:512].rearrange("p (a b) -> p a b", a=4), [PSK[7]], [("yaT", i)])
                yield

            def stage_b(i):
                ts_ = slice(128 * i, 128 * i + 128)
                xb_ = xl[i % 2]
                xk = ("xl", i % 2)
                xn = xnews[i % 2]
                for hf in range(2):
                    tb = (3, 6)[hf]
                    for c in range(8):
                        src = ysl_ if c < 4 else yaT
                        kk = ("ysl", b % 2, c) if c < 4 else ("yaT", i)
                        mm(PS[tb][:, :], src[:, c % 4, ts_], wout_bf[:, c, 512 * hf:512 * hf + 512], c == 0, c == 7,
                           [kk, "wout_bf"], [PSK[tb]])
                        if c % 2 == 1:
                            yield
                    tt("dve", xn[:, 512 * hf:512 * hf + 512], PS[tb][:, :], xb_[:, 512 * hf:512 * hf + 512],
                       ALU.add, [PSK[tb], xk], [("xnew", i % 2, hf)])
                    cp("dve", xnbs[i % 2][:, 512 * hf:512 * hf + 512], xn[:, 512 * hf:512 * hf + 512],
                       [("xnew", i % 2, hf)], [("xnb", i % 2, hf)])
                    yield

            def stage_c1(i):
                plk = ("pl", i % 2)
                for c in range(8):
                    tr(psb(7)[:, 128 * c:128 * c + 128], xnbs[i % 2][:, 128 * c:128 * c + 128], ident_bf,
                       [("xnb", i % 2, c // 4), "ident_bf"], [PSK[7]])
                    if c % 2 == 1:
                        yield
                cp("dve", xnT.rearrange("p a b -> p (a b)"), psb(7)[:, :], [PSK[7]], ["xnT"])
                for c in range(2):
                    tr(PS[6][:, 128 * c:128 * c + 128], pl[i % 2][:, 128 * c:128 * c + 128], ident_f, [plk, "ident_f"],
                       [PSK[6]])
                cp("dve", pT.rearrange("p a b -> p (a b)"), PS[6][:, 0:256], [PSK[6]], ["pT"])
                yield

            def stage_c2(i, hf):
                xb_ = xl[i % 2]
                xk = ("xl", i % 2)
                xn = xnews[i % 2]
                hsl = slice(512 * hf, 512 * hf + 512)
                tb = (3, 6)[hf]
                pk_ = PSK[tb]
                for c in range(8):
                    mm(PS[tb][:, :], xnT[:, c, :], pg_bf[:, c, hsl], c == 0, c == 7, ["xnT", "pg_bf"], [pk_])
                    if c % 2 == 1:
                        yield
                act(gate[:, hsl], PS[tb][:, :], AF.Tanh, [pk_], [("gate", hf)], scale=0.5)
                for c in range(2):
                    mm(PS[tb][:, :], pT[:, c, :], pp_bf[:, c, hsl], c == 0, c == 1, ["pT", "pp_bf"], [pk_])
                stt(gate[:, hsl], gate[:, hsl], 1.0, PS[tb][:, :], ALU.add, ALU.mult, [("gate", hf), pk_], [("gate", hf)])
                stt(xb_[:, hsl], gate[:, hsl], 0.5, xn[:, hsl], ALU.mult, ALU.add, [("gate", hf), ("xnew", i % 2, hf), xk], [xk])
                yield

            def store(i):
                tok = slice(t0 + 128 * i, t0 + 128 * i + 128)
                P.dma("sp", out_d[tok, :], xl[i % 2], [("xl", i % 2)], [("out", b, i)])
                fin_keys.append(("out", b, i))

            def gen():
                load_x(b, 0)
                load_x(b, 1)
                yield from stage_a(0)
                yield from stage_a(1)
                yield from stage_a2(0)
                yield from stage_a(2)
                yield from stage_a2(1)
                yield from stage_a(3)
                yield from stage_a2(2)
                yield from stage_a2(3)

                def fin(i):
                    yield from stage_c1(i)
                    yield from stage_c2(i, 0)
                    yield from stage_c2(i, 1)
                    store(i)
                    if i + 2 < 4:
                        load_x(b, i + 2)
                    yield

                yield from stage_b(0)
                yield from stage_b(1)
                yield from fin(0)
                yield from stage_b(2)
                yield from fin(1)
                yield from stage_b(3)
                yield from fin(2)
                yield from fin(3)

            return gen(), 112

        qzall = wst[1].bitcast(BF16)
        qz = [[qzall[:, 512 * (2 * c_ + hb_):512 * (2 * c_ + hb_) + 512] for hb_ in range(2)] for c_ in range(2)]
        for c_ in range(2):
            for hb_ in range(2):
                memset("pool", qz[c_][hb_], 0.0, [("qz", c_, hb_), ("wst", 1)])
        pending, pend_left = None, 0

        def advance(n):
            nonlocal pending, pend_left
            for _ in range(n):
                if pending is None:
                    return
                try:
                    next(pending)
                    pend_left = max(pend_left - 1, 1)
                except StopIteration:
                    pending, pend_left = None, 0

        for b in range(NBLK):
            t0 = 512 * b
            qb = qTl[b % 2]
            if b + 1 < NBLK:
                load_q(b + 1)
            for i in range(4):
                P.dma("sp", zatoks[b % 2][:, i, :], za_s[t0 + 128 * i:t0 + 128 * i + 128, :], [("za_s", b, i)],
                      [("zatok", b % 2, i)])
            for h in range(4):
                P.dma("sp", ysls[b % 2][:, h, :], ys_s[h, :, t0:t0 + 512], [("ys_s", h, b)], [("ysl", b % 2, h)])
            nkt = 4 * (b + 1)
            iters = []
            for h in range(4):
                for qh in range(2):
                    for kt in range(nkt):
                        j = kt - 4 * b
                        if j >= 0 and 128 * j >= 256 * (qh + 1):
                            continue
                        iters.append((h, kt, qh))
            n_it = len(iters)

            qz_done = set()

            def scores(it):
                h, kt, qh = it
                buf = scores.cnt % 3
                scores.cnt += 1
                j = kt - 4 * b
                q0 = max(128 * max(j, 0) - 256 * qh, 0)
                ks = slice(128 * kt, 128 * kt + 128)
                if h not in qz_done:
                    qz_done.add(h)
                    for c in range(2):
                        hs = slice(64 * c, 64 * c + 64)
                        cp("pool", qz[c][h % 2][hs, :], qb[hs, h, :], [("qTl", b % 2, h)], [("qz", c, h % 2)])
                for c in range(2):
                    mm(PS[buf][:, 256 * c + q0:256 * c + 256], kT_res[:, h, ks],
                       qz[c][h % 2][:, 256 * qh + q0:256 * qh + 256], True, True,
                       [("kT_res", h), ("qz", c, h % 2)], [("SC", buf)])
                return buf, q0

            scores.cnt = 0
            LA = 2
            sq_ = [scores(iters[k_]) for k_ in range(min(LA, n_it))]
            started = {}
            for idx, it in enumerate(iters):
                h, kt, qh = it
                if idx + LA < n_it:
                    sq_.append(scores(iters[idx + LA]))
                buf, q0 = sq_.pop(0)
                j = kt - 4 * b
                pt_i = pti % 4
                pti += 1
                pt = PTt[pt_i]
                pk = ("PT", pt_i)
                act(pt[:, :, q0:256], PS[buf].rearrange("p (c q) -> p c q", c=2)[:, :, q0:256], AF.Exp,
                    [("SC", buf)], [pk])
                if j >= 0 and 128 * j >= 256 * qh:
                    tt("dve", pt[:, :, q0:q0 + 128], pt[:, :, q0:q0 + 128],
                       tri.unsqueeze(1).to_broadcast([128, 2, 128]), ALU.mult, [pk, "cmask"], [pk])
                for qt in range(max(j, 2 * qh), 2 * qh + 2):
                    for c in range(2):
                        r = 2 * (qt - 2 * qh) + c
                        bank, col0 = r // 3, (r % 3) * 129
                        st_ = (h, qh, bank) not in started
                        started[(h, qh, bank)] = True
                        ql = 128 * (qt - 2 * qh)
                        lhs = pt[:, c, ql:ql + 128]
                        o_ap = OBk[bank][:, col0:col0 + 129]
                        rhs_ = v_res[:, kt, h, :]
                        P.op("pe", lambda e, o_ap=o_ap, lhs=lhs, rhs_=rhs_, st_=st_, sp_=False:
                             e.matmul(o_ap, lhsT=lhs, rhs=rhs_, start=st_, stop=sp_, skip_group_check=True),
                             [pk, ("v_res", kt), "v_ones"], [("OB", bank)])
                last_of_head = (idx + 1 == n_it) or (iters[idx + 1][0] != h) or (iters[idx + 1][2] != qh)
                if last_of_head:
                    for bank in range(2):
                        nreg = 3 if bank < 1 else 1
                        src = OBk[bank][:, 128:128 + 129 * (nreg - 1) + 1:129]
                        dst = rs[:, 3 * bank:3 * bank + nreg]
                        P.op("dve", lambda e, dst=dst, src=src: e.reciprocal(dst, src), [("OB", bank)], ["rs"])
                    ts("dve", rsn[:, 0:2], rs[:, 1:4:2], lamv[:, 1:2], None, ALU.mult, None, ["rs", "lamv"], ["rsn"])
                    for qt in range(2 * qh, 2 * qh + 2):
                        r0_, r1_ = 2 * (qt - 2 * qh), 2 * (qt - 2 * qh) + 1
                        oa = o_all[:, qt, 128 * h:128 * h + 128]
                        ts("dve", oa, OBk[r0_ // 3][:, (r0_ % 3) * 129:(r0_ % 3) * 129 + 128], rs[:, r0_:r0_ + 1], None,
                           ALU.mult, None, [("OB", r0_ // 3), "rs"], [("o_all", qt, h)])
                        stt(oa, OBk[r1_ // 3][:, (r1_ % 3) * 129:(r1_ % 3) * 129 + 128], rsn[:, qt - 2 * qh:qt - 2 * qh + 1], oa,
                            ALU.mult, ALU.add, [("OB", r1_ // 3), "rsn", ("o_all", qt, h)], [("o_all", qt, h)])
                if pending is not None:
                    advance(-(-pend_left // max(n_it - idx - 8, 1)))
            advance(10 ** 6)
            pending, pend_left = make_tail_units(b)
        advance(10 ** 6)
        P.emit(final_keys=fin_keys)
    return nc


_NC_CACHE = {}


def _core_inputs(b, x, p, positions, norm_w, w_in, ssm_lambda_re, ssm_lambda_im, ssm_log_dt,
                 ssm_b_re, ssm_b_im, ssm_c_re, ssm_c_im, ssm_d, glu_w, glu_b,
                 q_norm_w, k_norm_w, lambda_q1, lambda_k1, lambda_q2, lambda_k2,
                 subln_w, w_out, ple_w_proj, ple_w_gate):
    f = lambda a: np.ascontiguousarray(np.asarray(a, dtype=np.float32))
    vecs = np.concatenate([f(norm_w[0]).reshape(8, 128), f(ssm_d[0]).reshape(4, 128),
                           f(glu_b[0]).reshape(4, 128), f(subln_w[0]).reshape(1, 128)], axis=0)
    rows = np.concatenate([f(q_norm_w[0]), f(k_norm_w[0]), f(lambda_q1[0]), f(lambda_k1[0]),
                           f(lambda_q2[0]), f(lambda_k2[0]), f(subln_w[0])]).reshape(1, 512)
    lam = np.stack([f(ssm_lambda_re[0]).reshape(16, 128), f(ssm_lambda_im[0]).reshape(16, 128)], axis=1)
    return {
        "x": f(x[b]), "p": f(p[0, b]),
        "pos": np.ascontiguousarray(np.asarray(positions[b], dtype=np.int32).reshape(32, 128)),
        "vecs": np.ascontiguousarray(vecs), "rows": np.ascontiguousarray(rows),
        "w_in": f(w_in[0]), "lam": np.ascontiguousarray(lam), "log_dt": f(ssm_log_dt[0]).reshape(16, 2),
        "b_re": f(ssm_b_re[0]), "b_im": f(ssm_b_im[0]), "c_re": f(ssm_c_re[0]), "c_im": f(ssm_c_im[0]),
        "glu_w": f(glu_w[0]), "w_out": f(w_out[0]), "ple_w_proj": f(ple_w_proj[0]), "ple_w_gate": f(ple_w_gate[0]),
    }


def kernel(**inputs):
    if "nc" not in _NC_CACHE:
        _NC_CACHE["nc"] = build_program(DEBUG)
    nc = _NC_CACHE["nc"]
    in_maps = [_core_inputs(b, **inputs) for b in range(8)]
    res = run_bass_kernel_spmd(nc, in_maps, core_ids=list(range(8)))
    out = np.stack([np.asarray(r["out"], dtype=np.float32) for r in res.results], axis=0)
    return out
```

```python
import math
import contextlib
import numpy as np
import concourse.bass as bass
import concourse.mybir as mybir
from concourse.bass_utils import run_bass_kernel_spmd

F32 = mybir.dt.float32
BF16 = mybir.dt.bfloat16
I32 = mybir.dt.int32
ALU = mybir.AluOpType
AF = mybir.ActivationFunctionType
AX = mybir.AxisListType

SAME_ENGINE_SYNC = True
N_DMA_SEMS = 48
DEBUG = False

S = 4096
D = 1024
NBLK = 8
MS = 2
L = 8 * MS
NG = 8 * MS + 7
NH = 8 * MS + 1
NE = NG + NH
CPB = 512 // L
SCB = 64
EPS = 1e-6
TWO_PI = 2.0 * math.pi
CW1 = 6.28125
CW2 = TWO_PI - 6.28125
LAMBDA_INIT = 0.8 - 0.6 * math.exp(0.0)


class _Op:
    __slots__ = ("eng", "fn", "deps", "is_dma", "sem", "semval", "signal", "signo", "idx")


class Prog:
    ENGS = ("pe", "act", "dve", "pool", "sp")

    def __init__(self, nc):
        self.nc = nc
        self.ops = []
        self.last_w = {}
        self.readers = {}
        self.dma_rr = 0
        self.dma_sem_total = [0] * N_DMA_SEMS
        self.dma_sem_lastop = [None] * N_DMA_SEMS
        self.bar_deps = []
        self.need_bar = {e: False for e in self.ENGS}
        self.last_eng_op = {}

    def barrier(self):
        deps = [o for o in self.last_eng_op.values()]
        deps += [o for o in self.dma_sem_lastop if o is not None]
        self.bar_deps = deps
        for e in self.ENGS:
            self.need_bar[e] = True

    def _add(self, eng, fn, R, W, is_dma):
        op = _Op()
        op.eng, op.fn, op.is_dma = eng, fn, is_dma
        op.signal = False
        op.signo = 0
        op.sem = None
        op.semval = 0
        op.idx = len(self.ops)
        deps = []
        if self.need_bar[eng]:
            deps += self.bar_deps
            self.need_bar[eng] = False
        for k in R:
            w = self.last_w.get(k)
            if w is not None:
                deps.append(w)
        for k in W:
            w = self.last_w.get(k)
            if w is not None:
                deps.append(w)
            for r in self.readers.get(k, ()):
                deps.append(r)
        if is_dma:
            s = self.dma_rr
            self.dma_rr = (self.dma_rr + 1) % N_DMA_SEMS
            prev = self.dma_sem_lastop[s]
            if prev is not None:
                deps.append(prev)
            self.dma_sem_total[s] += 16
            op.sem = s
            op.semval = self.dma_sem_total[s]
            self.dma_sem_lastop[s] = op
        seen = set()
        dd = []
        for d in deps:
            if d is op or id(d) in seen:
                continue
            seen.add(id(d))
            if (not d.is_dma) and d.eng == eng and (eng == "pe" or not SAME_ENGINE_SYNC):
                continue
            dd.append(d)
            if not d.is_dma:
                d.signal = True
        op.deps = dd
        for k in W:
            self.last_w[k] = op
            self.readers[k] = []
        for k in R:
            if k not in W:
                self.readers.setdefault(k, []).append(op)
        self.ops.append(op)
        if not is_dma:
            self.last_eng_op[eng] = op
        return op

    def op(self, eng, fn, R=(), W=()):
        return self._add(eng, fn, tuple(R), tuple(W), False)

    def dma(self, q, out, in_, R=(), W=()):
        return self._add(q, lambda e: e.dma_start(out=out, in_=in_), tuple(R), tuple(W), True)

    def emit(self, final_keys=()):
        nc = self.nc
        self._add("sp", None, tuple(final_keys), (), False)
        cnt = {e: 0 for e in self.ENGS}
        for o in self.ops:
            if (not o.is_dma) and o.signal:
                cnt[o.eng] += 1
                o.signo = cnt[o.eng]
        with contextlib.ExitStack() as st:
            esem = {e: st.enter_context(nc.semaphore("sem_" + e)) for e in self.ENGS}
            dsem = [st.enter_context(nc.semaphore("dsem%d" % i)) for i in range(N_DMA_SEMS)]
            block = st.enter_context(nc.Block())
            per = {e: [o for o in self.ops if o.eng == e] for e in self.ENGS}

            def replay(e, eng):
                waited = {}
                for o in per[e]:
                    for d in o.deps:
                        if d.is_dma:
                            key, val, sem = ("d", d.sem), d.semval, dsem[d.sem]
                        else:
                            key, val, sem = ("e", d.eng), d.signo, esem[d.eng]
                        if waited.get(key, 0) >= val:
                            continue
                        waited[key] = val
                        eng.wait_ge(sem, val)
                    if o.fn is None:
                        continue
                    ins = o.fn(eng)
                    if o.is_dma:
                        ins.then_inc(dsem[o.sem], 16)
                    elif o.signal:
                        ins.then_inc(esem[e], 1)

            @block.sync
            def _(eng):
                replay("sp", eng)

            @block.scalar
            def _(eng):
                replay("act", eng)

            @block.vector
            def _(eng):
                replay("dve", eng)

            @block.gpsimd
            def _(eng):
                replay("pool", eng)

            @block.tensor
            def _(eng):
                replay("pe", eng)


class Region:
    def __init__(self, arena, base, size):
        self.arena, self.base, self.size, self.cur = arena, base, size, 0

    def reset(self):
        self.cur = 0

    def alloc(self, shape, dt, parts=None):
        esz = 2 if dt == BF16 else 4
        n = 1
        for s_ in shape[1:]:
            n *= s_
        nbytes = (n * esz + 31) // 32 * 32
        off = self.base + self.cur
        self.cur += nbytes
        assert self.cur <= self.size, ("region overflow", self.cur, self.size)
        v = self.arena[0:shape[0], off // 4:(off + nbytes) // 4]
        if dt != F32:
            v = v.bitcast(dt)
        v = v[:, 0:n]
        if len(shape) == 3:
            v = v.rearrange("p (a b) -> p a b", a=shape[1])
        elif len(shape) == 4:
            v = v.rearrange("p (a b c) -> p a b c", a=shape[1], b=shape[2])
        return v


def build_program(debug=False):
    nc = bass.Bass("TRN2", target_bir_lowering=False)
    P = Prog(nc)

    def din(name, shape, dt=F32):
        return nc.dram_tensor(name, list(shape), dt, kind="ExternalInput").ap()

    x_d = din("x", [S, D])
    p_d = din("p", [S, 256])
    pos_d = din("pos", [32, 128], I32)
    vec_d = din("vecs", [17, 128])
    row_d = din("rows", [1, 512])
    win_d = din("w_in", [D, 3072])
    lam_d = din("lam", [16, 2, 128])
    ldt_d = din("log_dt", [16, 2])
    bre_d = din("b_re", [32, 64, 16])
    bim_d = din("b_im", [32, 64, 16])
    cre_d = din("c_re", [32, 16, 64])
    cim_d = din("c_im", [32, 16, 64])
    glu_d = din("glu_w", [512, 512])
    wout_d = din("w_out", [D, D])
    pp_d = din("ple_w_proj", [256, D])
    pg_d = din("ple_w_gate", [D, D])
    out_d = nc.dram_tensor("out", [S, D], F32, kind="ExternalOutput").ap()
    ys_s = nc.dram_tensor("ys_s", [4, 128, S], BF16, kind="Internal").ap()
    u_s = nc.dram_tensor("u_s", [4, 128, S], BF16, kind="Internal").ap()
    zs_s = nc.dram_tensor("zs_s", [4, 128, S], BF16, kind="Internal").ap()
    za_s = nc.dram_tensor("za_s", [S, 512], BF16, kind="Internal").ap()
    qT_s = nc.dram_tensor("qT_s", [4, 128, S], BF16, kind="Internal").ap()
    kT_s = nc.dram_tensor("kT_s", [4, 128, S], BF16, kind="Internal").ap()
    v_s = nc.dram_tensor("v_s", [S, 512], BF16, kind="Internal").ap()
    dbg = {}
    if debug:
        for nm in ("ys", "qT", "kT"):
            dbg[nm] = nc.dram_tensor("dbg_" + nm, [4, 128, S], BF16, kind="ExternalOutput").ap()
        dbg["v"] = nc.dram_tensor("dbg_v", [S, 512], BF16, kind="ExternalOutput").ap()
        dbg["za"] = nc.dram_tensor("dbg_za", [S, 512], BF16, kind="ExternalOutput").ap()

    with contextlib.ExitStack() as st:
        ARENA_BYTES = 212480
        arena = st.enter_context(nc.sbuf_tensor("arena", [128, ARENA_BYTES // 4], F32))
        PQ = [st.enter_context(nc.psum_tensor("pq%d" % i, [128, 1024], F32)) for i in range(4)]
        PS = [PQ[i // 2][:, 512 * (i % 2):512 * (i % 2) + 512] for i in range(8)]
        PSK = ["ps%d" % i for i in range(8)]

        def psb(i):
            return PS[i].bitcast(BF16)

        PER = Region(arena, 0, 59136)
        RA = Region(arena, 59136, 65536)
        RB = Region(arena, 124672, 33792)
        RC = Region(arena, 158464, ARENA_BYTES - 158464)
        ident_bf = PER.alloc([128, 128], BF16)
        ident_f = PER.alloc([128, 128], F32)
        ones_f = PER.alloc([128, 128], F32)
        ones_bf = PER.alloc([128, 128], BF16)
        Wsel = PER.alloc([128, 8, 240], BF16)
        maskT = PER.alloc([128, MS, 128], BF16)
        cmask = PER.alloc([128, 4, 512], BF16)
        glu_bf = PER.alloc([128, 4, 512], BF16)
        W2_base = PER.cur
        wout_bf = PER.alloc([128, 8, 1024], BF16)
        pg_bf = PER.alloc([128, 8, 1024], BF16)
        pp_bf = PER.alloc([128, 2, 1024], BF16)
        W2 = Region(arena, W2_base, PER.cur - W2_base)
        vecT = PER.alloc([128, 17], F32)
        bcq = PER.alloc([128, 512], F32)
        bck = PER.alloc([128, 512], F32)
        cosT = PER.alloc([128, 32, 8], F32)
        sinT = PER.alloc([128, 32, 8], F32)
        APr = PER.alloc([128, 6, 16], F32)
        APi = PER.alloc([128, 6, 16], F32)
        carry = PER.alloc([128, 2, 16], F32)
        lamv = PER.alloc([128, 4], F32)
        sw08 = PER.alloc([128, 1], F32)
        epsv = PER.alloc([128, 1], F32)
        bcsw = PER.alloc([128, 128], F32)
        u1 = PER.alloc([128, 2, 16], F32)
        rmag = PER.alloc([128, 16], F32)
        nd = vecT[:, 0:8]
        dsk = vecT[:, 8:12]
        glb = vecT[:, 12:16]
        win_bf = RA.alloc([128, 8, 3072], BF16)
        M1 = RA.alloc([128, 32, MS, 128], BF16)
        Hr = RB.alloc([128, 16, NH, 16], BF16)
        nHi = RB.alloc([128, 16, NH, 16], BF16)
        Tt = RB.alloc([128, 32, MS, 128], BF16)

        def mm(out, lhsT, rhs, start, stop, R, W, tp=None):
            if tp is None:
                P.op("pe", lambda e: e.matmul(out, lhsT=lhsT, rhs=rhs, start=start, stop=stop), R, W)
            else:
                P.op("pe", lambda e: e.matmul(out, lhsT=lhsT, rhs=rhs, start=start, stop=stop, tile_position=tp), R, W)

        def tr(out, in_, ident, R, W):
            P.op("pe", lambda e: e.transpose(out, in_, ident), R, W)

        def act(out, in_, func, R, W, bias=None, scale=None, accum=None):
            kw = {}
            if bias is not None:
                kw["bias"] = bias
            if scale is not None:
                kw["scale"] = scale
            if accum is not None:
                kw["accum_out"] = accum
            P.op("act", lambda e: e.activation(out, in_, func, **kw), R, W)

        def tt(eng, out, a, b, op, R, W):
            P.op(eng, lambda e: e.tensor_tensor(out, a, b, op), R, W)

        def ts(eng, out, a, s1, s2, op0, op1, R, W):
            if op1 is None:
                P.op(eng, lambda e: e.tensor_scalar(out, a, s1, None, op0), R, W)
            else:
                P.op(eng, lambda e: e.tensor_scalar(out, a, s1, s2, op0, op1), R, W)

        def stt(out, a, s, b, op0, op1, R, W):
            P.op("dve", lambda e: e.scalar_tensor_tensor(out, a, s, b, op0, op1), R, W)

        def cp(eng, out, in_, R, W):
            if eng == "act":
                act(out, in_, AF.Copy, R, W)
            else:
                P.op(eng, lambda e: e.tensor_copy(out, in_), R, W)

        def iota(out, pattern, base, cm, W):
            P.op("pool", lambda e: e.iota(out, pattern=pattern, base=base, channel_multiplier=cm), (), W)

        def memset(eng, out, val, W):
            P.op(eng, lambda e: e.memset(out, val), (), W)

        def bc3(ap2, shape, axis):
            return ap2.unsqueeze(axis).to_broadcast(shape)

        def sincos(x, q, qi, r, s_out, c_out, key):
            ts("dve", q, x, 1.0 / TWO_PI, None, ALU.mult, None, [key + "x"], [key + "q"])
            cp("dve", qi, q, [key + "q"], [key + "qi"])
            cp("dve", q, qi, [key + "qi"], [key + "q"])
            stt(r, q, -CW1, x, ALU.mult, ALU.add, [key + "q", key + "x"], [key + "r"])
            stt(r, q, -CW2, r, ALU.mult, ALU.add, [key + "q", key + "r"], [key + "r"])
            ts("dve", x, r, -math.pi, math.pi, ALU.max, ALU.min, [key + "r"], [key + "x"])
            act(s_out, x, AF.Sin, [key + "x"], [key + "s"])
            ts("dve", x, r, math.pi / 2, None, ALU.add, None, [key + "r", key + "s"], [key + "x"])
            ts("dve", q, x, math.pi, -TWO_PI, ALU.is_gt, ALU.mult, [key + "x"], [key + "q"])
            tt("dve", x, x, q, ALU.add, [key + "q", key + "x"], [key + "x"])
            ts("dve", x, x, -math.pi, math.pi, ALU.max, ALU.min, [key + "x"], [key + "x"])
            act(c_out, x, AF.Sin, [key + "x"], [key + "c"])

        RC.reset()
        W2.reset()
        R0 = Region(arena, RA.base, RA.size)
        io_i = R0.alloc([128, 1920], I32)
        io_f = R0.alloc([128, 1920], F32)
        msk_f = R0.alloc([128, 1920], F32)
        iota(io_i[:, 0:128], [[1, 128]], 0, -1, ["io_i"])
        ts("dve", ident_f, io_i[:, 0:128], 0.0, None, ALU.is_equal, None, ["io_i"], ["ident_f"])
        cp("dve", ident_bf, ident_f, ["ident_f"], ["ident_bf"])
        memset("dve", ones_f, 1.0, ["ones_f"])
        memset("dve", ones_bf, 1.0, ["ones_bf"])
        memset("dve", epsv, EPS, ["epsv"])
        iota(io_i[:, 0:1920].rearrange("p (a b) -> p a b", a=8), [[16, 8], [1, 240]], -112, -1, ["io_i"])
        ts("dve", io_f[:, 0:1920], io_i[:, 0:1920], 0.0, None, ALU.is_equal, None, ["io_i"], ["io_f"])
        iota(io_i[:, 0:1920].rearrange("p (a b) -> p a b", a=8), [[0, 8], [1, 240]], 0, 0, ["io_i"])
        ts("dve", msk_f[:, 0:1920], io_i[:, 0:1920], 112.0, None, ALU.is_ge, None, ["io_i"], ["msk_f"])
        tt("dve", io_f[:, 0:1920], io_f[:, 0:1920], msk_f[:, 0:1920], ALU.mult, ["io_f", "msk_f"], ["io_f"])
        ts("dve", msk_f[:, 0:1920], io_i[:, 0:1920], 127.0, None, ALU.is_le, None, ["io_i"], ["msk_f"])
        tt("dve", Wsel.rearrange("p a b -> p (a b)"), io_f[:, 0:1920], msk_f[:, 0:1920], ALU.mult,
           ["io_f", "msk_f"], ["Wsel"])
        iota(io_i[:, 0:128].rearrange("p (a b) -> p a b", a=8), [[16, 8], [0, 16]], 0, -1, ["io_i"])
        memset("dve", maskT.rearrange("p a b -> p (a b)"), 1.0, ["maskT"])
        ts("dve", maskT[:, 0, :], io_i[:, 0:128], -15.0, None, ALU.is_ge, None, ["io_i", "maskT"], ["maskT"])
        for j in range(4):
            iota(io_i[:, 0:512], [[1, 512]], -128 * j, -1, ["io_i"])
            ts("dve", cmask[:, j, :], io_i[:, 0:512], 0.0, None, ALU.is_ge, None, ["io_i", "cmask"], ["cmask"])

        vec16 = R0.alloc([17, 128], F32)
        rowv = R0.alloc([1, 512], F32)
        lam16 = R0.alloc([16, 3, 128], F32)
        ldt16 = R0.alloc([16, 2], F32)
        pos_i = R0.alloc([32, 128], I32)
        pos_f = R0.alloc([32, 128], F32)
        P.dma("sp", vec16, vec_d, [], ["vec16"])
        P.dma("sp", rowv, row_d, [], ["rowv"])
        P.dma("sp", lam16[:, 0:2, :], lam_d, [], ["lam16a"])
        P.dma("sp", ldt16, ldt_d, [], ["ldt16"])
        P.dma("sp", pos_i, pos_d, [], ["pos_i"])
        cp("dve", lam16[:, 2, :].rearrange("p (a b) -> p a b", a=2), bc3(ldt16, [16, 2, 64], 2),
           ["ldt16"], ["lam16b"])
        cp("dve", pos_f, pos_i, ["pos_i"], ["pos_f"])
        tr(PS[0][:, 0:17], vec16, ident_f[0:17, 0:17], ["vec16", "ident_f"], [PSK[0]])
        cp("dve", vecT, PS[0][:, 0:17], [PSK[0]], ["vecT"])
        par = R0.alloc([128, 3, 16], F32)
        for i in range(3):
            tr(PS[1][:, 16 * i:16 * i + 16], lam16[:, i, :], ident_f[0:16, 0:16],
               ["lam16a", "lam16b", "ident_f"], [PSK[1]])
        cp("dve", par.rearrange("p a b -> p (a b)"), PS[1][:, 0:48], [PSK[1]], ["par"])
        posT = R0.alloc([128, 32], F32)
        tr(PS[2][:, 0:32], pos_f, ident_f[0:32, 0:32], ["pos_f", "ident_f"], [PSK[2]])
        cp("dve", posT, PS[2][:, 0:32], [PSK[2]], ["posT"])
        bcr = R0.alloc([128, 512], F32)
        mm(PS[3][:, 0:512], ones_f[0:1, :], rowv, True, True, ["ones_f", "rowv"], [PSK[3]])
        cp("dve", bcr, PS[3][:, 0:512], [PSK[3]], ["bcr"])
        ts("dve", bcsw, bcr[:, 384:512], 1.0 - LAMBDA_INIT, None, ALU.mult, None, ["bcr"], ["bcsw"])
        ts("dve", bcq.rearrange("p (a b) -> p a b", a=8), bc3(bcr[:, 0:64], [128, 8, 64], 1),
           0.125, None, ALU.mult, None, ["bcr"], ["bcq"])
        cp("dve", bck.rearrange("p (a b) -> p a b", a=8), bc3(bcr[:, 64:128], [128, 8, 64], 1), ["bcr"], ["bck"])
        lsc = R0.alloc([128, 128], F32)
        lsum = R0.alloc([128, 2], F32)
        tt("dve", lsc[:, 0:64], bcr[:, 128:192], bcr[:, 192:256], ALU.mult, ["bcr"], ["lsc"])
        tt("dve", lsc[:, 64:128], bcr[:, 256:320], bcr[:, 320:384], ALU.mult, ["bcr", "lsc"], ["lsc"])
        P.op("dve", lambda e: e.tensor_reduce(lsum, lsc.rearrange("p (a b) -> p a b", a=2), AX.X, ALU.add),
             ["lsc"], ["lsum"])
        act(lsum, lsum, AF.Exp, ["lsum"], ["lsum"])
        tt("dve", lamv[:, 0:1], lsum[:, 0:1], lsum[:, 1:2], ALU.subtract, ["lsum"], ["lamv"])
        ts("dve", lamv[:, 0:1], lamv[:, 0:1], LAMBDA_INIT, None, ALU.add, None, ["lamv"], ["lamv"])
        ts("dve", lamv[:, 1:2], lamv[:, 0:1], -1.0, None, ALU.mult, None, ["lamv"], ["lamv"])
        ts("dve", sw08, vecT[:, 16:17], 1.0 - LAMBDA_INIT, None, ALU.mult, None, ["vecT"], ["sw08"])
        invf = R0.alloc([128, 8], F32)
        for i in range(8):
            memset("dve", invf[:, i:i + 1], float(np.float32(500000.0) ** np.float32(-(2.0 * i) / 16.0)), ["invf"])
        ang = R0.alloc([128, 256], F32)
        aq = R0.alloc([128, 256], F32)
        aqi = R0.alloc([128, 256], I32)
        ar_ = R0.alloc([128, 256], F32)
        tt("dve", ang.rearrange("p (a b) -> p a b", a=32), bc3(posT, [128, 32, 8], 2), bc3(invf, [128, 32, 8], 1),
           ALU.mult, ["posT", "invf"], ["angx"])
        sincos(ang, aq, aqi, ar_, sinT.rearrange("p a b -> p (a b)"), cosT.rearrange("p a b -> p (a b)"), "ang")

        NEt = 16 * NE
        kv_i = R0.alloc([128, NE], I32)
        kv = R0.alloc([128, NE], F32)
        iota(kv_i[:, 0:NG], [[-1, NG]], L - 1, 0, ["kv_i"])
        iota(kv_i[:, NG:NE], [[1, NH]], 0, 0, ["kv_i"])
        cp("dve", kv, kv_i, ["kv_i"], ["kv"])
        dtv = R0.alloc([128, 16], F32)
        act(dtv, par[:, 2, :], AF.Exp, ["par"], ["dtv"])
        lrdt = R0.alloc([128, 16], F32)
        thv = R0.alloc([128, 16], F32)
        tt("dve", lrdt, par[:, 0, :], dtv, ALU.mult, ["par", "dtv"], ["lrdt"])
        tt("dve", thv, par[:, 1, :], dtv, ALU.mult, ["par", "dtv"], ["thv"])
        Emag = R0.alloc([128, 16, NE], F32)
        Eph = R0.alloc([128, 16, NE], F32)
        Eq = R0.alloc([128, 16, NE], F32)
        Eqi = R0.alloc([128, 16, NE], I32)
        Er = R0.alloc([128, 16, NE], F32)
        Ei = R0.alloc([128, 16, NE], F32)
        Ert = R0.alloc([128, 16, NE], F32)
        shp = [128, 16, NE]
        tt("dve", Emag, bc3(lrdt, shp, 2), bc3(kv, shp, 1), ALU.mult, ["lrdt", "kv"], ["Emag"])
        act(Emag, Emag, AF.Exp, ["Emag"], ["Emag"])
        tt("dve", Eph, bc3(thv, shp, 2), bc3(kv, shp, 1), ALU.mult, ["thv", "kv"], ["Ephx"])
        f2 = lambda a: a.rearrange("p a b -> p (a b)")
        sincos(f2(Eph), f2(Eq), f2(Eqi), f2(Ert), f2(Ei), f2(Er), "Eph")
        tt("dve", f2(Er), f2(Er), f2(Emag), ALU.mult, ["Ephc", "Emag"], ["Er"])
        tt("dve", f2(Ei), f2(Ei), f2(Emag), ALU.mult, ["Ephs", "Emag"], ["Ei"])
        c_nr = R0.alloc([128, 16], F32)
        c_den = R0.alloc([128, 16], F32)
        c_t = R0.alloc([128, 16], F32)
        c_r = R0.alloc([128, 16], F32)
        c_i = R0.alloc([128, 16], F32)
        lr, li = par[:, 0, :], par[:, 1, :]
        ni = Ei[:, :, NG + 1]
        ts("dve", c_nr, Er[:, :, NG + 1], -1.0, None, ALU.add, None, ["Er"], ["c_nr"])
        tt("dve", c_den, lr, lr, ALU.mult, ["par"], ["c_den"])
        tt("dve", c_t, li, li, ALU.mult, ["par"], ["c_t"])
        tt("dve", c_den, c_den, c_t, ALU.add, ["c_den", "c_t"], ["c_den"])
        P.op("dve", lambda e: e.reciprocal(c_den, c_den), ["c_den"], ["c_den"])
        tt("dve", c_r, c_nr, lr, ALU.mult, ["c_nr", "par"], ["c_r"])
        tt("dve", c_t, ni, li, ALU.mult, ["Ei", "par", "c_t"], ["c_t"])
        tt("dve", c_r, c_r, c_t, ALU.add, ["c_r", "c_t"], ["c_r"])
        tt("dve", c_r, c_r, c_den, ALU.mult, ["c_r", "c_den"], ["c_r"])
        tt("dve", c_i, ni, lr, ALU.mult, ["Ei", "par"], ["c_i"])
        tt("dve", c_t, c_nr, li, ALU.mult, ["c_nr", "par", "c_t"], ["c_t"])
        tt("dve", c_i, c_i, c_t, ALU.subtract, ["c_i", "c_t"], ["c_i"])
        tt("dve", c_i, c_i, c_den, ALU.mult, ["c_i", "c_den"], ["c_i"])
        cp("dve", APr[:, 0, :], Er[:, :, NG + L], ["Er"], ["APr"])
        cp("dve", APi[:, 0, :], Ei[:, :, NG + L], ["Ei"], ["APi"])
        sq1 = R0.alloc([128, 16], F32)
        sq2 = R0.alloc([128, 16], F32)
        for d_ in range(1, 6):
            tt("dve", sq1, APr[:, d_ - 1, :], APr[:, d_ - 1, :], ALU.mult, ["APr", "sq1"], ["sq1"])
            tt("dve", sq2, APi[:, d_ - 1, :], APi[:, d_ - 1, :], ALU.mult, ["APi", "sq2"], ["sq2"])
            tt("dve", APr[:, d_, :], sq1, sq2, ALU.subtract, ["sq1", "sq2", "APr"], ["APr"])
            tt("dve", sq1, APr[:, d_ - 1, :], APi[:, d_ - 1, :], ALU.mult, ["APr", "APi", "sq1"], ["sq1"])
            ts("dve", APi[:, d_, :], sq1, 2.0, None, ALU.mult, None, ["sq1", "APi"], ["APi"])
        cp("dve", rmag, Emag[:, :, NG + L], ["Emag"], ["rmag"])
        P.op("dve", lambda e: e.reciprocal(sq1, rmag), ["rmag", "sq1"], ["sq1"])
        tt("dve", u1[:, 0, :], APr[:, 0, :], sq1, ALU.mult, ["APr", "sq1"], ["u1"])
        tt("dve", u1[:, 1, :], APi[:, 0, :], sq1, ALU.mult, ["APi", "sq1", "u1"], ["u1"])
        Bre = R0.alloc([128, 16, 16], F32)
        Bim = R0.alloc([128, 16, 16], F32)
        bbr = R0.alloc([128, 16, 16], F32)
        bbi = R0.alloc([128, 16, 16], F32)
        bt = R0.alloc([128, 16, 16], F32)
        P.dma("sp", Bre, bre_d.rearrange("(gp g2) n q -> (g2 n) gp q", g2=2), [], ["Bre"])
        P.dma("sp", Bim, bim_d.rearrange("(gp g2) n q -> (g2 n) gp q", g2=2), [], ["Bim"])
        s3 = [128, 16, 16]
        tt("dve", bbr, Bre, bc3(c_r, s3, 2), ALU.mult, ["Bre", "c_r"], ["bbr"])
        tt("dve", bt, Bim, bc3(c_i, s3, 2), ALU.mult, ["Bim", "c_i"], ["bt"])
        tt("dve", bbr, bbr, bt, ALU.subtract, ["bbr", "bt"], ["bbr"])
        tt("dve", bbi, Bim, bc3(c_r, s3, 2), ALU.mult, ["Bim", "c_r"], ["bbi"])
        tt("dve", bt, Bre, bc3(c_i, s3, 2), ALU.mult, ["Bre", "c_i", "bt"], ["bt"])
        tt("dve", bbi, bbi, bt, ALU.add, ["bbi", "bt"], ["bbi"])
        Xc = R0.alloc([128, 4, 128], F32)
        Ctr = R0.alloc([128, 16, 16], F32)
        Cti = R0.alloc([128, 16, 16], F32)
        for ri, cd in enumerate((cre_d, cim_d)):
            for hf in range(2):
                for gpl in range(8):
                    for g2 in range(2):
                        g = 2 * (8 * hf + gpl) + g2
                        P.dma("sp" if (gpl % 2 == 0) else "act", Xc[16 * gpl:16 * gpl + 16, 2 * ri + hf, 64 * g2:64 * g2 + 64],
                              cd[g], [], [("Xc", ri, hf, gpl, g2)])
                tr(PS[4 + 2 * ri + hf][:, 0:128], Xc[:, 2 * ri + hf, :], ident_f,
                   [("Xc", ri, hf, a, b) for a in range(8) for b in range(2)] + ["ident_f"], [PSK[4 + 2 * ri + hf]])
                dst = (Ctr, Cti)[ri]
                cp("dve", dst[:, 8 * hf:8 * hf + 8, :].rearrange("p a b -> p (a b)"), PS[4 + 2 * ri + hf][:, 0:128],
                   [PSK[4 + 2 * ri + hf]], [("Ct", ri, hf)])
        CtK = [("Ct", ri, hf) for ri in range(2) for hf in range(2)]
        Gr = RC.alloc([128, 16, NG, 16], BF16)
        Gi = RC.alloc([128, 16, NG, 16], BF16)
        GC = 2
        g1 = W2.alloc([128, GC, NG, 16], F32)
        g2t = W2.alloc([128, GC, NG, 16], F32)
        g3 = W2.alloc([128, GC, NG, 16], F32)
        g4 = W2.alloc([128, GC, NG, 16], F32)
        for c0 in range(0, 16, GC):
            sl = slice(c0, c0 + GC)
            for (E0, n0, nn, Xr_, Xi_, Or_, Oi_, neg, kx) in (
                    (0, 0, NG, bbr, bbi, Gr, Gi, False, ["bbr", "bbi"]),
                    (NG, 0, NH, Ctr, Cti, Hr, nHi, True, CtK)):
                shp4 = [128, GC, nn, 16]
                er = Er[:, sl, E0:E0 + nn].unsqueeze(3).to_broadcast(shp4)
                ei = Ei[:, sl, E0:E0 + nn].unsqueeze(3).to_broadcast(shp4)
                xr = Xr_[:, sl, :].unsqueeze(2).to_broadcast(shp4)
                xi = Xi_[:, sl, :].unsqueeze(2).to_broadcast(shp4)
                a1, a2 = g1[:, :, 0:nn, :], g2t[:, :, 0:nn, :]
                a3, a4 = g3[:, :, 0:nn, :], g4[:, :, 0:nn, :]
                tt("dve", a1, er, xr, ALU.mult, ["Er"] + kx + ["g1"], ["g1"])
                tt("dve", a2, ei, xi, ALU.mult, ["Ei"] + kx + ["g2"], ["g2"])
                tt("dve", Or_[:, sl, :, :], a1, a2, ALU.subtract, ["g1", "g2"], [("GH", E0, c0, 0)])
                tt("pool", a3, er, xi, ALU.mult, ["Er"] + kx + ["g3"], ["g3"])
                tt("pool", a4, ei, xr, ALU.mult, ["Ei"] + kx + ["g4"], ["g4"])
                if neg:
                    fl = lambda a: a.rearrange("p a b c -> p a (b c)")
                    stt(fl(Oi_[:, sl, :, :]), fl(a3), -1.0, fl(a4), ALU.mult, ALU.subtract, ["g3", "g4"], [("GH", E0, c0, 1)])
                else:
                    tt("pool", Oi_[:, sl, :, :], a3, a4, ALU.add, ["g3", "g4"], [("GH", E0, c0, 1)])
        GK = [("GH", 0, c0, i) for c0 in range(0, 16, GC) for i in range(2)]
        HK = [("GH", NG, c0, i) for c0 in range(0, 16, GC) for i in range(2)]
        P.barrier()
        for g in range(32):
            gp, hf = g // 2, g % 2
            hs = slice(64 * hf, 64 * hf + 64)
            pb = 4 + (g % 2)
            for dl in range(MS):
                r0 = (L - 1) - 8 * dl
                o = PS[pb][:, 128 * dl:128 * dl + 128]
                mm(o, Gr[hs, gp, r0:r0 + 8, :].rearrange("p a b -> p (a b)"),
                   Hr[hs, gp, 0:8, :].rearrange("p a b -> p (a b)"), True, False, GK + HK, [PSK[pb]])
                mm(o, Gi[hs, gp, r0:r0 + 8, :].rearrange("p a b -> p (a b)"),
                   nHi[hs, gp, 0:8, :].rearrange("p a b -> p (a b)"), False, True, GK + HK, [PSK[pb]])
            tt("dve", Tt[:, g, :, :].rearrange("p a b -> p (a b)"), PS[pb][:, 0:128 * MS],
               maskT.rearrange("p a b -> p (a b)"), ALU.mult, [PSK[pb], "maskT"], ["Tt"])
            pt = 6 + (g % 2)
            for j in range(MS):
                for ri, Gx in enumerate((Gr, Gi)):
                    c0 = (j * 2 + ri) * 64
                    tr(psb(pt)[:, c0:c0 + 64], Gx[hs, gp, 8 * j:8 * j + 8, :].rearrange("p a b -> p (a b)"),
                       ident_bf[hs, hs], GK + ["ident_bf"], [PSK[pt]])
            cp("act", M1[:, g, :, :].rearrange("p a b -> p (a b)"), psb(pt)[:, 0:128 * MS], [PSK[pt]], ["M1"])
        wst = [RC.alloc([128, 1024], F32) for _ in range(2)]
        wi = 0

        def load_w(dst, src, rows_chunks, ncols, key, scale_col=None):
            nonlocal wi
            for c in range(rows_chunks):
                for n0 in range(0, ncols, 1024):
                    nn = min(1024, ncols - n0)
                    b = wi % 2
                    wi += 1
                    P.dma("sp", wst[b][:, 0:nn], src[128 * c:128 * c + 128, n0:n0 + nn], [], [("wst", b)])
                    eng = "act" if (wi % 2) else "dve"
                    if scale_col is not None:
                        if eng == "act":
                            act(dst[:, c, n0:n0 + nn], wst[b][:, 0:nn], AF.Copy, [("wst", b), "vecT"], [key],
                                scale=scale_col[:, c:c + 1])
                        else:
                            ts("dve", dst[:, c, n0:n0 + nn], wst[b][:, 0:nn], scale_col[:, c:c + 1], None,
                               ALU.mult, None, [("wst", b), "vecT"], [key])
                    else:
                        cp(eng, dst[:, c, n0:n0 + nn], wst[b][:, 0:nn], [("wst", b)], [key])

        load_w(win_bf, win_d, 8, 3072, "win_bf", scale_col=nd)
        load_w(glu_bf, glu_d, 4, 512, "glu_bf")
        W2.reset()
        CH_ = 2048 // L
        PTr = W2.alloc([128, 16, CH_], F32)
        PTi = W2.alloc([128, 16, CH_], F32)
        Rm = W2.alloc([128, 16, CH_], F32)
        pw = RC.alloc([128, 8, 2, 16], F32)
        dA = RC.alloc([128, 16, CH_ // 2], F32)
        dB = RC.alloc([128, 16, CH_ // 2], F32)
        memset("dve", PTr[:, :, 0:1], 1.0, ["PT"])
        memset("dve", PTi[:, :, 0:1], 0.0, ["PT"])
        cp("dve", pw[:, 0, :, :], u1, ["u1"], ["pw"])
        k_ = 0
        while (1 << k_) < CH_:
            m_ = 1 << k_
            if k_ > 0:
                pr, pi_ = pw[:, k_ - 1, 0, :], pw[:, k_ - 1, 1, :]
                tt("dve", dA[:, :, 0], pr, pr, ALU.mult, ["pw", "dA"], ["dA"])
                tt("dve", dB[:, :, 0], pi_, pi_, ALU.mult, ["pw", "dB"], ["dB"])
                tt("dve", pw[:, k_, 0, :], dA[:, :, 0], dB[:, :, 0], ALU.subtract, ["dA", "dB", "pw"], ["pw"])
                tt("dve", dA[:, :, 0], pr, pi_, ALU.mult, ["pw", "dA"], ["dA"])
                ts("dve", pw[:, k_, 1, :], dA[:, :, 0], 2.0, None, ALU.mult, None, ["dA", "pw"], ["pw"])
            shp_ = [128, 16, m_]
            br = pw[:, k_, 0, :].unsqueeze(2).to_broadcast(shp_)
            bi = pw[:, k_, 1, :].unsqueeze(2).to_broadcast(shp_)
            sr, si = PTr[:, :, 0:m_], PTi[:, :, 0:m_]
            a_, b_2 = dA[:, :, 0:m_], dB[:, :, 0:m_]
            tt("dve", a_, sr, br, ALU.mult, ["PT", "pw", "dA"], ["dA"])
            tt("dve", b_2, si, bi, ALU.mult, ["PT", "pw", "dB"], ["dB"])
            tt("dve", PTr[:, :, m_:2 * m_], a_, b_2, ALU.subtract, ["dA", "dB", "PT"], ["PT"])
            tt("dve", a_, sr, bi, ALU.mult, ["PT", "pw", "dA"], ["dA"])
            tt("dve", b_2, si, br, ALU.mult, ["PT", "pw", "dB"], ["dB"])
            tt("dve", PTi[:, :, m_:2 * m_], a_, b_2, ALU.add, ["dA", "dB", "PT"], ["PT"])
            k_ += 1
        cp("dve", Rm, rmag.unsqueeze(2).to_broadcast([128, 16, CH_]), ["rmag"], ["Rm"])
        memset("dve", Rm[:, :, 0:1], 0.0, ["Rm"])
        memset("dve", carry.rearrange("p a b -> p (a b)"), 0.0, ["carry"])
        P.barrier()

        RC.reset()
        xt = [RC.alloc([128, 1024], F32) for _ in range(2)]
        hbf = RC.alloc([128, 1024], BF16)
        hT = RC.alloc([128, 8, 512], BF16)
        uT = RC.alloc([128, 4, 512], BF16)
        zsT = RC.alloc([128, 4, 512], BF16)
        qsq = RC.alloc([128, 512], F32)
        w2_mark = W2.cur
        qns = [W2.alloc([128, 512], F32) for _ in range(2)]
        qtoks = [[W2.alloc([128, 512], BF16) for _ in range(2)] for _ in range(2)]
        qTb = RC.alloc([128, 4, 512], BF16)
        kTb = RC.alloc([128, 4, 512], BF16)
        vtok = [RC.alloc([128, 512], BF16) for _ in range(2)]
        st8 = RC.alloc([128, 8], F32)
        rt1 = RC.alloc([128, 8, 8], F32)
        rt2 = RC.alloc([128, 8, 8], F32)
        ss1 = RC.alloc([128, 4], F32)

        hTs = [hT, RC.alloc([128, 8, 512], BF16)]
        mhalf = RC.alloc([128, 8], F32)
        memset("dve", mhalf, -0.5, ["mhalf"])

        hbfs = [hbf, RC.alloc([128, 1024], BF16)]

        def rms_a(b, i):
            t0 = 512 * b
            xb_ = xt[i % 2]
            xk = ("xt", i % 2)
            hb, hk = hbfs[i % 2], ("hbf", i % 2)
            P.dma("sp", xb_, x_d[t0 + 128 * i:t0 + 128 * i + 128, :], [], [xk])
            act(hb, xb_, AF.Square, [xk], [hk, "ss1"], accum=ss1[:, 0:1])
            ts("dve", ss1[:, 1:2], ss1[:, 0:1], 1.0 / D, EPS, ALU.mult, ALU.add, ["ss1"], ["ss1b"])
            tt("pool", ss1[:, 3:4], ss1[:, 1:2], mhalf[:, 0:1], ALU.pow, ["ss1b", "mhalf"], ["ss1d"])
            act(hb, xb_, AF.Copy, [xk, "ss1d"], [hk], scale=ss1[:, 3:4])

        def rms_b(b, i):
            hb, hk = hbfs[i % 2], ("hbf", i % 2)
            for c in range(8):
                tr(psb(0)[:, 128 * c:128 * c + 128], hb[:, 128 * c:128 * c + 128], ident_bf,
                   [hk, "ident_bf"], [PSK[0]])
            cp("act", hTs[b % 2][:, :, 128 * i:128 * i + 128], psb(0).rearrange("p (a b) -> p a b", a=8),
               [PSK[0]], [("hT", b % 2, i)])

        def rms_sched(b, slot):
            order = {0: [("a", 0)], 1: [("a", 1)], 2: [("b", 0)], 3: [("a", 2)], 4: [("b", 1)], 5: [("a", 3)],
                     6: [("b", 2)], 7: [("b", 3)]}
            for kind, i in order[slot]:
                (rms_a if kind == "a" else rms_b)(b, i)

        for slot in range(8):
            rms_sched(0, slot)
        for b in range(NBLK):
            t0 = 512 * b
            hT = hTs[b % 2]
            hTK = [("hT", b % 2, i) for i in range(4)]
            for fi, f in enumerate(range(8)):
                pb = 1 + (fi % 2)
                for c in range(8):
                    mm(PS[pb][:, :], win_bf[:, c, 128 * f:128 * f + 128], hT[:, c, :], c == 0, c == 7,
                       ["win_bf"] + hTK, [PSK[pb]])
                if f < 4:
                    cp("act", uT[:, f, :], PS[pb][:, :], [PSK[pb]], [("uT", f)])
                    P.dma("act", u_s[f, :, t0:t0 + 512], uT[:, f, :], [("uT", f)], [("u_s", f, b)])
                else:
                    act(zsT[:, f - 4, :], PS[pb][:, :], AF.Silu, [PSK[pb]], [("zsT", f - 4)])
                    P.dma("act", zs_s[f - 4, :, t0:t0 + 512], zsT[:, f - 4, :], [("zsT", f - 4)], [("zs_s", f - 4, b)])
                if b + 1 < NBLK:
                    rms_sched(b + 1, fi)
            def qk_transposes(i):
                ts_ = slice(128 * i, 128 * i + 128)
                for wi_, which in enumerate(("q", "k")):
                    qt_ = qtoks[wi_][i % 2]
                    qk_ = ("qtok", wi_, i % 2)
                    pk3 = ("ps3", wi_)
                    for h in range(4):
                        tr(psb(3)[:, 512 * wi_ + 128 * h:512 * wi_ + 128 * h + 128], qt_[:, 128 * h:128 * h + 128], ident_bf,
                           [qk_, "ident_bf"], [pk3])
                    dstT = qTb if which == "q" else kTb
                    cp("act", dstT[:, :, ts_], psb(3)[:, 512 * wi_:512 * wi_ + 512].rearrange("p (a b) -> p a b", a=4),
                       [pk3], [(which + "Tb", i)])

            for i in range(4):
                ts_ = slice(128 * i, 128 * i + 128)
                for wi_, (which, col0) in enumerate((("q", 1024), ("k", 1536), ("v", 2048), ("za", 2560))):
                    pb = (4 + wi_ + 2 * (i % 2)) if wi_ < 2 else (wi_ - 1)
                    for c in range(8):
                        mm(PS[pb][:, :], hT[:, c, ts_], win_bf[:, c, col0:col0 + 512], c == 0, c == 7,
                           ["win_bf", ("hT", b % 2, i)], [PSK[pb]])
                    if which == "v":
                        vb = vtok[0]
                        cp("act", vb, PS[pb][:, :], [PSK[pb]], [("vtok", 0)])
                        P.dma("act", v_s[t0 + 128 * i:t0 + 128 * i + 128, :], vb, [("vtok", 0)],
                              [("v_s", b, i)])
                        continue
                    if which == "za":
                        vb = vtok[1]
                        act(vb, PS[pb][:, :], AF.Silu, [PSK[pb]], [("vtok", 1)])
                        P.dma("act", za_s[t0 + 128 * i:t0 + 128 * i + 128, :], vb, [("vtok", 1)],
                              [("za_s", b, i)])
                        continue
                    wq = bcq if which == "q" else bck
                    qn = qns[wi_]
                    qnk = "qn%d" % wi_
                    act(qsq, PS[pb][:, :], AF.Square, [PSK[pb]], ["qsq"])
                    P.op("dve", lambda e: e.tensor_reduce(st8, qsq.rearrange("p (a b) -> p a b", a=8), AX.X, ALU.add),
                         ["qsq"], ["st8"])
                    ts("dve", st8, st8, 1.0 / 64, EPS, ALU.mult, ALU.add, ["st8"], ["st8"])
                    tt("pool", st8, st8, mhalf, ALU.pow, ["st8", "mhalf"], ["st8"])
                    q3 = qn.rearrange("p (a b) -> p a b", a=8)
                    tt("dve", q3, PS[pb][:, :].rearrange("p (a b) -> p a b", a=8), bc3(st8, [128, 8, 64], 2),
                       ALU.mult, [PSK[pb], "st8"], [qnk])
                    tt("dve", qn, qn, wq, ALU.mult, [qnk, "bcq", "bck"], [qnk])
                    tile_idx = 4 * b + i
                    cs = cosT[:, tile_idx, :].unsqueeze(1).to_broadcast([128, 8, 8])
                    sn = sinT[:, tile_idx, :].unsqueeze(1).to_broadcast([128, 8, 8])
                    x1, x2 = q3[:, :, 0:8], q3[:, :, 8:16]
                    tt("dve", rt1, x1, cs, ALU.mult, [qnk, "angc"], ["rt1"])
                    tt("dve", rt2, x2, sn, ALU.mult, [qnk, "angs"], ["rt2"])
                    tt("dve", rt1, rt1, rt2, ALU.subtract, ["rt1", "rt2"], ["rt1"])
                    tt("dve", rt2, x2, cs, ALU.mult, [qnk, "angc", "rt2"], ["rt2"])
                    tt("dve", x2, x1, sn, ALU.mult, [qnk, "angs"], [qnk])
                    tt("dve", x2, x2, rt2, ALU.add, [qnk, "rt2"], [qnk])
                    cp("dve", x1, rt1, ["rt1", qnk], [qnk])
                    cp("pool", qtoks[wi_][i % 2], qn, [qnk], [("qtok", wi_, i % 2)])
                if i >= 1:
                    qk_transposes(i - 1)
            qk_transposes(3)
            for h in range(4):
                P.dma("act", qT_s[h, :, t0:t0 + 512], qTb[:, h, :], [("qTb", i) for i in range(4)], [("qT_s", h, b)])
                P.dma("act", kT_s[h, :, t0:t0 + 512], kTb[:, h, :], [("kTb", i) for i in range(4)], [("kT_s", h, b)])
        P.barrier()
        HT = 2048
        CH = HT // L
        SCH = HT // 8
        RA1 = Region(arena, RA.base, 49152)
        RC.reset()
        uTh = RA1.alloc([128, 4, HT], BF16)
        Ub = RA1.alloc([128, 32, SCH], BF16)
        Zt = [RA1.alloc([128, 16, CH], F32) for _ in range(2)]
        Zt += [RC.alloc([128, 16, CH], F32) for _ in range(4)]
        Zr, Zi, Zmr, Zmi, tA, tB = Zt
        zsl = [RC.alloc([128, 4, 512], BF16) for _ in range(2)]
        y2ps = [RC.alloc([128, SCH, 2], F32) for _ in range(2)]
        ge1s = [RC.alloc([128, SCH, 2], F32) for _ in range(2)]
        gsg = RC.alloc([128, 512], F32)
        ysb = [RC.alloc([128, 512], BF16) for _ in range(2)]
        W2.cur = w2_mark
        Xbf = [W2.alloc([128, 16, CH], BF16) for _ in range(2)]
        y2b_h = Ub.rearrange("p a b -> p (a b)").rearrange("p (c t) -> p c t", c=4)
        UK = [("Ub", g) for g in range(32)]
        f3 = lambda a: a.rearrange("p a b -> p (a b)")
        for hh in range(2):
            T0 = HT * hh
            for f in range(4):
                P.dma("sp", uTh[:, f, :], u_s[f, :, T0:T0 + HT], [("u_s", f, b_) for b_ in range(4 * hh, 4 * hh + 4)],
                      [("uTh", f)])
            for g in range(32):
                g8, gl = g // 8, g % 8
                pb = (g // 2) % 2
                for s_ in range(8):
                    j_, e_ = s_ // 2, s_ % 2
                    mm(PS[pb][32 * j_:32 * j_ + 32, SCH * (g % 2):SCH * (g % 2) + SCH],
                       Wsel[:, gl, 112 - 16 * e_:144 - 16 * e_], uTh[:, g8, s_:HT:8], e_ == 0, e_ == 1,
                       ["Wsel", ("uTh", g8)], [PSK[pb]], tp=(0, 32 * j_))
                if g % 2 == 1:
                    cp("act", Ub[:, g - 1:g + 1, :].rearrange("p a b -> p (a b)"), PS[pb][:, :], [PSK[pb]],
                       [("Ub", g - 1), ("Ub", g)])
            for gq in range(4):
                pz = 4 + 2 * (gq % 2)
                for g in range(8 * gq, 8 * gq + 8):
                    gp, hf = g // 2, g % 2
                    hs = slice(64 * hf, 64 * hf + 64)
                    for ri in range(2):
                        for j in range(MS):
                            mm(PS[pz + ri][hs, CH * (gp % 4):CH * (gp % 4) + CH], M1[:, g, j, 64 * ri:64 * ri + 64],
                               Ub[:, g, j:SCH:MS], j == 0, j == MS - 1, ["M1", ("Ub", g)], [PSK[pz + ri]])
                cp("dve", f3(Zr[:, 4 * gq:4 * gq + 4, :]), PS[pz][:, :], [PSK[pz]], [("Zr", gq)])
                cp("act", f3(Zi[:, 4 * gq:4 * gq + 4, :]), PS[pz + 1][:, :], [PSK[pz + 1]], [("Zi", gq)])
            ZrK = [("Zr", q_) for q_ in range(4)]
            ZiK = [("Zi", q_) for q_ in range(4)]
            a0r, a0i = APr[:, 0, :], APi[:, 0, :]
            cr_, ci_ = carry[:, 0, :], carry[:, 1, :]
            t1, t2 = tA[:, :, 0], tB[:, :, 0]
            tt("dve", t1, a0r, cr_, ALU.mult, ["APr", "carry", "tA"], ["tA"])
            tt("dve", t2, a0i, ci_, ALU.mult, ["APi", "carry", "tB"], ["tB"])
            tt("dve", t1, t1, t2, ALU.subtract, ["tA", "tB"], ["tA"])
            tt("dve", Zr[:, :, 0], Zr[:, :, 0], t1, ALU.add, ZrK + ["tA"], ZrK)
            tt("dve", t1, a0r, ci_, ALU.mult, ["APr", "carry", "tA"], ["tA"])
            tt("dve", t2, a0i, cr_, ALU.mult, ["APi", "carry", "tB"], ["tB"])
            tt("dve", t1, t1, t2, ALU.add, ["tA", "tB"], ["tA"])
            tt("dve", Zi[:, :, 0], Zi[:, :, 0], t1, ALU.add, ZiK + ["tA"], ZiK)
            tt("dve", tA, PTr, Zr, ALU.mult, ["PT", "tA"] + ZrK, ["tA"])
            tt("pool", tB, PTi, Zi, ALU.mult, ["PT", "tB"] + ZiK, ["tB"])
            tt("dve", Zmr, tA, tB, ALU.add, ["tA", "tB", "Zmr"], ["Zmr"])
            tt("dve", tA, PTr, Zi, ALU.mult, ["PT", "tA"] + ZiK, ["tA"])
            tt("pool", tB, PTi, Zr, ALU.mult, ["PT", "tB"] + ZrK, ["tB"])
            tt("dve", Zmi, tA, tB, ALU.subtract, ["tA", "tB", "Zmi"], ["Zmi"])
            P.op("dve", lambda e: e.tensor_tensor_scan(f3(Zr), f3(Rm), f3(Zmr), 0.0, ALU.mult, ALU.add),
                 ["Rm", "Zmr"] + ZrK, ZrK)
            P.op("dve", lambda e: e.tensor_tensor_scan(f3(Zi), f3(Rm), f3(Zmi), 0.0, ALU.mult, ALU.add),
                 ["Rm", "Zmi"] + ZiK, ZiK)
            tt("dve", tA, PTr, Zr, ALU.mult, ["PT", "tA"] + ZrK, ["tA"])
            tt("pool", tB, PTi, Zi, ALU.mult, ["PT", "tB"] + ZiK, ["tB"])
            tt("dve", Zmr, tA, tB, ALU.subtract, ["tA", "tB", "Zmr"], ["Zmr"])
            tt("dve", tA, PTr, Zi, ALU.mult, ["PT", "tA"] + ZiK, ["tA"])
            tt("pool", tB, PTi, Zr, ALU.mult, ["PT", "tB"] + ZrK, ["tB"])
            tt("dve", Zmi, tA, tB, ALU.add, ["tA", "tB", "Zmi"], ["Zmi"])
            for ri, (Xs, xk_) in enumerate(((Zmr, "Zmr"), (Zmi, "Zmi"))):
                cp("dve", Xbf[ri][:, :, 1:CH], Xs[:, :, 0:CH - 1], [xk_], [("Xbf", ri)])
                cp("dve", Xbf[ri][:, :, 0], carry[:, ri, :], ["carry", ("Xbf", ri)], [("Xbf", ri)])
            for ri, (Xs, xk_) in enumerate(((Zmr, "Zmr"), (Zmi, "Zmi"))):
                cp("dve", carry[:, ri, :], Xs[:, :, CH - 1], [xk_, ("Xbf", 0), ("Xbf", 1), "carry"], ["carry"])
            for g in range(32):
                gp, hf = g // 2, g % 2
                hs = slice(64 * hf, 64 * hf + 64)
                pb = (g // 2) % 2
                for j in range(MS):
                    o = PS[pb][:, SCH * (g % 2) + j:SCH * (g % 2) + SCH:MS]
                    for jp in range(j + 1):
                        mm(o, Tt[:, g, j - jp, :], Ub[:, g, jp:SCH:MS], jp == 0, False, ["Tt", ("Ub", g)], [PSK[pb]])
                    mm(o, Hr[hs, gp, 8 * j + 1:8 * j + 9, :].rearrange("p a b -> p (a b)"), Xbf[0][hs, gp, :],
                       False, False, HK + [("Xbf", 0)], [PSK[pb]])
                    mm(o, nHi[hs, gp, 8 * j + 1:8 * j + 9, :].rearrange("p a b -> p (a b)"), Xbf[1][hs, gp, :],
                       False, True, HK + [("Xbf", 1)], [PSK[pb]])
                if g % 2 == 1:
                    cp("act", Ub[:, g - 1:g + 1, :].rearrange("p a b -> p (a b)"), PS[pb][:, :], [PSK[pb]],
                       [("Ub", g - 1), ("Ub", g)])
            for ct in range(4):
                CK = [("Ub", 8 * ct + gl) for gl in range(8)]
                for t_ in range(8):
                    pbk = 4 + t_ // 2
                    for gl in range(8):
                        j_, e_ = gl // 2, gl % 2
                        mm(PS[pbk][32 * j_:32 * j_ + 32, SCH * (t_ % 2):SCH * (t_ % 2) + SCH],
                           Wsel[:, t_, 112 - 16 * e_:144 - 16 * e_], Ub[:, 8 * ct + gl, :], e_ == 0, e_ == 1,
                           ["Wsel", ("Ub", 8 * ct + gl)], [PSK[pbk]], tp=(0, 32 * j_))
                for tq in range(4):
                    pbk = 4 + tq
                    y2p, ge1 = y2ps[tq % 2], ge1s[tq % 2]
                    yk, gk = ("y2p", tq % 2), ("ge1", tq % 2)
                    uview = uTh[:, ct, :].rearrange("p (a b) -> p a b", b=8)[:, :, 2 * tq:2 * tq + 2]
                    stt(y2p, uview, dsk[:, ct:ct + 1], PS[pbk][:, :].rearrange("p (b a) -> p a b", b=2),
                        ALU.mult, ALU.add, [("uTh", ct), "vecT", PSK[pbk]], [yk])
                    yf, gf = f3(y2p), f3(ge1)
                    tt("pool", gf, yf, yf, ALU.mult, [yk], [gk])
                    ts("dve", gf, gf, 0.044715, 1.0, ALU.mult, ALU.add, [gk], [gk])
                    tt("dve", gf, gf, yf, ALU.mult, [gk, yk], [gk])
                    act(gf, gf, AF.Sigmoid, [gk], [gk], scale=1.5957691216057308)
                    tt("dve", y2b_h[:, ct, :].rearrange("p (a b) -> p a b", b=8)[:, :, 2 * tq:2 * tq + 2], y2p, ge1,
                       ALU.mult, [yk, gk] + CK, CK)
            for bi in range(4):
                bg = 4 * hh + bi
                tb0 = 512 * bi
                zl = zsl[bi % 2]
                for f in range(4):
                    P.dma("sp", zl[:, f, :], zs_s[f, :, T0 + tb0:T0 + tb0 + 512], [("zs_s", f, bg)], [("zsl", bi % 2, f)])
                for fo in range(4):
                    pb = fo % 2
                    for ci in range(4):
                        mm(PS[pb][:, :], glu_bf[:, ci, 128 * fo:128 * fo + 128], y2b_h[:, ci, tb0:tb0 + 512], ci == 0, ci == 3,
                           ["glu_bf"] + UK, [PSK[pb]])
                    act(gsg, PS[pb][:, :], AF.Sigmoid, [PSK[pb]], ["gsg"], bias=glb[:, fo:fo + 1])
                    tt("dve", gsg, gsg, y2b_h[:, fo, tb0:tb0 + 512], ALU.mult, ["gsg"] + UK, ["gsg"])
                    yo = ysb[fo % 2]
                    tt("dve", yo, gsg, zl[:, fo, :], ALU.mult, ["gsg", ("zsl", bi % 2, fo), ("ysb", fo % 2)], [("ysb", fo % 2)])
                    P.dma("sp", ys_s[fo, :, T0 + tb0:T0 + tb0 + 512], yo, [("ysb", fo % 2)], [("ys_s", fo, bg)])
        P.barrier()
        fin_keys = []
        if debug:
            RC.reset()
            dtile = RC.alloc([128, 4096], BF16)
            for nm, src in (("ys", ys_s), ("qT", qT_s), ("kT", kT_s)):
                for f in range(4):
                    P.dma("sp", dtile, src[f], [], ["dtile"])
                    P.dma("sp", dbg[nm][f], dtile, ["dtile"], [("dbg", nm, f)])
                    fin_keys.append(("dbg", nm, f))
            for nm, src in (("v", v_s), ("za", za_s)):
                for i in range(32):
                    P.dma("sp", dtile[:, 0:512], src[128 * i:128 * i + 128, :], [], ["dtile"])
                    P.dma("sp", dbg[nm][128 * i:128 * i + 128, :], dtile[:, 0:512], ["dtile"], [("dbg", nm, i)])
                    fin_keys.append(("dbg", nm, i))
            P.barrier()

        RAB = Region(arena, RA.base, RA.size + RB.size)
        RC.reset()
        kT_res = RAB.alloc([128, 4, S], BF16)
        v_res = RAB.alloc([128, 32, 4, 129], BF16)
        qTl = [RAB.alloc([128, 4, 512], BF16) for _ in range(2)]
        zatok = RAB.alloc([128, 4, 512], BF16)
        ysl = RAB.alloc([128, 4, 512], BF16)
        PTt = [RAB.alloc([128, 2, 512], BF16) for _ in range(4)]
        yaT = RAB.alloc([128, 4, 512], BF16)
        rs = RC.alloc([128, 8], F32)
        rsn = RC.alloc([128, 4], F32)
        o_all = RC.alloc([128, 4, 512], F32)
        sqt = RC.alloc([128, 512], F32)
        ss4 = RC.alloc([128, 4], F32)
        yatoks = [RC.alloc([128, 512], BF16) for _ in range(2)]
        xl = [RC.alloc([128, 1024], F32) for _ in range(2)]
        xnews = [RC.alloc([128, 1024], F32) for _ in range(2)]
        xnbs = [RC.alloc([128, 1024], BF16) for _ in range(2)]
        xnT = RC.alloc([128, 8, 128], BF16)
        pl = [RC.alloc([128, 256], F32) for _ in range(2)]
        pT = RC.alloc([128, 2, 128], BF16)
        gate = RC.alloc([128, 1024], F32)
        wst = [RC.alloc([128, 1024], F32) for _ in range(2)]
        tri = cmask[:, 0, 0:128]
        mhalf4 = RC.alloc([128, 4], F32)
        memset("dve", mhalf4, -0.5, ["mhalf4"])
        for h in range(4):
            P.dma("sp", kT_res[:, h, :], kT_s[h], [("kT_s", h, b_) for b_ in range(NBLK)], [("kT_res", h)])
        memset("dve", v_res[:, :, :, 128:129], 1.0, ["v_ones"])
        for i in range(32):
            P.dma("sp", v_res[:, i, :, 0:128], v_s[128 * i:128 * i + 128, :].rearrange("p (a b) -> p a b", a=4),
                  [("v_s", i // 4, i % 4)], [("v_res", i)])

        def load_q(b):
            for h in range(4):
                P.dma("sp", qTl[b % 2][:, h, :], qT_s[h, :, 512 * b:512 * b + 512], [("qT_s", h, b)], [("qTl", b % 2, h)])

        def load_x(b, i):
            tok = slice(512 * b + 128 * i, 512 * b + 128 * i + 128)
            P.dma("sp", xl[i % 2], x_d[tok, :], [], [("xl", i % 2)])
            P.dma("sp", pl[i % 2], p_d[tok, :], [], [("pl", i % 2)])

        load_q(0)
        load_w(wout_bf, wout_d, 8, 1024, "wout_bf")
        load_w(pg_bf, pg_d, 8, 1024, "pg_bf")
        load_w(pp_bf, pp_d, 2, 1024, "pp_bf")
        OBk = [PS[4], PS[5]]
        zatoks = [zatok, RAB.alloc([128, 4, 512], BF16)]
        ysls = [ysl, RC.alloc([128, 4, 512], BF16)]
        pti = 0

        def make_tail_units(b):
            t0 = 512 * b
            zat, ysl_ = zatoks[b % 2], ysls[b % 2]

            def stage_a(i):
                ts_ = slice(128 * i, 128 * i + 128)
                oK = [("o_all", i, h) for h in range(4)]
                oq = o_all[:, i, :]
                act(sqt, oq, AF.Square, oK, ["sqt"])
                P.op("dve", lambda e: e.tensor_reduce(ss4, sqt.rearrange("p (a b) -> p a b", a=4), AX.X, ALU.add),
                     ["sqt"], ["ss4"])
                ts("dve", ss4, ss4, 1.0 / 128, EPS, ALU.mult, ALU.add, ["ss4"], ["ss4"])
                tt("pool", ss4, ss4, mhalf4, ALU.pow, ["ss4", "mhalf4"], ["ss4"])
                o3 = oq.rearrange("p (a b) -> p a b", a=4)
                tt("dve", o3, o3, bc3(ss4, [128, 4, 128], 2), ALU.mult, oK + ["ss4"], oK)
                tt("dve", o3, o3, bcsw.unsqueeze(1).to_broadcast([128, 4, 128]), ALU.mult, oK + ["bcsw"], oK)
                tt("dve", yatoks[i % 2], oq, zat[:, i, :], ALU.mult, oK + [("zatok", b % 2, i)], [("yatok", i % 2)])
                yield

            def stage_a2(i):
                ts_ = slice(128 * i, 128 * i + 128)
                o7 = 512 * (i % 2)
                k7 = ("ps7h", i % 2)
                for h in range(4):
                    tr(psb(7)[:, o7 + 128 * h:o7 + 128 * h + 128], yatoks[i % 2][:, 128 * h:128 * h + 128], ident_bf,
                       [("yatok", i % 2), "ident_bf", PSK[7]], [k7])
                    if h % 2 == 1:
                        yield
                cp("dve", yaT[:, :, ts_], psb(7)[:, o7:o7 + 512].rearrange("p (a b) -> p a b", a=4), [k7], [("yaT", i)])
                yield

            def stage_b(i):
                ts_ = slice(128 * i, 128 * i + 128)
                xb_ = xl[i % 2]
                xk = ("xl", i % 2)
                xn = xnews[i % 2]
                for hf in range(2):
                    tb = (3, 6)[hf]
                    for c in range(8):
                        src = ysl_ if c < 4 else yaT
                        kk = ("ysl", b % 2, c) if c < 4 else ("yaT", i)
                        mm(PS[tb][:, :], src[:, c % 4, ts_], wout_bf[:, c, 512 * hf:512 * hf + 512], c == 0, c == 7,
                           [kk, "wout_bf"], [PSK[tb]])
                        if c % 2 == 1:
                            yield
                    tt("dve", xn[:, 512 * hf:512 * hf + 512], PS[tb][:, :], xb_[:, 512 * hf:512 * hf + 512],
                       ALU.add, [PSK[tb], xk], [("xnew", i % 2, hf)])
                    cp("dve", xnbs[i % 2][:, 512 * hf:512 * hf + 512], xn[:, 512 * hf:512 * hf + 512],
                       [("xnew", i % 2, hf)], [("xnb", i % 2, hf)])
                    yield

            def stage_c1(i):
                plk = ("pl", i % 2)
                for c in range(8):
                    tr(psb(7)[:, 128 * c:128 * c + 128], xnbs[i % 2][:, 128 * c:128 * c + 128], ident_bf,
                       [("xnb", i % 2, c // 4), "ident_bf"], [PSK[7], ("ps7h", 0), ("ps7h", 1)])
                    if c % 2 == 1:
                        yield
                cp("dve", xnT.rearrange("p a b -> p (a b)"), psb(7)[:, :], [PSK[7]], ["xnT"])
                for c in range(2):
                    tr(PS[6][:, 128 * c:128 * c + 128], pl[i % 2][:, 128 * c:128 * c + 128], ident_f, [plk, "ident_f"],
                       [PSK[6]])
                cp("dve", pT.rearrange("p a b -> p (a b)"), PS[6][:, 0:256], [PSK[6]], ["pT"])
                yield

            def stage_c2(i, hf):
                xb_ = xl[i % 2]
                xk = ("xl", i % 2)
                xn = xnews[i % 2]
                hsl = slice(512 * hf, 512 * hf + 512)
                tb = (3, 6)[hf]
                pk_ = PSK[tb]
                for c in range(8):
                    mm(PS[tb][:, :], xnT[:, c, :], pg_bf[:, c, hsl], c == 0, c == 7, ["xnT", "pg_bf"], [pk_])
                    if c % 2 == 1:
                        yield
                act(gate[:, hsl], PS[tb][:, :], AF.Tanh, [pk_], [("gate", hf)], scale=0.5)
                for c in range(2):
                    mm(PS[tb][:, :], pT[:, c, :], pp_bf[:, c, hsl], c == 0, c == 1, ["pT", "pp_bf"], [pk_])
                stt(gate[:, hsl], gate[:, hsl], 1.0, PS[tb][:, :], ALU.add, ALU.mult, [("gate", hf), pk_], [("gate", hf)])
                stt(xb_[:, hsl], gate[:, hsl], 0.5, xn[:, hsl], ALU.mult, ALU.add, [("gate", hf), ("xnew", i % 2, hf), xk], [xk])
                yield

            def store(i):
                tok = slice(t0 + 128 * i, t0 + 128 * i + 128)
                P.dma("sp", out_d[tok, :], xl[i % 2], [("xl", i % 2)], [("out", b, i)])
                fin_keys.append(("out", b, i))

            def gen():
                load_x(b, 0)
                load_x(b, 1)
                yield from stage_a(0)
                yield from stage_a(1)
                yield from stage_a2(0)
                yield from stage_a(2)
                yield from stage_a2(1)
                yield from stage_a(3)
                yield from stage_a2(2)
                yield from stage_a2(3)

                def fin(i):
                    yield from stage_c1(i)
                    yield from stage_c2(i, 0)
                    yield from stage_c2(i, 1)
                    store(i)
                    if i + 2 < 4:
                        load_x(b, i + 2)
                    yield

                yield from stage_b(0)
                yield from stage_b(1)
                yield from fin(0)
                yield from stage_b(2)
                yield from fin(1)
                yield from stage_b(3)
                yield from fin(2)
                yield from fin(3)

            return gen(), 112

        qzall = wst[1].bitcast(BF16)
        qz = [[qzall[:, 512 * (2 * c_ + hb_):512 * (2 * c_ + hb_) + 512] for hb_ in range(2)] for c_ in range(2)]
        for c_ in range(2):
            for hb_ in range(2):
                memset("pool", qz[c_][hb_], 0.0, [("qz", c_, hb_), ("wst", 1)])
        pending, pend_left = None, 0

        def advance(n):
            nonlocal pending, pend_left
            for _ in range(n):
                if pending is None:
                    return
                try:
                    next(pending)
                    pend_left = max(pend_left - 1, 1)
                except StopIteration:
                    pending, pend_left = None, 0

        for b in range(NBLK):
            t0 = 512 * b
            qb = qTl[b % 2]
            if b + 1 < NBLK:
                load_q(b + 1)
            for i in range(4):
                P.dma("sp", zatoks[b % 2][:, i, :], za_s[t0 + 128 * i:t0 + 128 * i + 128, :], [("za_s", b, i)],
                      [("zatok", b % 2, i)])
            for h in range(4):
                P.dma("sp", ysls[b % 2][:, h, :], ys_s[h, :, t0:t0 + 512], [("ys_s", h, b)], [("ysl", b % 2, h)])
            nkt = 4 * (b + 1)
            iters = []
            for h in range(4):
                for qh in range(2):
                    for kt in range(nkt):
                        j = kt - 4 * b
                        if j >= 0 and 128 * j >= 256 * (qh + 1):
                            continue
                        iters.append((h, kt, qh))
            n_it = len(iters)

            qz_done = set()

            def scores(it):
                h, kt, qh = it
                buf = scores.cnt % 3
                scores.cnt += 1
                j = kt - 4 * b
                q0 = max(128 * max(j, 0) - 256 * qh, 0)
                ks = slice(128 * kt, 128 * kt + 128)
                if h not in qz_done:
                    qz_done.add(h)
                    for c in range(2):
                        hs = slice(64 * c, 64 * c + 64)
                        cp("pool", qz[c][h % 2][hs, :], qb[hs, h, :], [("qTl", b % 2, h)], [("qz", c, h % 2)])
                for c in range(2):
                    mm(PS[buf][:, 256 * c + q0:256 * c + 256], kT_res[:, h, ks],
                       qz[c][h % 2][:, 256 * qh + q0:256 * qh + 256], True, True,
                       [("kT_res", h), ("qz", c, h % 2)], [("SC", buf)])
                return buf, q0

            scores.cnt = 0
            LA = 2
            sq_ = [scores(iters[k_]) for k_ in range(min(LA, n_it))]
            started = {}
            for idx, it in enumerate(iters):
                h, kt, qh = it
                if idx + LA < n_it:
                    sq_.append(scores(iters[idx + LA]))
                buf, q0 = sq_.pop(0)
                j = kt - 4 * b
                pt_i = pti % 4
                pti += 1
                pt = PTt[pt_i]
                pk = ("PT", pt_i)
                act(pt[:, :, q0:256], PS[buf].rearrange("p (c q) -> p c q", c=2)[:, :, q0:256], AF.Exp,
                    [("SC", buf)], [pk])
                if j >= 0 and 128 * j >= 256 * qh:
                    tt("dve", pt[:, :, q0:q0 + 128], pt[:, :, q0:q0 + 128],
                       tri.unsqueeze(1).to_broadcast([128, 2, 128]), ALU.mult, [pk, "cmask"], [pk])
                for qt in range(max(j, 2 * qh), 2 * qh + 2):
                    for c in range(2):
                        r = 2 * (qt - 2 * qh) + c
                        bank, col0 = r // 3, (r % 3) * 129
                        st_ = (h, qh, bank) not in started
                        started[(h, qh, bank)] = True
                        ql = 128 * (qt - 2 * qh)
                        lhs = pt[:, c, ql:ql + 128]
                        o_ap = OBk[bank][:, col0:col0 + 129]
                        rhs_ = v_res[:, kt, h, :]
                        P.op("pe", lambda e, o_ap=o_ap, lhs=lhs, rhs_=rhs_, st_=st_, sp_=False:
                             e.matmul(o_ap, lhsT=lhs, rhs=rhs_, start=st_, stop=sp_, skip_group_check=True),
                             [pk, ("v_res", kt), "v_ones"], [("OB", bank)])
                last_of_head = (idx + 1 == n_it) or (iters[idx + 1][0] != h) or (iters[idx + 1][2] != qh)
                if last_of_head:
                    for bank in range(2):
                        nreg = 3 if bank < 1 else 1
                        src = OBk[bank][:, 128:128 + 129 * (nreg - 1) + 1:129]
                        dst = rs[:, 3 * bank:3 * bank + nreg]
                        P.op("dve", lambda e, dst=dst, src=src: e.reciprocal(dst, src), [("OB", bank)], ["rs"])
                    ts("dve", rsn[:, 0:2], rs[:, 1:4:2], lamv[:, 1:2], None, ALU.mult, None, ["rs", "lamv"], ["rsn"])
                    for qt in range(2 * qh, 2 * qh + 2):
                        r0_, r1_ = 2 * (qt - 2 * qh), 2 * (qt - 2 * qh) + 1
                        oa = o_all[:, qt, 128 * h:128 * h + 128]
                        ts("dve", oa, OBk[r0_ // 3][:, (r0_ % 3) * 129:(r0_ % 3) * 129 + 128], rs[:, r0_:r0_ + 1], None,
                           ALU.mult, None, [("OB", r0_ // 3), "rs"], [("o_all", qt, h)])
                        stt(oa, OBk[r1_ // 3][:, (r1_ % 3) * 129:(r1_ % 3) * 129 + 128], rsn[:, qt - 2 * qh:qt - 2 * qh + 1], oa,
                            ALU.mult, ALU.add, [("OB", r1_ // 3), "rsn", ("o_all", qt, h)], [("o_all", qt, h)])
                if pending is not None:
                    advance(-(-pend_left // max(n_it - idx - 8, 1)))
            advance(10 ** 6)
            pending, pend_left = make_tail_units(b)
        advance(10 ** 6)
        P.emit(final_keys=fin_keys)
    return nc


_NC_CACHE = {}


def _core_inputs(b, x, p, positions, norm_w, w_in, ssm_lambda_re, ssm_lambda_im, ssm_log_dt,
                 ssm_b_re, ssm_b_im, ssm_c_re, ssm_c_im, ssm_d, glu_w, glu_b,
                 q_norm_w, k_norm_w, lambda_q1, lambda_k1, lambda_q2, lambda_k2,
                 subln_w, w_out, ple_w_proj, ple_w_gate):
    f = lambda a: np.ascontiguousarray(np.asarray(a, dtype=np.float32))
    vecs = np.concatenate([f(norm_w[0]).reshape(8, 128), f(ssm_d[0]).reshape(4, 128),
                           f(glu_b[0]).reshape(4, 128), f(subln_w[0]).reshape(1, 128)], axis=0)
    rows = np.concatenate([f(q_norm_w[0]), f(k_norm_w[0]), f(lambda_q1[0]), f(lambda_k1[0]),
                           f(lambda_q2[0]), f(lambda_k2[0]), f(subln_w[0])]).reshape(1, 512)
    lam = np.stack([f(ssm_lambda_re[0]).reshape(16, 128), f(ssm_lambda_im[0]).reshape(16, 128)], axis=1)
    return {
        "x": f(x[b]), "p": f(p[0, b]),
        "pos": np.ascontiguousarray(np.asarray(positions[b], dtype=np.int32).reshape(32, 128)),
        "vecs": np.ascontiguousarray(vecs), "rows": np.ascontiguousarray(rows),
        "w_in": f(w_in[0]), "lam": np.ascontiguousarray(lam), "log_dt": f(ssm_log_dt[0]).reshape(16, 2),
        "b_re": f(ssm_b_re[0]), "b_im": f(ssm_b_im[0]), "c_re": f(ssm_c_re[0]), "c_im": f(ssm_c_im[0]),
        "glu_w": f(glu_w[0]), "w_out": f(w_out[0]), "ple_w_proj": f(ple_w_proj[0]), "ple_w_gate": f(ple_w_gate[0]),
    }


def kernel(**inputs):
    if "nc" not in _NC_CACHE:
        _NC_CACHE["nc"] = build_program(DEBUG)
    nc = _NC_CACHE["nc"]
    in_maps = [_core_inputs(b, **inputs) for b in range(8)]
    res = run_bass_kernel_spmd(nc, in_maps, core_ids=list(range(8)))
    out = np.stack([np.asarray(r["out"], dtype=np.float32) for r in res.results], axis=0)
    return out
```

```python
import math
import contextlib
import numpy as np
import concourse.bass as bass
import concourse.mybir as mybir
from concourse.bass_utils import run_bass_kernel_spmd

F32 = mybir.dt.float32
BF16 = mybir.dt.bfloat16
I32 = mybir.dt.int32
ALU = mybir.AluOpType
AF = mybir.ActivationFunctionType
AX = mybir.AxisListType

SAME_ENGINE_SYNC = True
N_DMA_SEMS = 48
DEBUG = False

S = 4096
D = 1024
NBLK = 8
MS = 2
L = 8 * MS
NG = 8 * MS + 7
NH = 8 * MS + 1
NE = NG + NH
CPB = 512 // L
SCB = 64
EPS = 1e-6
TWO_PI = 2.0 * math.pi
CW1 = 6.28125
CW2 = TWO_PI - 6.28125
LAMBDA_INIT = 0.8 - 0.6 * math.exp(0.0)


class _Op:
    __slots__ = ("eng", "fn", "deps", "is_dma", "sem", "semval", "signal", "signo", "idx")


class Prog:
    ENGS = ("pe", "act", "dve", "pool", "sp")

    def __init__(self, nc):
        self.nc = nc
        self.ops = []
        self.last_w = {}
        self.readers = {}
        self.dma_rr = 0
        self.dma_sem_total = [0] * N_DMA_SEMS
        self.dma_sem_lastop = [None] * N_DMA_SEMS
        self.bar_deps = []
        self.need_bar = {e: False for e in self.ENGS}
        self.last_eng_op = {}

    def barrier(self):
        deps = [o for o in self.last_eng_op.values()]
        deps += [o for o in self.dma_sem_lastop if o is not None]
        self.bar_deps = deps
        for e in self.ENGS:
            self.need_bar[e] = True

    def _add(self, eng, fn, R, W, is_dma):
        op = _Op()
        op.eng, op.fn, op.is_dma = eng, fn, is_dma
        op.signal = False
        op.signo = 0
        op.sem = None
        op.semval = 0
        op.idx = len(self.ops)
        deps = []
        if self.need_bar[eng]:
            deps += self.bar_deps
            self.need_bar[eng] = False
        for k in R:
            w = self.last_w.get(k)
            if w is not None:
                deps.append(w)
        for k in W:
            w = self.last_w.get(k)
            if w is not None:
                deps.append(w)
            for r in self.readers.get(k, ()):
                deps.append(r)
        if is_dma:
            s = self.dma_rr
            self.dma_rr = (self.dma_rr + 1) % N_DMA_SEMS
            prev = self.dma_sem_lastop[s]
            if prev is not None:
                deps.append(prev)
            self.dma_sem_total[s] += 16
            op.sem = s
            op.semval = self.dma_sem_total[s]
            self.dma_sem_lastop[s] = op
        seen = set()
        dd = []
        for d in deps:
            if d is op or id(d) in seen:
                continue
            seen.add(id(d))
            if (not d.is_dma) and d.eng == eng and (eng == "pe" or not SAME_ENGINE_SYNC):
                continue
            dd.append(d)
            if not d.is_dma:
                d.signal = True
        op.deps = dd
        for k in W:
            self.last_w[k] = op
            self.readers[k] = []
        for k in R:
            if k not in W:
                self.readers.setdefault(k, []).append(op)
        self.ops.append(op)
        if not is_dma:
            self.last_eng_op[eng] = op
        return op

    def op(self, eng, fn, R=(), W=()):
        return self._add(eng, fn, tuple(R), tuple(W), False)

    def dma(self, q, out, in_, R=(), W=()):
        return self._add(q, lambda e: e.dma_start(out=out, in_=in_), tuple(R), tuple(W), True)

    def emit(self, final_keys=()):
        nc = self.nc
        self._add("sp", None, tuple(final_keys), (), False)
        cnt = {e: 0 for e in self.ENGS}
        for o in self.ops:
            if (not o.is_dma) and o.signal:
                cnt[o.eng] += 1
                o.signo = cnt[o.eng]
        with contextlib.ExitStack() as st:
            esem = {e: st.enter_context(nc.semaphore("sem_" + e)) for e in self.ENGS}
            dsem = [st.enter_context(nc.semaphore("dsem%d" % i)) for i in range(N_DMA_SEMS)]
            block = st.enter_context(nc.Block())
            per = {e: [o for o in self.ops if o.eng == e] for e in self.ENGS}

            def replay(e, eng):
                waited = {}
                for o in per[e]:
                    for d in o.deps:
                        if d.is_dma:
                            key, val, sem = ("d", d.sem), d.semval, dsem[d.sem]
                        else:
                            key, val, sem = ("e", d.eng), d.signo, esem[d.eng]
                        if waited.get(key, 0) >= val:
                            continue
                        waited[key] = val
                        eng.wait_ge(sem, val)
                    if o.fn is None:
                        continue
                    ins = o.fn(eng)
                    if o.is_dma:
                        ins.then_inc(dsem[o.sem], 16)
                    elif o.signal:
                        ins.then_inc(esem[e], 1)

            @block.sync
            def _(eng):
                replay("sp", eng)

            @block.scalar
            def _(eng):
                replay("act", eng)

            @block.vector
            def _(eng):
                replay("dve", eng)

            @block.gpsimd
            def _(eng):
                replay("pool", eng)

            @block.tensor
            def _(eng):
                replay("pe", eng)


class Region:
    def __init__(self, arena, base, size):
        self.arena, self.base, self.size, self.cur = arena, base, size, 0

    def reset(self):
        self.cur = 0

    def alloc(self, shape, dt, parts=None):
        esz = 2 if dt == BF16 else 4
        n = 1
        for s_ in shape[1:]:
            n *= s_
        nbytes = (n * esz + 31) // 32 * 32
        off = self.base + self.cur
        self.cur += nbytes
        assert self.cur <= self.size, ("region overflow", self.cur, self.size)
        v = self.arena[0:shape[0], off // 4:(off + nbytes) // 4]
        if dt != F32:
            v = v.bitcast(dt)
        v = v[:, 0:n]
        if len(shape) == 3:
            v = v.rearrange("p (a b) -> p a b", a=shape[1])
        elif len(shape) == 4:
            v = v.rearrange("p (a b c) -> p a b c", a=shape[1], b=shape[2])
        return v


def build_program(debug=False):
    nc = bass.Bass("TRN2", target_bir_lowering=False)
    P = Prog(nc)

    def din(name, shape, dt=F32):
        return nc.dram_tensor(name, list(shape), dt, kind="ExternalInput").ap()

    x_d = din("x", [S, D])
    p_d = din("p", [S, 256])
    pos_d = din("pos", [32, 128], I32)
    vec_d = din("vecs", [17, 128])
    row_d = din("rows", [1, 512])
    win_d = din("w_in", [D, 3072])
    lam_d = din("lam", [16, 2, 128])
    ldt_d = din("log_dt", [16, 2])
    bre_d = din("b_re", [32, 64, 16])
    bim_d = din("b_im", [32, 64, 16])
    cre_d = din("c_re", [32, 16, 64])
    cim_d = din("c_im", [32, 16, 64])
    glu_d = din("glu_w", [512, 512])
    wout_d = din("w_out", [D, D])
    pp_d = din("ple_w_proj", [256, D])
    pg_d = din("ple_w_gate", [D, D])
    out_d = nc.dram_tensor("out", [S, D], F32, kind="ExternalOutput").ap()
    ys_s = nc.dram_tensor("ys_s", [4, 128, S], BF16, kind="Internal").ap()
    u_s = nc.dram_tensor("u_s", [4, 128, S], BF16, kind="Internal").ap()
    zs_s = nc.dram_tensor("zs_s", [4, 128, S], BF16, kind="Internal").ap()
    za_s = nc.dram_tensor("za_s", [S, 512], BF16, kind="Internal").ap()
    qT_s = nc.dram_tensor("qT_s", [4, 128, S], BF16, kind="Internal").ap()
    kT_s = nc.dram_tensor("kT_s", [4, 128, S], BF16, kind="Internal").ap()
    v_s = nc.dram_tensor("v_s", [S, 512], BF16, kind="Internal").ap()
    dbg = {}
    if debug:
        for nm in ("ys", "qT", "kT"):
            dbg[nm] = nc.dram_tensor("dbg_" + nm, [4, 128, S], BF16, kind="ExternalOutput").ap()
        dbg["v"] = nc.dram_tensor("dbg_v", [S, 512], BF16, kind="ExternalOutput").ap()
        dbg["za"] = nc.dram_tensor("dbg_za", [S, 512], BF16, kind="ExternalOutput").ap()

    with contextlib.ExitStack() as st:
        ARENA_BYTES = 212480
        arena = st.enter_context(nc.sbuf_tensor("arena", [128, ARENA_BYTES // 4], F32))
        PQ = [st.enter_context(nc.psum_tensor("pq%d" % i, [128, 1024], F32)) for i in range(4)]
        PS = [PQ[i // 2][:, 512 * (i % 2):512 * (i % 2) + 512] for i in range(8)]
        PSK = ["ps%d" % i for i in range(8)]

        def psb(i):
            return PS[i].bitcast(BF16)

        PER = Region(arena, 0, 59136)
        RA = Region(arena, 59136, 65536)
        RB = Region(arena, 124672, 33792)
        RC = Region(arena, 158464, ARENA_BYTES - 158464)
        ident_bf = PER.alloc([128, 128], BF16)
        ident_f = PER.alloc([128, 128], F32)
        ones_f = PER.alloc([128, 128], F32)
        ones_bf = PER.alloc([128, 128], BF16)
        Wsel = PER.alloc([128, 8, 240], BF16)
        maskT = PER.alloc([128, MS, 128], BF16)
        cmask = PER.alloc([128, 4, 512], BF16)
        glu_bf = PER.alloc([128, 4, 512], BF16)
        W2_base = PER.cur
        wout_bf = PER.alloc([128, 8, 1024], BF16)
        pg_bf = PER.alloc([128, 8, 1024], BF16)
        pp_bf = PER.alloc([128, 2, 1024], BF16)
        W2 = Region(arena, W2_base, PER.cur - W2_base)
        vecT = PER.alloc([128, 17], F32)
        bcq = PER.alloc([128, 512], F32)
        bck = PER.alloc([128, 512], F32)
        cosT = PER.alloc([128, 32, 8], F32)
        sinT = PER.alloc([128, 32, 8], F32)
        APr = PER.alloc([128, 6, 16], F32)
        APi = PER.alloc([128, 6, 16], F32)
        carry = PER.alloc([128, 2, 16], F32)
        lamv = PER.alloc([128, 4], F32)
        sw08 = PER.alloc([128, 1], F32)
        epsv = PER.alloc([128, 1], F32)
        bcsw = PER.alloc([128, 128], F32)
        u1 = PER.alloc([128, 2, 16], F32)
        rmag = PER.alloc([128, 16], F32)
        nd = vecT[:, 0:8]
        dsk = vecT[:, 8:12]
        glb = vecT[:, 12:16]
        win_bf = RA.alloc([128, 8, 3072], BF16)
        M1 = RA.alloc([128, 32, MS, 128], BF16)
        Hr = RB.alloc([128, 16, NH, 16], BF16)
        nHi = RB.alloc([128, 16, NH, 16], BF16)
        Tt = RB.alloc([128, 32, MS, 128], BF16)

        def mm(out, lhsT, rhs, start, stop, R, W, tp=None):
            if tp is None:
                P.op("pe", lambda e: e.matmul(out, lhsT=lhsT, rhs=rhs, start=start, stop=stop), R, W)
            else:
                P.op("pe", lambda e: e.matmul(out, lhsT=lhsT, rhs=rhs, start=start, stop=stop, tile_position=tp), R, W)

        def tr(out, in_, ident, R, W):
            P.op("pe", lambda e: e.transpose(out, in_, ident), R, W)

        def act(out, in_, func, R, W, bias=None, scale=None, accum=None):
            kw = {}
            if bias is not None:
                kw["bias"] = bias
            if scale is not None:
                kw["scale"] = scale
            if accum is not None:
                kw["accum_out"] = accum
            P.op("act", lambda e: e.activation(out, in_, func, **kw), R, W)

        def tt(eng, out, a, b, op, R, W):
            P.op(eng, lambda e: e.tensor_tensor(out, a, b, op), R, W)

        def ts(eng, out, a, s1, s2, op0, op1, R, W):
            if op1 is None:
                P.op(eng, lambda e: e.tensor_scalar(out, a, s1, None, op0), R, W)
            else:
                P.op(eng, lambda e: e.tensor_scalar(out, a, s1, s2, op0, op1), R, W)

        def stt(out, a, s, b, op0, op1, R, W):
            P.op("dve", lambda e: e.scalar_tensor_tensor(out, a, s, b, op0, op1), R, W)

        def cp(eng, out, in_, R, W):
            if eng == "act":
                act(out, in_, AF.Copy, R, W)
            else:
                P.op(eng, lambda e: e.tensor_copy(out, in_), R, W)

        def iota(out, pattern, base, cm, W):
            P.op("pool", lambda e: e.iota(out, pattern=pattern, base=base, channel_multiplier=cm), (), W)

        def memset(eng, out, val, W):
            P.op(eng, lambda e: e.memset(out, val), (), W)

        def bc3(ap2, shape, axis):
            return ap2.unsqueeze(axis).to_broadcast(shape)

        def sincos(x, q, qi, r, s_out, c_out, key):
            ts("dve", q, x, 1.0 / TWO_PI, None, ALU.mult, None, [key + "x"], [key + "q"])
            cp("dve", qi, q, [key + "q"], [key + "qi"])
            cp("dve", q, qi, [key + "qi"], [key + "q"])
            stt(r, q, -CW1, x, ALU.mult, ALU.add, [key + "q", key + "x"], [key + "r"])
            stt(r, q, -CW2, r, ALU.mult, ALU.add, [key + "q", key + "r"], [key + "r"])
            ts("dve", x, r, -math.pi, math.pi, ALU.max, ALU.min, [key + "r"], [key + "x"])
            act(s_out, x, AF.Sin, [key + "x"], [key + "s"])
            ts("dve", x, r, math.pi / 2, None, ALU.add, None, [key + "r", key + "s"], [key + "x"])
            ts("dve", q, x, math.pi, -TWO_PI, ALU.is_gt, ALU.mult, [key + "x"], [key + "q"])
            tt("dve", x, x, q, ALU.add, [key + "q", key + "x"], [key + "x"])
            ts("dve", x, x, -math.pi, math.pi, ALU.max, ALU.min, [key + "x"], [key + "x"])
            act(c_out, x, AF.Sin, [key + "x"], [key + "c"])

        RC.reset()
        W2.reset()
        R0 = Region(arena, RA.base, RA.size)
        io_i = R0.alloc([128, 1920], I32)
        io_f = R0.alloc([128, 1920], F32)
        msk_f = R0.alloc([128, 1920], F32)
        iota(io_i[:, 0:128], [[1, 128]], 0, -1, ["io_i"])
        ts("dve", ident_f, io_i[:, 0:128], 0.0, None, ALU.is_equal, None, ["io_i"], ["ident_f"])
        cp("dve", ident_bf, ident_f, ["ident_f"], ["ident_bf"])
        memset("dve", ones_f, 1.0, ["ones_f"])
        memset("dve", ones_bf, 1.0, ["ones_bf"])
        memset("dve", epsv, EPS, ["epsv"])
        iota(io_i[:, 0:1920].rearrange("p (a b) -> p a b", a=8), [[16, 8], [1, 240]], -112, -1, ["io_i"])
        ts("dve", io_f[:, 0:1920], io_i[:, 0:1920], 0.0, None, ALU.is_equal, None, ["io_i"], ["io_f"])
        iota(io_i[:, 0:1920].rearrange("p (a b) -> p a b", a=8), [[0, 8], [1, 240]], 0, 0, ["io_i"])
        ts("dve", msk_f[:, 0:1920], io_i[:, 0:1920], 112.0, None, ALU.is_ge, None, ["io_i"], ["msk_f"])
        tt("dve", io_f[:, 0:1920], io_f[:, 0:1920], msk_f[:, 0:1920], ALU.mult, ["io_f", "msk_f"], ["io_f"])
        ts("dve", msk_f[:, 0:1920], io_i[:, 0:1920], 127.0, None, ALU.is_le, None, ["io_i"], ["msk_f"])
        tt("dve", Wsel.rearrange("p a b -> p (a b)"), io_f[:, 0:1920], msk_f[:, 0:1920], ALU.mult,
           ["io_f", "msk_f"], ["Wsel"])
        iota(io_i[:, 0:128].rearrange("p (a b) -> p a b", a=8), [[16, 8], [0, 16]], 0, -1, ["io_i"])
        memset("dve", maskT.rearrange("p a b -> p (a b)"), 1.0, ["maskT"])
        ts("dve", maskT[:, 0, :], io_i[:, 0:128], -15.0, None, ALU.is_ge, None, ["io_i", "maskT"], ["maskT"])
        for j in range(4):
            iota(io_i[:, 0:512], [[1, 512]], -128 * j, -1, ["io_i"])
            ts("dve", cmask[:, j, :], io_i[:, 0:512], 0.0, None, ALU.is_ge, None, ["io_i", "cmask"], ["cmask"])

        vec16 = R0.alloc([17, 128], F32)
        rowv = R0.alloc([1, 512], F32)
        lam16 = R0.alloc([16, 3, 128], F32)
        ldt16 = R0.alloc([16, 2], F32)
        pos_i = R0.alloc([32, 128], I32)
        pos_f = R0.alloc([32, 128], F32)
        P.dma("sp", vec16, vec_d, [], ["vec16"])
        P.dma("sp", rowv, row_d, [], ["rowv"])
        P.dma("sp", lam16[:, 0:2, :], lam_d, [], ["lam16a"])
        P.dma("sp", ldt16, ldt_d, [], ["ldt16"])
        P.dma("sp", pos_i, pos_d, [], ["pos_i"])
        cp("dve", lam16[:, 2, :].rearrange("p (a b) -> p a b", a=2), bc3(ldt16, [16, 2, 64], 2),
           ["ldt16"], ["lam16b"])
        cp("dve", pos_f, pos_i, ["pos_i"], ["pos_f"])
        tr(PS[0][:, 0:17], vec16, ident_f[0:17, 0:17], ["vec16", "ident_f"], [PSK[0]])
        cp("dve", vecT, PS[0][:, 0:17], [PSK[0]], ["vecT"])
        par = R0.alloc([128, 3, 16], F32)
        for i in range(3):
            tr(PS[1][:, 16 * i:16 * i + 16], lam16[:, i, :], ident_f[0:16, 0:16],
               ["lam16a", "lam16b", "ident_f"], [PSK[1]])
        cp("dve", par.rearrange("p a b -> p (a b)"), PS[1][:, 0:48], [PSK[1]], ["par"])
        posT = R0.alloc([128, 32], F32)
        tr(PS[2][:, 0:32], pos_f, ident_f[0:32, 0:32], ["pos_f", "ident_f"], [PSK[2]])
        cp("dve", posT, PS[2][:, 0:32], [PSK[2]], ["posT"])
        bcr = R0.alloc([128, 512], F32)
        mm(PS[3][:, 0:512], ones_f[0:1, :], rowv, True, True, ["ones_f", "rowv"], [PSK[3]])
        cp("dve", bcr, PS[3][:, 0:512], [PSK[3]], ["bcr"])
        ts("dve", bcsw, bcr[:, 384:512], 1.0 - LAMBDA_INIT, None, ALU.mult, None, ["bcr"], ["bcsw"])
        ts("dve", bcq.rearrange("p (a b) -> p a b", a=8), bc3(bcr[:, 0:64], [128, 8, 64], 1),
           0.125, None, ALU.mult, None, ["bcr"], ["bcq"])
        cp("dve", bck.rearrange("p (a b) -> p a b", a=8), bc3(bcr[:, 64:128], [128, 8, 64], 1), ["bcr"], ["bck"])
        lsc = R0.alloc([128, 128], F32)
        lsum = R0.alloc([128, 2], F32)
        tt("dve", lsc[:, 0:64], bcr[:, 128:192], bcr[:, 192:256], ALU.mult, ["bcr"], ["lsc"])
        tt("dve", lsc[:, 64:128], bcr[:, 256:320], bcr[:, 320:384], ALU.mult, ["bcr", "lsc"], ["lsc"])
        P.op("dve", lambda e: e.tensor_reduce(lsum, lsc.rearrange("p (a b) -> p a b", a=2), AX.X, ALU.add),
             ["lsc"], ["lsum"])
        act(lsum, lsum, AF.Exp, ["lsum"], ["lsum"])
        tt("dve", lamv[:, 0:1], lsum[:, 0:1], lsum[:, 1:2], ALU.subtract, ["lsum"], ["lamv"])
        ts("dve", lamv[:, 0:1], lamv[:, 0:1], LAMBDA_INIT, None, ALU.add, None, ["lamv"], ["lamv"])
        ts("dve", lamv[:, 1:2], lamv[:, 0:1], -1.0, None, ALU.mult, None, ["lamv"], ["lamv"])
        ts("dve", sw08, vecT[:, 16:17], 1.0 - LAMBDA_INIT, None, ALU.mult, None, ["vecT"], ["sw08"])
        invf = R0.alloc([128, 8], F32)
        for i in range(8):
            memset("dve", invf[:, i:i + 1], float(np.float32(500000.0) ** np.float32(-(2.0 * i) / 16.0)), ["invf"])
        ang = R0.alloc([128, 256], F32)
        aq = R0.alloc([128, 256], F32)
        aqi = R0.alloc([128, 256], I32)
        ar_ = R0.alloc([128, 256], F32)
        tt("dve", ang.rearrange("p (a b) -> p a b", a=32), bc3(posT, [128, 32, 8], 2), bc3(invf, [128, 32, 8], 1),
           ALU.mult, ["posT", "invf"], ["angx"])
        sincos(ang, aq, aqi, ar_, sinT.rearrange("p a b -> p (a b)"), cosT.rearrange("p a b -> p (a b)"), "ang")

        NEt = 16 * NE
        kv_i = R0.alloc([128, NE], I32)
        kv = R0.alloc([128, NE], F32)
        iota(kv_i[:, 0:NG], [[-1, NG]], L - 1, 0, ["kv_i"])
        iota(kv_i[:, NG:NE], [[1, NH]], 0, 0, ["kv_i"])
        cp("dve", kv, kv_i, ["kv_i"], ["kv"])
        dtv = R0.alloc([128, 16], F32)
        act(dtv, par[:, 2, :], AF.Exp, ["par"], ["dtv"])
        lrdt = R0.alloc([128, 16], F32)
        thv = R0.alloc([128, 16], F32)
        tt("dve", lrdt, par[:, 0, :], dtv, ALU.mult, ["par", "dtv"], ["lrdt"])
        tt("dve", thv, par[:, 1, :], dtv, ALU.mult, ["par", "dtv"], ["thv"])
        Emag = R0.alloc([128, 16, NE], F32)
        Eph = R0.alloc([128, 16, NE], F32)
        Eq = R0.alloc([128, 16, NE], F32)
        Eqi = R0.alloc([128, 16, NE], I32)
        Er = R0.alloc([128, 16, NE], F32)
        Ei = R0.alloc([128, 16, NE], F32)
        Ert = R0.alloc([128, 16, NE], F32)
        shp = [128, 16, NE]
        tt("dve", Emag, bc3(lrdt, shp, 2), bc3(kv, shp, 1), ALU.mult, ["lrdt", "kv"], ["Emag"])
        act(Emag, Emag, AF.Exp, ["Emag"], ["Emag"])
        tt("dve", Eph, bc3(thv, shp, 2), bc3(kv, shp, 1), ALU.mult, ["thv", "kv"], ["Ephx"])
        f2 = lambda a: a.rearrange("p a b -> p (a b)")
        sincos(f2(Eph), f2(Eq), f2(Eqi), f2(Ert), f2(Ei), f2(Er), "Eph")
        tt("dve", f2(Er), f2(Er), f2(Emag), ALU.mult, ["Ephc", "Emag"], ["Er"])
        tt("dve", f2(Ei), f2(Ei), f2(Emag), ALU.mult, ["Ephs", "Emag"], ["Ei"])
        c_nr = R0.alloc([128, 16], F32)
        c_den = R0.alloc([128, 16], F32)
        c_t = R0.alloc([128, 16], F32)
        c_r = R0.alloc([128, 16], F32)
        c_i = R0.alloc([128, 16], F32)
        lr, li = par[:, 0, :], par[:, 1, :]
        ni = Ei[:, :, NG + 1]
        ts("dve", c_nr, Er[:, :, NG + 1], -1.0, None, ALU.add, None, ["Er"], ["c_nr"])
        tt("dve", c_den, lr, lr, ALU.mult, ["par"], ["c_den"])
        tt("dve", c_t, li, li, ALU.mult, ["par"], ["c_t"])
        tt("dve", c_den, c_den, c_t, ALU.add, ["c_den", "c_t"], ["c_den"])
        P.op("dve", lambda e: e.reciprocal(c_den, c_den), ["c_den"], ["c_den"])
        tt("dve", c_r, c_nr, lr, ALU.mult, ["c_nr", "par"], ["c_r"])
        tt("dve", c_t, ni, li, ALU.mult, ["Ei", "par", "c_t"], ["c_t"])
        tt("dve", c_r, c_r, c_t, ALU.add, ["c_r", "c_t"], ["c_r"])
        tt("dve", c_r, c_r, c_den, ALU.mult, ["c_r", "c_den"], ["c_r"])
        tt("dve", c_i, ni, lr, ALU.mult, ["Ei", "par"], ["c_i"])
        tt("dve", c_t, c_nr, li, ALU.mult, ["c_nr", "par", "c_t"], ["c_t"])
        tt("dve", c_i, c_i, c_t, ALU.subtract, ["c_i", "c_t"], ["c_i"])
        tt("dve", c_i, c_i, c_den, ALU.mult, ["c_i", "c_den"], ["c_i"])
        cp("dve", APr[:, 0, :], Er[:, :, NG + L], ["Er"], ["APr"])
        cp("dve", APi[:, 0, :], Ei[:, :, NG + L], ["Ei"], ["APi"])
        sq1 = R0.alloc([128, 16], F32)
        sq2 = R0.alloc([128, 16], F32)
        for d_ in range(1, 6):
            tt("dve", sq1, APr[:, d_ - 1, :], APr[:, d_ - 1, :], ALU.mult, ["APr", "sq1"], ["sq1"])
            tt("dve", sq2, APi[:, d_ - 1, :], APi[:, d_ - 1, :], ALU.mult, ["APi", "sq2"], ["sq2"])
            tt("dve", APr[:, d_, :], sq1, sq2, ALU.subtract, ["sq1", "sq2", "APr"], ["APr"])
            tt("dve", sq1, APr[:, d_ - 1, :], APi[:, d_ - 1, :], ALU.mult, ["APr", "APi", "sq1"], ["sq1"])
            ts("dve", APi[:, d_, :], sq1, 2.0, None, ALU.mult, None, ["sq1", "APi"], ["APi"])
        cp("dve", rmag, Emag[:, :, NG + L], ["Emag"], ["rmag"])
        P.op("dve", lambda e: e.reciprocal(sq1, rmag), ["rmag", "sq1"], ["sq1"])
        tt("dve", u1[:, 0, :], APr[:, 0, :], sq1, ALU.mult, ["APr", "sq1"], ["u1"])
        tt("dve", u1[:, 1, :], APi[:, 0, :], sq1, ALU.mult, ["APi", "sq1", "u1"], ["u1"])
        Bre = R0.alloc([128, 16, 16], F32)
        Bim = R0.alloc([128, 16, 16], F32)
        bbr = R0.alloc([128, 16, 16], F32)
        bbi = R0.alloc([128, 16, 16], F32)
        bt = R0.alloc([128, 16, 16], F32)
        P.dma("sp", Bre, bre_d.rearrange("(gp g2) n q -> (g2 n) gp q", g2=2), [], ["Bre"])
        P.dma("sp", Bim, bim_d.rearrange("(gp g2) n q -> (g2 n) gp q", g2=2), [], ["Bim"])
        s3 = [128, 16, 16]
        tt("dve", bbr, Bre, bc3(c_r, s3, 2), ALU.mult, ["Bre", "c_r"], ["bbr"])
        tt("dve", bt, Bim, bc3(c_i, s3, 2), ALU.mult, ["Bim", "c_i"], ["bt"])
        tt("dve", bbr, bbr, bt, ALU.subtract, ["bbr", "bt"], ["bbr"])
        tt("dve", bbi, Bim, bc3(c_r, s3, 2), ALU.mult, ["Bim", "c_r"], ["bbi"])
        tt("dve", bt, Bre, bc3(c_i, s3, 2), ALU.mult, ["Bre", "c_i", "bt"], ["bt"])
        tt("dve", bbi, bbi, bt, ALU.add, ["bbi", "bt"], ["bbi"])
        Xc = R0.alloc([128, 4, 128], F32)
        Ctr = R0.alloc([128, 16, 16], F32)
        Cti = R0.alloc([128, 16, 16], F32)
        for ri, cd in enumerate((cre_d, cim_d)):
            for hf in range(2):
                for gpl in range(8):
                    for g2 in range(2):
                        g = 2 * (8 * hf + gpl) + g2
                        P.dma("sp" if (gpl % 2 == 0) else "act", Xc[16 * gpl:16 * gpl + 16, 2 * ri + hf, 64 * g2:64 * g2 + 64],
                              cd[g], [], [("Xc", ri, hf, gpl, g2)])
                tr(PS[4 + 2 * ri + hf][:, 0:128], Xc[:, 2 * ri + hf, :], ident_f,
                   [("Xc", ri, hf, a, b) for a in range(8) for b in range(2)] + ["ident_f"], [PSK[4 + 2 * ri + hf]])
                dst = (Ctr, Cti)[ri]
                cp("dve", dst[:, 8 * hf:8 * hf + 8, :].rearrange("p a b -> p (a b)"), PS[4 + 2 * ri + hf][:, 0:128],
                   [PSK[4 + 2 * ri + hf]], [("Ct", ri, hf)])
        CtK = [("Ct", ri, hf) for ri in range(2) for hf in range(2)]
        Gr = RC.alloc([128, 16, NG, 16], BF16)
        Gi = RC.alloc([128, 16, NG, 16], BF16)
        GC = 2
        g1 = W2.alloc([128, GC, NG, 16], F32)
        g2t = W2.alloc([128, GC, NG, 16], F32)
        g3 = W2.alloc([128, GC, NG, 16], F32)
        g4 = W2.alloc([128, GC, NG, 16], F32)
        for c0 in range(0, 16, GC):
            sl = slice(c0, c0 + GC)
            for (E0, n0, nn, Xr_, Xi_, Or_, Oi_, neg, kx) in (
                    (0, 0, NG, bbr, bbi, Gr, Gi, False, ["bbr", "bbi"]),
                    (NG, 0, NH, Ctr, Cti, Hr, nHi, True, CtK)):
                shp4 = [128, GC, nn, 16]
                er = Er[:, sl, E0:E0 + nn].unsqueeze(3).to_broadcast(shp4)
                ei = Ei[:, sl, E0:E0 + nn].unsqueeze(3).to_broadcast(shp4)
                xr = Xr_[:, sl, :].unsqueeze(2).to_broadcast(shp4)
                xi = Xi_[:, sl, :].unsqueeze(2).to_broadcast(shp4)
                a1, a2 = g1[:, :, 0:nn, :], g2t[:, :, 0:nn, :]
                a3, a4 = g3[:, :, 0:nn, :], g4[:, :, 0:nn, :]
                tt("dve", a1, er, xr, ALU.mult, ["Er"] + kx + ["g1"], ["g1"])
                tt("dve", a2, ei, xi, ALU.mult, ["Ei"] + kx + ["g2"], ["g2"])
                tt("dve", Or_[:, sl, :, :], a1, a2, ALU.subtract, ["g1", "g2"], [("GH", E0, c0, 0)])
                tt("pool", a3, er, xi, ALU.mult, ["Er"] + kx + ["g3"], ["g3"])
                tt("pool", a4, ei, xr, ALU.mult, ["Ei"] + kx + ["g4"], ["g4"])
                if neg:
                    fl = lambda a: a.rearrange("p a b c -> p a (b c)")
                    stt(fl(Oi_[:, sl, :, :]), fl(a3), -1.0, fl(a4), ALU.mult, ALU.subtract, ["g3", "g4"], [("GH", E0, c0, 1)])
                else:
                    tt("pool", Oi_[:, sl, :, :], a3, a4, ALU.add, ["g3", "g4"], [("GH", E0, c0, 1)])
        GK = [("GH", 0, c0, i) for c0 in range(0, 16, GC) for i in range(2)]
        HK = [("GH", NG, c0, i) for c0 in range(0, 16, GC) for i in range(2)]
        P.barrier()
        for g in range(32):
            gp, hf = g // 2, g % 2
            hs = slice(64 * hf, 64 * hf + 64)
            pb = 4 + (g % 2)
            for dl in range(MS):
                r0 = (L - 1) - 8 * dl
                o = PS[pb][:, 128 * dl:128 * dl + 128]
                mm(o, Gr[hs, gp, r0:r0 + 8, :].rearrange("p a b -> p (a b)"),
                   Hr[hs, gp, 0:8, :].rearrange("p a b -> p (a b)"), True, False, GK + HK, [PSK[pb]])
                mm(o, Gi[hs, gp, r0:r0 + 8, :].rearrange("p a b -> p (a b)"),
                   nHi[hs, gp, 0:8, :].rearrange("p a b -> p (a b)"), False, True, GK + HK, [PSK[pb]])
            tt("dve", Tt[:, g, :, :].rearrange("p a b -> p (a b)"), PS[pb][:, 0:128 * MS],
               maskT.rearrange("p a b -> p (a b)"), ALU.mult, [PSK[pb], "maskT"], ["Tt"])
            pt = 6 + (g % 2)
            for j in range(MS):
                for ri, Gx in enumerate((Gr, Gi)):
                    c0 = (j * 2 + ri) * 64
                    tr(psb(pt)[:, c0:c0 + 64], Gx[hs, gp, 8 * j:8 * j + 8, :].rearrange("p a b -> p (a b)"),
                       ident_bf[hs, hs], GK + ["ident_bf"], [PSK[pt]])
            cp("act", M1[:, g, :, :].rearrange("p a b -> p (a b)"), psb(pt)[:, 0:128 * MS], [PSK[pt]], ["M1"])
        wst = [RC.alloc([128, 1024], F32) for _ in range(2)]
        wi = 0

        def load_w(dst, src, rows_chunks, ncols, key, scale_col=None):
            nonlocal wi
            for c in range(rows_chunks):
                for n0 in range(0, ncols, 1024):
                    nn = min(1024, ncols - n0)
                    b = wi % 2
                    wi += 1
                    P.dma("sp", wst[b][:, 0:nn], src[128 * c:128 * c + 128, n0:n0 + nn], [], [("wst", b)])
                    eng = "act" if (wi % 2) else "dve"
                    if scale_col is not None:
                        if eng == "act":
                            act(dst[:, c, n0:n0 + nn], wst[b][:, 0:nn], AF.Copy, [("wst", b), "vecT"], [key],
                                scale=scale_col[:, c:c + 1])
                        else:
                            ts("dve", dst[:, c, n0:n0 + nn], wst[b][:, 0:nn], scale_col[:, c:c + 1], None,
                               ALU.mult, None, [("wst", b), "vecT"], [key])
                    else:
                        cp(eng, dst[:, c, n0:n0 + nn], wst[b][:, 0:nn], [("wst", b)], [key])

        load_w(win_bf, win_d, 8, 3072, "win_bf", scale_col=nd)
        load_w(glu_bf, glu_d, 4, 512, "glu_bf")
        W2.reset()
        CH_ = 2048 // L
        PTr = W2.alloc([128, 16, CH_], F32)
        PTi = W2.alloc([128, 16, CH_], F32)
        Rm = W2.alloc([128, 16, CH_], F32)
        pw = RC.alloc([128, 8, 2, 16], F32)
        dA = RC.alloc([128, 16, CH_ // 2], F32)
        dB = RC.alloc([128, 16, CH_ // 2], F32)
        memset("dve", PTr[:, :, 0:1], 1.0, ["PT"])
        memset("dve", PTi[:, :, 0:1], 0.0, ["PT"])
        cp("dve", pw[:, 0, :, :], u1, ["u1"], ["pw"])
        k_ = 0
        while (1 << k_) < CH_:
            m_ = 1 << k_
            if k_ > 0:
                pr, pi_ = pw[:, k_ - 1, 0, :], pw[:, k_ - 1, 1, :]
                tt("dve", dA[:, :, 0], pr, pr, ALU.mult, ["pw", "dA"], ["dA"])
                tt("dve", dB[:, :, 0], pi_, pi_, ALU.mult, ["pw", "dB"], ["dB"])
                tt("dve", pw[:, k_, 0, :], dA[:, :, 0], dB[:, :, 0], ALU.subtract, ["dA", "dB", "pw"], ["pw"])
                tt("dve", dA[:, :, 0], pr, pi_, ALU.mult, ["pw", "dA"], ["dA"])
                ts("dve", pw[:, k_, 1, :], dA[:, :, 0], 2.0, None, ALU.mult, None, ["dA", "pw"], ["pw"])
            shp_ = [128, 16, m_]
            br = pw[:, k_, 0, :].unsqueeze(2).to_broadcast(shp_)
            bi = pw[:, k_, 1, :].unsqueeze(2).to_broadcast(shp_)
            sr, si = PTr[:, :, 0:m_], PTi[:, :, 0:m_]
            a_, b_2 = dA[:, :, 0:m_], dB[:, :, 0:m_]
            tt("dve", a_, sr, br, ALU.mult, ["PT", "pw", "dA"], ["dA"])
            tt("dve", b_2, si, bi, ALU.mult, ["PT", "pw", "dB"], ["dB"])
            tt("dve", PTr[:, :, m_:2 * m_], a_, b_2, ALU.subtract, ["dA", "dB", "PT"], ["PT"])
            tt("dve", a_, sr, bi, ALU.mult, ["PT", "pw", "dA"], ["dA"])
            tt("dve", b_2, si, br, ALU.mult, ["PT", "pw", "dB"], ["dB"])
            tt("dve", PTi[:, :, m_:2 * m_], a_, b_2, ALU.add, ["dA", "dB", "PT"], ["PT"])
            k_ += 1
        cp("dve", Rm, rmag.unsqueeze(2).to_broadcast([128, 16, CH_]), ["rmag"], ["Rm"])
        memset("dve", Rm[:, :, 0:1], 0.0, ["Rm"])
        memset("dve", carry.rearrange("p a b -> p (a b)"), 0.0, ["carry"])
        P.barrier()

        RC.reset()
        xt = [RC.alloc([128, 1024], F32) for _ in range(2)]
        hbf = RC.alloc([128, 1024], BF16)
        hT = RC.alloc([128, 8, 512], BF16)
        uT = RC.alloc([128, 4, 512], BF16)
        zsT = RC.alloc([128, 4, 512], BF16)
        qsq = RC.alloc([128, 512], F32)
        w2_mark = W2.cur
        qns = [W2.alloc([128, 512], F32) for _ in range(2)]
        qtoks = [[W2.alloc([128, 512], BF16) for _ in range(2)] for _ in range(2)]
        qTb = RC.alloc([128, 4, 512], BF16)
        kTb = RC.alloc([128, 4, 512], BF16)
        vtok = [RC.alloc([128, 512], BF16) for _ in range(2)]
        st8 = RC.alloc([128, 8], F32)
        rt1 = RC.alloc([128, 8, 8], F32)
        rt2 = RC.alloc([128, 8, 8], F32)
        ss1 = RC.alloc([128, 4], F32)

        hTs = [hT, RC.alloc([128, 8, 512], BF16)]
        mhalf = RC.alloc([128, 8], F32)
        memset("dve", mhalf, -0.5, ["mhalf"])

        hbfs = [hbf, RC.alloc([128, 1024], BF16)]

        def rms_a(b, i):
            t0 = 512 * b
            xb_ = xt[i % 2]
            xk = ("xt", i % 2)
            hb, hk = hbfs[i % 2], ("hbf", i % 2)
            P.dma("sp", xb_, x_d[t0 + 128 * i:t0 + 128 * i + 128, :], [], [xk])
            act(hb, xb_, AF.Square, [xk], [hk, "ss1"], accum=ss1[:, 0:1])
            ts("dve", ss1[:, 1:2], ss1[:, 0:1], 1.0 / D, EPS, ALU.mult, ALU.add, ["ss1"], ["ss1b"])
            tt("pool", ss1[:, 3:4], ss1[:, 1:2], mhalf[:, 0:1], ALU.pow, ["ss1b", "mhalf"], ["ss1d"])
            act(hb, xb_, AF.Copy, [xk, "ss1d"], [hk], scale=ss1[:, 3:4])

        def rms_b(b, i):
            hb, hk = hbfs[i % 2], ("hbf", i % 2)
            for c in range(8):
                tr(psb(0)[:, 128 * c:128 * c + 128], hb[:, 128 * c:128 * c + 128], ident_bf,
                   [hk, "ident_bf"], [PSK[0]])
            cp("act", hTs[b % 2][:, :, 128 * i:128 * i + 128], psb(0).rearrange("p (a b) -> p a b", a=8),
               [PSK[0]], [("hT", b % 2, i)])

        def rms_sched(b, slot):
            order = {0: [("a", 0)], 1: [("a", 1)], 2: [("b", 0)], 3: [("a", 2)], 4: [("b", 1)], 5: [("a", 3)],
                     6: [("b", 2)], 7: [("b", 3)]}
            for kind, i in order[slot]:
                (rms_a if kind == "a" else rms_b)(b, i)

        for slot in range(8):
            rms_sched(0, slot)
        for b in range(NBLK):
            t0 = 512 * b
            hT = hTs[b % 2]
            hTK = [("hT", b % 2, i) for i in range(4)]
            for fi, f in enumerate(range(8)):
                pb = 1 + (fi % 2)
                for c in range(8):
                    mm(PS[pb][:, :], win_bf[:, c, 128 * f:128 * f + 128], hT[:, c, :], c == 0, c == 7,
                       ["win_bf"] + hTK, [PSK[pb]])
                if f < 4:
                    cp("act", uT[:, f, :], PS[pb][:, :], [PSK[pb]], [("uT", f)])
                    P.dma("act", u_s[f, :, t0:t0 + 512], uT[:, f, :], [("uT", f)], [("u_s", f, b)])
                else:
                    act(zsT[:, f - 4, :], PS[pb][:, :], AF.Silu, [PSK[pb]], [("zsT", f - 4)])
                    P.dma("act", zs_s[f - 4, :, t0:t0 + 512], zsT[:, f - 4, :], [("zsT", f - 4)], [("zs_s", f - 4, b)])
                if b + 1 < NBLK:
                    rms_sched(b + 1, fi)
            def qk_transposes(i):
                ts_ = slice(128 * i, 128 * i + 128)
                for wi_, which in enumerate(("q", "k")):
                    qt_ = qtoks[wi_][i % 2]
                    qk_ = ("qtok", wi_, i % 2)
                    pk3 = ("ps3", wi_)
                    for h in range(4):
                        tr(psb(3)[:, 512 * wi_ + 128 * h:512 * wi_ + 128 * h + 128], qt_[:, 128 * h:128 * h + 128], ident_bf,
                           [qk_, "ident_bf"], [pk3])
                    dstT = qTb if which == "q" else kTb
                    cp("act", dstT[:, :, ts_], psb(3)[:, 512 * wi_:512 * wi_ + 512].rearrange("p (a b) -> p a b", a=4),
                       [pk3], [(which + "Tb", i)])

            for i in range(4):
                ts_ = slice(128 * i, 128 * i + 128)
                for wi_, (which, col0) in enumerate((("q", 1024), ("k", 1536), ("v", 2048), ("za", 2560))):
                    pb = (4 + wi_ + 2 * (i % 2)) if wi_ < 2 else (wi_ - 1)
                    for c in range(8):
                        mm(PS[pb][:, :], hT[:, c, ts_], win_bf[:, c, col0:col0 + 512], c == 0, c == 7,
                           ["win_bf", ("hT", b % 2, i)], [PSK[pb]])
                    if which == "v":
                        vb = vtok[0]
                        cp("act", vb, PS[pb][:, :], [PSK[pb]], [("vtok", 0)])
                        P.dma("act", v_s[t0 + 128 * i:t0 + 128 * i + 128, :], vb, [("vtok", 0)],
                              [("v_s", b, i)])
                        continue
                    if which == "za":
                        vb = vtok[1]
                        act(vb, PS[pb][:, :], AF.Silu, [PSK[pb]], [("vtok", 1)])
                        P.dma("act", za_s[t0 + 128 * i:t0 + 128 * i + 128, :], vb, [("vtok", 1)],
                              [("za_s", b, i)])
                        continue
                    wq = bcq if which == "q" else bck
                    qn = qns[wi_]
                    qnk = "qn%d" % wi_
                    act(qsq, PS[pb][:, :], AF.Square, [PSK[pb]], ["qsq"])
                    P.op("dve", lambda e: e.tensor_reduce(st8, qsq.rearrange("p (a b) -> p a b", a=8), AX.X, ALU.add),
                         ["qsq"], ["st8"])
                    ts("dve", st8, st8, 1.0 / 64, EPS, ALU.mult, ALU.add, ["st8"], ["st8"])
                    tt("pool", st8, st8, mhalf, ALU.pow, ["st8", "mhalf"], ["st8"])
                    q3 = qn.rearrange("p (a b) -> p a b", a=8)
                    tt("dve", q3, PS[pb][:, :].rearrange("p (a b) -> p a b", a=8), bc3(st8, [128, 8, 64], 2),
                       ALU.mult, [PSK[pb], "st8"], [qnk])
                    tt("dve", qn, qn, wq, ALU.mult, [qnk, "bcq", "bck"], [qnk])
                    tile_idx = 4 * b + i
                    cs = cosT[:, tile_idx, :].unsqueeze(1).to_broadcast([128, 8, 8])
                    sn = sinT[:, tile_idx, :].unsqueeze(1).to_broadcast([128, 8, 8])
                    x1, x2 = q3[:, :, 0:8], q3[:, :, 8:16]
                    tt("dve", rt1, x1, cs, ALU.mult, [qnk, "angc"], ["rt1"])
                    tt("dve", rt2, x2, sn, ALU.mult, [qnk, "angs"], ["rt2"])
                    tt("dve", rt1, rt1, rt2, ALU.subtract, ["rt1", "rt2"], ["rt1"])
                    tt("dve", rt2, x2, cs, ALU.mult, [qnk, "angc", "rt2"], ["rt2"])
                    tt("dve", x2, x1, sn, ALU.mult, [qnk, "angs"], [qnk])
                    tt("dve", x2, x2, rt2, ALU.add, [qnk, "rt2"], [qnk])
                    cp("dve", x1, rt1, ["rt1", qnk], [qnk])
                    cp("pool", qtoks[wi_][i % 2], qn, [qnk], [("qtok", wi_, i % 2)])
                if i >= 1:
                    qk_transposes(i - 1)
            qk_transposes(3)
            for h in range(4):
                P.dma("act", qT_s[h, :, t0:t0 + 512], qTb[:, h, :], [("qTb", i) for i in range(4)], [("qT_s", h, b)])
                P.dma("act", kT_s[h, :, t0:t0 + 512], kTb[:, h, :], [("kTb", i) for i in range(4)], [("kT_s", h, b)])
        P.barrier()
        HT = 2048
        CH = HT // L
        SCH = HT // 8
        RA1 = Region(arena, RA.base, 49152)
        RC.reset()
        uTh = RA1.alloc([128, 4, HT], BF16)
        Ub = RA1.alloc([128, 32, SCH], BF16)
        Zt = [RA1.alloc([128, 16, CH], F32) for _ in range(2)]
        Zt += [RC.alloc([128, 16, CH], F32) for _ in range(4)]
        Zr, Zi, Zmr, Zmi, tA, tB = Zt
        zsl = [RC.alloc([128, 4, 512], BF16) for _ in range(2)]
        y2ps = [RC.alloc([128, SCH, 2], F32) for _ in range(2)]
        ge1s = [RC.alloc([128, SCH, 2], F32) for _ in range(2)]
        gsg = RC.alloc([128, 512], F32)
        ysb = [RC.alloc([128, 512], BF16) for _ in range(2)]
        W2.cur = w2_mark
        Xbf = [W2.alloc([128, 16, CH], BF16) for _ in range(2)]
        y2b_h = Ub.rearrange("p a b -> p (a b)").rearrange("p (c t) -> p c t", c=4)
        UK = [("Ub", g) for g in range(32)]
        f3 = lambda a: a.rearrange("p a b -> p (a b)")
        for hh in range(2):
            T0 = HT * hh
            for f in range(4):
                P.dma("sp", uTh[:, f, :], u_s[f, :, T0:T0 + HT], [("u_s", f, b_) for b_ in range(4 * hh, 4 * hh + 4)],
                      [("uTh", f)])
            for g in range(32):
                g8, gl = g // 8, g % 8
                pb = (g // 2) % 2
                for s_ in range(8):
                    j_, e_ = s_ // 2, s_ % 2
                    mm(PS[pb][32 * j_:32 * j_ + 32, SCH * (g % 2):SCH * (g % 2) + SCH],
                       Wsel[:, gl, 112 - 16 * e_:144 - 16 * e_], uTh[:, g8, s_:HT:8], e_ == 0, e_ == 1,
                       ["Wsel", ("uTh", g8)], [PSK[pb]], tp=(0, 32 * j_))
                if g % 2 == 1:
                    cp("act", Ub[:, g - 1:g + 1, :].rearrange("p a b -> p (a b)"), PS[pb][:, :], [PSK[pb]],
                       [("Ub", g - 1), ("Ub", g)])
            for gq in range(4):
                pz = 4 + 2 * (gq % 2)
                for g in range(8 * gq, 8 * gq + 8):
                    gp, hf = g // 2, g % 2
                    hs = slice(64 * hf, 64 * hf + 64)
                    for ri in range(2):
                        for j in range(MS):
                            mm(PS[pz + ri][hs, CH * (gp % 4):CH * (gp % 4) + CH], M1[:, g, j, 64 * ri:64 * ri + 64],
                               Ub[:, g, j:SCH:MS], j == 0, j == MS - 1, ["M1", ("Ub", g)], [PSK[pz + ri]])
                cp("dve", f3(Zr[:, 4 * gq:4 * gq + 4, :]), PS[pz][:, :], [PSK[pz]], [("Zr", gq)])
                cp("act", f3(Zi[:, 4 * gq:4 * gq + 4, :]), PS[pz + 1][:, :], [PSK[pz + 1]], [("Zi", gq)])
            ZrK = [("Zr", q_) for q_ in range(4)]
            ZiK = [("Zi", q_) for q_ in range(4)]
            a0r, a0i = APr[:, 0, :], APi[:, 0, :]
            cr_, ci_ = carry[:, 0, :], carry[:, 1, :]
            t1, t2 = tA[:, :, 0], tB[:, :, 0]
            tt("dve", t1, a0r, cr_, ALU.mult, ["APr", "carry", "tA"], ["tA"])
            tt("dve", t2, a0i, ci_, ALU.mult, ["APi", "carry", "tB"], ["tB"])
            tt("dve", t1, t1, t2, ALU.subtract, ["tA", "tB"], ["tA"])
            tt("dve", Zr[:, :, 0], Zr[:, :, 0], t1, ALU.add, ZrK + ["tA"], ZrK)
            tt("dve", t1, a0r, ci_, ALU.mult, ["APr", "carry", "tA"], ["tA"])
            tt("dve", t2, a0i, cr_, ALU.mult, ["APi", "carry", "tB"], ["tB"])
            tt("dve", t1, t1, t2, ALU.add, ["tA", "tB"], ["tA"])
            tt("dve", Zi[:, :, 0], Zi[:, :, 0], t1, ALU.add, ZiK + ["tA"], ZiK)
            tt("dve", tA, PTr, Zr, ALU.mult, ["PT", "tA"] + ZrK, ["tA"])
            tt("pool", tB, PTi, Zi, ALU.mult, ["PT", "tB"] + ZiK, ["tB"])
            tt("dve", Zmr, tA, tB, ALU.add, ["tA", "tB", "Zmr"], ["Zmr"])
            tt("dve", tA, PTr, Zi, ALU.mult, ["PT", "tA"] + ZiK, ["tA"])
            tt("pool", tB, PTi, Zr, ALU.mult, ["PT", "tB"] + ZrK, ["tB"])
            tt("dve", Zmi, tA, tB, ALU.subtract, ["tA", "tB", "Zmi"], ["Zmi"])
            P.op("dve", lambda e: e.tensor_tensor_scan(f3(Zr), f3(Rm), f3(Zmr), 0.0, ALU.mult, ALU.add),
                 ["Rm", "Zmr"] + ZrK, ZrK)
            P.op("dve", lambda e: e.tensor_tensor_scan(f3(Zi), f3(Rm), f3(Zmi), 0.0, ALU.mult, ALU.add),
                 ["Rm", "Zmi"] + ZiK, ZiK)
            tt("dve", tA, PTr, Zr, ALU.mult, ["PT", "tA"] + ZrK, ["tA"])
            tt("pool", tB, PTi, Zi, ALU.mult, ["PT", "tB"] + ZiK, ["tB"])
            tt("dve", Zmr, tA, tB, ALU.subtract, ["tA", "tB", "Zmr"], ["Zmr"])
            tt("dve", tA, PTr, Zi, ALU.mult, ["PT", "tA"] + ZiK, ["tA"])
            tt("pool", tB, PTi, Zr, ALU.mult, ["PT", "tB"] + ZrK, ["tB"])
            tt("dve", Zmi, tA, tB, ALU.add, ["tA", "tB", "Zmi"], ["Zmi"])
            for ri, (Xs, xk_) in enumerate(((Zmr, "Zmr"), (Zmi, "Zmi"))):
                cp("dve", Xbf[ri][:, :, 1:CH], Xs[:, :, 0:CH - 1], [xk_], [("Xbf", ri)])
                cp("dve", Xbf[ri][:, :, 0], carry[:, ri, :], ["carry", ("Xbf", ri)], [("Xbf", ri)])
            for ri, (Xs, xk_) in enumerate(((Zmr, "Zmr"), (Zmi, "Zmi"))):
                cp("dve", carry[:, ri, :], Xs[:, :, CH - 1], [xk_, ("Xbf", 0), ("Xbf", 1), "carry"], ["carry"])
            for g in range(32):
                gp, hf = g // 2, g % 2
                hs = slice(64 * hf, 64 * hf + 64)
                pb = (g // 2) % 2
                for j in range(MS):
                    o = PS[pb][:, SCH * (g % 2) + j:SCH * (g % 2) + SCH:MS]
                    for jp in range(j + 1):
                        mm(o, Tt[:, g, j - jp, :], Ub[:, g, jp:SCH:MS], jp == 0, False, ["Tt", ("Ub", g)], [PSK[pb]])
                    mm(o, Hr[hs, gp, 8 * j + 1:8 * j + 9, :].rearrange("p a b -> p (a b)"), Xbf[0][hs, gp, :],
                       False, False, HK + [("Xbf", 0)], [PSK[pb]])
                    mm(o, nHi[hs, gp, 8 * j + 1:8 * j + 9, :].rearrange("p a b -> p (a b)"), Xbf[1][hs, gp, :],
                       False, True, HK + [("Xbf", 1)], [PSK[pb]])
                if g % 2 == 1:
                    cp("act", Ub[:, g - 1:g + 1, :].rearrange("p a b -> p (a b)"), PS[pb][:, :], [PSK[pb]],
                       [("Ub", g - 1), ("Ub", g)])
            for ct in range(4):
                CK = [("Ub", 8 * ct + gl) for gl in range(8)]
                for t_ in range(8):
                    pbk = 4 + t_ // 2
                    for gl in range(8):
                        j_, e_ = gl // 2, gl % 2
                        mm(PS[pbk][32 * j_:32 * j_ + 32, SCH * (t_ % 2):SCH * (t_ % 2) + SCH],
                           Wsel[:, t_, 112 - 16 * e_:144 - 16 * e_], Ub[:, 8 * ct + gl, :], e_ == 0, e_ == 1,
                           ["Wsel", ("Ub", 8 * ct + gl)], [PSK[pbk]], tp=(0, 32 * j_))
                for tq in range(4):
                    pbk = 4 + tq
                    y2p, ge1 = y2ps[tq % 2], ge1s[tq % 2]
                    yk, gk = ("y2p", tq % 2), ("ge1", tq % 2)
                    uview = uTh[:, ct, :].rearrange("p (a b) -> p a b", b=8)[:, :, 2 * tq:2 * tq + 2]
                    stt(y2p, uview, dsk[:, ct:ct + 1], PS[pbk][:, :].rearrange("p (b a) -> p a b", b=2),
                        ALU.mult, ALU.add, [("uTh", ct), "vecT", PSK[pbk]], [yk])
                    yf, gf = f3(y2p), f3(ge1)
                    tt("pool", gf, yf, yf, ALU.mult, [yk], [gk])
                    ts("dve", gf, gf, 0.044715, 1.0, ALU.mult, ALU.add, [gk], [gk])
                    tt("dve", gf, gf, yf, ALU.mult, [gk, yk], [gk])
                    act(gf, gf, AF.Sigmoid, [gk], [gk], scale=1.5957691216057308)
                    tt("dve", y2b_h[:, ct, :].rearrange("p (a b) -> p a b", b=8)[:, :, 2 * tq:2 * tq + 2], y2p, ge1,
                       ALU.mult, [yk, gk] + CK, CK)
            for bi in range(4):
                bg = 4 * hh + bi
                tb0 = 512 * bi
                zl = zsl[bi % 2]
                for f in range(4):
                    P.dma("sp", zl[:, f, :], zs_s[f, :, T0 + tb0:T0 + tb0 + 512], [("zs_s", f, bg)], [("zsl", bi % 2, f)])
                for fo in range(4):
                    pb = fo % 2
                    for ci in range(4):
                        mm(PS[pb][:, :], glu_bf[:, ci, 128 * fo:128 * fo + 128], y2b_h[:, ci, tb0:tb0 + 512], ci == 0, ci == 3,
                           ["glu_bf"] + UK, [PSK[pb]])
                    act(gsg, PS[pb][:, :], AF.Sigmoid, [PSK[pb]], ["gsg"], bias=glb[:, fo:fo + 1])
                    tt("dve", gsg, gsg, y2b_h[:, fo, tb0:tb0 + 512], ALU.mult, ["gsg"] + UK, ["gsg"])
                    yo = ysb[fo % 2]
                    tt("dve", yo, gsg, zl[:, fo, :], ALU.mult, ["gsg", ("zsl", bi % 2, fo), ("ysb", fo % 2)], [("ysb", fo % 2)])
                    P.dma("sp", ys_s[fo, :, T0 + tb0:T0 + tb0 + 512], yo, [("ysb", fo % 2)], [("ys_s", fo, bg)])
        P.barrier()
        fin_keys = []
        if debug:
            RC.reset()
            dtile = RC.alloc([128, 4096], BF16)
            for nm, src in (("ys", ys_s), ("qT", qT_s), ("kT", kT_s)):
                for f in range(4):
                    P.dma("sp", dtile, src[f], [], ["dtile"])
                    P.dma("sp", dbg[nm][f], dtile, ["dtile"], [("dbg", nm, f)])
                    fin_keys.append(("dbg", nm, f))
            for nm, src in (("v", v_s), ("za", za_s)):
                for i in range(32):
                    P.dma("sp", dtile[:, 0:512], src[128 * i:128 * i + 128, :], [], ["dtile"])
                    P.dma("sp", dbg[nm][128 * i:128 * i + 128, :], dtile[:, 0:512], ["dtile"], [("dbg", nm, i)])
                    fin_keys.append(("dbg", nm, i))
            P.barrier()

        RAB = Region(arena, RA.base, RA.size + RB.size)
        RC.reset()
        kT_res = RAB.alloc([128, 4, S], BF16)
        v_res = RAB.alloc([128, 32, 4, 129], BF16)
        qTl = [RAB.alloc([128, 4, 512], BF16) for _ in range(2)]
        zatok = RAB.alloc([128, 4, 512], BF16)
        ysl = RAB.alloc([128, 4, 512], BF16)
        PTt = [RAB.alloc([128, 2, 512], BF16) for _ in range(4)]
        yaT = RAB.alloc([128, 4, 512], BF16)
        rs = RC.alloc([128, 8], F32)
        rsn = RC.alloc([128, 4], F32)
        o_all = RC.alloc([128, 4, 512], F32)
        sqt = RC.alloc([128, 512], F32)
        ss4 = RC.alloc([128, 4], F32)
        yatoks = [RC.alloc([128, 512], BF16) for _ in range(2)]
        xl = [RC.alloc([128, 1024], F32) for _ in range(2)]
        xnews = [RC.alloc([128, 1024], F32) for _ in range(2)]
        xnbs = [RC.alloc([128, 1024], BF16) for _ in range(2)]
        xnT = RC.alloc([128, 8, 128], BF16)
        pl = [RC.alloc([128, 256], F32) for _ in range(2)]
        pT = RC.alloc([128, 2, 128], BF16)
        gate = RC.alloc([128, 1024], F32)
        wst = [RC.alloc([128, 1024], F32) for _ in range(2)]
        tri = cmask[:, 0, 0:128]
        mhalf4 = RC.alloc([128, 4], F32)
        memset("dve", mhalf4, -0.5, ["mhalf4"])
        memset("dve", v_res[:, :, :, 128:129], 1.0, ["v_ones"])

        def load_kv(blk):
            for h in range(4):
                P.dma("sp", kT_res[:, h, 512 * blk:512 * blk + 512], kT_s[h, :, 512 * blk:512 * blk + 512],
                      [("kT_s", h, blk)], [("kT_res", h, blk)])
            for i in range(4 * blk, 4 * blk + 4):
                P.dma("sp", v_res[:, i, :, 0:128], v_s[128 * i:128 * i + 128, :].rearrange("p (a b) -> p a b", a=4),
                      [("v_s", i // 4, i % 4)], [("v_res", i)])

        def load_q(b):
            for h in range(4):
                P.dma("sp", qTl[b % 2][:, h, :], qT_s[h, :, 512 * b:512 * b + 512], [("qT_s", h, b)], [("qTl", b % 2, h)])

        def load_x(b, i):
            tok = slice(512 * b + 128 * i, 512 * b + 128 * i + 128)
            P.dma("sp", xl[i % 2], x_d[tok, :], [], [("xl", i % 2)])
            P.dma("sp", pl[i % 2], p_d[tok, :], [], [("pl", i % 2)])

        load_q(0)
        load_kv(0)
        load_kv(1)
        load_w(wout_bf, wout_d, 8, 1024, "wout_bf")
        load_kv(2)
        load_w(pg_bf, pg_d, 8, 1024, "pg_bf")
        load_kv(3)
        load_w(pp_bf, pp_d, 2, 1024, "pp_bf")
        for blk_ in range(4, NBLK):
            load_kv(blk_)
        OBk = [PS[4], PS[5]]
        zatoks = [zatok, RAB.alloc([128, 4, 512], BF16)]
        ysls = [ysl, RC.alloc([128, 4, 512], BF16)]
        pti = 0

        def make_tail_units(b):
            t0 = 512 * b
            zat, ysl_ = zatoks[b % 2], ysls[b % 2]

            def stage_a(i):
                ts_ = slice(128 * i, 128 * i + 128)
                oK = [("o_all", i, h) for h in range(4)]
                oq = o_all[:, i, :]
                act(sqt, oq, AF.Square, oK, ["sqt"])
                P.op("dve", lambda e: e.tensor_reduce(ss4, sqt.rearrange("p (a b) -> p a b", a=4), AX.X, ALU.add),
                     ["sqt"], ["ss4"])
                ts("dve", ss4, ss4, 1.0 / 128, EPS, ALU.mult, ALU.add, ["ss4"], ["ss4"])
                tt("pool", ss4, ss4, mhalf4, ALU.pow, ["ss4", "mhalf4"], ["ss4"])
                o3 = oq.rearrange("p (a b) -> p a b", a=4)
                tt("dve", o3, o3, bc3(ss4, [128, 4, 128], 2), ALU.mult, oK + ["ss4"], oK)
                tt("dve", o3, o3, bcsw.unsqueeze(1).to_broadcast([128, 4, 128]), ALU.mult, oK + ["bcsw"], oK)
                tt("dve", yatoks[i % 2], oq, zat[:, i, :], ALU.mult, oK + [("zatok", b % 2, i)], [("yatok", i % 2)])
                yield

            def stage_a2(i):
                ts_ = slice(128 * i, 128 * i + 128)
                for h in range(4):
                    tr(psb(7)[:, 128 * h:128 * h + 128], yatoks[i % 2][:, 128 * h:128 * h + 128], ident_bf,
                       [("yatok", i % 2), "ident_bf"], [PSK[7]])
                    if h % 2 == 1:
                        yield
                cp("dve", yaT[:, :, ts_], psb(7)[:, 0:512].rearrange("p (a b) -> p a b", a=4), [PSK[7]], [("yaT", i)])
                yield

            def stage_b(i):
                ts_ = slice(128 * i, 128 * i + 128)
                xb_ = xl[i % 2]
                xk = ("xl", i % 2)
                xn = xnews[i % 2]
                for hf in range(2):
                    tb = (3, 6)[hf]
                    for c in range(8):
                        src = ysl_ if c < 4 else yaT
                        kk = ("ysl", b % 2, c) if c < 4 else ("yaT", i)
                        mm(PS[tb][:, :], src[:, c % 4, ts_], wout_bf[:, c, 512 * hf:512 * hf + 512], c == 0, c == 7,
                           [kk, "wout_bf"], [PSK[tb]])
                        if c % 2 == 1:
                            yield
                    tt("dve", xn[:, 512 * hf:512 * hf + 512], PS[tb][:, :], xb_[:, 512 * hf:512 * hf + 512],
                       ALU.add, [PSK[tb], xk], [("xnew", i % 2, hf)])
                    cp("dve", xnbs[i % 2][:, 512 * hf:512 * hf + 512], xn[:, 512 * hf:512 * hf + 512],
                       [("xnew", i % 2, hf)], [("xnb", i % 2, hf)])
                    yield

            def stage_c1(i):
                plk = ("pl", i % 2)
                for c in range(8):
                    tr(psb(7)[:, 128 * c:128 * c + 128], xnbs[i % 2][:, 128 * c:128 * c + 128], ident_bf,
                       [("xnb", i % 2, c // 4), "ident_bf"], [PSK[7]])
                    if c % 2 == 1:
                        yield
                cp("dve", xnT.rearrange("p a b -> p (a b)"), psb(7)[:, :], [PSK[7]], ["xnT"])
                for c in range(2):
                    tr(PS[6][:, 128 * c:128 * c + 128], pl[i % 2][:, 128 * c:128 * c + 128], ident_f, [plk, "ident_f"],
                       [PSK[6]])
                cp("dve", pT.rearrange("p a b -> p (a b)"), PS[6][:, 0:256], [PSK[6]], ["pT"])
                yield

            def stage_c2(i, hf):
                xb_ = xl[i % 2]
                xk = ("xl", i % 2)
                xn = xnews[i % 2]
                hsl = slice(512 * hf, 512 * hf + 512)
                tb = (3, 6)[hf]
                pk_ = PSK[tb]
                for c in range(8):
                    mm(PS[tb][:, :], xnT[:, c, :], pg_bf[:, c, hsl], c == 0, c == 7, ["xnT", "pg_bf"], [pk_])
                    if c % 2 == 1:
                        yield
                act(gate[:, hsl], PS[tb][:, :], AF.Tanh, [pk_], [("gate", hf)], scale=0.5)
                for c in range(2):
                    mm(PS[tb][:, :], pT[:, c, :], pp_bf[:, c, hsl], c == 0, c == 1, ["pT", "pp_bf"], [pk_])
                stt(gate[:, hsl], gate[:, hsl], 1.0, PS[tb][:, :], ALU.add, ALU.mult, [("gate", hf), pk_], [("gate", hf)])
                stt(xb_[:, hsl], gate[:, hsl], 0.5, xn[:, hsl], ALU.mult, ALU.add, [("gate", hf), ("xnew", i % 2, hf), xk], [xk])
                yield

            def store(i):
                tok = slice(t0 + 128 * i, t0 + 128 * i + 128)
                P.dma("sp", out_d[tok, :], xl[i % 2], [("xl", i % 2)], [("out", b, i)])
                fin_keys.append(("out", b, i))

            def gen():
                load_x(b, 0)
                load_x(b, 1)
                yield from stage_a(0)
                yield from stage_a(1)
                yield from stage_a2(0)
                yield from stage_a(2)
                yield from stage_a2(1)
                yield from stage_a(3)
                yield from stage_a2(2)
                yield from stage_a2(3)

                def fin(i):
                    yield from stage_c1(i)
                    yield from stage_c2(i, 0)
                    yield from stage_c2(i, 1)
                    store(i)
                    if i + 2 < 4:
                        load_x(b, i + 2)
                    yield

                yield from stage_b(0)
                yield from stage_b(1)
                yield from fin(0)
                yield from stage_b(2)
                yield from fin(1)
                yield from stage_b(3)
                yield from fin(2)
                yield from fin(3)

            return gen(), 112

        qzall = wst[1].bitcast(BF16)
        qz = [[qzall[:, 512 * (2 * c_ + hb_):512 * (2 * c_ + hb_) + 512] for hb_ in range(2)] for c_ in range(2)]
        for c_ in range(2):
            for hb_ in range(2):
                memset("pool", qz[c_][hb_], 0.0, [("qz", c_, hb_), ("wst", 1)])
        pending, pend_left = None, 0

        def advance(n):
            nonlocal pending, pend_left
            for _ in range(n):
                if pending is None:
                    return
                try:
                    next(pending)
                    pend_left = max(pend_left - 1, 1)
                except StopIteration:
                    pending, pend_left = None, 0

        for b in range(NBLK):
            t0 = 512 * b
            qb = qTl[b % 2]
            if b + 1 < NBLK:
                load_q(b + 1)
            for i in range(4):
                P.dma("sp", zatoks[b % 2][:, i, :], za_s[t0 + 128 * i:t0 + 128 * i + 128, :], [("za_s", b, i)],
                      [("zatok", b % 2, i)])
            for h in range(4):
                P.dma("sp", ysls[b % 2][:, h, :], ys_s[h, :, t0:t0 + 512], [("ys_s", h, b)], [("ysl", b % 2, h)])
            nkt = 4 * (b + 1)
            iters = []
            for h in range(4):
                for qh in range(2):
                    for kt in range(nkt):
                        j = kt - 4 * b
                        if j >= 0 and 128 * j >= 256 * (qh + 1):
                            continue
                        iters.append((h, kt, qh))
            n_it = len(iters)

            qz_done = set()

            def scores(it):
                h, kt, qh = it
                buf = scores.cnt % 3
                scores.cnt += 1
                j = kt - 4 * b
                q0 = max(128 * max(j, 0) - 256 * qh, 0)
                ks = slice(128 * kt, 128 * kt + 128)
                if h not in qz_done:
                    qz_done.add(h)
                    for c in range(2):
                        hs = slice(64 * c, 64 * c + 64)
                        cp("pool", qz[c][h % 2][hs, :], qb[hs, h, :], [("qTl", b % 2, h)], [("qz", c, h % 2)])
                for c in range(2):
                    mm(PS[buf][:, 256 * c + q0:256 * c + 256], kT_res[:, h, ks],
                       qz[c][h % 2][:, 256 * qh + q0:256 * qh + 256], True, True,
                       [("kT_res", h, kt // 4), ("qz", c, h % 2)], [("SC", buf)])
                return buf, q0

            scores.cnt = 0
            LA = 2
            sq_ = [scores(iters[k_]) for k_ in range(min(LA, n_it))]
            started = {}
            for idx, it in enumerate(iters):
                h, kt, qh = it
                if idx + LA < n_it:
                    sq_.append(scores(iters[idx + LA]))
                buf, q0 = sq_.pop(0)
                j = kt - 4 * b
                pt_i = pti % 4
                pti += 1
                pt = PTt[pt_i]
                pk = ("PT", pt_i)
                act(pt[:, :, q0:256], PS[buf].rearrange("p (c q) -> p c q", c=2)[:, :, q0:256], AF.Exp,
                    [("SC", buf)], [pk])
                if j >= 0 and 128 * j >= 256 * qh:
                    tt("dve", pt[:, :, q0:q0 + 128], pt[:, :, q0:q0 + 128],
                       tri.unsqueeze(1).to_broadcast([128, 2, 128]), ALU.mult, [pk, "cmask"], [pk])
                for qt in range(max(j, 2 * qh), 2 * qh + 2):
                    for c in range(2):
                        r = 2 * (qt - 2 * qh) + c
                        bank, col0 = r // 3, (r % 3) * 129
                        st_ = (h, qh, bank) not in started
                        started[(h, qh, bank)] = True
                        ql = 128 * (qt - 2 * qh)
                        lhs = pt[:, c, ql:ql + 128]
                        o_ap = OBk[bank][:, col0:col0 + 129]
                        rhs_ = v_res[:, kt, h, :]
                        P.op("pe", lambda e, o_ap=o_ap, lhs=lhs, rhs_=rhs_, st_=st_, sp_=False:
                             e.matmul(o_ap, lhsT=lhs, rhs=rhs_, start=st_, stop=sp_, skip_group_check=True),
                             [pk, ("v_res", kt), "v_ones"], [("OB", bank)])
                last_of_head = (idx + 1 == n_it) or (iters[idx + 1][0] != h) or (iters[idx + 1][2] != qh)
                if last_of_head:
                    for bank in range(2):
                        nreg = 3 if bank < 1 else 1
                        src = OBk[bank][:, 128:128 + 129 * (nreg - 1) + 1:129]
                        dst = rs[:, 3 * bank:3 * bank + nreg]
                        P.op("dve", lambda e, dst=dst, src=src: e.reciprocal(dst, src), [("OB", bank)], ["rs"])
                    ts("dve", rsn[:, 0:2], rs[:, 1:4:2], lamv[:, 1:2], None, ALU.mult, None, ["rs", "lamv"], ["rsn"])
                    for qt in range(2 * qh, 2 * qh + 2):
                        r0_, r1_ = 2 * (qt - 2 * qh), 2 * (qt - 2 * qh) + 1
                        oa = o_all[:, qt, 128 * h:128 * h + 128]
                        ts("dve", oa, OBk[r0_ // 3][:, (r0_ % 3) * 129:(r0_ % 3) * 129 + 128], rs[:, r0_:r0_ + 1], None,
                           ALU.mult, None, [("OB", r0_ // 3), "rs"], [("o_all", qt, h)])
                        stt(oa, OBk[r1_ // 3][:, (r1_ % 3) * 129:(r1_ % 3) * 129 + 128], rsn[:, qt - 2 * qh:qt - 2 * qh + 1], oa,
                            ALU.mult, ALU.add, [("OB", r1_ // 3), "rsn", ("o_all", qt, h)], [("o_all", qt, h)])
                if pending is not None:
                    advance(-(-pend_left // max(n_it - idx - 8, 1)))
            advance(10 ** 6)
            pending, pend_left = make_tail_units(b)
        advance(10 ** 6)
        P.emit(final_keys=fin_keys)
    return nc


_NC_CACHE = {}


def _core_inputs(b, x, p, positions, norm_w, w_in, ssm_lambda_re, ssm_lambda_im, ssm_log_dt,
                 ssm_b_re, ssm_b_im, ssm_c_re, ssm_c_im, ssm_d, glu_w, glu_b,
                 q_norm_w, k_norm_w, lambda_q1, lambda_k1, lambda_q2, lambda_k2,
                 subln_w, w_out, ple_w_proj, ple_w_gate):
    f = lambda a: np.ascontiguousarray(np.asarray(a, dtype=np.float32))
    vecs = np.concatenate([f(norm_w[0]).reshape(8, 128), f(ssm_d[0]).reshape(4, 128),
                           f(glu_b[0]).reshape(4, 128), f(subln_w[0]).reshape(1, 128)], axis=0)
    rows = np.concatenate([f(q_norm_w[0]), f(k_norm_w[0]), f(lambda_q1[0]), f(lambda_k1[0]),
                           f(lambda_q2[0]), f(lambda_k2[0]), f(subln_w[0])]).reshape(1, 512)
    lam = np.stack([f(ssm_lambda_re[0]).reshape(16, 128), f(ssm_lambda_im[0]).reshape(16, 128)], axis=1)
    return {
        "x": f(x[b]), "p": f(p[0, b]),
        "pos": np.ascontiguousarray(np.asarray(positions[b], dtype=np.int32).reshape(32, 128)),
        "vecs": np.ascontiguousarray(vecs), "rows": np.ascontiguousarray(rows),
        "w_in": f(w_in[0]), "lam": np.ascontiguousarray(lam), "log_dt": f(ssm_log_dt[0]).reshape(16, 2),
        "b_re": f(ssm_b_re[0]), "b_im": f(ssm_b_im[0]), "c_re": f(ssm_c_re[0]), "c_im": f(ssm_c_im[0]),
        "glu_w": f(glu_w[0]), "w_out": f(w_out[0]), "ple_w_proj": f(ple_w_proj[0]), "ple_w_gate": f(ple_w_gate[0]),
    }


def kernel(**inputs):
    if "nc" not in _NC_CACHE:
        _NC_CACHE["nc"] = build_program(DEBUG)
    nc = _NC_CACHE["nc"]
    in_maps = [_core_inputs(b, **inputs) for b in range(8)]
    res = run_bass_kernel_spmd(nc, in_maps, core_ids=list(range(8)))
    out = np.stack([np.asarray(r["out"], dtype=np.float32) for r in res.results], axis=0)
    return out
```
